# Optimizing a Trainium2 kernel written in Bass

```python
import math
import jax, jax.numpy as jnp
from jax import lax
import numpy as np

D_MODEL = 2048
BATCH = 4
SEQ = 2048
DEPTH = 4

N_DIFF_HEADS = 8
DIFF_HEAD_DIM = 64
DIFF_WIDTH = N_DIFF_HEADS * 2 * DIFF_HEAD_DIM
N_GDN_HEADS = 8
GDN_HEAD_DIM = 128
GDN_WIDTH = N_GDN_HEADS * GDN_HEAD_DIM
CONV_WIDTH = 5
CHUNK = 64
Q_BLOCK = 128
FFN_HIDDEN = ((8 * D_MODEL // 3 + 255) // 256) * 256
ROPE_THETA = 10000.0
NORM_EPS = 1e-6
N_MOD = 6
IN_WIDTHS = (DIFF_WIDTH, DIFF_WIDTH, DIFF_WIDTH,
             GDN_WIDTH, GDN_WIDTH, GDN_WIDTH, GDN_WIDTH,
             2 * N_GDN_HEADS, 2 * N_GDN_HEADS,
             2 * D_MODEL)
IN_COLS = 3 * DIFF_WIDTH + 4 * GDN_WIDTH + 4 * N_GDN_HEADS + 2 * D_MODEL

kernel_name = 'hybrid_diffattn_gdn_encoder'


def _split(t, widths):
    out, start = [], 0
    for w in widths:
        out.append(t[..., start:start + w])
        start += w
    return out


def rms_norm(x, w):
    xf = x.astype(jnp.float32)
    y = xf * lax.rsqrt(jnp.mean(xf * xf, axis=-1, keepdims=True) + NORM_EPS)
    return (y * w.astype(jnp.float32)).astype(x.dtype)


def l2_norm(xf):
    return xf * lax.rsqrt(jnp.sum(xf * xf, axis=-1, keepdims=True) + NORM_EPS)


def rope_tables(positions):
    half = DIFF_HEAD_DIM // 2
    inv_freq = ROPE_THETA ** (-jnp.arange(half, dtype=jnp.float32) * 2.0 / DIFF_HEAD_DIM)
    ang = positions.astype(jnp.float32)[..., None] * inv_freq
    return jnp.cos(ang)[:, :, None, None, :], jnp.sin(ang)[:, :, None, None, :]


def apply_rope(x, cos, sin):
    half = DIFF_HEAD_DIM // 2
    cos = cos.astype(x.dtype)
    sin = sin.astype(x.dtype)
    x1, x2 = x[..., :half], x[..., half:]
    return jnp.concatenate([x1 * cos - x2 * sin, x2 * cos + x1 * sin], axis=-1)


def diff_attention(q, k, v, cos, sin, qn_w, kn_w, lam_vecs, subln_w, lambda_init):
    B, S, H, _, Dh = q.shape
    q = apply_rope(rms_norm(q, qn_w), cos, sin)
    k = apply_rope(rms_norm(k, kn_w), cos, sin)
    lv = lam_vecs.astype(jnp.float32)
    lam = jnp.exp(jnp.sum(lv[0] * lv[1])) - jnp.exp(jnp.sum(lv[2] * lv[3])) + lambda_init
    scale = DIFF_HEAD_DIM ** -0.5
    qb = q.reshape(B, S // Q_BLOCK, Q_BLOCK, H, 2, Dh).transpose(1, 0, 3, 4, 2, 5)
    kt = k.transpose(0, 2, 3, 1, 4)
    vt = v.transpose(0, 2, 1, 3)

    def block(q_blk):
        s = jnp.einsum('bhmqd,bhmkd->bhmqk', q_blk, kt).astype(jnp.float32) * scale
        p = jax.nn.softmax(s, axis=-1)
        p_diff = p[:, :, 0] - lam * p[:, :, 1]
        return jnp.einsum('bhqk,bhkv->bhqv', p_diff.astype(vt.dtype), vt)

    o = lax.map(block, qb)
    o = o.transpose(1, 0, 3, 2, 4).reshape(B, S, H, 2 * Dh)
    o = rms_norm(o, subln_w) * (1.0 - lambda_init)
    return o.reshape(B, S, H * 2 * Dh)


def short_conv(x, w):
    C = x.shape[-1]
    pad = CONV_WIDTH // 2
    return lax.conv_general_dilated(x, w[:, None, :].astype(x.dtype), window_strides=(1,),
                                    padding=[(pad, pad)], dimension_numbers=('NWC', 'WIO', 'NWC'),
                                    feature_group_count=C)


def gated_delta_chunked(q, k, v, g, beta):
    B, S, H, K = q.shape
    V = v.shape[-1]
    N = S // CHUNK

    def to_chunks(t):
        return jnp.swapaxes(t.reshape((B, N, CHUNK) + t.shape[2:]), 2, 3)

    qc, kc, vc, gc, bc = to_chunks(q), to_chunks(k), to_chunks(v), to_chunks(g), to_chunks(beta)
    g_cum = jnp.cumsum(gc, axis=-1)
    tri = jnp.tril(jnp.ones((CHUNK, CHUNK), dtype=bool))
    strict = jnp.tril(jnp.ones((CHUNK, CHUNK), dtype=bool), -1)
    diff = g_cum[..., :, None] - g_cum[..., None, :]
    decay = jnp.where(tri, jnp.exp(jnp.where(tri, diff, 0.0)), 0.0)
    k_beta = kc * bc[..., None]
    v_beta = vc * bc[..., None]
    L = jnp.where(strict, jnp.einsum('bnhik,bnhjk->bnhij', k_beta, kc) * decay, 0.0)
    A = jnp.eye(CHUNK, dtype=jnp.float32) + L
    u = lax.linalg.triangular_solve(A, v_beta, left_side=True, lower=True, unit_diagonal=True)
    w = lax.linalg.triangular_solve(A, k_beta * jnp.exp(g_cum)[..., None], left_side=True,
                                    lower=True, unit_diagonal=True)
    attn = jnp.einsum('bnhik,bnhjk->bnhij', qc, kc) * decay
    q_dec = qc * jnp.exp(g_cum)[..., None]
    k_tail = kc * jnp.exp(g_cum[..., -1:] - g_cum)[..., None]
    chunk_decay = jnp.exp(g_cum[..., -1])

    def step(state, inp):
        u_n, w_n, qd_n, at_n, kt_n, cd_n = inp
        v_new = u_n - jnp.einsum('bhck,bhkv->bhcv', w_n, state)
        o = jnp.einsum('bhck,bhkv->bhcv', qd_n, state) + jnp.einsum('bhij,bhjv->bhiv', at_n, v_new)
        state = state * cd_n[..., None, None] + jnp.einsum('bhck,bhcv->bhkv', kt_n, v_new)
        return state, o

    xs = tuple(jnp.moveaxis(t, 1, 0) for t in (u, w, q_dec, attn, k_tail, chunk_decay))
    s0 = jnp.zeros((B, H, K, V), jnp.float32)
    _, o = lax.scan(step, s0, xs)
    return o.transpose(1, 0, 3, 2, 4).reshape(B, S, H, V)


def gated_deltanet(q, k, v, z, b, a, conv_w, a_log, dt_bias, norm_w):
    B, S, _ = q.shape
    H, Dh = N_GDN_HEADS, GDN_HEAD_DIM
    qkv = jax.nn.silu(short_conv(jnp.concatenate([q, k, v], axis=-1), conv_w))
    q, k, v = jnp.split(qkv, 3, axis=-1)
    qf = l2_norm(q.reshape(B, S, H, Dh).astype(jnp.float32)) * (Dh ** -0.5)
    kf = l2_norm(k.reshape(B, S, H, Dh).astype(jnp.float32))
    vf = v.reshape(B, S, H, Dh).astype(jnp.float32)
    beta = jax.nn.sigmoid(b.astype(jnp.float32))
    g = -jnp.exp(a_log.astype(jnp.float32)) * jax.nn.softplus(a.astype(jnp.float32) + dt_bias.astype(jnp.float32))
    o_fwd = gated_delta_chunked(qf, kf, vf, g[:, :, 0], beta[:, :, 0])
    flip = lambda t: jnp.flip(t, axis=1)
    o_bwd = flip(gated_delta_chunked(flip(qf), flip(kf), flip(vf), flip(g[:, :, 1]), flip(beta[:, :, 1])))
    o = rms_norm(o_fwd + o_bwd, norm_w) * jax.nn.silu(z.reshape(B, S, H, Dh).astype(jnp.float32))
    return o.reshape(B, S, H * Dh).astype(z.dtype)


def hybrid_mixer(h, cos, sin, lambda_init, w_in, qn_w, kn_w, lam_vecs, subln_w, conv_w, a_log, dt_bias,
                 gdn_norm_w, w_branch_diff, w_branch_gdn, w_out):
    B, S, _ = h.shape
    proj = h @ w_in
    dq, dk, dv, gq, gk, gv, gz, gb, ga, gates = _split(proj, IN_WIDTHS)
    qk_shape = (B, S, N_DIFF_HEADS, 2, DIFF_HEAD_DIM)
    y_diff = diff_attention(dq.reshape(qk_shape), dk.reshape(qk_shape),
                            dv.reshape(B, S, N_DIFF_HEADS, 2 * DIFF_HEAD_DIM), cos, sin,
                            qn_w, kn_w, lam_vecs, subln_w, lambda_init)
    y_gdn = gated_deltanet(gq, gk, gv, gz, gb.reshape(B, S, 2, N_GDN_HEADS), ga.reshape(B, S, 2, N_GDN_HEADS),
                           conv_w, a_log, dt_bias, gdn_norm_w)
    g_diff, g_gdn = jnp.split(jax.nn.sigmoid(gates), 2, axis=-1)
    merged = g_diff * (y_diff @ w_branch_diff) + g_gdn * (y_gdn @ w_branch_gdn)
    return merged @ w_out


def swiglu(h, w_up, w_down):
    gate, up = jnp.split(h @ w_up, 2, axis=-1)
    return (jax.nn.silu(gate) * up) @ w_down


def setup_inputs(seed: int = 0) -> dict:
    key = jax.random.key(seed)
    ks = jax.random.split(key, 24)
    f32 = jnp.float32

    def nrm(k, shape, scale):
        return jax.random.normal(k, shape, f32) * scale

    x = nrm(ks[0], (BATCH, SEQ, D_MODEL), 1.0)
    c = nrm(ks[1], (BATCH, D_MODEL), 1.0)
    offsets = jax.random.randint(ks[2], (BATCH, 1), 0, 4096, dtype=jnp.int32)
    positions = offsets + jnp.arange(SEQ, dtype=jnp.int32)[None, :]
    ada_w = nrm(ks[3], (DEPTH, D_MODEL, N_MOD * D_MODEL), 0.5 * D_MODEL ** -0.5)
    ada_b = nrm(ks[4], (DEPTH, N_MOD * D_MODEL), 0.01)
    norm_mix_w = 1.0 + nrm(ks[5], (DEPTH, D_MODEL), 0.02)
    norm_ffn_w = 1.0 + nrm(ks[6], (DEPTH, D_MODEL), 0.02)
    w_in = nrm(ks[7], (DEPTH, D_MODEL, IN_COLS), D_MODEL ** -0.5)
    diff_qn_w = 1.0 + nrm(ks[8], (DEPTH, DIFF_HEAD_DIM), 0.02)
    diff_kn_w = 1.0 + nrm(ks[9], (DEPTH, DIFF_HEAD_DIM), 0.02)
    diff_lambda = nrm(ks[10], (DEPTH, 4, DIFF_HEAD_DIM), 0.1)
    diff_subln_w = 1.0 + nrm(ks[11], (DEPTH, 2 * DIFF_HEAD_DIM), 0.02)
    gdn_conv_w = nrm(ks[12], (DEPTH, CONV_WIDTH, 3 * GDN_WIDTH), CONV_WIDTH ** -0.5)
    gdn_a_log = jnp.log(jax.random.uniform(ks[13], (DEPTH, 2, N_GDN_HEADS), f32, 1.0, 16.0))
    u = jax.random.uniform(ks[14], (DEPTH, 2, N_GDN_HEADS), f32)
    dt = jnp.exp(u * (math.log(0.1) - math.log(0.001)) + math.log(0.001))
    gdn_dt_bias = dt + jnp.log(-jnp.expm1(-dt))
    gdn_norm_w = 1.0 + nrm(ks[15], (DEPTH, GDN_HEAD_DIM), 0.02)
    w_branch_diff = nrm(ks[16], (DEPTH, DIFF_WIDTH, D_MODEL), DIFF_WIDTH ** -0.5)
    w_branch_gdn = nrm(ks[17], (DEPTH, GDN_WIDTH, D_MODEL), GDN_WIDTH ** -0.5)
    w_out = nrm(ks[18], (DEPTH, D_MODEL, D_MODEL), D_MODEL ** -0.5)
    ffn_w_up = nrm(ks[19], (DEPTH, D_MODEL, 2 * FFN_HIDDEN), D_MODEL ** -0.5)
    ffn_w_down = nrm(ks[20], (DEPTH, FFN_HIDDEN, D_MODEL), FFN_HIDDEN ** -0.5)
    return {'x': x, 'c': c, 'positions': positions, 'ada_w': ada_w, 'ada_b': ada_b,
            'norm_mix_w': norm_mix_w, 'norm_ffn_w': norm_ffn_w, 'w_in': w_in,
            'diff_qn_w': diff_qn_w, 'diff_kn_w': diff_kn_w, 'diff_lambda': diff_lambda,
            'diff_subln_w': diff_subln_w, 'gdn_conv_w': gdn_conv_w, 'gdn_a_log': gdn_a_log,
            'gdn_dt_bias': gdn_dt_bias, 'gdn_norm_w': gdn_norm_w, 'w_branch_diff': w_branch_diff,
            'w_branch_gdn': w_branch_gdn, 'w_out': w_out, 'ffn_w_up': ffn_w_up, 'ffn_w_down': ffn_w_down}


def reference(x, c, positions, ada_w, ada_b, norm_mix_w, norm_ffn_w, w_in, diff_qn_w, diff_kn_w, diff_lambda,
              diff_subln_w, gdn_conv_w, gdn_a_log, gdn_dt_bias, gdn_norm_w, w_branch_diff, w_branch_gdn, w_out,
              ffn_w_up, ffn_w_down):
    cos, sin = rope_tables(positions)
    c_act = jax.nn.silu(c)
    for layer in range(DEPTH):
        lambda_init = 0.8 - 0.6 * math.exp(-0.3 * layer)
        mod = (c_act @ ada_w[layer] + ada_b[layer])[:, None, :]
        sh_m, sc_m, gt_m, sh_f, sc_f, gt_f = jnp.split(mod, N_MOD, axis=-1)
        h = rms_norm(x, norm_mix_w[layer]) * (1.0 + sc_m) + sh_m
        y = hybrid_mixer(h, cos, sin, lambda_init, w_in[layer], diff_qn_w[layer], diff_kn_w[layer],
                         diff_lambda[layer], diff_subln_w[layer], gdn_conv_w[layer], gdn_a_log[layer],
                         gdn_dt_bias[layer], gdn_norm_w[layer], w_branch_diff[layer], w_branch_gdn[layer],
                         w_out[layer])
        x = x + gt_m * y
        h = rms_norm(x, norm_ffn_w[layer]) * (1.0 + sc_f) + sh_f
        x = x + gt_f * swiglu(h, ffn_w_up[layer], ffn_w_down[layer])
    return x
```

```python
import math
import types
from contextlib import ExitStack

import numpy as np
import concourse.bass as bass
import concourse.mybir as mybir
from concourse.bass_utils import run_bass_kernel_spmd

F32 = mybir.dt.float32
BF16 = mybir.dt.bfloat16
I32 = mybir.dt.int32
AF = mybir.ActivationFunctionType
ALU = mybir.AluOpType
AX = mybir.AxisListType

D = 2048
KC = 16
NH = 8
FFN = 5632
FC = 44
IN_COLS = 11296
EPS = 1e-6
COMPUTE = ('pe', 'act', 'dve', 'pool')


class Buf:
    __slots__ = ('name', 'w', 'r', 'sem', 'cnt')

    def __init__(self, name):
        self.name = name
        self.w = []
        self.r = []
        self.sem = None
        self.cnt = 0


def _snap(fn):
    if fn is None or fn.__closure__ is None:
        return fn
    cells = []
    for c in fn.__closure__:
        try:
            cells.append(types.CellType(c.cell_contents))
        except ValueError:
            cells.append(c)
    return types.FunctionType(fn.__code__, fn.__globals__, fn.__name__, fn.__defaults__, tuple(cells))


class Sched:
    def __init__(self, nc, stack):
        self.nc = nc
        self.stack = stack
        self.ops = {e: [] for e in COMPUTE + ('sp',)}
        self.seq = {e: 0 for e in COMPUTE}
        self.esem = {e: stack.enter_context(nc.semaphore('s_' + e)) for e in COMPUTE}
        self.waited = {e: {} for e in COMPUTE + ('sp',)}
        self.nsem = 0
        self.nbuf = 0
        self.dbufs = []
        self.named = {}

    def buf(self, name=None):
        self.nbuf += 1
        if name is None:
            return Buf('b%d' % self.nbuf)
        if name not in self.named:
            self.named[name] = Buf(name)
        return self.named[name]

    def bufs(self, n, name='b'):
        return [self.buf('%s%d' % (name, i)) for i in range(n)]

    def _dsem(self, b):
        if b.sem is None:
            b.sem = self.stack.enter_context(self.nc.semaphore('d%d' % self.nsem))
            self.nsem += 1
            self.dbufs.append(b)
        return b.sem

    def _deps(self, eng, reads, writes):
        evs = []
        for b in reads:
            evs.extend(b.w)
        for b in writes:
            evs.extend(b.w)
            evs.extend(b.r)
        waits = {}
        wd = self.waited[eng]
        for ev in evs:
            sem, val, src = ev[0], ev[1], ev[2]
            if src == 'pe' and eng == 'pe':
                continue
            if src == 'dma':
                val = ev[3].cnt
            if wd.get(sem, 0) >= val:
                continue
            if waits.get(sem, 0) < val:
                waits[sem] = val
        for sem, val in waits.items():
            wd[sem] = val
        return list(waits.items())

    def op(self, eng, fn, reads=(), writes=()):
        waits = self._deps(eng, reads, writes)
        self.seq[eng] += 1
        ev = (self.esem[eng], self.seq[eng], eng)
        for b in reads:
            b.r.append(ev)
        for b in writes:
            b.w = [ev]
            b.r = []
        self.ops[eng].append((waits, _snap(fn), self.esem[eng], 1))

    def dma(self, fn, reads=(), writes=(), q='sp', more=False, store=False, inc=16):
        d = writes[0]
        if more or store:
            saved = d.w
            d.w = []
        waits = self._deps(q, reads, writes)
        sbuf_side = reads[0] if store else d
        sem = self._dsem(sbuf_side)
        sbuf_side.cnt += inc
        ev = (sem, sbuf_side.cnt, 'dma', sbuf_side)
        for b in reads:
            b.r.append(ev)
        if more or store:
            d.w = [x for x in saved if x[0] is not sem] + [ev]
        else:
            d.w = [ev]
            d.r = []
        self.ops[q].append((waits, _snap(fn), sem, inc))

    def barrier(self):
        tgt = [(self.esem[e], self.seq[e]) for e in COMPUTE if self.seq[e] > 0]
        tgt += [(b.sem, b.cnt) for b in self.dbufs if b.cnt > 0 and not b.name.startswith('wt')]
        for eng in ('pe', 'act', 'dve', 'sp'):
            wd = self.waited[eng]
            waits = []
            for (s, v) in tgt:
                if eng in COMPUTE and s is self.esem[eng]:
                    continue
                if wd.get(s, 0) < v:
                    wd[s] = v
                    waits.append((s, v))
            if waits:
                self.ops[eng].append((waits, None, None, 0))

    def final_wait(self, eng, bufs):
        waits = self._deps(eng, bufs, ())
        self.ops[eng].append((waits, None, None, 0))

    def emit(self):
        nc = self.nc
        with nc.Block() as block:
            def run(e, name):
                for (waits, fn, sem, inc) in self.ops[name]:
                    for (s, v) in waits:
                        e.wait_ge(s, v)
                    if fn is not None:
                        fn(e).then_inc(sem, inc)

            @block.tensor
            def _(e):
                run(e, 'pe')

            @block.scalar
            def _(e):
                run(e, 'act')

            @block.vector
            def _(e):
                run(e, 'dve')

            @block.gpsimd
            def _(e):
                run(e, 'pool')

            @block.sync
            def _(e):
                run(e, 'sp')


def bc_mid(ap2, n):
    P, Fd = ap2.shape
    return ap2.unsqueeze(1).to_broadcast([P, n, Fd])


def bc_last(ap2, n):
    P, G = ap2.shape
    return ap2.unsqueeze(2).to_broadcast([P, G, n])


def build_program(NT, DEPTH, PAIR, dbg=None):
    dbg = dbg or {}
    CDT = BF16 if dbg.get('_chain16') else F32
    TT = NT // 128
    CH = min(512, NT)
    NCH = NT // CH
    NK = 2 * NT if PAIR else NT
    KT = NK // 128
    lam_init = [0.8 - 0.6 * math.exp(-0.3 * l) for l in range(DEPTH)]

    nc = bass.Bass("TRN2", target_bir_lowering=False)

    def din(name, shape, dt=F32):
        return nc.dram_tensor(name, list(shape), dt, kind="ExternalInput").ap()

    xT_d = din("xT", [D, NT])
    pos_d = din("posT", [128, TT], I32)
    c_d = din("cT", [128, KC])
    invf_d = din("invf", [128, 32])
    sel_d = din("sel", [128, 4])
    ada_w_d = din("ada_w", [DEPTH, D, 6 * D])
    ada_b_d = din("ada_bT", [DEPTH, 128, 96])
    nmw_d = din("nmwT", [DEPTH, 128, KC])
    nfw_d = din("nfwT", [DEPTH, 128, KC])
    w_in_d = din("w_in", [DEPTH, D, IN_COLS])
    w_dir_d = din("w_dir", [DEPTH, D, 32])
    qnw_d = din("qn_w", [DEPTH, 64])
    knw_d = din("kn_w", [DEPTH, 64])
    lamv_d = din("lamv", [DEPTH, 256])
    subln_d = din("sublnT", [DEPTH, 128, 1])
    conv_d = din("convT", [DEPTH, 128, 24, 5])
    alog_d = din("a_log", [DEPTH, 16])
    dtb_d = din("dt_bias", [DEPTH, 16])
    gnw_d = din("gnwT", [DEPTH, 128, 1])
    wbd_d = din("w_bd", [DEPTH, 1024, D])
    wbg_d = din("w_bg", [DEPTH, 1024, D])
    wout_d = din("w_out", [DEPTH, D, D])
    wup_d = din("w_up", [DEPTH, D, 2 * FFN])
    wdn_d = din("w_down", [DEPTH, FFN, D])
    out_d = nc.dram_tensor("outT", [D, NT], F32, kind="ExternalOutput").ap()
    dbg_d = {k: nc.dram_tensor("dbg_" + k, list(shp), F32, kind="ExternalOutput").ap()
             for k, shp in dbg.items() if not k.startswith('_')}

    def dscr(name, shape, dt=BF16):
        return nc.dram_tensor(name, list(shape), dt).ap()

    qT_s = dscr("qT_s", [NH, 128, NT])
    kT_s = dscr("kT_s", [NH * 128, NT])
    v_s = dscr("v_s", [NT, 1024])
    g_s = dscr("g_s", [24, 128, NT])
    z_s = dscr("z_s", [NH, 128, NT])
    gg_s = dscr("gg_s", [32, 128, NT])
    gq_s = dscr("gq_s", [NH, 128, NT])
    gk_s = dscr("gk_s", [NH, 128, NT])
    gkt_s = dscr("gkt_s", [NH, NT, 128])
    gvt_s = dscr("gvt_s", [NH, NT, 128])
    o1_s = dscr("o1_s", [NH, 128, NT], F32)
    if PAIR:
        halo_snd = dscr("halo_snd", [2 * 128, 48])
        halo_rcv = dscr("halo_rcv", [2 * 128, 48])
        st_snd = dscr("st_snd", [128, NH * 128], F32)
        st_rcv = dscr("st_rcv", [128, NH * 128], F32)
        kv_snd_k = dscr("kv_snd_k", [2 * NH * 128, NT])
        kv_rcv_k = dscr("kv_rcv_k", [2 * NH * 128, NT])
        kv_snd_v = dscr("kv_snd_v", [2 * NT, 1024])
        kv_rcv_v = dscr("kv_rcv_v", [2 * NT, 1024])

    def allreduce(e, src, dst):
        return e.collective_compute("AllReduce", ALU.add, replica_groups=GROUPS, ins=[src.opt()], outs=[dst.opt()])
    GROUPS = [[2 * i, 2 * i + 1] for i in range(dbg.get('_ncores', 8) // 2)]

    with ExitStack() as stack:
        S = Sched(nc, stack)
        sb = lambda name, shape, dt=F32: nc.alloc_sbuf_tensor("sb_" + name, list(shape), dt)

        xT = sb("xT", [128, KC, NT]);            BxT = [[S.buf() for _ in range(NCH)] for _ in range(KC)]
        hT = sb("hT", [128, KC, NT], BF16);      BhT = [S.buf() for _ in range(NCH)]
        NW = 3
        wt = [sb("wt%d" % i, [128, 16 * 512], BF16) for i in range(NW)]
        Bwt = S.bufs(NW, 'wt')
        ident_f = sb("ident_f", [128, 128]);     ident_b = sb("ident_b", [128, 128], BF16)
        ones_f = sb("ones_f", [128, 128]);       ones_b = sb("ones_b", [128, 128], BF16)
        m_Ui = sb("m_Ui", [128, 128]); m_Li = sb("m_Li", [128, 128])
        m_Us = sb("m_Us", [128, 128]); m_Ls = sb("m_Ls", [128, 128])
        Bconst = S.buf('const')
        cosT = sb("cosT", [128, TT, 32]); sinT = sb("sinT", [128, TT, 32]); Brope = S.buf('rope')
        cact = sb("cact", [128, KC], BF16); Bcact = S.buf('cact')
        mod = sb("mod", [128, 96]); Bmod = S.buf('mod')
        modA = sb("modA", [128, 2, KC]); BmodA = S.buf('modA')
        lyr = sb("lyr", [128, 1024]); Blyr = S.buf('lyr')
        betaT = sb("betaT", [128, TT, 16]); gT = sb("gT", [128, TT, 16]); Bbg = S.buf('bg')
        sel = sb("sel", [128, 4]); Bsel = S.buf('sel')
        ARENA = 48 * 1024
        arena = sb("arena", [128, ARENA // 4])
        psum = [nc.alloc_psum_tensor("ps%d" % i, [128, 512], F32) for i in range(8)]
        Bps = S.bufs(8, 'ps')

        def P(i, n=512):
            return psum[i][:, 0:n]

        class Arena:
            def __init__(self):
                self.off = 0

            def reset(self):
                S.barrier()
                self.off = 0

            def get(self, shape, dt=F32):
                esz = 4 if dt == F32 or dt == I32 else 2
                n = int(np.prod(shape))
                nbytes = (n * esz + 31) // 32 * 32
                assert self.off + nbytes <= ARENA, ("arena overflow", self.off, nbytes)
                v = arena[:, self.off // 4:(self.off + nbytes) // 4]
                self.off += nbytes
                if dt != F32:
                    v = v.bitcast(dt)
                v = v[:, 0:n]
                if len(shape) == 2:
                    return v.rearrange("p (a b) -> p a b", b=shape[1])
                if len(shape) == 3:
                    return v.rearrange("p (a b c) -> p a b c", b=shape[1], c=shape[2])
                return v

        AR = Arena()

        class WStream:
            def __init__(self):
                self.descs = []
                self.issued = 0
                self.taken = 0
                self.loaded = {}

            def add(self, tag, src, kch, ncols):
                self.descs.append((tag, src, kch, ncols))

            def _issue(self, i):
                tag, src, kch, ncols = self.descs[i]
                slot = i % NW
                t = wt[slot][:, 0:kch * ncols].rearrange("p (k n) -> p k n", n=ncols)
                first = True
                for k0 in range(0, kch, 16):
                    k1 = min(kch, k0 + 16)
                    S.dma(lambda e, t=t, src=src, k0=k0, k1=k1: e.dma_start(out=t[:, k0:k1, :], in_=src[:, k0:k1, :]),
                          writes=[Bwt[slot]], q='pool', more=not first)
                    first = False
                self.loaded[i] = (t, Bwt[slot])

            def get(self, tag):
                i = self.taken
                assert self.descs[i][0] == tag, (self.descs[i][0], tag)
                while self.issued < len(self.descs) and self.issued <= i + 1:
                    self._issue(self.issued)
                    self.issued += 1
                self.taken += 1
                return self.loaded.pop(i)

        WS = WStream()

        def wview(w2d, c0, n):
            return w2d.rearrange("(k p) n -> p k n", p=128)[:, :, c0:c0 + n]

        for l in range(DEPTH):
            for j in range(24):
                WS.add(('ada', l, j), wview(ada_w_d[l], j * 512, 512), KC, 512)
            for j in range(6):
                WS.add(('tm', l, j), wview(w_in_d[l], j * 512, 512), KC, 512)
            WS.add(('dir', l), wview(w_dir_d[l], 0, 32), KC, 32)
            for j in range(16):
                WS.add(('fm', l, j), wview(w_in_d[l], 3072 + j * 512 + (32 if j >= 8 else 0), 512), KC, 512)
            for j in range(4):
                WS.add(('bd', l, j), wview(wbd_d[l], j * 512, 512), 8, 512)
                WS.add(('bg', l, j), wview(wbg_d[l], j * 512, 512), 8, 512)
            for j in range(4):
                WS.add(('out', l, j), wview(wout_d[l], j * 512, 512), KC, 512)
            for n in range(NCH):
                for j in range(11):
                    WS.add(('upg', l, n, j), wview(wup_d[l], j * 512, 512), KC, 512)
                    WS.add(('upu', l, n, j), wview(wup_d[l], FFN + j * 512, 512), KC, 512)
                for m in range(16):
                    WS.add(('dn', l, n, m), wview(wdn_d[l], m * 128, 128), FC, 128)

        Bout = S.buf('out')

        def setup():
            S.op('pool', lambda e: e.memset(ident_f[:], 0.0), writes=[Bconst])
            S.op('pool', lambda e: e.affine_select(out=ident_f[:], in_=ident_f[:], pattern=[[-1, 128]],
                                                   compare_op=ALU.not_equal, fill=1.0, base=0, channel_multiplier=1),
                 reads=[Bconst], writes=[Bconst])
            S.op('pool', lambda e: e.memset(ones_f[:], 1.0), reads=[Bconst], writes=[Bconst])
            for (m, op, sgn) in ((m_Ls, ALU.is_gt, 1), (m_Li, ALU.is_ge, 1), (m_Us, ALU.is_gt, -1), (m_Ui, ALU.is_ge, -1)):
                S.op('pool', lambda e, m=m, op=op, sgn=sgn: e.affine_select(out=m[:], in_=ones_f[:], pattern=[[-sgn, 128]],
                                                                             compare_op=op, fill=0.0, base=0, channel_multiplier=sgn),
                     reads=[Bconst], writes=[Bconst])
            S.op('pool', lambda e: e.tensor_copy(ident_b[:], ident_f[:]), reads=[Bconst], writes=[Bconst])
            S.op('pool', lambda e: e.tensor_copy(ones_b[:], ones_f[:]), reads=[Bconst], writes=[Bconst])
            xv = xT_d.rearrange("(k p) n -> p k n", p=128)
            for kc in range(KC):
                for n in range(NCH):
                    S.dma(lambda e, kc=kc, n=n: e.dma_start(out=xT[:, kc, n * CH:(n + 1) * CH], in_=xv[:, kc, n * CH:(n + 1) * CH]),
                          writes=[BxT[kc][n]])
            S.dma(lambda e: e.dma_start(out=sel[:], in_=sel_d[:, :]), writes=[Bsel])
            AR.reset()
            posi = AR.get([TT], I32); posf = AR.get([TT]); invf = AR.get([32]); ang = AR.get([TT, 32]); kf = AR.get([TT, 32])
            ki = AR.get([TT, 32], I32); cf = AR.get([KC])
            Bt = S.buf('setup_t')
            S.dma(lambda e: e.dma_start(out=posi, in_=pos_d[:, :]), writes=[Bt])
            S.dma(lambda e: e.dma_start(out=invf, in_=invf_d[:, :]), writes=[Bt], more=True)
            S.dma(lambda e: e.dma_start(out=cf, in_=c_d[:, :]), writes=[Bt], more=True)
            S.op('dve', lambda e: e.tensor_copy(posf, posi), reads=[Bt], writes=[Bt])
            for t in range(TT):
                S.op('dve', lambda e, t=t: e.tensor_scalar(ang[:, t, :], invf, posf[:, t:t + 1], None, op0=ALU.mult),
                     reads=[Bt], writes=[Bt])
            def reduce_sin(dst, shift):
                S.op('dve', lambda e: e.tensor_scalar(kf, ang, shift, 1.0 / (2 * math.pi), op0=ALU.add, op1=ALU.mult), reads=[Bt, Brope], writes=[Bt])
                S.op('dve', lambda e: e.tensor_copy(ki, kf), reads=[Bt], writes=[Bt])
                S.op('dve', lambda e: e.tensor_copy(kf, ki), reads=[Bt], writes=[Bt])
                S.op('dve', lambda e: e.scalar_tensor_tensor(out=kf, in0=kf, scalar=-2 * math.pi, in1=ang, op0=ALU.mult, op1=ALU.add),
                     reads=[Bt], writes=[Bt])
                S.op('dve', lambda e: e.tensor_scalar(kf, kf, shift, None, op0=ALU.add), reads=[Bt], writes=[Bt])
                S.op('dve', lambda e: e.tensor_scalar(kf, kf, -math.pi, math.pi, op0=ALU.max, op1=ALU.min), reads=[Bt], writes=[Bt])
                S.op('act', lambda e: e.activation(out=dst, in_=kf, func=AF.Sin), reads=[Bt, Brope], writes=[Brope])
            reduce_sin(sinT[:], 0.0)
            reduce_sin(cosT[:], math.pi / 2)
            S.op('act', lambda e: e.activation(out=cact[:], in_=cf, func=AF.Silu), reads=[Bt], writes=[Bcact])

        L_QNW, L_KNW = 0, 64
        L_LAM = 128
        L_CONV = 384
        L_ALOG, L_DTB = 504, 520
        L_ADAB = 536
        L_NMW, L_NFW = 632, 648
        L_SUB, L_GNW = 664, 665
        L_LAMC = 666
        L_NEGA = 668
        L_TMP = 700

        def load_layer_params(l):
            def ld(off, n, src, more=True):
                S.dma(lambda e: e.dma_start(out=lyr[:, off:off + n], in_=src), writes=[Blyr], more=more)
            S.dma(lambda e: e.dma_start(out=lyr[:, L_QNW:L_QNW + 64], in_=qnw_d[l].partition_broadcast(128)), writes=[Blyr])
            ld(L_KNW, 64, knw_d[l].partition_broadcast(128))
            ld(L_LAM, 256, lamv_d[l].partition_broadcast(128))
            ld(L_CONV, 120, conv_d[l].rearrange("p a b -> p (a b)"))
            ld(L_ALOG, 16, alog_d[l].partition_broadcast(128))
            ld(L_DTB, 16, dtb_d[l].partition_broadcast(128))
            ld(L_ADAB, 96, ada_b_d[l])
            ld(L_NMW, 16, nmw_d[l])
            ld(L_NFW, 16, nfw_d[l])
            ld(L_SUB, 1, subln_d[l])
            ld(L_GNW, 1, gnw_d[l])
            rw = dict(reads=[Blyr], writes=[Blyr])
            S.op('dve', lambda e: e.tensor_scalar(lyr[:, L_QNW:L_QNW + 64], lyr[:, L_QNW:L_QNW + 64], 0.125, None, op0=ALU.mult), **rw)
            S.op('dve', lambda e: e.tensor_tensor(lyr[:, L_TMP:L_TMP + 64], lyr[:, L_LAM:L_LAM + 64], lyr[:, L_LAM + 64:L_LAM + 128], op=ALU.mult), **rw)
            S.op('dve', lambda e: e.tensor_tensor(lyr[:, L_TMP + 64:L_TMP + 128], lyr[:, L_LAM + 128:L_LAM + 192], lyr[:, L_LAM + 192:L_LAM + 256], op=ALU.mult), **rw)
            S.op('dve', lambda e: e.tensor_reduce(out=lyr[:, L_TMP + 128:L_TMP + 130], in_=lyr[:, L_TMP:L_TMP + 128].rearrange("p (a b) -> p a b", b=64),
                                                  axis=AX.X, op=ALU.add), **rw)
            S.op('act', lambda e: e.activation(out=lyr[:, L_TMP + 128:L_TMP + 130], in_=lyr[:, L_TMP + 128:L_TMP + 130], func=AF.Exp), **rw)
            S.op('dve', lambda e: e.tensor_tensor(lyr[:, L_LAMC:L_LAMC + 1], lyr[:, L_TMP + 128:L_TMP + 129], lyr[:, L_TMP + 129:L_TMP + 130], op=ALU.subtract), **rw)
            S.op('dve', lambda e: e.tensor_scalar(lyr[:, L_LAMC:L_LAMC + 1], lyr[:, L_LAMC:L_LAMC + 1], lam_init[l], -1.0, op0=ALU.add, op1=ALU.mult), **rw)
            S.op('act', lambda e: e.activation(out=lyr[:, L_NEGA:L_NEGA + 16], in_=lyr[:, L_ALOG:L_ALOG + 16], func=AF.Exp), **rw)
            S.op('dve', lambda e: e.tensor_scalar(lyr[:, L_NEGA:L_NEGA + 16], lyr[:, L_NEGA:L_NEGA + 16], -1.0, None, op0=ALU.mult), **rw)

        def ada_mod(l):
            pm = 7
            for j in range(24):
                w, Bw = WS.get(('ada', l, j))
                for m in range(4):
                    col = j * 4 + m
                    for kc in range(KC):
                        S.op('pe', lambda e, w=w, m=m, kc=kc, col=col: e.matmul(psum[pm][:, col:col + 1], w[:, kc, m * 128:(m + 1) * 128],
                                                                                   cact[:, kc:kc + 1], start=(kc == 0), stop=(kc == KC - 1)),
                             reads=[Bw, Bcact], writes=[Bps[pm]])
            S.op('dve', lambda e: e.tensor_tensor(mod[:], psum[pm][:, 0:96], lyr[:, L_ADAB:L_ADAB + 96], op=ALU.add),
                 reads=[Bps[pm], Blyr], writes=[Bmod])
            S.op('dve', lambda e: e.scalar_tensor_tensor(out=modA[:, 0, :], in0=mod[:, 16:32], scalar=1.0, in1=lyr[:, L_NMW:L_NMW + 16],
                                                         op0=ALU.add, op1=ALU.mult), reads=[Bmod, Blyr], writes=[BmodA])
            S.op('dve', lambda e: e.scalar_tensor_tensor(out=modA[:, 1, :], in0=mod[:, 64:80], scalar=1.0, in1=lyr[:, L_NFW:L_NFW + 16],
                                                         op0=ALU.add, op1=ALU.mult), reads=[Bmod, Blyr, BmodA], writes=[BmodA])

        def rsqrt_ops(dst, src, scale, rd, wr, eps=EPS, post=1.0):
            S.op('act', lambda e: e.activation(out=dst, in_=src, func=AF.Ln, scale=scale, bias=eps), reads=rd, writes=wr)
            S.op('act', lambda e: e.activation(out=dst, in_=dst, func=AF.Exp, scale=-0.5, bias=math.log(post)), reads=wr, writes=wr)

        def modnorm(which):
            sh0 = 0 if which == 0 else 48
            AR.reset()
            sq = [AR.get([CH]) for _ in range(2)]; Bsq = S.bufs(2, 'sq')
            rstd = AR.get([CH]); Brs = S.buf('rstd')
            tmp = [AR.get([CH]) for _ in range(2)]; Btmp = S.bufs(2, 'tmp')
            for n in range(NCH):
                cs = slice(n * CH, (n + 1) * CH)
                for kc in range(KC):
                    i = kc % 2
                    S.op('act', lambda e, i=i, kc=kc: e.activation(out=sq[i], in_=xT[:, kc, cs], func=AF.Square),
                         reads=[BxT[kc][n]], writes=[Bsq[i]])
                    S.op('pe', lambda e, i=i, kc=kc: e.matmul(P(6, CH), ones_f[:], sq[i], start=(kc == 0), stop=(kc == KC - 1)),
                         reads=[Bsq[i], Bconst], writes=[Bps[6]])
                rsqrt_ops(rstd, P(6, CH), 1.0 / D, [Bps[6]], [Brs])
                for kc in range(KC):
                    i = kc % 2
                    S.op('dve', lambda e, i=i, kc=kc: e.scalar_tensor_tensor(out=tmp[i], in0=xT[:, kc, cs], scalar=modA[:, which, kc:kc + 1], in1=rstd,
                                                                            op0=ALU.mult, op1=ALU.mult),
                         reads=[BxT[kc][n], BmodA, Brs], writes=[Btmp[i]])
                    S.op('act', lambda e, i=i, kc=kc: e.activation(out=hT[:, kc, cs], in_=tmp[i], func=AF.Identity,
                                                                   bias=mod[:, sh0 + kc:sh0 + kc + 1], scale=1.0),
                         reads=[Btmp[i], Bmod], writes=[BhT[n]])

        def fm_block(w, Bw, kch, ncols, rhs_fn, rhs_bufs_fn, evac, pbanks, cnt):
            for m in range(ncols // 128):
                for n in range(NCH):
                    pb = pbanks[cnt[0] % len(pbanks)]
                    cnt[0] += 1
                    for kc in range(kch):
                        S.op('pe', lambda e, pb=pb, m=m, n=n, kc=kc: e.matmul(P(pb, CH), w[:, kc, m * 128:(m + 1) * 128], rhs_fn(kc, n),
                                                                             start=(kc == 0), stop=(kc == kch - 1)),
                             reads=[Bw] + rhs_bufs_fn(kc, n), writes=[Bps[pb]])
                    evac(m, n, pb)

        def in_proj(l):
            AR.reset()
            sqb = [AR.get([512]) for _ in range(2)]; Bsqb = S.bufs(2, 'sqb')
            ss8 = [AR.get([8]) for _ in range(2)]; Bss8 = S.bufs(2, 'ss8')
            qn = [AR.get([512]) for _ in range(2)]; Bqn = S.bufs(2, 'qn')
            rot = [AR.get([512]) for _ in range(2)]; Brot = S.bufs(2, 'rot')
            rt2 = [AR.get([512]) for _ in range(2)]; Brt2 = S.bufs(2, 'rt2')
            qb = [AR.get([512], BF16) for _ in range(2)]; Bqb = S.bufs(2, 'qb')
            qtb = [AR.get([4, 128], BF16) for _ in range(2)]; Bqtb = S.bufs(2, 'qtb')
            vb_ = [AR.get([512], BF16) for _ in range(3)]; Bvb = S.bufs(3, 'vb')
            fo = [AR.get([CH], BF16) for _ in range(3)]; Bfo = S.bufs(3, 'fo')
            sm = AR.get([64]); Bsm = S.buf('sm')
            if PAIR:
                mk = [AR.get([2, 512], BF16) for _ in range(2)]; Bmk = S.bufs(2, 'mk')
                hm = [AR.get([2, 2], BF16) for _ in range(3)]; Bhm = S.bufs(3, 'hm')
            Bq_s = S.buf('q_s'); Bk_s = S.buf('k_s'); Bv_s = S.buf('v_s')
            cnt = [0]
            ev = [0]
            for j in range(6):
                w, Bw = WS.get(('tm', l, j))
                for tt in range(TT):
                    pb = cnt[0] % 4
                    cnt[0] += 1
                    for kc in range(KC):
                        S.op('pe', lambda e, pb=pb, tt=tt, kc=kc, w=w: e.matmul(P(pb), hT[:, kc, tt * 128:(tt + 1) * 128], w[:, kc, :],
                                                                               start=(kc == 0), stop=(kc == KC - 1)),
                             reads=[Bw, BhT[(tt * 128) // CH]], writes=[Bps[pb]])
                    if j < 4:
                        i = ev[0] % 2
                        ev[0] += 1
                        isq = (j < 2)
                        woff = L_QNW if isq else L_KNW
                        S.op('act', lambda e, i=i, pb=pb: e.activation(out=sqb[i], in_=P(pb), func=AF.Square), reads=[Bps[pb]], writes=[Bsqb[i]])
                        S.op('dve', lambda e, i=i: e.tensor_reduce(out=ss8[i], in_=sqb[i].rearrange("p (g d) -> p g d", d=64), axis=AX.X, op=ALU.add),
                             reads=[Bsqb[i]], writes=[Bss8[i]])
                        rsqrt_ops(ss8[i], ss8[i], 1.0 / 64, [Bss8[i]], [Bss8[i]])
                        S.op('dve', lambda e, i=i, pb=pb: e.tensor_tensor(qn[i].rearrange("p (g d) -> p g d", d=64), P(pb).rearrange("p (g d) -> p g d", d=64),
                                                                          bc_last(ss8[i], 64), op=ALU.mult),
                             reads=[Bps[pb], Bss8[i]], writes=[Bqn[i]])
                        S.op('dve', lambda e, i=i, woff=woff: e.tensor_tensor(qn[i].rearrange("p (g d) -> p g d", d=64), qn[i].rearrange("p (g d) -> p g d", d=64),
                                                                             bc_mid(lyr[:, woff:woff + 64], 8), op=ALU.mult),
                             reads=[Bqn[i], Blyr], writes=[Bqn[i]])
                        q4 = qn[i].rearrange("p (g t f) -> p g t f", t=2, f=32)
                        r4 = rot[i].rearrange("p (g t f) -> p g t f", t=2, f=32)
                        s4 = rt2[i].rearrange("p (g t f) -> p g t f", t=2, f=32)
                        cb = bc_mid(cosT[:, tt, :], 8)
                        sn = bc_mid(sinT[:, tt, :], 8)
                        for t_ in range(2):
                            S.op('dve', lambda e, t_=t_, q4=q4, r4=r4, cb=cb: e.tensor_tensor(r4[:, :, t_, :], q4[:, :, t_, :], cb, op=ALU.mult),
                                 reads=[Bqn[i], Brope, Brot[i]], writes=[Brot[i]])
                            S.op('dve', lambda e, t_=t_, q4=q4, s4=s4, sn=sn: e.tensor_tensor(s4[:, :, t_, :], q4[:, :, 1 - t_, :], sn, op=ALU.mult),
                                 reads=[Bqn[i], Brope, Brt2[i]], writes=[Brt2[i]])
                        b4 = qb[i].rearrange("p (g t f) -> p g t f", t=2, f=32)
                        S.op('dve', lambda e, r4=r4, s4=s4, b4=b4: e.tensor_tensor(b4[:, :, 0, :], r4[:, :, 0, :], s4[:, :, 0, :], op=ALU.subtract),
                             reads=[Brot[i], Brt2[i], Bqb[i]], writes=[Bqb[i]])
                        S.op('dve', lambda e, r4=r4, s4=s4, b4=b4: e.tensor_tensor(b4[:, :, 1, :], r4[:, :, 1, :], s4[:, :, 1, :], op=ALU.add),
                             reads=[Brot[i], Brt2[i], Bqb[i]], writes=[Bqb[i]])
                        tb = 4 + (ev[0] % 2)
                        pT = psum[tb][:].bitcast(BF16)
                        for hh in range(4):
                            S.op('pe', lambda e, hh=hh, i=i, pT=pT: e.transpose(pT[:, hh * 128:(hh + 1) * 128], qb[i][:, hh * 128:(hh + 1) * 128], ident_b[:]),
                                 reads=[Bqb[i], Bconst], writes=[Bps[tb]])
                        S.op('act', lambda e, i=i, pT=pT: e.activation(out=qtb[i], in_=pT[:, 0:512].rearrange("p (a b) -> p a b", b=128), func=AF.Copy),
                             reads=[Bps[tb]], writes=[Bqtb[i]])
                        h0 = (j % 2) * 4
                        if isq:
                            dst = qT_s[h0:h0 + 4, :, tt * 128:(tt + 1) * 128].rearrange("h p t -> p h t")
                            S.dma(lambda e, dst=dst, i=i: e.dma_start(out=dst, in_=qtb[i]), reads=[Bqtb[i]], writes=[Bq_s], store=True)
                        elif not PAIR:
                            dst = kT_s.rearrange("(h p) t -> h p t", p=128)[h0:h0 + 4, :, tt * 128:(tt + 1) * 128].rearrange("h p t -> p h t")
                            S.dma(lambda e, dst=dst, i=i: e.dma_start(out=dst, in_=qtb[i]), reads=[Bqtb[i]], writes=[Bk_s], store=True)
                        else:
                            for r in range(2):
                                S.op('dve', lambda e, i=i, r=r: e.tensor_scalar(mk[i][:, r, :], qtb[i].rearrange("p a b -> p (a b)"), sel[:, 2 + r:3 + r], None, op0=ALU.mult),
                                     reads=[Bqtb[i], Bsel, Bmk[i]], writes=[Bmk[i]])
                            for r in range(2):
                                dst = kv_snd_k.rearrange("(r h p) t -> r h p t", h=NH, p=128)[r, h0:h0 + 4, :, tt * 128:(tt + 1) * 128].rearrange("h p t -> p h t")
                                S.dma(lambda e, dst=dst, i=i, r=r: e.dma_start(out=dst, in_=mk[i][:, r, :].rearrange("p (a b) -> p a b", b=128)), reads=[Bmk[i]], writes=[Bk_s], store=True)
                    else:
                        i = ev[0] % 3
                        ev[0] += 1
                        S.op('act', lambda e, i=i, pb=pb: e.activation(out=vb_[i], in_=P(pb), func=AF.Copy), reads=[Bps[pb]], writes=[Bvb[i]])
                        c0 = (j - 4) * 512
                        if not PAIR:
                            S.dma(lambda e, i=i, tt=tt, c0=c0: e.dma_start(out=v_s[tt * 128:(tt + 1) * 128, c0:c0 + 512], in_=vb_[i]),
                                  reads=[Bvb[i]], writes=[Bv_s], store=True)
                        else:
                            i2 = ev[0] % 2
                            for r in range(2):
                                S.op('dve', lambda e, i=i, i2=i2, r=r: e.tensor_scalar(mk[i2][:, r, :], vb_[i], sel[:, 2 + r:3 + r], None, op0=ALU.mult),
                                     reads=[Bvb[i], Bsel, Bmk[i2]], writes=[Bmk[i2]])
                            for r in range(2):
                                S.dma(lambda e, i2=i2, tt=tt, c0=c0, r=r: e.dma_start(out=kv_snd_v[r * NT + tt * 128:r * NT + (tt + 1) * 128, c0:c0 + 512], in_=mk[i2][:, r, :]),
                                      reads=[Bmk[i2]], writes=[Bv_s], store=True)
            w, Bw = WS.get(('dir', l))
            for tt in range(TT):
                pb = 4 + tt % 2
                for kc in range(KC):
                    S.op('pe', lambda e, pb=pb, tt=tt, kc=kc, w=w: e.matmul(psum[pb][:, 0:32], hT[:, kc, tt * 128:(tt + 1) * 128], w[:, kc, :],
                                                                           start=(kc == 0), stop=(kc == KC - 1)),
                         reads=[Bw, BhT[(tt * 128) // CH]], writes=[Bps[pb]])
                S.op('act', lambda e, pb=pb: e.activation(out=sm[:, 0:16], in_=psum[pb][:, 0:16], func=AF.Exp, scale=-1.0), reads=[Bps[pb]], writes=[Bsm])
                S.op('dve', lambda e: e.tensor_scalar(sm[:, 0:16], sm[:, 0:16], 1.0, None, op0=ALU.add), reads=[Bsm], writes=[Bsm])
                S.op('dve', lambda e, tt=tt: e.reciprocal(betaT[:, tt, :], sm[:, 0:16]), reads=[Bsm, Bbg], writes=[Bbg])
                S.op('dve', lambda e, pb=pb: e.tensor_tensor(sm[:, 16:32], psum[pb][:, 16:32], lyr[:, L_DTB:L_DTB + 16], op=ALU.add),
                     reads=[Bps[pb], Blyr, Bsm], writes=[Bsm])
                S.op('dve', lambda e: e.tensor_scalar(sm[:, 32:48], sm[:, 16:32], 0.0, None, op0=ALU.min), reads=[Bsm], writes=[Bsm])
                S.op('dve', lambda e: e.tensor_scalar(sm[:, 48:64], sm[:, 16:32], 0.0, None, op0=ALU.max), reads=[Bsm], writes=[Bsm])
                S.op('dve', lambda e: e.tensor_tensor(sm[:, 32:48], sm[:, 32:48], sm[:, 48:64], op=ALU.subtract), reads=[Bsm], writes=[Bsm])
                S.op('act', lambda e: e.activation(out=sm[:, 32:48], in_=sm[:, 32:48], func=AF.Exp), reads=[Bsm], writes=[Bsm])
                S.op('act', lambda e: e.activation(out=sm[:, 32:48], in_=sm[:, 32:48], func=AF.Ln, bias=1.0, scale=1.0), reads=[Bsm], writes=[Bsm])
                S.op('dve', lambda e: e.tensor_tensor(sm[:, 32:48], sm[:, 32:48], sm[:, 48:64], op=ALU.add), reads=[Bsm], writes=[Bsm])
                S.op('dve', lambda e, tt=tt: e.tensor_tensor(gT[:, tt, :], sm[:, 32:48], lyr[:, L_NEGA:L_NEGA + 16], op=ALU.mult),
                     reads=[Bsm, Blyr, Bbg], writes=[Bbg])
            Bg_s = S.buf('g_s'); Bz_s = S.buf('z_s'); Bgg_s = S.buf('gg_s'); Bhs = S.buf('halo_snd')
            fcnt = [0]
            for j in range(16):
                w, Bw = WS.get(('fm', l, j))

                def evac(m, n, pb, j=j):
                    i = fcnt[0] % 3
                    fcnt[0] += 1
                    ch = j * 4 + m
                    cs = slice(n * CH, (n + 1) * CH)
                    if ch < 24:
                        S.op('dve', lambda e: e.tensor_copy(fo[i], P(pb, CH)), reads=[Bps[pb]], writes=[Bfo[i]])
                        S.dma(lambda e: e.dma_start(out=g_s[ch, :, cs], in_=fo[i]), reads=[Bfo[i]], writes=[Bg_s], store=True)
                        if PAIR and n == NCH - 1:
                            ih = ch % 3
                            for r in range(2):
                                S.op('dve', lambda e, r=r: e.tensor_scalar(hm[ih][:, r, :], fo[i][:, CH - 2:CH], sel[:, 2 + r:3 + r], None, op0=ALU.mult),
                                     reads=[Bfo[i], Bsel, Bhm[ih]], writes=[Bhm[ih]])
                            for r in range(2):
                                S.dma(lambda e, r=r: e.dma_start(out=halo_snd[r * 128:(r + 1) * 128, 2 * ch:2 * ch + 2], in_=hm[ih][:, r, :]), reads=[Bhm[ih]], writes=[Bhs], store=True)
                    elif ch < 32:
                        S.op('act', lambda e: e.activation(out=fo[i], in_=P(pb, CH), func=AF.Silu), reads=[Bps[pb]], writes=[Bfo[i]])
                        S.dma(lambda e: e.dma_start(out=z_s[ch - 24, :, cs], in_=fo[i]), reads=[Bfo[i]], writes=[Bz_s], store=True)
                    else:
                        S.op('act', lambda e: e.activation(out=fo[i], in_=P(pb, CH), func=AF.Sigmoid), reads=[Bps[pb]], writes=[Bfo[i]])
                        S.dma(lambda e: e.dma_start(out=gg_s[ch - 32, :, cs], in_=fo[i]), reads=[Bfo[i]], writes=[Bgg_s], store=True)

                fm_block(w, Bw, KC, 512, lambda kc, n: hT[:, kc, n * CH:(n + 1) * CH], lambda kc, n: [BhT[n]], evac, [0, 1, 2, 3], cnt)
            return dict(q=Bq_s, k=Bk_s, v=Bv_s, g=Bg_s, z=Bz_s, gg=Bgg_s, hs=Bhs)


        ydT = hT[:, 0:8, :]
        ygT = hT[:, 8:16, :]
        Byd = S.bufs(NH, 'yd'); Byg = S.bufs(2, 'yg')

        def gdn_prep(l, sc):
            AR.reset()
            gin = [AR.get([NT + 4], BF16) for _ in range(2)]; Bgin = S.bufs(2, 'gin')
            acc = [AR.get([NT]) for _ in range(2)]; Bacc = S.bufs(2, 'acc')
            sqt = [AR.get([CH]) for _ in range(2)]; Bsqt = S.bufs(2, 'sqt')
            rs = [AR.get([CH]) for _ in range(2)]; Brs_ = S.bufs(2, 'rs')
            nb = [AR.get([NT], BF16) for _ in range(2)]; Bnb = S.bufs(2, 'nb')
            tk = [AR.get([TT, 128], BF16) for _ in range(2)]; Btk = S.bufs(2, 'tk')
            Bgq = S.buf('gq_s'); Bgk = S.buf('gk_s'); Bgkt = S.buf('gkt_s'); Bgvt = S.buf('gvt_s')
            it = 0
            if PAIR:
                Bhr = S.buf('halo_rcv')
                S.dma(lambda e: allreduce(e, halo_snd, halo_rcv), reads=[sc['hs']], writes=[Bhr], q='pool', inc=1)
                Bkr = S.buf('kv_rcv_k'); Bvr = S.buf('kv_rcv_v')
                S.dma(lambda e: allreduce(e, kv_snd_k, kv_rcv_k), reads=[sc['k']], writes=[Bkr], q='pool', inc=1)
                S.dma(lambda e: allreduce(e, kv_snd_v, kv_rcv_v), reads=[sc['v']], writes=[Bvr], q='pool', inc=1)
                sc['k'] = Bkr
                sc['v'] = Bvr
                hl = AR.get([2, 48], BF16); hlf = AR.get([48]); hl2 = AR.get([48], BF16); Bhl = S.buf('hl')
                S.dma(lambda e: e.dma_start(out=hl, in_=halo_rcv.rearrange("(r p) x -> p r x", p=128)), reads=[Bhr], writes=[Bhl])
                S.op('dve', lambda e: e.tensor_scalar(hlf, hl[:, 0, :], sel[:, 0:1], None, op0=ALU.mult), reads=[Bhl, Bsel], writes=[Bhl])
                S.op('dve', lambda e: e.scalar_tensor_tensor(out=hl2, in0=hl[:, 1, :], scalar=sel[:, 1:2], in1=hlf, op0=ALU.mult, op1=ALU.add), reads=[Bhl, Bsel], writes=[Bhl])

                def halo_fill(gt, Bg, ch):
                    S.op('dve', lambda e: e.tensor_copy(gt[:, NT + 2:NT + 3], hl2[:, 2 * ch + 1:2 * ch + 2]), reads=[Bhl, Bg], writes=[Bg])
                    S.op('dve', lambda e: e.tensor_copy(gt[:, NT + 3:NT + 4], hl2[:, 2 * ch:2 * ch + 1]), reads=[Bhl, Bg], writes=[Bg])
            for h in range(NH):
                for qi in range(3):
                    ch = qi * 8 + h
                    i = it % 2
                    it += 1
                    S.dma(lambda e, i=i, ch=ch: e.dma_start(out=gin[i][:, 2:NT + 2], in_=g_s[ch, :, :]), reads=[sc['g']], writes=[Bgin[i]])
                    S.op('dve', lambda e, i=i: e.memset(gin[i][:, 0:2], 0.0), reads=[Bgin[i]], writes=[Bgin[i]])
                    if PAIR:
                        halo_fill(gin[i], Bgin[i], ch)
                    else:
                        S.op('dve', lambda e, i=i: e.memset(gin[i][:, NT + 2:NT + 4], 0.0), reads=[Bgin[i]], writes=[Bgin[i]])
                    cw = L_CONV + ch * 5
                    S.op('dve', lambda e, i=i, cw=cw: e.tensor_scalar(acc[i], gin[i][:, 0:NT], lyr[:, cw:cw + 1], None, op0=ALU.mult),
                         reads=[Bgin[i], Blyr], writes=[Bacc[i]])
                    for tap in range(1, 5):
                        S.op('dve', lambda e, i=i, cw=cw, tap=tap: e.scalar_tensor_tensor(out=acc[i], in0=gin[i][:, tap:tap + NT], scalar=lyr[:, cw + tap:cw + tap + 1],
                                                                                        in1=acc[i], op0=ALU.mult, op1=ALU.add),
                             reads=[Bgin[i], Blyr, Bacc[i]], writes=[Bacc[i]])
                    S.op('act', lambda e, i=i: e.activation(out=acc[i], in_=acc[i], func=AF.Silu), reads=[Bacc[i]], writes=[Bacc[i]])
                    if qi < 2:
                        post = (128.0 ** -0.5) if qi == 0 else 1.0
                        for n in range(NCH):
                            cs = slice(n * CH, (n + 1) * CH)
                            j = n % 2
                            S.op('act', lambda e, i=i, j=j, cs=cs: e.activation(out=sqt[j], in_=acc[i][:, cs], func=AF.Square), reads=[Bacc[i]], writes=[Bsqt[j]])
                            S.op('pe', lambda e, j=j: e.matmul(P(6, CH), ones_f[:], sqt[j], start=True, stop=True), reads=[Bsqt[j], Bconst], writes=[Bps[6]])
                            rsqrt_ops(rs[j], P(6, CH), 1.0, [Bps[6]], [Brs_[j]], post=post)
                            S.op('dve', lambda e, i=i, j=j, cs=cs: e.tensor_tensor(nb[i][:, cs], acc[i][:, cs], rs[j], op=ALU.mult),
                                 reads=[Bacc[i], Brs_[j], Bnb[i]], writes=[Bnb[i]])
                    else:
                        S.op('dve', lambda e, i=i: e.tensor_copy(nb[i], acc[i]), reads=[Bacc[i]], writes=[Bnb[i]])
                    if qi == 0:
                        S.dma(lambda e, i=i, h=h: e.dma_start(out=gq_s[h, :, :], in_=nb[i]), reads=[Bnb[i]], writes=[Bgq], store=True)
                    else:
                        if qi == 1:
                            S.dma(lambda e, i=i, h=h: e.dma_start(out=gk_s[h, :, :], in_=nb[i]), reads=[Bnb[i]], writes=[Bgk], store=True)
                        for t0 in range(0, TT, 4):
                            tb = 4 + ((t0 // 4) % 2)
                            pT = psum[tb][:].bitcast(BF16)
                            nt_ = min(4, TT - t0)
                            for t in range(nt_):
                                S.op('pe', lambda e, i=i, t=t, t0=t0, pT=pT: e.transpose(pT[:, t * 128:(t + 1) * 128], nb[i][:, (t0 + t) * 128:(t0 + t + 1) * 128], ident_b[:]),
                                     reads=[Bnb[i], Bconst], writes=[Bps[tb]])
                            S.op('act', lambda e, i=i, t0=t0, nt_=nt_, pT=pT: e.activation(out=tk[i][:, t0:t0 + nt_, :], in_=pT[:, 0:nt_ * 128].rearrange("p (a b) -> p a b", b=128), func=AF.Copy),
                                 reads=[Bps[tb], Btk[i]], writes=[Btk[i]])
                        dst = (gkt_s if qi == 1 else gvt_s)[h].rearrange("(t p) d -> p t d", p=128)
                        S.dma(lambda e, i=i, dst=dst: e.dma_start(out=dst, in_=tk[i]), reads=[Btk[i]], writes=[Bgkt if qi == 1 else Bgvt], store=True)
            return dict(gq=Bgq, gk=Bgk, gkt=Bgkt, gvt=Bgvt)

        Sst = sb("Sst", [128, NH, 128]); BSst = S.bufs(2, 'Sst')

        def gdn_scan(l, ph, sc, gp, Bo1, zero_state):
            AR.reset()
            TRI = m_Ui if ph == 0 else m_Li
            MST = m_Ls if ph == 0 else m_Us
            MIN = m_Ui if ph == 0 else m_Li
            tiles = list(range(TT)) if ph == 0 else list(range(TT - 1, -1, -1))
            dsl = slice(ph * 8, ph * 8 + 8)
            st8 = AR.get([64]); Bst8 = S.buf('st8')
            qT4 = [AR.get([4, 128], BF16)] * 2; kT4 = [AR.get([4, 128], BF16)] * 2
            kt4 = [AR.get([4, 128], BF16)] * 2; vt4 = [AR.get([4, 128], BF16)] * 2
            Bld = [S.buf('ld')] * 2
            Gb4 = AR.get([4, 128]); BGb = S.buf('Gb')
            ta = AR.get([4, 128]); tb_ = AR.get([4, 128]); Er = AR.get([4, 128]); Bta = S.buf('ta'); Btb = S.buf('tb'); BEr = S.buf('Er')
            L4 = AR.get([4, 128], CDT); At4 = AR.get([4, 128], BF16); BL4 = S.buf('L4'); BAt4 = S.buf('At4')
            Xb = [AR.get([4, 128], CDT) for _ in range(2)]; Yb = [AR.get([4, 128], CDT) for _ in range(2)]; Tb = [AR.get([4, 128], CDT) for _ in range(2)]
            Tfin = AR.get([4, 128], BF16); BTfin = S.buf('Tfin')
            identc = ident_f if CDT == F32 else ident_b
            BXb = S.bufs(2, 'Xb'); BYb = S.bufs(2, 'Yb'); BTb = S.bufs(2, 'Tb')
            kbg4 = AR.get([4, 128], BF16); ktl4 = AR.get([4, 128], BF16); vb4 = AR.get([4, 128], BF16)
            nw4 = AR.get([4, 128], BF16); qd4 = AR.get([4, 128], BF16); vn4 = AR.get([4, 128], BF16)
            Bkbg = S.buf('kbg'); Bktl = S.buf('ktl'); Bvb4 = S.buf('vb4'); Bnw = S.buf('nw'); Bqd = S.buf('qd'); Bvn = S.buf('vn')
            Sb4 = [AR.get([4, 128], BF16) for _ in range(2)]; BSb = S.bufs(2, 'Sb')
            ot = [AR.get([4, 128])] * 2; Bot = [S.buf('ot')] * 2
            o1t = [AR.get([4, 128])] * 2; Bo1t = [S.buf('o1t')] * 2
            zt = [AR.get([4, 128], BF16)] * 2; Bzt = [S.buf('zt')] * 2
            sq4 = AR.get([4, 128]); Bsq4 = S.buf('sq4'); rs4 = AR.get([4, 128]); Brs4 = S.buf('rs4')
            Er2 = AR.get([4, 128]); BEr2 = S.buf('Er2')
            for g in range(2):
                hs = slice(g * 4, g * 4 + 4)
                if zero_state:
                    S.op('dve', lambda e, hs=hs: e.memset(Sst[:, hs, :], 0.0), reads=[BSst[g]], writes=[BSst[g]])
                S.op('act', lambda e, g=g, hs=hs: e.activation(out=Sb4[g], in_=Sst[:, hs, :], func=AF.Copy), reads=[BSst[g]], writes=[BSb[g]])
            it = 0
            rot = [0]

            def rbank():
                rot[0] += 1
                return 4 + (rot[0] % 3)

            for tt in tiles:
                ts_ = slice(tt * 128, (tt + 1) * 128)
                S.op('pe', lambda e, tt=tt: e.matmul(psum[0][:, 0:8], TRI[:], gT[:, tt, dsl], start=True, stop=True), reads=[Bconst, Bbg], writes=[Bps[0]])
                S.op('pe', lambda e, tt=tt: e.matmul(psum[0][:, 8:16], ones_f[:], gT[:, tt, dsl], start=True, stop=True), reads=[Bconst, Bbg], writes=[Bps[0]])
                S.op('dve', lambda e: e.tensor_copy(st8[:, 0:8], psum[0][:, 0:8]), reads=[Bps[0], Bst8], writes=[Bst8])
                S.op('act', lambda e: e.activation(out=st8[:, 8:16], in_=psum[0][:, 0:8], func=AF.Exp), reads=[Bps[0], Bst8], writes=[Bst8])
                S.op('act', lambda e: e.activation(out=st8[:, 32:40], in_=psum[0][:, 0:8], func=AF.Copy, scale=-1.0), reads=[Bps[0], Bst8], writes=[Bst8])
                S.op('act', lambda e: e.activation(out=st8[:, 16:24], in_=psum[0][:, 8:16], func=AF.Exp), reads=[Bps[0], Bst8], writes=[Bst8])
                S.op('dve', lambda e: e.tensor_tensor(st8[:, 24:32], psum[0][:, 8:16], st8[:, 0:8], op=ALU.subtract), reads=[Bps[0], Bst8], writes=[Bst8])
                S.op('act', lambda e: e.activation(out=st8[:, 24:32], in_=st8[:, 24:32], func=AF.Exp), reads=[Bst8], writes=[Bst8])
                S.op('dve', lambda e, tt=tt: e.tensor_tensor(st8[:, 8:16], st8[:, 8:16], betaT[:, tt, dsl], op=ALU.mult), reads=[Bst8, Bbg], writes=[Bst8])
                if dbg.get('_cut') == 1:
                    return
                for g in range(2):
                    hs = slice(g * 4, g * 4 + 4)
                    h0 = g * 4
                    i = it % 2
                    it += 1
                    S.dma(lambda e, i=i, h0=h0, ts_=ts_: e.dma_start(out=qT4[i], in_=gq_s[h0:h0 + 4, :, ts_].rearrange("h p t -> p h t")), reads=[gp['gq']], writes=[Bld[i]])
                    S.dma(lambda e, i=i, h0=h0, ts_=ts_: e.dma_start(out=kT4[i], in_=gk_s[h0:h0 + 4, :, ts_].rearrange("h p t -> p h t")), reads=[gp['gk']], writes=[Bld[i]], more=True)
                    S.dma(lambda e, i=i, h0=h0, ts_=ts_: e.dma_start(out=kt4[i], in_=gkt_s[h0:h0 + 4, ts_, :].rearrange("h p d -> p h d")), reads=[gp['gkt']], writes=[Bld[i]], more=True)
                    S.dma(lambda e, i=i, h0=h0, ts_=ts_: e.dma_start(out=vt4[i], in_=gvt_s[h0:h0 + 4, ts_, :].rearrange("h p d -> p h d")), reads=[gp['gvt']], writes=[Bld[i]], more=True)
                    g4 = gT[:, tt, ph * 8 + h0:ph * 8 + h0 + 4]
                    be4 = betaT[:, tt, ph * 8 + h0:ph * 8 + h0 + 4]
                    gc4 = st8[:, h0:h0 + 4]; skbg4 = st8[:, 8 + h0:12 + h0]; cd4 = st8[:, 16 + h0:20 + h0]; skt4 = st8[:, 24 + h0:28 + h0]
                    if dbg.get('_cut') == 11:
                        return
                    for hh in range(4):
                        S.op('dve', lambda e, g4=g4, hh=hh: e.tensor_scalar(Gb4[:, hh, :], ones_f[:], g4[:, hh:hh + 1], None, op0=ALU.mult),
                             reads=[Bbg, BGb, Bconst], writes=[BGb])
                    for hh in range(4):
                        S.op('pe', lambda e, hh=hh: e.matmul(psum[1][:, hh * 128:(hh + 1) * 128], Gb4[:, hh, :], TRI[:], start=True, stop=True),
                             reads=[BGb, Bconst], writes=[Bps[1]])
                    if dbg.get('_cut') == 12:
                        if 'pb' in dbg_d:
                            S.op('act', lambda e: e.activation(out=Er, in_=psum[1][:].rearrange("p (a b) -> p a b", b=128), func=AF.Copy), reads=[Bps[1], BEr], writes=[BEr])
                            S.dma(lambda e: e.dma_start(out=dbg_d['pb'], in_=Er.rearrange("p a b -> p (a b)")), reads=[BEr], writes=[Bout], store=True)
                            S.dma(lambda e: e.dma_start(out=dbg_d['st8'], in_=st8), reads=[Bst8], writes=[Bout], store=True)
                            S.dma(lambda e: e.dma_start(out=dbg_d['gb'], in_=Gb4.rearrange("p a b -> p (a b)")), reads=[BGb], writes=[Bout], store=True)
                        return
                    pB = psum[1][:].rearrange("p (a b) -> p a b", b=128)
                    ngc4 = st8[:, 32 + h0:36 + h0]
                    for hh in range(4):
                        S.op('act', lambda e, hh=hh, ngc4=ngc4: e.activation(out=ta[:, hh, :], in_=psum[1][:, hh * 128:(hh + 1) * 128], func=AF.Relu,
                                                                           bias=ngc4[:, hh:hh + 1], scale=1.0), reads=[Bps[1], Bst8, Bta], writes=[Bta])
                        S.op('act', lambda e, hh=hh, gc4=gc4: e.activation(out=tb_[:, hh, :], in_=psum[1][:, hh * 128:(hh + 1) * 128], func=AF.Relu,
                                                                          bias=gc4[:, hh:hh + 1], scale=-1.0), reads=[Bps[1], Bst8, Btb], writes=[Btb])
                    S.op('act', lambda e, pB=pB: e.activation(out=Er, in_=pB, func=AF.Exp), reads=[Bps[1], BEr], writes=[BEr])
                    if dbg.get('_cut') == 13:
                        return
                    S.op('act', lambda e: e.activation(out=ta, in_=ta, func=AF.Exp, scale=-1.0), reads=[Bta], writes=[Bta])
                    S.op('act', lambda e: e.activation(out=tb_, in_=tb_, func=AF.Exp, scale=-1.0), reads=[Btb], writes=[Btb])
                    if dbg.get('_cut') == 15:
                        return
                    S.op('dve', lambda e: e.tensor_tensor(ta, ta, bc_mid(MST[:], 4), op=ALU.mult), reads=[Bta, Bconst], writes=[Bta])
                    S.op('dve', lambda e, be4=be4: e.tensor_tensor(ta, ta, bc_last(be4, 128), op=ALU.mult), reads=[Bta, Bbg], writes=[Bta])
                    S.op('dve', lambda e: e.tensor_tensor(tb_, tb_, bc_mid(MIN[:], 4), op=ALU.mult), reads=[Btb, Bconst], writes=[Btb])
                    if dbg.get('_cut') == 2:
                        return
                    for hh in range(4):
                        S.op('pe', lambda e, hh=hh, i=i: e.matmul(psum[2][:, hh * 128:(hh + 1) * 128], kT4[i][:, hh, :], kT4[i][:, hh, :], start=True, stop=True),
                             reads=[Bld[i]], writes=[Bps[2]])
                    for hh in range(4):
                        S.op('pe', lambda e, hh=hh, i=i: e.matmul(psum[3][:, hh * 128:(hh + 1) * 128], kT4[i][:, hh, :], qT4[i][:, hh, :], start=True, stop=True),
                             reads=[Bld[i]], writes=[Bps[3]])
                    pK = psum[2][:].rearrange("p (a b) -> p a b", b=128)
                    pQ = psum[3][:].rearrange("p (a b) -> p a b", b=128)
                    S.op('act', lambda e, pK=pK: e.activation(out=Er2, in_=pK, func=AF.Copy), reads=[Bps[2], BEr2], writes=[BEr2])
                    S.op('dve', lambda e: e.tensor_tensor(L4, Er2, ta, op=ALU.mult), reads=[BEr2, Bta, BL4], writes=[BL4])
                    S.op('act', lambda e, pQ=pQ: e.activation(out=Er2, in_=pQ, func=AF.Copy), reads=[Bps[3], BEr2], writes=[BEr2])
                    S.op('dve', lambda e: e.tensor_tensor(At4, Er2, tb_, op=ALU.mult), reads=[BEr2, Btb, BAt4], writes=[BAt4])
                    if dbg.get('_cut') == 3:
                        return
                    bk = rbank()
                    pT = psum[bk][:].bitcast(CDT) if CDT != F32 else psum[bk][:]
                    for hh in range(4):
                        S.op('pe', lambda e, hh=hh, pT=pT: e.transpose(pT[:, hh * 128:(hh + 1) * 128], L4[:, hh, :], identc[:]), reads=[BL4, Bconst], writes=[Bps[bk]])
                    pT3 = pT[:, 0:512].rearrange("p (a b) -> p a b", b=128)
                    S.op('act', lambda e, pT3=pT3: e.activation(out=Yb[0], in_=pT3, func=AF.Copy), reads=[Bps[bk], BYb[0]], writes=[BYb[0]])
                    S.op('dve', lambda e: e.tensor_tensor(Tb[0], bc_mid(identc[:], 4), Yb[0], op=ALU.subtract), reads=[BYb[0], Bconst, BTb[0]], writes=[BTb[0]])
                    if dbg.get('_cut') == 4:
                        return
                    Xc, BXc = L4, BL4
                    Yc, BYc = Yb[0], BYb[0]
                    Tc, BTc = Tb[0], BTb[0]
                    for lev in range(1, 7):
                        Xn, BXn = Xb[lev % 2], BXb[lev % 2]
                        Yn, BYn = Yb[lev % 2], BYb[lev % 2]
                        Tn, BTn = Tb[lev % 2], BTb[lev % 2]
                        bx = rbank()
                        for hh in range(4):
                            S.op('pe', lambda e, hh=hh, bx=bx, Xc=Xc, Yc=Yc: e.matmul(psum[bx][:, hh * 128:(hh + 1) * 128], Yc[:, hh, :], Xc[:, hh, :], start=True, stop=True),
                                 reads=[BXc, BYc], writes=[Bps[bx]])
                        if lev < 6:
                            by = rbank()
                            for hh in range(4):
                                S.op('pe', lambda e, hh=hh, by=by, Xc=Xc, Yc=Yc: e.matmul(psum[by][:, hh * 128:(hh + 1) * 128], Xc[:, hh, :], Yc[:, hh, :], start=True, stop=True),
                                     reads=[BXc, BYc], writes=[Bps[by]])
                        S.op('act', lambda e, bx=bx, Xn=Xn: e.activation(out=Xn, in_=psum[bx][:].rearrange("p (a b) -> p a b", b=128), func=AF.Copy),
                             reads=[Bps[bx], BXn], writes=[BXn])
                        if lev < 6:
                            S.op('act', lambda e, by=by, Yn=Yn: e.activation(out=Yn, in_=psum[by][:].rearrange("p (a b) -> p a b", b=128), func=AF.Copy), reads=[Bps[by], BYn], writes=[BYn])
                        bt = rbank()
                        for hh in range(4):
                            S.op('pe', lambda e, hh=hh, bt=bt, Xn=Xn, Tc=Tc: e.matmul(psum[bt][:, hh * 128:(hh + 1) * 128], Xn[:, hh, :], Tc[:, hh, :], start=True, stop=True),
                                 reads=[BXn, BTc], writes=[Bps[bt]])
                        S.op('act', lambda e, bt=bt: e.activation(out=Er2, in_=psum[bt][:].rearrange("p (a b) -> p a b", b=128), func=AF.Copy), reads=[Bps[bt], BEr2], writes=[BEr2])
                        S.op('dve', lambda e, Tn=Tn, Tc=Tc: e.tensor_tensor(Tn, Er2, Tc, op=ALU.add), reads=[BEr2, BTc, BTn], writes=[BTn])
                        Xc, BXc, Yc, BYc, Tc, BTc = Xn, BXn, Yn, BYn, Tn, BTn
                    if dbg.get('_cut') == 5:
                        return
                    S.op('act', lambda e, Tc=Tc: e.activation(out=Tfin, in_=Tc, func=AF.Copy), reads=[BTc, BTfin], writes=[BTfin])
                    Tc, BTc = Tfin, BTfin
                    S.op('dve', lambda e, i=i, skbg4=skbg4: e.tensor_tensor(kbg4, kt4[i], bc_last(skbg4, 128), op=ALU.mult), reads=[Bld[i], Bst8, Bkbg], writes=[Bkbg])
                    S.op('dve', lambda e, i=i, skt4=skt4: e.tensor_tensor(ktl4, kt4[i], bc_last(skt4, 128), op=ALU.mult), reads=[Bld[i], Bst8, Bktl], writes=[Bktl])
                    S.op('dve', lambda e, i=i, be4=be4: e.tensor_tensor(vb4, vt4[i], bc_last(be4, 128), op=ALU.mult), reads=[Bld[i], Bbg, Bvb4], writes=[Bvb4])
                    S.op('dve', lambda e, i=i: e.tensor_tensor(qd4, qT4[i], Er, op=ALU.mult), reads=[Bld[i], BEr, Bqd], writes=[Bqd])
                    bw = rbank()
                    for hh in range(4):
                        S.op('pe', lambda e, hh=hh, bw=bw, Tc=Tc: e.matmul(psum[bw][:, hh * 128:(hh + 1) * 128], kbg4[:, hh, :], Tc[:, hh, :], start=True, stop=True),
                             reads=[Bkbg, BTc], writes=[Bps[bw]])
                    S.op('act', lambda e, bw=bw: e.activation(out=nw4, in_=psum[bw][:].rearrange("p (a b) -> p a b", b=128), func=AF.Copy, scale=-1.0),
                         reads=[Bps[bw], Bnw], writes=[Bnw])
                    if dbg.get('_cut') == 6:
                        return
                    for hh in range(4):
                        S.op('pe', lambda e, hh=hh, Tc=Tc: e.matmul(psum[1][:, hh * 128:(hh + 1) * 128], Tc[:, hh, :], vb4[:, hh, :], start=True, stop=False),
                             reads=[BTc, Bvb4], writes=[Bps[1]])
                        S.op('pe', lambda e, hh=hh, g=g: e.matmul(psum[1][:, hh * 128:(hh + 1) * 128], nw4[:, hh, :], Sb4[g][:, hh, :], start=False, stop=True),
                             reads=[Bnw, BSb[g]], writes=[Bps[1]])
                    S.op('act', lambda e: e.activation(out=vn4, in_=psum[1][:].rearrange("p (a b) -> p a b", b=128), func=AF.Copy), reads=[Bps[1], Bvn], writes=[Bvn])
                    for hh in range(4):
                        S.op('pe', lambda e, hh=hh, g=g: e.matmul(psum[2][:, hh * 128:(hh + 1) * 128], Sb4[g][:, hh, :], qd4[:, hh, :], start=True, stop=False),
                             reads=[BSb[g], Bqd], writes=[Bps[2]])
                        S.op('pe', lambda e, hh=hh: e.matmul(psum[2][:, hh * 128:(hh + 1) * 128], vn4[:, hh, :], At4[:, hh, :], start=False, stop=True),
                             reads=[Bvn, BAt4], writes=[Bps[2]])
                    for hh in range(4):
                        S.op('pe', lambda e, hh=hh: e.matmul(psum[3][:, hh * 128:(hh + 1) * 128], ktl4[:, hh, :], vn4[:, hh, :], start=True, stop=True),
                             reads=[Bktl, Bvn], writes=[Bps[3]])
                    if dbg.get('_cut') == 7:
                        return
                    S.op('dve', lambda e, hs=hs, cd4=cd4: e.tensor_tensor(Sst[:, hs, :], Sst[:, hs, :], bc_last(cd4, 128), op=ALU.mult), reads=[BSst[g], Bst8], writes=[BSst[g]])
                    S.op('act', lambda e: e.activation(out=Er2, in_=psum[3][:].rearrange("p (a b) -> p a b", b=128), func=AF.Copy), reads=[Bps[3], BEr2], writes=[BEr2])
                    S.op('dve', lambda e, hs=hs: e.tensor_tensor(Sst[:, hs, :], Sst[:, hs, :], Er2, op=ALU.add), reads=[BSst[g], BEr2], writes=[BSst[g]])
                    S.op('act', lambda e, g=g, hs=hs: e.activation(out=Sb4[g], in_=Sst[:, hs, :], func=AF.Copy), reads=[BSst[g], BSb[g]], writes=[BSb[g]])
                    if dbg.get('_cut') == 8:
                        return
                    pO = psum[2][:].rearrange("p (a b) -> p a b", b=128)
                    if ph == 0:
                        S.op('act', lambda e, i=i, pO=pO: e.activation(out=ot[i], in_=pO, func=AF.Copy), reads=[Bps[2], Bot[i]], writes=[Bot[i]])
                        S.dma(lambda e, i=i, h0=h0, ts_=ts_: e.dma_start(out=o1_s[h0:h0 + 4, :, ts_].rearrange("h p t -> p h t"), in_=ot[i]),
                              reads=[Bot[i]], writes=[Bo1], store=True)
                    else:
                        S.dma(lambda e, i=i, h0=h0, ts_=ts_: e.dma_start(out=o1t[i], in_=o1_s[h0:h0 + 4, :, ts_].rearrange("h p t -> p h t")), reads=[Bo1], writes=[Bo1t[i]])
                        S.dma(lambda e, i=i, h0=h0, ts_=ts_: e.dma_start(out=zt[i], in_=z_s[h0:h0 + 4, :, ts_].rearrange("h p t -> p h t")), reads=[sc['z']], writes=[Bzt[i]])
                        S.op('act', lambda e, i=i, pO=pO: e.activation(out=ot[i], in_=pO, func=AF.Copy), reads=[Bps[2], Bot[i]], writes=[Bot[i]])
                        S.op('dve', lambda e, i=i: e.tensor_tensor(ot[i], ot[i], o1t[i], op=ALU.add), reads=[Bo1t[i], Bot[i]], writes=[Bot[i]])
                        S.op('act', lambda e, i=i: e.activation(out=sq4, in_=ot[i], func=AF.Square), reads=[Bot[i], Bsq4], writes=[Bsq4])
                        S.op('pe', lambda e: e.matmul(psum[7][:], ones_f[:], sq4.rearrange("p a b -> p (a b)"), start=True, stop=True), reads=[Bsq4, Bconst], writes=[Bps[7]])
                        rsqrt_ops(rs4.rearrange("p a b -> p (a b)"), psum[7][:], 1.0 / 128, [Bps[7], Brs4], [Brs4])
                        S.op('dve', lambda e, i=i: e.scalar_tensor_tensor(out=ot[i], in0=ot[i], scalar=lyr[:, L_GNW:L_GNW + 1], in1=rs4, op0=ALU.mult, op1=ALU.mult),
                             reads=[Bot[i], Blyr, Brs4], writes=[Bot[i]])
                        S.op('dve', lambda e, i=i, h0=h0, ts_=ts_: e.tensor_tensor(ygT[:, h0:h0 + 4, ts_], ot[i], zt[i], op=ALU.mult),
                             reads=[Bot[i], Bzt[i], Byg[g]], writes=[Byg[g]])


        def exchange(l, sc):
            AR.reset()
            Bss = S.buf('st_snd'); Bsr = S.buf('st_rcv')
            for g in range(2):
                S.dma(lambda e, g=g: e.dma_start(out=st_snd[:, g * 512:(g + 1) * 512], in_=Sst[:, g * 4:g * 4 + 4, :].rearrange("p a b -> p (a b)")),
                      reads=[BSst[g]], writes=[Bss], store=True)
            S.dma(lambda e: allreduce(e, st_snd, st_rcv), reads=[Bss], writes=[Bsr], q='pool', inc=1)
            sr = AR.get([1024]); Bsrt = S.buf('sr')
            S.dma(lambda e: e.dma_start(out=sr, in_=st_rcv[:, :]), reads=[Bsr], writes=[Bsrt])
            for g in range(2):
                gs = slice(g * 512, (g + 1) * 512)
                Sg = Sst[:, g * 4:g * 4 + 4, :].rearrange("p a b -> p (a b)")
                S.op('dve', lambda e, gs=gs, Sg=Sg: e.tensor_tensor(Sg, sr[:, gs], Sg, op=ALU.subtract), reads=[Bsrt, BSst[g]], writes=[BSst[g]])

        def attention(l, sc, nparts):
            AR.reset()
            q_sb = [AR.get([NT], BF16) for _ in range(2)]; k_sb = [AR.get([NK], BF16) for _ in range(2)]
            v_sb = [AR.get([KT, 128], BF16) for _ in range(2)]; Bqkv = S.bufs(2, 'qkv')
            pTt = [AR.get([CH], BF16) for _ in range(3)]; BpT = S.bufs(3, 'pT')
            om = [AR.get([CH]) for _ in range(2)]; Bom = S.bufs(2, 'om')
            rd = AR.get([CH]); Brd = S.buf('rd')
            oc = AR.get([CH]); Boc = S.buf('oc'); sqa = AR.get([CH]); Bsqa = S.buf('sqa'); rsa = AR.get([CH]); Brsa = S.buf('rsa')
            post = 1.0 - lam_init[l]
            pc = 0
            for h in range(NH):
                i = h % 2
                S.dma(lambda e, i=i, h=h: e.dma_start(out=q_sb[i], in_=qT_s[h, :, :]), reads=[sc['q']], writes=[Bqkv[i]])
                ksrc = (kv_rcv_k if PAIR else kT_s).rearrange("(r h p) t -> r h p t", h=NH, p=128)
                vsrc = (kv_rcv_v if PAIR else v_s).rearrange("(r t) v -> r t v", t=NT)
                for part in range(nparts):
                    S.dma(lambda e, i=i, h=h, part=part, ksrc=ksrc: e.dma_start(out=k_sb[i][:, part * NT:(part + 1) * NT], in_=ksrc[part, h, :, :]),
                          reads=[sc['k']], writes=[Bqkv[i]], more=True)
                    S.dma(lambda e, i=i, h=h, part=part, vsrc=vsrc: e.dma_start(out=v_sb[i][:, part * TT:(part + 1) * TT, :],
                                                                     in_=vsrc[part, :, h * 128:(h + 1) * 128].rearrange("(t p) v -> p t v", p=128)),
                          reads=[sc['v']], writes=[Bqkv[i]], more=True)
                for n in range(NCH):
                    cs = slice(n * CH, (n + 1) * CH)
                    for m in range(2):
                        ms = slice(m * 64, (m + 1) * 64)
                        for kt in range(KT):
                            sbk = kt % 2
                            r = pc % 3
                            pc += 1
                            S.op('pe', lambda e, i=i, ms=ms, kt=kt, sbk=sbk, cs=cs: e.matmul(P(sbk, CH), k_sb[i][ms, kt * 128:(kt + 1) * 128], q_sb[i][ms, cs], start=True, stop=True),
                                 reads=[Bqkv[i]], writes=[Bps[sbk]])
                            S.op('act', lambda e, r=r, sbk=sbk: e.activation(out=pTt[r], in_=P(sbk, CH), func=AF.Exp), reads=[Bps[sbk]], writes=[BpT[r]])
                            S.op('pe', lambda e, i=i, r=r, kt=kt, m=m: e.matmul(P(2 + m, CH), v_sb[i][:, kt, :], pTt[r], start=(kt == 0), stop=(kt == KT - 1)),
                                 reads=[Bqkv[i], BpT[r]], writes=[Bps[2 + m]])
                            S.op('pe', lambda e, r=r, kt=kt, m=m: e.matmul(P(4 + m, CH), ones_b[:], pTt[r], start=(kt == 0), stop=(kt == KT - 1)),
                                 reads=[Bconst, BpT[r]], writes=[Bps[4 + m]])
                        S.op('act', lambda e, m=m: e.activation(out=rd, in_=P(4 + m, CH), func=AF.Copy), reads=[Bps[4 + m], Brd], writes=[Brd])
                        S.op('dve', lambda e: e.reciprocal(rd, rd), reads=[Brd], writes=[Brd])
                        S.op('act', lambda e, m=m: e.activation(out=om[m], in_=P(2 + m, CH), func=AF.Copy), reads=[Bps[2 + m], Bom[m]], writes=[Bom[m]])
                        S.op('dve', lambda e, m=m: e.tensor_tensor(om[m], om[m], rd, op=ALU.mult), reads=[Brd, Bom[m]], writes=[Bom[m]])
                    S.op('dve', lambda e: e.scalar_tensor_tensor(out=oc, in0=om[1], scalar=lyr[:, L_LAMC:L_LAMC + 1], in1=om[0], op0=ALU.mult, op1=ALU.add),
                         reads=[Bom[0], Bom[1], Blyr, Boc], writes=[Boc])
                    S.op('act', lambda e: e.activation(out=sqa, in_=oc, func=AF.Square), reads=[Boc, Bsqa], writes=[Bsqa])
                    S.op('pe', lambda e: e.matmul(P(6, CH), ones_f[:], sqa, start=True, stop=True), reads=[Bsqa, Bconst], writes=[Bps[6]])
                    rsqrt_ops(rsa, P(6, CH), 1.0 / 128, [Bps[6], Brsa], [Brsa], post=post)
                    S.op('dve', lambda e, h=h, cs=cs: e.scalar_tensor_tensor(out=ydT[:, h, cs], in0=oc, scalar=lyr[:, L_SUB:L_SUB + 1], in1=rsa, op0=ALU.mult, op1=ALU.mult),
                         reads=[Boc, Blyr, Brsa, Byd[h]], writes=[Byd[h]])

        def out_proj(l, sc):
            AR.reset()
            mg = AR.get([KC, NT], BF16); Bmg = [[S.buf() for _ in range(NCH)] for _ in range(KC)]
            gd = [AR.get([CH], BF16) for _ in range(2)]; gg = [AR.get([CH], BF16) for _ in range(2)]; Bgt = S.bufs(2, 'gt')
            t1 = [AR.get([CH]) for _ in range(2)]; Bt1 = S.bufs(2, 't1')
            t2 = [AR.get([CH]) for _ in range(2)]; Bt2 = S.bufs(2, 't2')
            it = 0
            for j in range(4):
                wd_, Bwd = WS.get(('bd', l, j))
                wg_, Bwg = WS.get(('bg', l, j))
                for m in range(4):
                    mc = j * 4 + m
                    for n in range(NCH):
                        cs = slice(n * CH, (n + 1) * CH)
                        i = it % 2
                        it += 1
                        p1, p2 = (0, 1) if i == 0 else (2, 3)
                        for kc in range(8):
                            S.op('pe', lambda e, kc=kc, m=m, cs=cs, p1=p1, wd_=wd_: e.matmul(P(p1, CH), wd_[:, kc, m * 128:(m + 1) * 128], ydT[:, kc, cs], start=(kc == 0), stop=(kc == 7)),
                                 reads=[Bwd, Byd[kc]], writes=[Bps[p1]])
                        for kc in range(8):
                            S.op('pe', lambda e, kc=kc, m=m, cs=cs, p2=p2, wg_=wg_: e.matmul(P(p2, CH), wg_[:, kc, m * 128:(m + 1) * 128], ygT[:, kc, cs], start=(kc == 0), stop=(kc == 7)),
                                 reads=[Bwg, Byg[kc // 4]], writes=[Bps[p2]])
                        S.dma(lambda e, i=i, mc=mc, cs=cs: e.dma_start(out=gd[i], in_=gg_s[mc, :, cs]), reads=[sc['gg']], writes=[Bgt[i]])
                        S.dma(lambda e, i=i, mc=mc, cs=cs: e.dma_start(out=gg[i], in_=gg_s[16 + mc, :, cs]), reads=[sc['gg']], writes=[Bgt[i]], more=True)
                        S.op('act', lambda e, i=i, p1=p1: e.activation(out=t1[i], in_=P(p1, CH), func=AF.Copy), reads=[Bps[p1], Bt1[i]], writes=[Bt1[i]])
                        S.op('act', lambda e, i=i, p2=p2: e.activation(out=t2[i], in_=P(p2, CH), func=AF.Copy), reads=[Bps[p2], Bt2[i]], writes=[Bt2[i]])
                        S.op('dve', lambda e, i=i: e.tensor_tensor(t1[i], t1[i], gd[i], op=ALU.mult), reads=[Bgt[i], Bt1[i]], writes=[Bt1[i]])
                        S.op('dve', lambda e, i=i: e.tensor_tensor(t2[i], t2[i], gg[i], op=ALU.mult), reads=[Bgt[i], Bt2[i]], writes=[Bt2[i]])
                        S.op('dve', lambda e, i=i, mc=mc, cs=cs: e.tensor_tensor(mg[:, mc, cs], t1[i], t2[i], op=ALU.add), reads=[Bt1[i], Bt2[i], Bmg[mc][n]], writes=[Bmg[mc][n]])
            it = 0
            for j in range(4):
                w, Bw = WS.get(('out', l, j))
                for m in range(4):
                    mc = j * 4 + m
                    for n in range(NCH):
                        cs = slice(n * CH, (n + 1) * CH)
                        pb = 4 + it % 2
                        it += 1
                        for kc in range(KC):
                            S.op('pe', lambda e, kc=kc, m=m, cs=cs, pb=pb, w=w: e.matmul(P(pb, CH), w[:, kc, m * 128:(m + 1) * 128], mg[:, kc, cs], start=(kc == 0), stop=(kc == KC - 1)),
                                 reads=[Bw, Bmg[kc][n]], writes=[Bps[pb]])
                        i2 = it % 2
                        S.op('act', lambda e, mc=mc, pb=pb, i2=i2: e.activation(out=t1[i2], in_=P(pb, CH), func=AF.Copy, scale=mod[:, 32 + mc:33 + mc]), reads=[Bps[pb], Bmod, Bt1[i2]], writes=[Bt1[i2]])
                        S.op('dve', lambda e, mc=mc, cs=cs, i2=i2: e.tensor_tensor(xT[:, mc, cs], xT[:, mc, cs], t1[i2], op=ALU.add), reads=[Bt1[i2], BxT[mc][n]], writes=[BxT[mc][n]])

        def ffn(l):
            modnorm(1)
            AR.reset()
            aT = AR.get([FC, CH], BF16); BaT = S.bufs(FC, 'aT')
            sg_off = AR.off
            sg = [AR.get([CH], BF16) for _ in range(2)]; Bsg = S.bufs(2, 'sg')
            uc = [AR.get([CH], BF16) for _ in range(2)]; Buc = S.bufs(2, 'uc')
            xdf = arena[:, sg_off // 4:sg_off // 4 + CH]
            it = 0
            for n in range(NCH):
                cs = slice(n * CH, (n + 1) * CH)
                for j in range(11):
                    wg_, Bwg = WS.get(('upg', l, n, j))
                    wu_, Bwu = WS.get(('upu', l, n, j))
                    for m in range(4):
                        jc = j * 4 + m
                        i = it % 2
                        it += 1
                        p1, p2 = (0, 1) if i == 0 else (2, 3)
                        for kc in range(KC):
                            S.op('pe', lambda e, kc=kc, m=m, p1=p1, wg_=wg_: e.matmul(P(p1, CH), wg_[:, kc, m * 128:(m + 1) * 128], hT[:, kc, cs], start=(kc == 0), stop=(kc == KC - 1)),
                                 reads=[Bwg, BhT[n]], writes=[Bps[p1]])
                        for kc in range(KC):
                            S.op('pe', lambda e, kc=kc, m=m, p2=p2, wu_=wu_: e.matmul(P(p2, CH), wu_[:, kc, m * 128:(m + 1) * 128], hT[:, kc, cs], start=(kc == 0), stop=(kc == KC - 1)),
                                 reads=[Bwu, BhT[n]], writes=[Bps[p2]])
                        S.op('act', lambda e, i=i, p1=p1: e.activation(out=sg[i], in_=P(p1, CH), func=AF.Silu), reads=[Bps[p1], Bsg[i]], writes=[Bsg[i]])
                        S.op('act', lambda e, i=i, p2=p2: e.activation(out=uc[i], in_=P(p2, CH), func=AF.Copy), reads=[Bps[p2], Buc[i]], writes=[Buc[i]])
                        S.op('dve', lambda e, i=i, jc=jc: e.tensor_tensor(aT[:, jc, :], sg[i], uc[i], op=ALU.mult), reads=[Bsg[i], Buc[i], BaT[jc]], writes=[BaT[jc]])
                for m in range(KC):
                    w, Bw = WS.get(('dn', l, n, m))
                    pb = 4 + m % 2
                    for kc in range(FC):
                        S.op('pe', lambda e, kc=kc, pb=pb, w=w: e.matmul(P(pb, CH), w[:, kc, :], aT[:, kc, :], start=(kc == 0), stop=(kc == FC - 1)),
                             reads=[Bw, BaT[kc]], writes=[Bps[pb]])
                    S.op('act', lambda e, m=m, pb=pb: e.activation(out=xdf, in_=P(pb, CH), func=AF.Copy, scale=mod[:, 80 + m:81 + m]), reads=[Bps[pb], Bmod], writes=[Bsg[0], Bsg[1]])
                    S.op('dve', lambda e, m=m: e.tensor_tensor(xT[:, m, cs], xT[:, m, cs], xdf, op=ALU.add), reads=[Bsg[0], Bsg[1], BxT[m][n]], writes=[BxT[m][n]])


        setup()
        stage = dbg.get('_stage', None)
        for l in range(DEPTH):
            load_layer_params(l)
            ada_mod(l)
            modnorm(0)
            sc = in_proj(l)
            if stage == 'inproj':
                break
            gp = gdn_prep(l, sc)
            if stage == 'prep':
                break
            Bo1 = S.buf('o1_s')
            gdn_scan(l, 0, sc, gp, Bo1, True)
            if stage == 'scan0':
                break
            if PAIR:
                exchange(l, sc)
            attention(l, sc, 2 if PAIR else 1)
            if stage == 'attn':
                break
            gdn_scan(l, 1, sc, gp, Bo1, not PAIR)
            if stage == 'mixer':
                break
            out_proj(l, sc)
            if stage == 'outproj':
                break
            ffn(l)
        def dump_sb(name, src, rd):
            S.dma(lambda e: e.dma_start(out=dbg_d[name], in_=src), reads=rd, writes=[Bout], store=True)
        S.barrier()
        if 'mod' in dbg_d:
            dump_sb('mod', mod[:], [Bmod])
        if 'cos' in dbg_d:
            dump_sb('cos', cosT[:], [Brope]); dump_sb('sin', sinT[:], [Brope])
        if 'beta' in dbg_d:
            dump_sb('beta', betaT[:], [Bbg]); dump_sb('g', gT[:], [Bbg])
        if 'hT' in dbg_d:
            AR.reset()
            t = AR.get([NT]); Bt_ = S.buf()
            for kc in range(KC):
                S.op('dve', lambda e, kc=kc: e.tensor_copy(t, hT[:, kc, :]), reads=BhT, writes=[Bt_])
                S.dma(lambda e, kc=kc: e.dma_start(out=dbg_d['hT'][kc * 128:(kc + 1) * 128, :], in_=t), reads=[Bt_], writes=[Bout], store=True)
        for nm, view in (('yd', ydT), ('yg', ygT)):
            if nm in dbg_d:
                AR.reset()
                t = AR.get([NT]); Bt_ = S.buf()
                for kc in range(8):
                    S.op('dve', lambda e, kc=kc, view=view: e.tensor_copy(t, view[:, kc, :]), reads=Byd + Byg, writes=[Bt_])
                    S.dma(lambda e, kc=kc, nm=nm: e.dma_start(out=dbg_d[nm][kc * 128:(kc + 1) * 128, :], in_=t), reads=[Bt_], writes=[Bout], store=True)
        ov = out_d.rearrange("(k p) n -> p k n", p=128)
        for kc in range(KC):
            S.dma(lambda e, kc=kc: e.dma_start(out=ov[:, kc, :], in_=xT[:, kc, :]), reads=BxT[kc], writes=[Bout], store=True)
        S.final_wait('sp', [Bout])
        S.emit()
    return nc


def prep_inputs(inp, NT, PAIR, DEPTH, n_cores=8):
    f32 = np.float32
    x = np.asarray(inp['x'], f32)
    B, SEQ, _ = x.shape
    c = np.asarray(inp['c'], f32)
    pos = np.asarray(inp['positions']).astype(np.int32)
    w_in = np.asarray(inp['w_in'], f32)
    conv_w = np.asarray(inp['gdn_conv_w'], f32)
    a_log = np.asarray(inp['gdn_a_log'], f32)
    dt_bias = np.asarray(inp['gdn_dt_bias'], f32)
    invf = (np.float32(10000.0) ** (-np.arange(32, dtype=np.float32) * np.float32(2.0) / np.float32(64))).astype(f32)
    shared = {
        'invf': np.ascontiguousarray(np.broadcast_to(invf[None, :], (128, 32))),
        'ada_w': np.asarray(inp['ada_w'], f32),
        'ada_bT': np.ascontiguousarray(np.asarray(inp['ada_b'], f32).reshape(DEPTH, 96, 128).transpose(0, 2, 1)),
        'nmwT': np.ascontiguousarray(np.asarray(inp['norm_mix_w'], f32).reshape(DEPTH, 16, 128).transpose(0, 2, 1)),
        'nfwT': np.ascontiguousarray(np.asarray(inp['norm_ffn_w'], f32).reshape(DEPTH, 16, 128).transpose(0, 2, 1)),
        'w_in': w_in,
        'qn_w': np.asarray(inp['diff_qn_w'], f32),
        'kn_w': np.asarray(inp['diff_kn_w'], f32),
        'lamv': np.ascontiguousarray(np.asarray(inp['diff_lambda'], f32).reshape(DEPTH, 256)),
        'sublnT': np.ascontiguousarray(np.asarray(inp['diff_subln_w'], f32).reshape(DEPTH, 128, 1)),
        'gnwT': np.ascontiguousarray(np.asarray(inp['gdn_norm_w'], f32).reshape(DEPTH, 128, 1)),
        'w_bd': np.asarray(inp['w_branch_diff'], f32),
        'w_bg': np.asarray(inp['w_branch_gdn'], f32),
        'w_out': np.asarray(inp['w_out'], f32),
        'w_up': np.asarray(inp['ffn_w_up'], f32),
        'w_down': np.asarray(inp['ffn_w_down'], f32),
    }
    role = {}
    for r in (0, 1):
        dmap = [0, 1] if r == 0 else [1, 0]
        cols = [7168 + d * 8 + h for d in dmap for h in range(8)] + [7184 + d * 8 + h for d in dmap for h in range(8)]
        cw = conv_w if r == 0 else conv_w[:, ::-1, :]
        role[r] = {
            'w_dir': np.ascontiguousarray(w_in[:, :, cols]),
            'a_log': np.ascontiguousarray(a_log[:, dmap, :].reshape(DEPTH, 16)),
            'dt_bias': np.ascontiguousarray(dt_bias[:, dmap, :].reshape(DEPTH, 16)),
            'convT': np.ascontiguousarray(cw.reshape(DEPTH, 5, 24, 128).transpose(0, 3, 2, 1)),
            'sel': np.ascontiguousarray(np.broadcast_to(np.array([[0.0, 1.0, 1.0, 0.0] if r == 0 else [1.0, 0.0, 0.0, 1.0]], f32), (128, 4))),
        }
    maps = []
    meta = []
    for core in range(n_cores):
        if PAIR:
            b, r = core // 2, core % 2
            tok = np.arange(NT) if r == 0 else (SEQ - 1 - np.arange(NT))
        else:
            b, r = core % B, 0
            tok = np.arange(NT)
        m = dict(shared)
        m.update(role[r])
        m['xT'] = np.ascontiguousarray(x[b, tok, :].T)
        m['posT'] = np.ascontiguousarray(pos[b, tok].reshape(NT // 128, 128).T)
        m['cT'] = np.ascontiguousarray(c[b].reshape(16, 128).T)
        maps.append(m)
        meta.append((b, tok))
    return maps, meta


_PROG_CACHE = {}


def kernel(**inputs):
    x = np.asarray(inputs['x'])
    B, SEQ, _ = x.shape
    DEPTH = int(np.asarray(inputs['ada_w']).shape[0])
    NT = SEQ // 2
    key = (NT, DEPTH)
    if key not in _PROG_CACHE:
        _PROG_CACHE[key] = build_program(NT, DEPTH, True, {})
    nc = _PROG_CACHE[key]
    maps, meta = prep_inputs(inputs, NT, True, DEPTH, n_cores=8)
    res = run_bass_kernel_spmd(nc, maps, core_ids=list(range(8)))
    out = np.empty((B, SEQ, D), np.float32)
    for core in range(8):
        b, tok = meta[core]
        out[b, tok, :] = np.asarray(res.results[core]['outT'], np.float32).T
    return out
```

```python
import math
import types
from contextlib import ExitStack

import numpy as np
import concourse.bass as bass
import concourse.mybir as mybir
from concourse.bass_utils import run_bass_kernel_spmd

F32 = mybir.dt.float32
BF16 = mybir.dt.bfloat16
I32 = mybir.dt.int32
AF = mybir.ActivationFunctionType
ALU = mybir.AluOpType
AX = mybir.AxisListType

D = 2048
KC = 16
NH = 8
FFN = 5632
FC = 44
IN_COLS = 11296
EPS = 1e-6
COMPUTE = ('pe', 'act', 'dve', 'pool')


class Buf:
    __slots__ = ('name', 'w', 'r', 'sem', 'cnt')

    def __init__(self, name):
        self.name = name
        self.w = []
        self.r = []
        self.sem = None
        self.cnt = 0


def _snap(fn):
    if fn is None or fn.__closure__ is None:
        return fn
    cells = []
    for c in fn.__closure__:
        try:
            cells.append(types.CellType(c.cell_contents))
        except ValueError:
            cells.append(c)
    return types.FunctionType(fn.__code__, fn.__globals__, fn.__name__, fn.__defaults__, tuple(cells))


class Sched:
    def __init__(self, nc, stack):
        self.nc = nc
        self.stack = stack
        self.ops = {e: [] for e in COMPUTE + ('sp',)}
        self.seq = {e: 0 for e in COMPUTE}
        self.esem = {e: stack.enter_context(nc.semaphore('s_' + e)) for e in COMPUTE}
        self.waited = {e: {} for e in COMPUTE + ('sp',)}
        self.nsem = 0
        self.nbuf = 0
        self.dbufs = []
        self.named = {}

    def buf(self, name=None):
        self.nbuf += 1
        if name is None:
            return Buf('b%d' % self.nbuf)
        if name not in self.named:
            self.named[name] = Buf(name)
        return self.named[name]

    def bufs(self, n, name='b'):
        return [self.buf('%s%d' % (name, i)) for i in range(n)]

    def _dsem(self, b):
        if b.sem is None:
            b.sem = self.stack.enter_context(self.nc.semaphore('d%d' % self.nsem))
            self.nsem += 1
            self.dbufs.append(b)
        return b.sem

    def _deps(self, eng, reads, writes):
        evs = []
        for b in reads:
            evs.extend(b.w)
        for b in writes:
            evs.extend(b.w)
            evs.extend(b.r)
        waits = {}
        wd = self.waited[eng]
        for ev in evs:
            sem, val, src = ev[0], ev[1], ev[2]
            if src == 'pe' and eng == 'pe':
                continue
            if src == 'dma':
                val = ev[3].cnt
            if wd.get(sem, 0) >= val:
                continue
            if waits.get(sem, 0) < val:
                waits[sem] = val
        for sem, val in waits.items():
            wd[sem] = val
        return list(waits.items())

    def op(self, eng, fn, reads=(), writes=()):
        waits = self._deps(eng, reads, writes)
        self.seq[eng] += 1
        ev = (self.esem[eng], self.seq[eng], eng)
        for b in reads:
            b.r.append(ev)
        for b in writes:
            b.w = [ev]
            b.r = []
        self.ops[eng].append((waits, _snap(fn), self.esem[eng], 1))

    def dma(self, fn, reads=(), writes=(), q='sp', more=False, store=False, inc=16):
        d = writes[0]
        if more or store:
            saved = d.w
            d.w = []
        waits = self._deps(q, reads, writes)
        sbuf_side = reads[0] if store else d
        sem = self._dsem(sbuf_side)
        sbuf_side.cnt += inc
        ev = (sem, sbuf_side.cnt, 'dma', sbuf_side)
        for b in reads:
            b.r.append(ev)
        if more or store:
            d.w = [x for x in saved if x[0] is not sem] + [ev]
        else:
            d.w = [ev]
            d.r = []
        self.ops[q].append((waits, _snap(fn), sem, inc))

    def barrier(self):
        tgt = [(self.esem[e], self.seq[e]) for e in COMPUTE if self.seq[e] > 0]
        tgt += [(b.sem, b.cnt) for b in self.dbufs if b.cnt > 0 and not b.name.startswith('wt')]
        for eng in ('pe', 'act', 'dve', 'sp'):
            wd = self.waited[eng]
            waits = []
            for (s, v) in tgt:
                if eng in COMPUTE and s is self.esem[eng]:
                    continue
                if wd.get(s, 0) < v:
                    wd[s] = v
                    waits.append((s, v))
            if waits:
                self.ops[eng].append((waits, None, None, 0))

    def final_wait(self, eng, bufs):
        waits = self._deps(eng, bufs, ())
        self.ops[eng].append((waits, None, None, 0))

    def emit(self):
        nc = self.nc
        with nc.Block() as block:
            def run(e, name):
                for (waits, fn, sem, inc) in self.ops[name]:
                    for (s, v) in waits:
                        e.wait_ge(s, v)
                    if fn is not None:
                        fn(e).then_inc(sem, inc)

            @block.tensor
            def _(e):
                run(e, 'pe')

            @block.scalar
            def _(e):
                run(e, 'act')

            @block.vector
            def _(e):
                run(e, 'dve')

            @block.gpsimd
            def _(e):
                run(e, 'pool')

            @block.sync
            def _(e):
                run(e, 'sp')


def bc_mid(ap2, n):
    P, Fd = ap2.shape
    return ap2.unsqueeze(1).to_broadcast([P, n, Fd])


def bc_last(ap2, n):
    P, G = ap2.shape
    return ap2.unsqueeze(2).to_broadcast([P, G, n])


def build_program(NT, DEPTH, PAIR, dbg=None):
    dbg = dbg or {}
    CDT = BF16 if dbg.get('_chain16') else F32
    TT = NT // 128
    CH = min(512, NT)
    NCH = NT // CH
    NK = 2 * NT if PAIR else NT
    KT = NK // 128
    lam_init = [0.8 - 0.6 * math.exp(-0.3 * l) for l in range(DEPTH)]

    nc = bass.Bass("TRN2", target_bir_lowering=False)

    def din(name, shape, dt=F32):
        return nc.dram_tensor(name, list(shape), dt, kind="ExternalInput").ap()

    xT_d = din("xT", [D, NT])
    pos_d = din("posT", [128, TT], I32)
    c_d = din("cT", [128, KC])
    invf_d = din("invf", [128, 32])
    sel_d = din("sel", [128, 4])
    ada_w_d = din("ada_w", [DEPTH, D, 6 * D])
    ada_b_d = din("ada_bT", [DEPTH, 128, 96])
    nmw_d = din("nmwT", [DEPTH, 128, KC])
    nfw_d = din("nfwT", [DEPTH, 128, KC])
    w_in_d = din("w_in", [DEPTH, D, IN_COLS])
    w_dir_d = din("w_dir", [DEPTH, D, 32])
    qnw_d = din("qn_w", [DEPTH, 64])
    knw_d = din("kn_w", [DEPTH, 64])
    lamv_d = din("lamv", [DEPTH, 256])
    subln_d = din("sublnT", [DEPTH, 128, 1])
    conv_d = din("convT", [DEPTH, 128, 24, 5])
    alog_d = din("a_log", [DEPTH, 16])
    dtb_d = din("dt_bias", [DEPTH, 16])
    gnw_d = din("gnwT", [DEPTH, 128, 1])
    wbd_d = din("w_bd", [DEPTH, 1024, D])
    wbg_d = din("w_bg", [DEPTH, 1024, D])
    wout_d = din("w_out", [DEPTH, D, D])
    wup_d = din("w_up", [DEPTH, D, 2 * FFN])
    wdn_d = din("w_down", [DEPTH, FFN, D])
    out_d = nc.dram_tensor("outT", [D, NT], F32, kind="ExternalOutput").ap()
    dbg_d = {k: nc.dram_tensor("dbg_" + k, list(shp), F32, kind="ExternalOutput").ap()
             for k, shp in dbg.items() if not k.startswith('_')}

    def dscr(name, shape, dt=BF16):
        return nc.dram_tensor(name, list(shape), dt).ap()

    qT_s = dscr("qT_s", [NH, 128, NT])
    kT_s = dscr("kT_s", [NH * 128, NT])
    v_s = dscr("v_s", [NT, 1024])
    g_s = dscr("g_s", [24, 128, NT])
    z_s = dscr("z_s", [NH, 128, NT])
    gg_s = dscr("gg_s", [32, 128, NT])
    gq_s = dscr("gq_s", [NH, 128, NT])
    gk_s = dscr("gk_s", [NH, 128, NT])
    gkt_s = dscr("gkt_s", [NH, NT, 128])
    gvt_s = dscr("gvt_s", [NH, NT, 128])
    o1_s = dscr("o1_s", [NH, 128, NT], F32)
    if PAIR:
        halo_snd = dscr("halo_snd", [2 * 128, 48])
        halo_rcv = dscr("halo_rcv", [2 * 128, 48])
        st_snd = dscr("st_snd", [128, NH * 128], F32)
        st_rcv = dscr("st_rcv", [128, NH * 128], F32)
        kv_snd_k = dscr("kv_snd_k", [2 * NH * 128, NT])
        kv_rcv_k = dscr("kv_rcv_k", [2 * NH * 128, NT])
        kv_snd_v = dscr("kv_snd_v", [2 * NT, 1024])
        kv_rcv_v = dscr("kv_rcv_v", [2 * NT, 1024])

    def allreduce(e, src, dst):
        return e.collective_compute("AllReduce", ALU.add, replica_groups=GROUPS, ins=[src.opt()], outs=[dst.opt()])
    GROUPS = [[2 * i, 2 * i + 1] for i in range(dbg.get('_ncores', 8) // 2)]

    with ExitStack() as stack:
        S = Sched(nc, stack)
        sb = lambda name, shape, dt=F32: nc.alloc_sbuf_tensor("sb_" + name, list(shape), dt)

        xT = sb("xT", [128, KC, NT]);            BxT = [[S.buf() for _ in range(NCH)] for _ in range(KC)]
        hT = sb("hT", [128, KC, NT], BF16);      BhT = [S.buf() for _ in range(NCH)]
        NW = 3
        wt = [sb("wt%d" % i, [128, 16 * 512], BF16) for i in range(NW)]
        Bwt = S.bufs(NW, 'wt')
        ident_f = sb("ident_f", [128, 128]);     ident_b = sb("ident_b", [128, 128], BF16)
        ones_f = sb("ones_f", [128, 128]);       ones_b = sb("ones_b", [128, 128], BF16)
        m_Ui = sb("m_Ui", [128, 128]); m_Li = sb("m_Li", [128, 128])
        m_Us = sb("m_Us", [128, 128]); m_Ls = sb("m_Ls", [128, 128])
        Bconst = S.buf('const')
        cosT = sb("cosT", [128, TT, 32]); sinT = sb("sinT", [128, TT, 32]); Brope = S.buf('rope')
        cact = sb("cact", [128, KC], BF16); Bcact = S.buf('cact')
        mod = sb("mod", [128, 96]); Bmod = S.buf('mod')
        modA = sb("modA", [128, 2, KC]); BmodA = S.buf('modA')
        lyr = sb("lyr", [128, 1024]); Blyr = S.buf('lyr')
        betaT = sb("betaT", [128, TT, 16]); gT = sb("gT", [128, TT, 16]); Bbg = S.buf('bg')
        sel = sb("sel", [128, 4]); Bsel = S.buf('sel')
        ARENA = 48 * 1024
        arena = sb("arena", [128, ARENA // 4])
        psum = [nc.alloc_psum_tensor("ps%d" % i, [128, 512], F32) for i in range(8)]
        Bps = S.bufs(8, 'ps')

        def P(i, n=512):
            return psum[i][:, 0:n]

        class Arena:
            def __init__(self):
                self.off = 0

            def reset(self):
                S.barrier()
                self.off = 0

            def get(self, shape, dt=F32):
                esz = 4 if dt == F32 or dt == I32 else 2
                n = int(np.prod(shape))
                nbytes = (n * esz + 31) // 32 * 32
                assert self.off + nbytes <= ARENA, ("arena overflow", self.off, nbytes)
                v = arena[:, self.off // 4:(self.off + nbytes) // 4]
                self.off += nbytes
                if dt != F32:
                    v = v.bitcast(dt)
                v = v[:, 0:n]
                if len(shape) == 2:
                    return v.rearrange("p (a b) -> p a b", b=shape[1])
                if len(shape) == 3:
                    return v.rearrange("p (a b c) -> p a b c", b=shape[1], c=shape[2])
                return v

        AR = Arena()

        class WStream:
            def __init__(self):
                self.descs = []
                self.issued = 0
                self.taken = 0
                self.loaded = {}

            def add(self, tag, src, kch, ncols):
                self.descs.append((tag, src, kch, ncols))

            def _issue(self, i):
                tag, src, kch, ncols = self.descs[i]
                slot = i % NW
                t = wt[slot][:, 0:kch * ncols].rearrange("p (k n) -> p k n", n=ncols)
                first = True
                for k0 in range(0, kch, 16):
                    k1 = min(kch, k0 + 16)
                    S.dma(lambda e, t=t, src=src, k0=k0, k1=k1: e.dma_start(out=t[:, k0:k1, :], in_=src[:, k0:k1, :]),
                          writes=[Bwt[slot]], q='pool', more=not first)
                    first = False
                self.loaded[i] = (t, Bwt[slot])

            def get(self, tag):
                i = self.taken
                assert self.descs[i][0] == tag, (self.descs[i][0], tag)
                while self.issued < len(self.descs) and self.issued <= i + NW - 1:
                    self._issue(self.issued)
                    self.issued += 1
                self.taken += 1
                return self.loaded.pop(i)

        WS = WStream()

        def wview(w2d, c0, n):
            return w2d.rearrange("(k p) n -> p k n", p=128)[:, :, c0:c0 + n]

        for l in range(DEPTH):
            for j in range(24):
                WS.add(('ada', l, j), wview(ada_w_d[l], j * 512, 512), KC, 512)
            for j in range(6):
                WS.add(('tm', l, j), wview(w_in_d[l], j * 512, 512), KC, 512)
            WS.add(('dir', l), wview(w_dir_d[l], 0, 32), KC, 32)
            for j in range(16):
                WS.add(('fm', l, j), wview(w_in_d[l], 3072 + j * 512 + (32 if j >= 8 else 0), 512), KC, 512)
            for j in range(4):
                WS.add(('bd', l, j), wview(wbd_d[l], j * 512, 512), 8, 512)
                WS.add(('bg', l, j), wview(wbg_d[l], j * 512, 512), 8, 512)
            for j in range(4):
                WS.add(('out', l, j), wview(wout_d[l], j * 512, 512), KC, 512)
            for n in range(NCH):
                for j in range(11):
                    WS.add(('upg', l, n, j), wview(wup_d[l], j * 512, 512), KC, 512)
                    WS.add(('upu', l, n, j), wview(wup_d[l], FFN + j * 512, 512), KC, 512)
                for m in range(16):
                    WS.add(('dn', l, n, m), wview(wdn_d[l], m * 128, 128), FC, 128)

        Bout = S.buf('out')

        def setup():
            S.op('pool', lambda e: e.memset(ident_f[:], 0.0), writes=[Bconst])
            S.op('pool', lambda e: e.affine_select(out=ident_f[:], in_=ident_f[:], pattern=[[-1, 128]],
                                                   compare_op=ALU.not_equal, fill=1.0, base=0, channel_multiplier=1),
                 reads=[Bconst], writes=[Bconst])
            S.op('pool', lambda e: e.memset(ones_f[:], 1.0), reads=[Bconst], writes=[Bconst])
            for (m, op, sgn) in ((m_Ls, ALU.is_gt, 1), (m_Li, ALU.is_ge, 1), (m_Us, ALU.is_gt, -1), (m_Ui, ALU.is_ge, -1)):
                S.op('pool', lambda e, m=m, op=op, sgn=sgn: e.affine_select(out=m[:], in_=ones_f[:], pattern=[[-sgn, 128]],
                                                                             compare_op=op, fill=0.0, base=0, channel_multiplier=sgn),
                     reads=[Bconst], writes=[Bconst])
            S.op('pool', lambda e: e.tensor_copy(ident_b[:], ident_f[:]), reads=[Bconst], writes=[Bconst])
            S.op('pool', lambda e: e.tensor_copy(ones_b[:], ones_f[:]), reads=[Bconst], writes=[Bconst])
            xv = xT_d.rearrange("(k p) n -> p k n", p=128)
            for kc in range(KC):
                for n in range(NCH):
                    S.dma(lambda e, kc=kc, n=n: e.dma_start(out=xT[:, kc, n * CH:(n + 1) * CH], in_=xv[:, kc, n * CH:(n + 1) * CH]),
                          writes=[BxT[kc][n]])
            S.dma(lambda e: e.dma_start(out=sel[:], in_=sel_d[:, :]), writes=[Bsel])
            AR.reset()
            posi = AR.get([TT], I32); posf = AR.get([TT]); invf = AR.get([32]); ang = AR.get([TT, 32]); kf = AR.get([TT, 32])
            ki = AR.get([TT, 32], I32); cf = AR.get([KC])
            Bt = S.buf('setup_t')
            S.dma(lambda e: e.dma_start(out=posi, in_=pos_d[:, :]), writes=[Bt])
            S.dma(lambda e: e.dma_start(out=invf, in_=invf_d[:, :]), writes=[Bt], more=True)
            S.dma(lambda e: e.dma_start(out=cf, in_=c_d[:, :]), writes=[Bt], more=True)
            S.op('dve', lambda e: e.tensor_copy(posf, posi), reads=[Bt], writes=[Bt])
            for t in range(TT):
                S.op('dve', lambda e, t=t: e.tensor_scalar(ang[:, t, :], invf, posf[:, t:t + 1], None, op0=ALU.mult),
                     reads=[Bt], writes=[Bt])
            def reduce_sin(dst, shift):
                S.op('dve', lambda e: e.tensor_scalar(kf, ang, shift, 1.0 / (2 * math.pi), op0=ALU.add, op1=ALU.mult), reads=[Bt, Brope], writes=[Bt])
                S.op('dve', lambda e: e.tensor_copy(ki, kf), reads=[Bt], writes=[Bt])
                S.op('dve', lambda e: e.tensor_copy(kf, ki), reads=[Bt], writes=[Bt])
                S.op('dve', lambda e: e.scalar_tensor_tensor(out=kf, in0=kf, scalar=-2 * math.pi, in1=ang, op0=ALU.mult, op1=ALU.add),
                     reads=[Bt], writes=[Bt])
                S.op('dve', lambda e: e.tensor_scalar(kf, kf, shift, None, op0=ALU.add), reads=[Bt], writes=[Bt])
                S.op('dve', lambda e: e.tensor_scalar(kf, kf, -math.pi, math.pi, op0=ALU.max, op1=ALU.min), reads=[Bt], writes=[Bt])
                S.op('act', lambda e: e.activation(out=dst, in_=kf, func=AF.Sin), reads=[Bt, Brope], writes=[Brope])
            reduce_sin(sinT[:], 0.0)
            reduce_sin(cosT[:], math.pi / 2)
            S.op('act', lambda e: e.activation(out=cact[:], in_=cf, func=AF.Silu), reads=[Bt], writes=[Bcact])

        L_QNW, L_KNW = 0, 64
        L_LAM = 128
        L_CONV = 384
        L_ALOG, L_DTB = 504, 520
        L_ADAB = 536
        L_NMW, L_NFW = 632, 648
        L_SUB, L_GNW = 664, 665
        L_LAMC = 666
        L_NEGA = 668
        L_TMP = 700

        def load_layer_params(l):
            def ld(off, n, src, more=True):
                S.dma(lambda e: e.dma_start(out=lyr[:, off:off + n], in_=src), writes=[Blyr], more=more)
            S.dma(lambda e: e.dma_start(out=lyr[:, L_QNW:L_QNW + 64], in_=qnw_d[l].partition_broadcast(128)), writes=[Blyr])
            ld(L_KNW, 64, knw_d[l].partition_broadcast(128))
            ld(L_LAM, 256, lamv_d[l].partition_broadcast(128))
            ld(L_CONV, 120, conv_d[l].rearrange("p a b -> p (a b)"))
            ld(L_ALOG, 16, alog_d[l].partition_broadcast(128))
            ld(L_DTB, 16, dtb_d[l].partition_broadcast(128))
            ld(L_ADAB, 96, ada_b_d[l])
            ld(L_NMW, 16, nmw_d[l])
            ld(L_NFW, 16, nfw_d[l])
            ld(L_SUB, 1, subln_d[l])
            ld(L_GNW, 1, gnw_d[l])
            rw = dict(reads=[Blyr], writes=[Blyr])
            S.op('dve', lambda e: e.tensor_scalar(lyr[:, L_QNW:L_QNW + 64], lyr[:, L_QNW:L_QNW + 64], 0.125, None, op0=ALU.mult), **rw)
            S.op('dve', lambda e: e.tensor_tensor(lyr[:, L_TMP:L_TMP + 64], lyr[:, L_LAM:L_LAM + 64], lyr[:, L_LAM + 64:L_LAM + 128], op=ALU.mult), **rw)
            S.op('dve', lambda e: e.tensor_tensor(lyr[:, L_TMP + 64:L_TMP + 128], lyr[:, L_LAM + 128:L_LAM + 192], lyr[:, L_LAM + 192:L_LAM + 256], op=ALU.mult), **rw)
            S.op('dve', lambda e: e.tensor_reduce(out=lyr[:, L_TMP + 128:L_TMP + 130], in_=lyr[:, L_TMP:L_TMP + 128].rearrange("p (a b) -> p a b", b=64),
                                                  axis=AX.X, op=ALU.add), **rw)
            S.op('act', lambda e: e.activation(out=lyr[:, L_TMP + 128:L_TMP + 130], in_=lyr[:, L_TMP + 128:L_TMP + 130], func=AF.Exp), **rw)
            S.op('dve', lambda e: e.tensor_tensor(lyr[:, L_LAMC:L_LAMC + 1], lyr[:, L_TMP + 128:L_TMP + 129], lyr[:, L_TMP + 129:L_TMP + 130], op=ALU.subtract), **rw)
            S.op('dve', lambda e: e.tensor_scalar(lyr[:, L_LAMC:L_LAMC + 1], lyr[:, L_LAMC:L_LAMC + 1], lam_init[l], -1.0, op0=ALU.add, op1=ALU.mult), **rw)
            S.op('act', lambda e: e.activation(out=lyr[:, L_NEGA:L_NEGA + 16], in_=lyr[:, L_ALOG:L_ALOG + 16], func=AF.Exp), **rw)
            S.op('dve', lambda e: e.tensor_scalar(lyr[:, L_NEGA:L_NEGA + 16], lyr[:, L_NEGA:L_NEGA + 16], -1.0, None, op0=ALU.mult), **rw)

        def ada_mod(l):
            pm = 7
            for j in range(24):
                w, Bw = WS.get(('ada', l, j))
                for m in range(4):
                    col = j * 4 + m
                    for kc in range(KC):
                        S.op('pe', lambda e, w=w, m=m, kc=kc, col=col: e.matmul(psum[pm][:, col:col + 1], w[:, kc, m * 128:(m + 1) * 128],
                                                                                   cact[:, kc:kc + 1], start=(kc == 0), stop=(kc == KC - 1)),
                             reads=[Bw, Bcact], writes=[Bps[pm]])
            S.op('dve', lambda e: e.tensor_tensor(mod[:], psum[pm][:, 0:96], lyr[:, L_ADAB:L_ADAB + 96], op=ALU.add),
                 reads=[Bps[pm], Blyr], writes=[Bmod])
            S.op('dve', lambda e: e.scalar_tensor_tensor(out=modA[:, 0, :], in0=mod[:, 16:32], scalar=1.0, in1=lyr[:, L_NMW:L_NMW + 16],
                                                         op0=ALU.add, op1=ALU.mult), reads=[Bmod, Blyr], writes=[BmodA])
            S.op('dve', lambda e: e.scalar_tensor_tensor(out=modA[:, 1, :], in0=mod[:, 64:80], scalar=1.0, in1=lyr[:, L_NFW:L_NFW + 16],
                                                         op0=ALU.add, op1=ALU.mult), reads=[Bmod, Blyr, BmodA], writes=[BmodA])

        def rsqrt_ops(dst, src, scale, rd, wr, eps=EPS, post=1.0):
            S.op('act', lambda e: e.activation(out=dst, in_=src, func=AF.Ln, scale=scale, bias=eps), reads=rd, writes=wr)
            S.op('act', lambda e: e.activation(out=dst, in_=dst, func=AF.Exp, scale=-0.5, bias=math.log(post)), reads=wr, writes=wr)

        def modnorm(which):
            sh0 = 0 if which == 0 else 48
            AR.reset()
            sq = [AR.get([CH]) for _ in range(2)]; Bsq = S.bufs(2, 'sq')
            rstd = AR.get([CH]); Brs = S.buf('rstd')
            tmp = [AR.get([CH]) for _ in range(2)]; Btmp = S.bufs(2, 'tmp')
            for n in range(NCH):
                cs = slice(n * CH, (n + 1) * CH)
                for kc in range(KC):
                    i = kc % 2
                    S.op('act', lambda e, i=i, kc=kc: e.activation(out=sq[i], in_=xT[:, kc, cs], func=AF.Square),
                         reads=[BxT[kc][n]], writes=[Bsq[i]])
                    S.op('pe', lambda e, i=i, kc=kc: e.matmul(P(6, CH), ones_f[:], sq[i], start=(kc == 0), stop=(kc == KC - 1)),
                         reads=[Bsq[i], Bconst], writes=[Bps[6]])
                rsqrt_ops(rstd, P(6, CH), 1.0 / D, [Bps[6]], [Brs])
                for kc in range(KC):
                    i = kc % 2
                    S.op('dve', lambda e, i=i, kc=kc: e.scalar_tensor_tensor(out=tmp[i], in0=xT[:, kc, cs], scalar=modA[:, which, kc:kc + 1], in1=rstd,
                                                                            op0=ALU.mult, op1=ALU.mult),
                         reads=[BxT[kc][n], BmodA, Brs], writes=[Btmp[i]])
                    S.op('act', lambda e, i=i, kc=kc: e.activation(out=hT[:, kc, cs], in_=tmp[i], func=AF.Identity,
                                                                   bias=mod[:, sh0 + kc:sh0 + kc + 1], scale=1.0),
                         reads=[Btmp[i], Bmod], writes=[BhT[n]])

        def fm_block(w, Bw, kch, ncols, rhs_fn, rhs_bufs_fn, evac, pbanks, cnt):
            for m in range(ncols // 128):
                for n in range(NCH):
                    pb = pbanks[cnt[0] % len(pbanks)]
                    cnt[0] += 1
                    for kc in range(kch):
                        S.op('pe', lambda e, pb=pb, m=m, n=n, kc=kc: e.matmul(P(pb, CH), w[:, kc, m * 128:(m + 1) * 128], rhs_fn(kc, n),
                                                                             start=(kc == 0), stop=(kc == kch - 1)),
                             reads=[Bw] + rhs_bufs_fn(kc, n), writes=[Bps[pb]])
                    evac(m, n, pb)

        def in_proj(l):
            AR.reset()
            sqb = [AR.get([512]) for _ in range(2)]; Bsqb = S.bufs(2, 'sqb')
            ss8 = [AR.get([8]) for _ in range(2)]; Bss8 = S.bufs(2, 'ss8')
            qn = [AR.get([512]) for _ in range(2)]; Bqn = S.bufs(2, 'qn')
            rot = [AR.get([512]) for _ in range(2)]; Brot = S.bufs(2, 'rot')
            rt2 = [AR.get([512]) for _ in range(2)]; Brt2 = S.bufs(2, 'rt2')
            qb = [AR.get([512], BF16) for _ in range(2)]; Bqb = S.bufs(2, 'qb')
            qtb = [AR.get([4, 128], BF16) for _ in range(2)]; Bqtb = S.bufs(2, 'qtb')
            vb_ = [AR.get([512], BF16) for _ in range(3)]; Bvb = S.bufs(3, 'vb')
            fo = [AR.get([CH], BF16) for _ in range(3)]; Bfo = S.bufs(3, 'fo')
            sm = AR.get([64]); Bsm = S.buf('sm')
            if PAIR:
                mk = [AR.get([2, 512], BF16) for _ in range(2)]; Bmk = S.bufs(2, 'mk')
                hm = [AR.get([2, 2], BF16) for _ in range(3)]; Bhm = S.bufs(3, 'hm')
            Bq_s = S.buf('q_s'); Bk_s = S.buf('k_s'); Bv_s = S.buf('v_s')
            cnt = [0]
            ev = [0]
            for j in range(6):
                w, Bw = WS.get(('tm', l, j))
                for tt in range(TT):
                    pb = cnt[0] % 4
                    cnt[0] += 1
                    for kc in range(KC):
                        S.op('pe', lambda e, pb=pb, tt=tt, kc=kc, w=w: e.matmul(P(pb), hT[:, kc, tt * 128:(tt + 1) * 128], w[:, kc, :],
                                                                               start=(kc == 0), stop=(kc == KC - 1)),
                             reads=[Bw, BhT[(tt * 128) // CH]], writes=[Bps[pb]])
                    if j < 4:
                        i = ev[0] % 2
                        ev[0] += 1
                        isq = (j < 2)
                        woff = L_QNW if isq else L_KNW
                        S.op('act', lambda e, i=i, pb=pb: e.activation(out=sqb[i], in_=P(pb), func=AF.Square), reads=[Bps[pb]], writes=[Bsqb[i]])
                        S.op('dve', lambda e, i=i: e.tensor_reduce(out=ss8[i], in_=sqb[i].rearrange("p (g d) -> p g d", d=64), axis=AX.X, op=ALU.add),
                             reads=[Bsqb[i]], writes=[Bss8[i]])
                        rsqrt_ops(ss8[i], ss8[i], 1.0 / 64, [Bss8[i]], [Bss8[i]])
                        S.op('dve', lambda e, i=i, pb=pb: e.tensor_tensor(qn[i].rearrange("p (g d) -> p g d", d=64), P(pb).rearrange("p (g d) -> p g d", d=64),
                                                                          bc_last(ss8[i], 64), op=ALU.mult),
                             reads=[Bps[pb], Bss8[i]], writes=[Bqn[i]])
                        S.op('dve', lambda e, i=i, woff=woff: e.tensor_tensor(qn[i].rearrange("p (g d) -> p g d", d=64), qn[i].rearrange("p (g d) -> p g d", d=64),
                                                                             bc_mid(lyr[:, woff:woff + 64], 8), op=ALU.mult),
                             reads=[Bqn[i], Blyr], writes=[Bqn[i]])
                        q4 = qn[i].rearrange("p (g t f) -> p g t f", t=2, f=32)
                        r4 = rot[i].rearrange("p (g t f) -> p g t f", t=2, f=32)
                        s4 = rt2[i].rearrange("p (g t f) -> p g t f", t=2, f=32)
                        cb = bc_mid(cosT[:, tt, :], 8)
                        sn = bc_mid(sinT[:, tt, :], 8)
                        for t_ in range(2):
                            S.op('dve', lambda e, t_=t_, q4=q4, r4=r4, cb=cb: e.tensor_tensor(r4[:, :, t_, :], q4[:, :, t_, :], cb, op=ALU.mult),
                                 reads=[Bqn[i], Brope, Brot[i]], writes=[Brot[i]])
                            S.op('dve', lambda e, t_=t_, q4=q4, s4=s4, sn=sn: e.tensor_tensor(s4[:, :, t_, :], q4[:, :, 1 - t_, :], sn, op=ALU.mult),
                                 reads=[Bqn[i], Brope, Brt2[i]], writes=[Brt2[i]])
                        b4 = qb[i].rearrange("p (g t f) -> p g t f", t=2, f=32)
                        S.op('dve', lambda e, r4=r4, s4=s4, b4=b4: e.tensor_tensor(b4[:, :, 0, :], r4[:, :, 0, :], s4[:, :, 0, :], op=ALU.subtract),
                             reads=[Brot[i], Brt2[i], Bqb[i]], writes=[Bqb[i]])
                        S.op('dve', lambda e, r4=r4, s4=s4, b4=b4: e.tensor_tensor(b4[:, :, 1, :], r4[:, :, 1, :], s4[:, :, 1, :], op=ALU.add),
                             reads=[Brot[i], Brt2[i], Bqb[i]], writes=[Bqb[i]])
                        tb = 4 + (ev[0] % 2)
                        pT = psum[tb][:].bitcast(BF16)
                        for hh in range(4):
                            S.op('pe', lambda e, hh=hh, i=i, pT=pT: e.transpose(pT[:, hh * 128:(hh + 1) * 128], qb[i][:, hh * 128:(hh + 1) * 128], ident_b[:]),
                                 reads=[Bqb[i], Bconst], writes=[Bps[tb]])
                        S.op('act', lambda e, i=i, pT=pT: e.activation(out=qtb[i], in_=pT[:, 0:512].rearrange("p (a b) -> p a b", b=128), func=AF.Copy),
                             reads=[Bps[tb]], writes=[Bqtb[i]])
                        h0 = (j % 2) * 4
                        if isq:
                            dst = qT_s[h0:h0 + 4, :, tt * 128:(tt + 1) * 128].rearrange("h p t -> p h t")
                            S.dma(lambda e, dst=dst, i=i: e.dma_start(out=dst, in_=qtb[i]), reads=[Bqtb[i]], writes=[Bq_s], store=True)
                        elif not PAIR:
                            dst = kT_s.rearrange("(h p) t -> h p t", p=128)[h0:h0 + 4, :, tt * 128:(tt + 1) * 128].rearrange("h p t -> p h t")
                            S.dma(lambda e, dst=dst, i=i: e.dma_start(out=dst, in_=qtb[i]), reads=[Bqtb[i]], writes=[Bk_s], store=True)
                        else:
                            for r in range(2):
                                S.op('dve', lambda e, i=i, r=r: e.tensor_scalar(mk[i][:, r, :], qtb[i].rearrange("p a b -> p (a b)"), sel[:, 2 + r:3 + r], None, op0=ALU.mult),
                                     reads=[Bqtb[i], Bsel, Bmk[i]], writes=[Bmk[i]])
                            for r in range(2):
                                dst = kv_snd_k.rearrange("(r h p) t -> r h p t", h=NH, p=128)[r, h0:h0 + 4, :, tt * 128:(tt + 1) * 128].rearrange("h p t -> p h t")
                                S.dma(lambda e, dst=dst, i=i, r=r: e.dma_start(out=dst, in_=mk[i][:, r, :].rearrange("p (a b) -> p a b", b=128)), reads=[Bmk[i]], writes=[Bk_s], store=True)
                    else:
                        i = ev[0] % 3
                        ev[0] += 1
                        S.op('act', lambda e, i=i, pb=pb: e.activation(out=vb_[i], in_=P(pb), func=AF.Copy), reads=[Bps[pb]], writes=[Bvb[i]])
                        c0 = (j - 4) * 512
                        if not PAIR:
                            S.dma(lambda e, i=i, tt=tt, c0=c0: e.dma_start(out=v_s[tt * 128:(tt + 1) * 128, c0:c0 + 512], in_=vb_[i]),
                                  reads=[Bvb[i]], writes=[Bv_s], store=True)
                        else:
                            i2 = ev[0] % 2
                            for r in range(2):
                                S.op('dve', lambda e, i=i, i2=i2, r=r: e.tensor_scalar(mk[i2][:, r, :], vb_[i], sel[:, 2 + r:3 + r], None, op0=ALU.mult),
                                     reads=[Bvb[i], Bsel, Bmk[i2]], writes=[Bmk[i2]])
                            for r in range(2):
                                S.dma(lambda e, i2=i2, tt=tt, c0=c0, r=r: e.dma_start(out=kv_snd_v[r * NT + tt * 128:r * NT + (tt + 1) * 128, c0:c0 + 512], in_=mk[i2][:, r, :]),
                                      reads=[Bmk[i2]], writes=[Bv_s], store=True)
            w, Bw = WS.get(('dir', l))
            for tt in range(TT):
                pb = 4 + tt % 2
                for kc in range(KC):
                    S.op('pe', lambda e, pb=pb, tt=tt, kc=kc, w=w: e.matmul(psum[pb][:, 0:32], hT[:, kc, tt * 128:(tt + 1) * 128], w[:, kc, :],
                                                                           start=(kc == 0), stop=(kc == KC - 1)),
                         reads=[Bw, BhT[(tt * 128) // CH]], writes=[Bps[pb]])
                S.op('act', lambda e, pb=pb: e.activation(out=sm[:, 0:16], in_=psum[pb][:, 0:16], func=AF.Exp, scale=-1.0), reads=[Bps[pb]], writes=[Bsm])
                S.op('dve', lambda e: e.tensor_scalar(sm[:, 0:16], sm[:, 0:16], 1.0, None, op0=ALU.add), reads=[Bsm], writes=[Bsm])
                S.op('dve', lambda e, tt=tt: e.reciprocal(betaT[:, tt, :], sm[:, 0:16]), reads=[Bsm, Bbg], writes=[Bbg])
                S.op('dve', lambda e, pb=pb: e.tensor_tensor(sm[:, 16:32], psum[pb][:, 16:32], lyr[:, L_DTB:L_DTB + 16], op=ALU.add),
                     reads=[Bps[pb], Blyr, Bsm], writes=[Bsm])
                S.op('dve', lambda e: e.tensor_scalar(sm[:, 32:48], sm[:, 16:32], 0.0, None, op0=ALU.min), reads=[Bsm], writes=[Bsm])
                S.op('dve', lambda e: e.tensor_scalar(sm[:, 48:64], sm[:, 16:32], 0.0, None, op0=ALU.max), reads=[Bsm], writes=[Bsm])
                S.op('dve', lambda e: e.tensor_tensor(sm[:, 32:48], sm[:, 32:48], sm[:, 48:64], op=ALU.subtract), reads=[Bsm], writes=[Bsm])
                S.op('act', lambda e: e.activation(out=sm[:, 32:48], in_=sm[:, 32:48], func=AF.Exp), reads=[Bsm], writes=[Bsm])
                S.op('act', lambda e: e.activation(out=sm[:, 32:48], in_=sm[:, 32:48], func=AF.Ln, bias=1.0, scale=1.0), reads=[Bsm], writes=[Bsm])
                S.op('dve', lambda e: e.tensor_tensor(sm[:, 32:48], sm[:, 32:48], sm[:, 48:64], op=ALU.add), reads=[Bsm], writes=[Bsm])
                S.op('dve', lambda e, tt=tt: e.tensor_tensor(gT[:, tt, :], sm[:, 32:48], lyr[:, L_NEGA:L_NEGA + 16], op=ALU.mult),
                     reads=[Bsm, Blyr, Bbg], writes=[Bbg])
            Bg_s = S.buf('g_s'); Bz_s = S.buf('z_s'); Bgg_s = S.buf('gg_s'); Bhs = S.buf('halo_snd')
            fcnt = [0]
            for j in range(16):
                w, Bw = WS.get(('fm', l, j))

                def evac(m, n, pb, j=j):
                    i = fcnt[0] % 3
                    fcnt[0] += 1
                    ch = j * 4 + m
                    cs = slice(n * CH, (n + 1) * CH)
                    if ch < 24:
                        S.op('dve', lambda e: e.tensor_copy(fo[i], P(pb, CH)), reads=[Bps[pb]], writes=[Bfo[i]])
                        S.dma(lambda e: e.dma_start(out=g_s[ch, :, cs], in_=fo[i]), reads=[Bfo[i]], writes=[Bg_s], store=True)
                        if PAIR and n == NCH - 1:
                            ih = ch % 3
                            for r in range(2):
                                S.op('dve', lambda e, r=r: e.tensor_scalar(hm[ih][:, r, :], fo[i][:, CH - 2:CH], sel[:, 2 + r:3 + r], None, op0=ALU.mult),
                                     reads=[Bfo[i], Bsel, Bhm[ih]], writes=[Bhm[ih]])
                            for r in range(2):
                                S.dma(lambda e, r=r: e.dma_start(out=halo_snd[r * 128:(r + 1) * 128, 2 * ch:2 * ch + 2], in_=hm[ih][:, r, :]), reads=[Bhm[ih]], writes=[Bhs], store=True)
                    elif ch < 32:
                        S.op('act', lambda e: e.activation(out=fo[i], in_=P(pb, CH), func=AF.Silu), reads=[Bps[pb]], writes=[Bfo[i]])
                        S.dma(lambda e: e.dma_start(out=z_s[ch - 24, :, cs], in_=fo[i]), reads=[Bfo[i]], writes=[Bz_s], store=True)
                    else:
                        S.op('act', lambda e: e.activation(out=fo[i], in_=P(pb, CH), func=AF.Sigmoid), reads=[Bps[pb]], writes=[Bfo[i]])
                        S.dma(lambda e: e.dma_start(out=gg_s[ch - 32, :, cs], in_=fo[i]), reads=[Bfo[i]], writes=[Bgg_s], store=True)

                fm_block(w, Bw, KC, 512, lambda kc, n: hT[:, kc, n * CH:(n + 1) * CH], lambda kc, n: [BhT[n]], evac, [0, 1, 2, 3], cnt)
            return dict(q=Bq_s, k=Bk_s, v=Bv_s, g=Bg_s, z=Bz_s, gg=Bgg_s, hs=Bhs)


        ydT = hT[:, 0:8, :]
        ygT = hT[:, 8:16, :]
        Byd = S.bufs(NH, 'yd'); Byg = S.bufs(2, 'yg')

        def gdn_prep(l, sc):
            AR.reset()
            gin = [AR.get([NT + 4], BF16) for _ in range(2)]; Bgin = S.bufs(2, 'gin')
            acc = [AR.get([NT]) for _ in range(2)]; Bacc = S.bufs(2, 'acc')
            sqt = [AR.get([CH]) for _ in range(2)]; Bsqt = S.bufs(2, 'sqt')
            rs = [AR.get([CH]) for _ in range(2)]; Brs_ = S.bufs(2, 'rs')
            nb = [AR.get([NT], BF16) for _ in range(2)]; Bnb = S.bufs(2, 'nb')
            tk = [AR.get([TT, 128], BF16) for _ in range(2)]; Btk = S.bufs(2, 'tk')
            Bgq = S.buf('gq_s'); Bgk = S.buf('gk_s'); Bgkt = S.buf('gkt_s'); Bgvt = S.buf('gvt_s')
            it = 0
            if PAIR:
                Bhr = S.buf('halo_rcv')
                S.dma(lambda e: allreduce(e, halo_snd, halo_rcv), reads=[sc['hs']], writes=[Bhr], q='pool', inc=1)
                Bkr = S.buf('kv_rcv_k'); Bvr = S.buf('kv_rcv_v')
                S.dma(lambda e: allreduce(e, kv_snd_k, kv_rcv_k), reads=[sc['k']], writes=[Bkr], q='pool', inc=1)
                S.dma(lambda e: allreduce(e, kv_snd_v, kv_rcv_v), reads=[sc['v']], writes=[Bvr], q='pool', inc=1)
                sc['k'] = Bkr
                sc['v'] = Bvr
                hl = AR.get([2, 48], BF16); hlf = AR.get([48]); hl2 = AR.get([48], BF16); Bhl = S.buf('hl')
                S.dma(lambda e: e.dma_start(out=hl, in_=halo_rcv.rearrange("(r p) x -> p r x", p=128)), reads=[Bhr], writes=[Bhl])
                S.op('dve', lambda e: e.tensor_scalar(hlf, hl[:, 0, :], sel[:, 0:1], None, op0=ALU.mult), reads=[Bhl, Bsel], writes=[Bhl])
                S.op('dve', lambda e: e.scalar_tensor_tensor(out=hl2, in0=hl[:, 1, :], scalar=sel[:, 1:2], in1=hlf, op0=ALU.mult, op1=ALU.add), reads=[Bhl, Bsel], writes=[Bhl])

                def halo_fill(gt, Bg, ch):
                    S.op('dve', lambda e: e.tensor_copy(gt[:, NT + 2:NT + 3], hl2[:, 2 * ch + 1:2 * ch + 2]), reads=[Bhl, Bg], writes=[Bg])
                    S.op('dve', lambda e: e.tensor_copy(gt[:, NT + 3:NT + 4], hl2[:, 2 * ch:2 * ch + 1]), reads=[Bhl, Bg], writes=[Bg])
            for h in range(NH):
                for qi in range(3):
                    ch = qi * 8 + h
                    i = it % 2
                    it += 1
                    S.dma(lambda e, i=i, ch=ch: e.dma_start(out=gin[i][:, 2:NT + 2], in_=g_s[ch, :, :]), reads=[sc['g']], writes=[Bgin[i]])
                    S.op('dve', lambda e, i=i: e.memset(gin[i][:, 0:2], 0.0), reads=[Bgin[i]], writes=[Bgin[i]])
                    if PAIR:
                        halo_fill(gin[i], Bgin[i], ch)
                    else:
                        S.op('dve', lambda e, i=i: e.memset(gin[i][:, NT + 2:NT + 4], 0.0), reads=[Bgin[i]], writes=[Bgin[i]])
                    cw = L_CONV + ch * 5
                    S.op('dve', lambda e, i=i, cw=cw: e.tensor_scalar(acc[i], gin[i][:, 0:NT], lyr[:, cw:cw + 1], None, op0=ALU.mult),
                         reads=[Bgin[i], Blyr], writes=[Bacc[i]])
                    for tap in range(1, 5):
                        S.op('dve', lambda e, i=i, cw=cw, tap=tap: e.scalar_tensor_tensor(out=acc[i], in0=gin[i][:, tap:tap + NT], scalar=lyr[:, cw + tap:cw + tap + 1],
                                                                                        in1=acc[i], op0=ALU.mult, op1=ALU.add),
                             reads=[Bgin[i], Blyr, Bacc[i]], writes=[Bacc[i]])
                    S.op('act', lambda e, i=i: e.activation(out=acc[i], in_=acc[i], func=AF.Silu), reads=[Bacc[i]], writes=[Bacc[i]])
                    if qi < 2:
                        post = (128.0 ** -0.5) if qi == 0 else 1.0
                        for n in range(NCH):
                            cs = slice(n * CH, (n + 1) * CH)
                            j = n % 2
                            S.op('act', lambda e, i=i, j=j, cs=cs: e.activation(out=sqt[j], in_=acc[i][:, cs], func=AF.Square), reads=[Bacc[i]], writes=[Bsqt[j]])
                            S.op('pe', lambda e, j=j: e.matmul(P(6, CH), ones_f[:], sqt[j], start=True, stop=True), reads=[Bsqt[j], Bconst], writes=[Bps[6]])
                            rsqrt_ops(rs[j], P(6, CH), 1.0, [Bps[6]], [Brs_[j]], post=post)
                            S.op('dve', lambda e, i=i, j=j, cs=cs: e.tensor_tensor(nb[i][:, cs], acc[i][:, cs], rs[j], op=ALU.mult),
                                 reads=[Bacc[i], Brs_[j], Bnb[i]], writes=[Bnb[i]])
                    else:
                        S.op('dve', lambda e, i=i: e.tensor_copy(nb[i], acc[i]), reads=[Bacc[i]], writes=[Bnb[i]])
                    if qi == 0:
                        S.dma(lambda e, i=i, h=h: e.dma_start(out=gq_s[h, :, :], in_=nb[i]), reads=[Bnb[i]], writes=[Bgq], store=True)
                    else:
                        if qi == 1:
                            S.dma(lambda e, i=i, h=h: e.dma_start(out=gk_s[h, :, :], in_=nb[i]), reads=[Bnb[i]], writes=[Bgk], store=True)
                        for t0 in range(0, TT, 4):
                            tb = 4 + ((t0 // 4) % 2)
                            pT = psum[tb][:].bitcast(BF16)
                            nt_ = min(4, TT - t0)
                            for t in range(nt_):
                                S.op('pe', lambda e, i=i, t=t, t0=t0, pT=pT: e.transpose(pT[:, t * 128:(t + 1) * 128], nb[i][:, (t0 + t) * 128:(t0 + t + 1) * 128], ident_b[:]),
                                     reads=[Bnb[i], Bconst], writes=[Bps[tb]])
                            S.op('act', lambda e, i=i, t0=t0, nt_=nt_, pT=pT: e.activation(out=tk[i][:, t0:t0 + nt_, :], in_=pT[:, 0:nt_ * 128].rearrange("p (a b) -> p a b", b=128), func=AF.Copy),
                                 reads=[Bps[tb], Btk[i]], writes=[Btk[i]])
                        dst = (gkt_s if qi == 1 else gvt_s)[h].rearrange("(t p) d -> p t d", p=128)
                        S.dma(lambda e, i=i, dst=dst: e.dma_start(out=dst, in_=tk[i]), reads=[Btk[i]], writes=[Bgkt if qi == 1 else Bgvt], store=True)
            return dict(gq=Bgq, gk=Bgk, gkt=Bgkt, gvt=Bgvt)

        Sst = sb("Sst", [128, NH, 128]); BSst = S.bufs(2, 'Sst')

        def gdn_scan(l, ph, sc, gp, Bo1, zero_state):
            AR.reset()
            TRI = m_Ui if ph == 0 else m_Li
            MST = m_Ls if ph == 0 else m_Us
            MIN = m_Ui if ph == 0 else m_Li
            tiles = list(range(TT)) if ph == 0 else list(range(TT - 1, -1, -1))
            dsl = slice(ph * 8, ph * 8 + 8)
            st8 = AR.get([64]); Bst8 = S.buf('st8')
            qT4 = [AR.get([4, 128], BF16)] * 2; kT4 = [AR.get([4, 128], BF16)] * 2
            kt4 = [AR.get([4, 128], BF16)] * 2; vt4 = [AR.get([4, 128], BF16)] * 2
            Bld = [S.buf('ld')] * 2
            Gb4 = AR.get([4, 128]); BGb = S.buf('Gb')
            ta = AR.get([4, 128]); tb_ = AR.get([4, 128]); Er = AR.get([4, 128]); Bta = S.buf('ta'); Btb = S.buf('tb'); BEr = S.buf('Er')
            L4 = AR.get([4, 128], CDT); At4 = AR.get([4, 128], BF16); BL4 = S.buf('L4'); BAt4 = S.buf('At4')
            Xb = [AR.get([4, 128], CDT) for _ in range(2)]; Yb = [AR.get([4, 128], CDT) for _ in range(2)]; Tb = [AR.get([4, 128], CDT) for _ in range(2)]
            Tfin = AR.get([4, 128], BF16); BTfin = S.buf('Tfin')
            identc = ident_f if CDT == F32 else ident_b
            BXb = S.bufs(2, 'Xb'); BYb = S.bufs(2, 'Yb'); BTb = S.bufs(2, 'Tb')
            kbg4 = AR.get([4, 128], BF16); ktl4 = AR.get([4, 128], BF16); vb4 = AR.get([4, 128], BF16)
            nw4 = AR.get([4, 128], BF16); qd4 = AR.get([4, 128], BF16); vn4 = AR.get([4, 128], BF16)
            Bkbg = S.buf('kbg'); Bktl = S.buf('ktl'); Bvb4 = S.buf('vb4'); Bnw = S.buf('nw'); Bqd = S.buf('qd'); Bvn = S.buf('vn')
            Sb4 = [AR.get([4, 128], BF16) for _ in range(2)]; BSb = S.bufs(2, 'Sb')
            ot = [AR.get([4, 128])] * 2; Bot = [S.buf('ot')] * 2
            o1t = [AR.get([4, 128])] * 2; Bo1t = [S.buf('o1t')] * 2
            zt = [AR.get([4, 128], BF16)] * 2; Bzt = [S.buf('zt')] * 2
            sq4 = AR.get([4, 128]); Bsq4 = S.buf('sq4'); rs4 = AR.get([4, 128]); Brs4 = S.buf('rs4')
            Er2 = AR.get([4, 128]); BEr2 = S.buf('Er2')
            for g in range(2):
                hs = slice(g * 4, g * 4 + 4)
                if zero_state:
                    S.op('dve', lambda e, hs=hs: e.memset(Sst[:, hs, :], 0.0), reads=[BSst[g]], writes=[BSst[g]])
                S.op('act', lambda e, g=g, hs=hs: e.activation(out=Sb4[g], in_=Sst[:, hs, :], func=AF.Copy), reads=[BSst[g]], writes=[BSb[g]])
            it = 0
            rot = [0]

            def rbank():
                rot[0] += 1
                return 4 + (rot[0] % 3)

            for tt in tiles:
                ts_ = slice(tt * 128, (tt + 1) * 128)
                S.op('pe', lambda e, tt=tt: e.matmul(psum[0][:, 0:8], TRI[:], gT[:, tt, dsl], start=True, stop=True), reads=[Bconst, Bbg], writes=[Bps[0]])
                S.op('pe', lambda e, tt=tt: e.matmul(psum[0][:, 8:16], ones_f[:], gT[:, tt, dsl], start=True, stop=True), reads=[Bconst, Bbg], writes=[Bps[0]])
                S.op('dve', lambda e: e.tensor_copy(st8[:, 0:8], psum[0][:, 0:8]), reads=[Bps[0], Bst8], writes=[Bst8])
                S.op('act', lambda e: e.activation(out=st8[:, 8:16], in_=psum[0][:, 0:8], func=AF.Exp), reads=[Bps[0], Bst8], writes=[Bst8])
                S.op('act', lambda e: e.activation(out=st8[:, 32:40], in_=psum[0][:, 0:8], func=AF.Copy, scale=-1.0), reads=[Bps[0], Bst8], writes=[Bst8])
                S.op('act', lambda e: e.activation(out=st8[:, 16:24], in_=psum[0][:, 8:16], func=AF.Exp), reads=[Bps[0], Bst8], writes=[Bst8])
                S.op('dve', lambda e: e.tensor_tensor(st8[:, 24:32], psum[0][:, 8:16], st8[:, 0:8], op=ALU.subtract), reads=[Bps[0], Bst8], writes=[Bst8])
                S.op('act', lambda e: e.activation(out=st8[:, 24:32], in_=st8[:, 24:32], func=AF.Exp), reads=[Bst8], writes=[Bst8])
                S.op('dve', lambda e, tt=tt: e.tensor_tensor(st8[:, 8:16], st8[:, 8:16], betaT[:, tt, dsl], op=ALU.mult), reads=[Bst8, Bbg], writes=[Bst8])
                if dbg.get('_cut') == 1:
                    return
                for g in range(2):
                    hs = slice(g * 4, g * 4 + 4)
                    h0 = g * 4
                    i = it % 2
                    it += 1
                    S.dma(lambda e, i=i, h0=h0, ts_=ts_: e.dma_start(out=qT4[i], in_=gq_s[h0:h0 + 4, :, ts_].rearrange("h p t -> p h t")), reads=[gp['gq']], writes=[Bld[i]])
                    S.dma(lambda e, i=i, h0=h0, ts_=ts_: e.dma_start(out=kT4[i], in_=gk_s[h0:h0 + 4, :, ts_].rearrange("h p t -> p h t")), reads=[gp['gk']], writes=[Bld[i]], more=True)
                    S.dma(lambda e, i=i, h0=h0, ts_=ts_: e.dma_start(out=kt4[i], in_=gkt_s[h0:h0 + 4, ts_, :].rearrange("h p d -> p h d")), reads=[gp['gkt']], writes=[Bld[i]], more=True)
                    S.dma(lambda e, i=i, h0=h0, ts_=ts_: e.dma_start(out=vt4[i], in_=gvt_s[h0:h0 + 4, ts_, :].rearrange("h p d -> p h d")), reads=[gp['gvt']], writes=[Bld[i]], more=True)
                    g4 = gT[:, tt, ph * 8 + h0:ph * 8 + h0 + 4]
                    be4 = betaT[:, tt, ph * 8 + h0:ph * 8 + h0 + 4]
                    gc4 = st8[:, h0:h0 + 4]; skbg4 = st8[:, 8 + h0:12 + h0]; cd4 = st8[:, 16 + h0:20 + h0]; skt4 = st8[:, 24 + h0:28 + h0]
                    if dbg.get('_cut') == 11:
                        return
                    for hh in range(4):
                        S.op('dve', lambda e, g4=g4, hh=hh: e.tensor_scalar(Gb4[:, hh, :], ones_f[:], g4[:, hh:hh + 1], None, op0=ALU.mult),
                             reads=[Bbg, BGb, Bconst], writes=[BGb])
                    for hh in range(4):
                        S.op('pe', lambda e, hh=hh: e.matmul(psum[1][:, hh * 128:(hh + 1) * 128], Gb4[:, hh, :], TRI[:], start=True, stop=True),
                             reads=[BGb, Bconst], writes=[Bps[1]])
                    if dbg.get('_cut') == 12:
                        if 'pb' in dbg_d:
                            S.op('act', lambda e: e.activation(out=Er, in_=psum[1][:].rearrange("p (a b) -> p a b", b=128), func=AF.Copy), reads=[Bps[1], BEr], writes=[BEr])
                            S.dma(lambda e: e.dma_start(out=dbg_d['pb'], in_=Er.rearrange("p a b -> p (a b)")), reads=[BEr], writes=[Bout], store=True)
                            S.dma(lambda e: e.dma_start(out=dbg_d['st8'], in_=st8), reads=[Bst8], writes=[Bout], store=True)
                            S.dma(lambda e: e.dma_start(out=dbg_d['gb'], in_=Gb4.rearrange("p a b -> p (a b)")), reads=[BGb], writes=[Bout], store=True)
                        return
                    pB = psum[1][:].rearrange("p (a b) -> p a b", b=128)
                    ngc4 = st8[:, 32 + h0:36 + h0]
                    for hh in range(4):
                        S.op('act', lambda e, hh=hh, ngc4=ngc4: e.activation(out=ta[:, hh, :], in_=psum[1][:, hh * 128:(hh + 1) * 128], func=AF.Relu,
                                                                           bias=ngc4[:, hh:hh + 1], scale=1.0), reads=[Bps[1], Bst8, Bta], writes=[Bta])
                        S.op('act', lambda e, hh=hh, gc4=gc4: e.activation(out=tb_[:, hh, :], in_=psum[1][:, hh * 128:(hh + 1) * 128], func=AF.Relu,
                                                                          bias=gc4[:, hh:hh + 1], scale=-1.0), reads=[Bps[1], Bst8, Btb], writes=[Btb])
                    S.op('act', lambda e, pB=pB: e.activation(out=Er, in_=pB, func=AF.Exp), reads=[Bps[1], BEr], writes=[BEr])
                    if dbg.get('_cut') == 13:
                        return
                    S.op('act', lambda e: e.activation(out=ta, in_=ta, func=AF.Exp, scale=-1.0), reads=[Bta], writes=[Bta])
                    S.op('act', lambda e: e.activation(out=tb_, in_=tb_, func=AF.Exp, scale=-1.0), reads=[Btb], writes=[Btb])
                    if dbg.get('_cut') == 15:
                        return
                    S.op('dve', lambda e: e.tensor_tensor(ta, ta, bc_mid(MST[:], 4), op=ALU.mult), reads=[Bta, Bconst], writes=[Bta])
                    S.op('dve', lambda e, be4=be4: e.tensor_tensor(ta, ta, bc_last(be4, 128), op=ALU.mult), reads=[Bta, Bbg], writes=[Bta])
                    S.op('dve', lambda e: e.tensor_tensor(tb_, tb_, bc_mid(MIN[:], 4), op=ALU.mult), reads=[Btb, Bconst], writes=[Btb])
                    if dbg.get('_cut') == 2:
                        return
                    for hh in range(4):
                        S.op('pe', lambda e, hh=hh, i=i: e.matmul(psum[2][:, hh * 128:(hh + 1) * 128], kT4[i][:, hh, :], kT4[i][:, hh, :], start=True, stop=True),
                             reads=[Bld[i]], writes=[Bps[2]])
                    for hh in range(4):
                        S.op('pe', lambda e, hh=hh, i=i: e.matmul(psum[3][:, hh * 128:(hh + 1) * 128], kT4[i][:, hh, :], qT4[i][:, hh, :], start=True, stop=True),
                             reads=[Bld[i]], writes=[Bps[3]])
                    pK = psum[2][:].rearrange("p (a b) -> p a b", b=128)
                    pQ = psum[3][:].rearrange("p (a b) -> p a b", b=128)
                    S.op('act', lambda e, pK=pK: e.activation(out=Er2, in_=pK, func=AF.Copy), reads=[Bps[2], BEr2], writes=[BEr2])
                    S.op('dve', lambda e: e.tensor_tensor(L4, Er2, ta, op=ALU.mult), reads=[BEr2, Bta, BL4], writes=[BL4])
                    S.op('act', lambda e, pQ=pQ: e.activation(out=Er2, in_=pQ, func=AF.Copy), reads=[Bps[3], BEr2], writes=[BEr2])
                    S.op('dve', lambda e: e.tensor_tensor(At4, Er2, tb_, op=ALU.mult), reads=[BEr2, Btb, BAt4], writes=[BAt4])
                    if dbg.get('_cut') == 3:
                        return
                    bk = rbank()
                    pT = psum[bk][:].bitcast(CDT) if CDT != F32 else psum[bk][:]
                    for hh in range(4):
                        S.op('pe', lambda e, hh=hh, pT=pT: e.transpose(pT[:, hh * 128:(hh + 1) * 128], L4[:, hh, :], identc[:]), reads=[BL4, Bconst], writes=[Bps[bk]])
                    pT3 = pT[:, 0:512].rearrange("p (a b) -> p a b", b=128)
                    S.op('act', lambda e, pT3=pT3: e.activation(out=Yb[0], in_=pT3, func=AF.Copy), reads=[Bps[bk], BYb[0]], writes=[BYb[0]])
                    S.op('dve', lambda e: e.tensor_tensor(Tb[0], bc_mid(identc[:], 4), Yb[0], op=ALU.subtract), reads=[BYb[0], Bconst, BTb[0]], writes=[BTb[0]])
                    if dbg.get('_cut') == 4:
                        return
                    Xc, BXc = L4, BL4
                    Yc, BYc = Yb[0], BYb[0]
                    Tc, BTc = Tb[0], BTb[0]
                    for lev in range(1, 7):
                        Xn, BXn = Xb[lev % 2], BXb[lev % 2]
                        Yn, BYn = Yb[lev % 2], BYb[lev % 2]
                        Tn, BTn = Tb[lev % 2], BTb[lev % 2]
                        bx = rbank()
                        for hh in range(4):
                            S.op('pe', lambda e, hh=hh, bx=bx, Xc=Xc, Yc=Yc: e.matmul(psum[bx][:, hh * 128:(hh + 1) * 128], Yc[:, hh, :], Xc[:, hh, :], start=True, stop=True),
                                 reads=[BXc, BYc], writes=[Bps[bx]])
                        if lev < 6:
                            by = rbank()
                            for hh in range(4):
                                S.op('pe', lambda e, hh=hh, by=by, Xc=Xc, Yc=Yc: e.matmul(psum[by][:, hh * 128:(hh + 1) * 128], Xc[:, hh, :], Yc[:, hh, :], start=True, stop=True),
                                     reads=[BXc, BYc], writes=[Bps[by]])
                        S.op('act', lambda e, bx=bx, Xn=Xn: e.activation(out=Xn, in_=psum[bx][:].rearrange("p (a b) -> p a b", b=128), func=AF.Copy),
                             reads=[Bps[bx], BXn], writes=[BXn])
                        if lev < 6:
                            S.op('act', lambda e, by=by, Yn=Yn: e.activation(out=Yn, in_=psum[by][:].rearrange("p (a b) -> p a b", b=128), func=AF.Copy), reads=[Bps[by], BYn], writes=[BYn])
                        bt = rbank()
                        for hh in range(4):
                            S.op('pe', lambda e, hh=hh, bt=bt, Xn=Xn, Tc=Tc: e.matmul(psum[bt][:, hh * 128:(hh + 1) * 128], Xn[:, hh, :], Tc[:, hh, :], start=True, stop=True),
                                 reads=[BXn, BTc], writes=[Bps[bt]])
                        S.op('act', lambda e, bt=bt: e.activation(out=Er2, in_=psum[bt][:].rearrange("p (a b) -> p a b", b=128), func=AF.Copy), reads=[Bps[bt], BEr2], writes=[BEr2])
                        S.op('dve', lambda e, Tn=Tn, Tc=Tc: e.tensor_tensor(Tn, Er2, Tc, op=ALU.add), reads=[BEr2, BTc, BTn], writes=[BTn])
                        Xc, BXc, Yc, BYc, Tc, BTc = Xn, BXn, Yn, BYn, Tn, BTn
                    if dbg.get('_cut') == 5:
                        return
                    S.op('act', lambda e, Tc=Tc: e.activation(out=Tfin, in_=Tc, func=AF.Copy), reads=[BTc, BTfin], writes=[BTfin])
                    Tc, BTc = Tfin, BTfin
                    S.op('dve', lambda e, i=i, skbg4=skbg4: e.tensor_tensor(kbg4, kt4[i], bc_last(skbg4, 128), op=ALU.mult), reads=[Bld[i], Bst8, Bkbg], writes=[Bkbg])
                    S.op('dve', lambda e, i=i, skt4=skt4: e.tensor_tensor(ktl4, kt4[i], bc_last(skt4, 128), op=ALU.mult), reads=[Bld[i], Bst8, Bktl], writes=[Bktl])
                    S.op('dve', lambda e, i=i, be4=be4: e.tensor_tensor(vb4, vt4[i], bc_last(be4, 128), op=ALU.mult), reads=[Bld[i], Bbg, Bvb4], writes=[Bvb4])
                    S.op('dve', lambda e, i=i: e.tensor_tensor(qd4, qT4[i], Er, op=ALU.mult), reads=[Bld[i], BEr, Bqd], writes=[Bqd])
                    bw = rbank()
                    for hh in range(4):
                        S.op('pe', lambda e, hh=hh, bw=bw, Tc=Tc: e.matmul(psum[bw][:, hh * 128:(hh + 1) * 128], kbg4[:, hh, :], Tc[:, hh, :], start=True, stop=True),
                             reads=[Bkbg, BTc], writes=[Bps[bw]])
                    S.op('act', lambda e, bw=bw: e.activation(out=nw4, in_=psum[bw][:].rearrange("p (a b) -> p a b", b=128), func=AF.Copy, scale=-1.0),
                         reads=[Bps[bw], Bnw], writes=[Bnw])
                    if dbg.get('_cut') == 6:
                        return
                    for hh in range(4):
                        S.op('pe', lambda e, hh=hh, Tc=Tc: e.matmul(psum[1][:, hh * 128:(hh + 1) * 128], Tc[:, hh, :], vb4[:, hh, :], start=True, stop=False),
                             reads=[BTc, Bvb4], writes=[Bps[1]])
                        S.op('pe', lambda e, hh=hh, g=g: e.matmul(psum[1][:, hh * 128:(hh + 1) * 128], nw4[:, hh, :], Sb4[g][:, hh, :], start=False, stop=True),
                             reads=[Bnw, BSb[g]], writes=[Bps[1]])
                    S.op('act', lambda e: e.activation(out=vn4, in_=psum[1][:].rearrange("p (a b) -> p a b", b=128), func=AF.Copy), reads=[Bps[1], Bvn], writes=[Bvn])
                    for hh in range(4):
                        S.op('pe', lambda e, hh=hh, g=g: e.matmul(psum[2][:, hh * 128:(hh + 1) * 128], Sb4[g][:, hh, :], qd4[:, hh, :], start=True, stop=False),
                             reads=[BSb[g], Bqd], writes=[Bps[2]])
                        S.op('pe', lambda e, hh=hh: e.matmul(psum[2][:, hh * 128:(hh + 1) * 128], vn4[:, hh, :], At4[:, hh, :], start=False, stop=True),
                             reads=[Bvn, BAt4], writes=[Bps[2]])
                    for hh in range(4):
                        S.op('pe', lambda e, hh=hh: e.matmul(psum[3][:, hh * 128:(hh + 1) * 128], ktl4[:, hh, :], vn4[:, hh, :], start=True, stop=True),
                             reads=[Bktl, Bvn], writes=[Bps[3]])
                    if dbg.get('_cut') == 7:
                        return
                    S.op('dve', lambda e, hs=hs, cd4=cd4: e.tensor_tensor(Sst[:, hs, :], Sst[:, hs, :], bc_last(cd4, 128), op=ALU.mult), reads=[BSst[g], Bst8], writes=[BSst[g]])
                    S.op('act', lambda e: e.activation(out=Er2, in_=psum[3][:].rearrange("p (a b) -> p a b", b=128), func=AF.Copy), reads=[Bps[3], BEr2], writes=[BEr2])
                    S.op('dve', lambda e, hs=hs: e.tensor_tensor(Sst[:, hs, :], Sst[:, hs, :], Er2, op=ALU.add), reads=[BSst[g], BEr2], writes=[BSst[g]])
                    S.op('act', lambda e, g=g, hs=hs: e.activation(out=Sb4[g], in_=Sst[:, hs, :], func=AF.Copy), reads=[BSst[g], BSb[g]], writes=[BSb[g]])
                    if dbg.get('_cut') == 8:
                        return
                    pO = psum[2][:].rearrange("p (a b) -> p a b", b=128)
                    if ph == 0:
                        S.op('act', lambda e, i=i, pO=pO: e.activation(out=ot[i], in_=pO, func=AF.Copy), reads=[Bps[2], Bot[i]], writes=[Bot[i]])
                        S.dma(lambda e, i=i, h0=h0, ts_=ts_: e.dma_start(out=o1_s[h0:h0 + 4, :, ts_].rearrange("h p t -> p h t"), in_=ot[i]),
                              reads=[Bot[i]], writes=[Bo1], store=True)
                    else:
                        S.dma(lambda e, i=i, h0=h0, ts_=ts_: e.dma_start(out=o1t[i], in_=o1_s[h0:h0 + 4, :, ts_].rearrange("h p t -> p h t")), reads=[Bo1], writes=[Bo1t[i]])
                        S.dma(lambda e, i=i, h0=h0, ts_=ts_: e.dma_start(out=zt[i], in_=z_s[h0:h0 + 4, :, ts_].rearrange("h p t -> p h t")), reads=[sc['z']], writes=[Bzt[i]])
                        S.op('act', lambda e, i=i, pO=pO: e.activation(out=ot[i], in_=pO, func=AF.Copy), reads=[Bps[2], Bot[i]], writes=[Bot[i]])
                        S.op('dve', lambda e, i=i: e.tensor_tensor(ot[i], ot[i], o1t[i], op=ALU.add), reads=[Bo1t[i], Bot[i]], writes=[Bot[i]])
                        S.op('act', lambda e, i=i: e.activation(out=sq4, in_=ot[i], func=AF.Square), reads=[Bot[i], Bsq4], writes=[Bsq4])
                        S.op('pe', lambda e: e.matmul(psum[7][:], ones_f[:], sq4.rearrange("p a b -> p (a b)"), start=True, stop=True), reads=[Bsq4, Bconst], writes=[Bps[7]])
                        rsqrt_ops(rs4.rearrange("p a b -> p (a b)"), psum[7][:], 1.0 / 128, [Bps[7], Brs4], [Brs4])
                        S.op('dve', lambda e, i=i: e.scalar_tensor_tensor(out=ot[i], in0=ot[i], scalar=lyr[:, L_GNW:L_GNW + 1], in1=rs4, op0=ALU.mult, op1=ALU.mult),
                             reads=[Bot[i], Blyr, Brs4], writes=[Bot[i]])
                        S.op('dve', lambda e, i=i, h0=h0, ts_=ts_: e.tensor_tensor(ygT[:, h0:h0 + 4, ts_], ot[i], zt[i], op=ALU.mult),
                             reads=[Bot[i], Bzt[i], Byg[g]], writes=[Byg[g]])


        def exchange(l, sc):
            AR.reset()
            Bss = S.buf('st_snd'); Bsr = S.buf('st_rcv')
            for g in range(2):
                S.dma(lambda e, g=g: e.dma_start(out=st_snd[:, g * 512:(g + 1) * 512], in_=Sst[:, g * 4:g * 4 + 4, :].rearrange("p a b -> p (a b)")),
                      reads=[BSst[g]], writes=[Bss], store=True)
            S.dma(lambda e: allreduce(e, st_snd, st_rcv), reads=[Bss], writes=[Bsr], q='pool', inc=1)
            sr = AR.get([1024]); Bsrt = S.buf('sr')
            S.dma(lambda e: e.dma_start(out=sr, in_=st_rcv[:, :]), reads=[Bsr], writes=[Bsrt])
            for g in range(2):
                gs = slice(g * 512, (g + 1) * 512)
                Sg = Sst[:, g * 4:g * 4 + 4, :].rearrange("p a b -> p (a b)")
                S.op('dve', lambda e, gs=gs, Sg=Sg: e.tensor_tensor(Sg, sr[:, gs], Sg, op=ALU.subtract), reads=[Bsrt, BSst[g]], writes=[BSst[g]])

        def attention(l, sc, nparts):
            AR.reset()
            q_sb = [AR.get([NT], BF16) for _ in range(2)]; k_sb = [AR.get([NK], BF16) for _ in range(2)]
            v_sb = [AR.get([KT, 128], BF16) for _ in range(2)]; Bqkv = S.bufs(2, 'qkv')
            pTt = [AR.get([CH], BF16) for _ in range(3)]; BpT = S.bufs(3, 'pT')
            om = [AR.get([CH]) for _ in range(2)]; Bom = S.bufs(2, 'om')
            rd = AR.get([CH]); Brd = S.buf('rd')
            oc = AR.get([CH]); Boc = S.buf('oc'); sqa = AR.get([CH]); Bsqa = S.buf('sqa'); rsa = AR.get([CH]); Brsa = S.buf('rsa')
            post = 1.0 - lam_init[l]
            pc = 0
            for h in range(NH):
                i = h % 2
                S.dma(lambda e, i=i, h=h: e.dma_start(out=q_sb[i], in_=qT_s[h, :, :]), reads=[sc['q']], writes=[Bqkv[i]])
                ksrc = (kv_rcv_k if PAIR else kT_s).rearrange("(r h p) t -> r h p t", h=NH, p=128)
                vsrc = (kv_rcv_v if PAIR else v_s).rearrange("(r t) v -> r t v", t=NT)
                for part in range(nparts):
                    S.dma(lambda e, i=i, h=h, part=part, ksrc=ksrc: e.dma_start(out=k_sb[i][:, part * NT:(part + 1) * NT], in_=ksrc[part, h, :, :]),
                          reads=[sc['k']], writes=[Bqkv[i]], more=True)
                    S.dma(lambda e, i=i, h=h, part=part, vsrc=vsrc: e.dma_start(out=v_sb[i][:, part * TT:(part + 1) * TT, :],
                                                                     in_=vsrc[part, :, h * 128:(h + 1) * 128].rearrange("(t p) v -> p t v", p=128)),
                          reads=[sc['v']], writes=[Bqkv[i]], more=True)
                for n in range(NCH):
                    cs = slice(n * CH, (n + 1) * CH)
                    for m in range(2):
                        ms = slice(m * 64, (m + 1) * 64)
                        for kt in range(KT):
                            sbk = kt % 2
                            r = pc % 3
                            pc += 1
                            S.op('pe', lambda e, i=i, ms=ms, kt=kt, sbk=sbk, cs=cs: e.matmul(P(sbk, CH), k_sb[i][ms, kt * 128:(kt + 1) * 128], q_sb[i][ms, cs], start=True, stop=True),
                                 reads=[Bqkv[i]], writes=[Bps[sbk]])
                            S.op('act', lambda e, r=r, sbk=sbk: e.activation(out=pTt[r], in_=P(sbk, CH), func=AF.Exp), reads=[Bps[sbk]], writes=[BpT[r]])
                            S.op('pe', lambda e, i=i, r=r, kt=kt, m=m: e.matmul(P(2 + m, CH), v_sb[i][:, kt, :], pTt[r], start=(kt == 0), stop=(kt == KT - 1)),
                                 reads=[Bqkv[i], BpT[r]], writes=[Bps[2 + m]])
                            S.op('pe', lambda e, r=r, kt=kt, m=m: e.matmul(P(4 + m, CH), ones_b[:], pTt[r], start=(kt == 0), stop=(kt == KT - 1)),
                                 reads=[Bconst, BpT[r]], writes=[Bps[4 + m]])
                        S.op('act', lambda e, m=m: e.activation(out=rd, in_=P(4 + m, CH), func=AF.Copy), reads=[Bps[4 + m], Brd], writes=[Brd])
                        S.op('dve', lambda e: e.reciprocal(rd, rd), reads=[Brd], writes=[Brd])
                        S.op('act', lambda e, m=m: e.activation(out=om[m], in_=P(2 + m, CH), func=AF.Copy), reads=[Bps[2 + m], Bom[m]], writes=[Bom[m]])
                        S.op('dve', lambda e, m=m: e.tensor_tensor(om[m], om[m], rd, op=ALU.mult), reads=[Brd, Bom[m]], writes=[Bom[m]])
                    S.op('dve', lambda e: e.scalar_tensor_tensor(out=oc, in0=om[1], scalar=lyr[:, L_LAMC:L_LAMC + 1], in1=om[0], op0=ALU.mult, op1=ALU.add),
                         reads=[Bom[0], Bom[1], Blyr, Boc], writes=[Boc])
                    S.op('act', lambda e: e.activation(out=sqa, in_=oc, func=AF.Square), reads=[Boc, Bsqa], writes=[Bsqa])
                    S.op('pe', lambda e: e.matmul(P(6, CH), ones_f[:], sqa, start=True, stop=True), reads=[Bsqa, Bconst], writes=[Bps[6]])
                    rsqrt_ops(rsa, P(6, CH), 1.0 / 128, [Bps[6], Brsa], [Brsa], post=post)
                    S.op('dve', lambda e, h=h, cs=cs: e.scalar_tensor_tensor(out=ydT[:, h, cs], in0=oc, scalar=lyr[:, L_SUB:L_SUB + 1], in1=rsa, op0=ALU.mult, op1=ALU.mult),
                         reads=[Boc, Blyr, Brsa, Byd[h]], writes=[Byd[h]])

        def out_proj(l, sc):
            AR.reset()
            mg = AR.get([KC, NT], BF16); Bmg = [[S.buf() for _ in range(NCH)] for _ in range(KC)]
            gd = [AR.get([CH], BF16) for _ in range(2)]; gg = [AR.get([CH], BF16) for _ in range(2)]; Bgt = S.bufs(2, 'gt')
            t1 = [AR.get([CH]) for _ in range(2)]; Bt1 = S.bufs(2, 't1')
            t2 = [AR.get([CH]) for _ in range(2)]; Bt2 = S.bufs(2, 't2')
            it = 0
            for j in range(4):
                for which in range(2):
                    w_, Bw_ = WS.get((('bd', 'bg')[which], l, j))
                    for m in range(4):
                        mc = j * 4 + m
                        for n in range(NCH):
                            cs = slice(n * CH, (n + 1) * CH)
                            i = it % 2
                            it += 1
                            p1 = it % 4
                            src = ydT if which == 0 else ygT
                            for kc in range(8):
                                rdb = Byd[kc] if which == 0 else Byg[kc // 4]
                                S.op('pe', lambda e, kc=kc, m=m, cs=cs, p1=p1, w_=w_, src=src: e.matmul(P(p1, CH), w_[:, kc, m * 128:(m + 1) * 128], src[:, kc, cs], start=(kc == 0), stop=(kc == 7)),
                                     reads=[Bw_, rdb], writes=[Bps[p1]])
                            S.dma(lambda e, i=i, mc=mc, cs=cs, which=which: e.dma_start(out=gd[i], in_=gg_s[16 * which + mc, :, cs]), reads=[sc['gg']], writes=[Bgt[i]])
                            S.op('act', lambda e, i=i, p1=p1: e.activation(out=t1[i], in_=P(p1, CH), func=AF.Copy), reads=[Bps[p1], Bt1[i]], writes=[Bt1[i]])
                            if which == 0:
                                S.op('dve', lambda e, i=i, mc=mc, cs=cs: e.tensor_tensor(mg[:, mc, cs], t1[i], gd[i], op=ALU.mult), reads=[Bgt[i], Bt1[i], Bmg[mc][n]], writes=[Bmg[mc][n]])
                            else:
                                S.op('dve', lambda e, i=i: e.tensor_tensor(t1[i], t1[i], gd[i], op=ALU.mult), reads=[Bgt[i], Bt1[i]], writes=[Bt1[i]])
                                S.op('dve', lambda e, i=i, mc=mc, cs=cs: e.tensor_tensor(mg[:, mc, cs], mg[:, mc, cs], t1[i], op=ALU.add), reads=[Bt1[i], Bmg[mc][n]], writes=[Bmg[mc][n]])
            it = 0
            for j in range(4):
                w, Bw = WS.get(('out', l, j))
                for m in range(4):
                    mc = j * 4 + m
                    for n in range(NCH):
                        cs = slice(n * CH, (n + 1) * CH)
                        pb = 4 + it % 2
                        it += 1
                        for kc in range(KC):
                            S.op('pe', lambda e, kc=kc, m=m, cs=cs, pb=pb, w=w: e.matmul(P(pb, CH), w[:, kc, m * 128:(m + 1) * 128], mg[:, kc, cs], start=(kc == 0), stop=(kc == KC - 1)),
                                 reads=[Bw, Bmg[kc][n]], writes=[Bps[pb]])
                        i2 = it % 2
                        S.op('act', lambda e, mc=mc, pb=pb, i2=i2: e.activation(out=t1[i2], in_=P(pb, CH), func=AF.Copy, scale=mod[:, 32 + mc:33 + mc]), reads=[Bps[pb], Bmod, Bt1[i2]], writes=[Bt1[i2]])
                        S.op('dve', lambda e, mc=mc, cs=cs, i2=i2: e.tensor_tensor(xT[:, mc, cs], xT[:, mc, cs], t1[i2], op=ALU.add), reads=[Bt1[i2], BxT[mc][n]], writes=[BxT[mc][n]])

        def ffn(l):
            modnorm(1)
            AR.reset()
            aT = AR.get([FC, CH], BF16); BaT = S.bufs(FC, 'aT')
            sg_off = AR.off
            sg = [AR.get([CH], BF16) for _ in range(2)]; Bsg = S.bufs(2, 'sg')
            uc = [AR.get([CH], BF16) for _ in range(2)]; Buc = S.bufs(2, 'uc')
            xdf = arena[:, sg_off // 4:sg_off // 4 + CH]
            it = 0
            for n in range(NCH):
                cs = slice(n * CH, (n + 1) * CH)
                for j in range(11):
                    for which in range(2):
                        w_, Bw_ = WS.get((('upg', 'upu')[which], l, n, j))
                        for m in range(4):
                            jc = j * 4 + m
                            i = it % 2
                            it += 1
                            p1 = it % 4
                            for kc in range(KC):
                                S.op('pe', lambda e, kc=kc, m=m, p1=p1, w_=w_: e.matmul(P(p1, CH), w_[:, kc, m * 128:(m + 1) * 128], hT[:, kc, cs], start=(kc == 0), stop=(kc == KC - 1)),
                                     reads=[Bw_, BhT[n]], writes=[Bps[p1]])
                            if which == 0:
                                S.op('act', lambda e, p1=p1, jc=jc: e.activation(out=aT[:, jc, :], in_=P(p1, CH), func=AF.Silu), reads=[Bps[p1], BaT[jc]], writes=[BaT[jc]])
                            else:
                                S.op('act', lambda e, i=i, p1=p1: e.activation(out=uc[i], in_=P(p1, CH), func=AF.Copy), reads=[Bps[p1], Buc[i]], writes=[Buc[i]])
                                S.op('dve', lambda e, i=i, jc=jc: e.tensor_tensor(aT[:, jc, :], aT[:, jc, :], uc[i], op=ALU.mult), reads=[Buc[i], BaT[jc]], writes=[BaT[jc]])
                for m in range(KC):
                    w, Bw = WS.get(('dn', l, n, m))
                    pb = 4 + m % 2
                    for kc in range(FC):
                        S.op('pe', lambda e, kc=kc, pb=pb, w=w: e.matmul(P(pb, CH), w[:, kc, :], aT[:, kc, :], start=(kc == 0), stop=(kc == FC - 1)),
                             reads=[Bw, BaT[kc]], writes=[Bps[pb]])
                    S.op('act', lambda e, m=m, pb=pb: e.activation(out=xdf, in_=P(pb, CH), func=AF.Copy, scale=mod[:, 80 + m:81 + m]), reads=[Bps[pb], Bmod], writes=[Bsg[0], Bsg[1]])
                    S.op('dve', lambda e, m=m: e.tensor_tensor(xT[:, m, cs], xT[:, m, cs], xdf, op=ALU.add), reads=[Bsg[0], Bsg[1], BxT[m][n]], writes=[BxT[m][n]])


        setup()
        stage = dbg.get('_stage', None)
        for l in range(DEPTH):
            load_layer_params(l)
            ada_mod(l)
            modnorm(0)
            sc = in_proj(l)
            if stage == 'inproj':
                break
            gp = gdn_prep(l, sc)
            if stage == 'prep':
                break
            Bo1 = S.buf('o1_s')
            gdn_scan(l, 0, sc, gp, Bo1, True)
            if stage == 'scan0':
                break
            if PAIR:
                exchange(l, sc)
            attention(l, sc, 2 if PAIR else 1)
            if stage == 'attn':
                break
            gdn_scan(l, 1, sc, gp, Bo1, not PAIR)
            if stage == 'mixer':
                break
            out_proj(l, sc)
            if stage == 'outproj':
                break
            ffn(l)
        def dump_sb(name, src, rd):
            S.dma(lambda e: e.dma_start(out=dbg_d[name], in_=src), reads=rd, writes=[Bout], store=True)
        S.barrier()
        if 'mod' in dbg_d:
            dump_sb('mod', mod[:], [Bmod])
        if 'cos' in dbg_d:
            dump_sb('cos', cosT[:], [Brope]); dump_sb('sin', sinT[:], [Brope])
        if 'beta' in dbg_d:
            dump_sb('beta', betaT[:], [Bbg]); dump_sb('g', gT[:], [Bbg])
        if 'hT' in dbg_d:
            AR.reset()
            t = AR.get([NT]); Bt_ = S.buf()
            for kc in range(KC):
                S.op('dve', lambda e, kc=kc: e.tensor_copy(t, hT[:, kc, :]), reads=BhT, writes=[Bt_])
                S.dma(lambda e, kc=kc: e.dma_start(out=dbg_d['hT'][kc * 128:(kc + 1) * 128, :], in_=t), reads=[Bt_], writes=[Bout], store=True)
        for nm, view in (('yd', ydT), ('yg', ygT)):
            if nm in dbg_d:
                AR.reset()
                t = AR.get([NT]); Bt_ = S.buf()
                for kc in range(8):
                    S.op('dve', lambda e, kc=kc, view=view: e.tensor_copy(t, view[:, kc, :]), reads=Byd + Byg, writes=[Bt_])
                    S.dma(lambda e, kc=kc, nm=nm: e.dma_start(out=dbg_d[nm][kc * 128:(kc + 1) * 128, :], in_=t), reads=[Bt_], writes=[Bout], store=True)
        ov = out_d.rearrange("(k p) n -> p k n", p=128)
        for kc in range(KC):
            S.dma(lambda e, kc=kc: e.dma_start(out=ov[:, kc, :], in_=xT[:, kc, :]), reads=BxT[kc], writes=[Bout], store=True)
        S.final_wait('sp', [Bout])
        S.emit()
    return nc


def prep_inputs(inp, NT, PAIR, DEPTH, n_cores=8):
    f32 = np.float32
    x = np.asarray(inp['x'], f32)
    B, SEQ, _ = x.shape
    c = np.asarray(inp['c'], f32)
    pos = np.asarray(inp['positions']).astype(np.int32)
    w_in = np.asarray(inp['w_in'], f32)
    conv_w = np.asarray(inp['gdn_conv_w'], f32)
    a_log = np.asarray(inp['gdn_a_log'], f32)
    dt_bias = np.asarray(inp['gdn_dt_bias'], f32)
    invf = (np.float32(10000.0) ** (-np.arange(32, dtype=np.float32) * np.float32(2.0) / np.float32(64))).astype(f32)
    shared = {
        'invf': np.ascontiguousarray(np.broadcast_to(invf[None, :], (128, 32))),
        'ada_w': np.asarray(inp['ada_w'], f32),
        'ada_bT': np.ascontiguousarray(np.asarray(inp['ada_b'], f32).reshape(DEPTH, 96, 128).transpose(0, 2, 1)),
        'nmwT': np.ascontiguousarray(np.asarray(inp['norm_mix_w'], f32).reshape(DEPTH, 16, 128).transpose(0, 2, 1)),
        'nfwT': np.ascontiguousarray(np.asarray(inp['norm_ffn_w'], f32).reshape(DEPTH, 16, 128).transpose(0, 2, 1)),
        'w_in': w_in,
        'qn_w': np.asarray(inp['diff_qn_w'], f32),
        'kn_w': np.asarray(inp['diff_kn_w'], f32),
        'lamv': np.ascontiguousarray(np.asarray(inp['diff_lambda'], f32).reshape(DEPTH, 256)),
        'sublnT': np.ascontiguousarray(np.asarray(inp['diff_subln_w'], f32).reshape(DEPTH, 128, 1)),
        'gnwT': np.ascontiguousarray(np.asarray(inp['gdn_norm_w'], f32).reshape(DEPTH, 128, 1)),
        'w_bd': np.asarray(inp['w_branch_diff'], f32),
        'w_bg': np.asarray(inp['w_branch_gdn'], f32),
        'w_out': np.asarray(inp['w_out'], f32),
        'w_up': np.asarray(inp['ffn_w_up'], f32),
        'w_down': np.asarray(inp['ffn_w_down'], f32),
    }
    role = {}
    for r in (0, 1):
        dmap = [0, 1] if r == 0 else [1, 0]
        cols = [7168 + d * 8 + h for d in dmap for h in range(8)] + [7184 + d * 8 + h for d in dmap for h in range(8)]
        cw = conv_w if r == 0 else conv_w[:, ::-1, :]
        role[r] = {
            'w_dir': np.ascontiguousarray(w_in[:, :, cols]),
            'a_log': np.ascontiguousarray(a_log[:, dmap, :].reshape(DEPTH, 16)),
            'dt_bias': np.ascontiguousarray(dt_bias[:, dmap, :].reshape(DEPTH, 16)),
            'convT': np.ascontiguousarray(cw.reshape(DEPTH, 5, 24, 128).transpose(0, 3, 2, 1)),
            'sel': np.ascontiguousarray(np.broadcast_to(np.array([[0.0, 1.0, 1.0, 0.0] if r == 0 else [1.0, 0.0, 0.0, 1.0]], f32), (128, 4))),
        }
    maps = []
    meta = []
    for core in range(n_cores):
        if PAIR:
            b, r = core // 2, core % 2
            tok = np.arange(NT) if r == 0 else (SEQ - 1 - np.arange(NT))
        else:
            b, r = core % B, 0
            tok = np.arange(NT)
        m = dict(shared)
        m.update(role[r])
        m['xT'] = np.ascontiguousarray(x[b, tok, :].T)
        m['posT'] = np.ascontiguousarray(pos[b, tok].reshape(NT // 128, 128).T)
        m['cT'] = np.ascontiguousarray(c[b].reshape(16, 128).T)
        maps.append(m)
        meta.append((b, tok))
    return maps, meta


_PROG_CACHE = {}


def kernel(**inputs):
    x = np.asarray(inputs['x'])
    B, SEQ, _ = x.shape
    DEPTH = int(np.asarray(inputs['ada_w']).shape[0])
    NT = SEQ // 2
    key = (NT, DEPTH)
    if key not in _PROG_CACHE:
        _PROG_CACHE[key] = build_program(NT, DEPTH, True, {})
    nc = _PROG_CACHE[key]
    maps, meta = prep_inputs(inputs, NT, True, DEPTH, n_cores=8)
    res = run_bass_kernel_spmd(nc, maps, core_ids=list(range(8)))
    out = np.empty((B, SEQ, D), np.float32)
    for core in range(8):
        b, tok = meta[core]
        out[b, tok, :] = np.asarray(res.results[core]['outT'], np.float32).T
    return out
```

```python
import math
import types
from contextlib import ExitStack

import numpy as np
import concourse.bass as bass
import concourse.mybir as mybir
from concourse.bass_utils import run_bass_kernel_spmd

F32 = mybir.dt.float32
BF16 = mybir.dt.bfloat16
I32 = mybir.dt.int32
AF = mybir.ActivationFunctionType
ALU = mybir.AluOpType
AX = mybir.AxisListType

D = 2048
KC = 16
NH = 8
FFN = 5632
FC = 44
IN_COLS = 11296
EPS = 1e-6
COMPUTE = ('pe', 'act', 'dve', 'pool')


class Buf:
    __slots__ = ('name', 'w', 'r', 'sem', 'cnt')

    def __init__(self, name):
        self.name = name
        self.w = []
        self.r = []
        self.sem = None
        self.cnt = 0


def _snap(fn):
    if fn is None or fn.__closure__ is None:
        return fn
    cells = []
    for c in fn.__closure__:
        try:
            cells.append(types.CellType(c.cell_contents))
        except ValueError:
            cells.append(c)
    return types.FunctionType(fn.__code__, fn.__globals__, fn.__name__, fn.__defaults__, tuple(cells))


class Sched:
    def __init__(self, nc, stack):
        self.nc = nc
        self.stack = stack
        self.ops = {e: [] for e in COMPUTE + ('sp',)}
        self.seq = {e: 0 for e in COMPUTE}
        self.esem = {e: stack.enter_context(nc.semaphore('s_' + e)) for e in COMPUTE}
        self.waited = {e: {} for e in COMPUTE + ('sp',)}
        self.nsem = 0
        self.nbuf = 0
        self.dbufs = []
        self.named = {}

    def buf(self, name=None):
        self.nbuf += 1
        if name is None:
            return Buf('b%d' % self.nbuf)
        if name not in self.named:
            self.named[name] = Buf(name)
        return self.named[name]

    def bufs(self, n, name='b'):
        return [self.buf('%s%d' % (name, i)) for i in range(n)]

    def _dsem(self, b):
        if b.sem is None:
            b.sem = self.stack.enter_context(self.nc.semaphore('d%d' % self.nsem))
            self.nsem += 1
            self.dbufs.append(b)
        return b.sem

    def _deps(self, eng, reads, writes):
        evs = []
        for b in reads:
            evs.extend(b.w)
        for b in writes:
            evs.extend(b.w)
            evs.extend(b.r)
        waits = {}
        wd = self.waited[eng]
        for ev in evs:
            sem, val, src = ev[0], ev[1], ev[2]
            if src == 'pe' and eng == 'pe':
                continue
            if src == 'dma':
                val = ev[3].cnt
            if wd.get(sem, 0) >= val:
                continue
            if waits.get(sem, 0) < val:
                waits[sem] = val
        for sem, val in waits.items():
            wd[sem] = val
        return list(waits.items())

    def op(self, eng, fn, reads=(), writes=()):
        waits = self._deps(eng, reads, writes)
        self.seq[eng] += 1
        ev = (self.esem[eng], self.seq[eng], eng)
        for b in reads:
            b.r.append(ev)
        for b in writes:
            b.w = [ev]
            b.r = []
        self.ops[eng].append((waits, _snap(fn), self.esem[eng], 1))

    def dma(self, fn, reads=(), writes=(), q='sp', more=False, store=False, inc=16):
        d = writes[0]
        if more or store:
            saved = d.w
            d.w = []
        waits = self._deps(q, reads, writes)
        sbuf_side = reads[0] if store else d
        sem = self._dsem(sbuf_side)
        sbuf_side.cnt += inc
        ev = (sem, sbuf_side.cnt, 'dma', sbuf_side)
        for b in reads:
            b.r.append(ev)
        if more or store:
            d.w = [x for x in saved if x[0] is not sem] + [ev]
        else:
            d.w = [ev]
            d.r = []
        self.ops[q].append((waits, _snap(fn), sem, inc))

    def barrier(self):
        tgt = [(self.esem[e], self.seq[e]) for e in COMPUTE if self.seq[e] > 0]
        tgt += [(b.sem, b.cnt) for b in self.dbufs if b.cnt > 0 and not b.name.startswith('wt')]
        for eng in ('pe', 'act', 'dve', 'sp'):
            wd = self.waited[eng]
            waits = []
            for (s, v) in tgt:
                if eng in COMPUTE and s is self.esem[eng]:
                    continue
                if wd.get(s, 0) < v:
                    wd[s] = v
                    waits.append((s, v))
            if waits:
                self.ops[eng].append((waits, None, None, 0))

    def final_wait(self, eng, bufs):
        waits = self._deps(eng, bufs, ())
        self.ops[eng].append((waits, None, None, 0))

    def emit(self):
        nc = self.nc
        with nc.Block() as block:
            def run(e, name):
                for (waits, fn, sem, inc) in self.ops[name]:
                    for (s, v) in waits:
                        e.wait_ge(s, v)
                    if fn is not None:
                        fn(e).then_inc(sem, inc)

            @block.tensor
            def _(e):
                run(e, 'pe')

            @block.scalar
            def _(e):
                run(e, 'act')

            @block.vector
            def _(e):
                run(e, 'dve')

            @block.gpsimd
            def _(e):
                run(e, 'pool')

            @block.sync
            def _(e):
                run(e, 'sp')


def bc_mid(ap2, n):
    P, Fd = ap2.shape
    return ap2.unsqueeze(1).to_broadcast([P, n, Fd])


def bc_last(ap2, n):
    P, G = ap2.shape
    return ap2.unsqueeze(2).to_broadcast([P, G, n])


def build_program(NT, DEPTH, PAIR, dbg=None):
    dbg = dbg or {}
    CDT = BF16 if dbg.get('_chain16') else F32
    TT = NT // 128
    CH = min(512, NT)
    NCH = NT // CH
    NK = 2 * NT if PAIR else NT
    KT = NK // 128
    lam_init = [0.8 - 0.6 * math.exp(-0.3 * l) for l in range(DEPTH)]

    nc = bass.Bass("TRN2", target_bir_lowering=False)

    def din(name, shape, dt=F32):
        return nc.dram_tensor(name, list(shape), dt, kind="ExternalInput").ap()

    xT_d = din("xT", [D, NT])
    pos_d = din("posT", [128, TT], I32)
    c_d = din("cT", [128, KC])
    invf_d = din("invf", [128, 32])
    sel_d = din("sel", [128, 4])
    ada_w_d = din("ada_w", [DEPTH, D, 6 * D])
    ada_b_d = din("ada_bT", [DEPTH, 128, 96])
    nmw_d = din("nmwT", [DEPTH, 128, KC])
    nfw_d = din("nfwT", [DEPTH, 128, KC])
    w_in_d = din("w_in", [DEPTH, D, IN_COLS])
    w_dir_d = din("w_dir", [DEPTH, D, 32])
    qnw_d = din("qn_w", [DEPTH, 64])
    knw_d = din("kn_w", [DEPTH, 64])
    lamv_d = din("lamv", [DEPTH, 256])
    subln_d = din("sublnT", [DEPTH, 128, 1])
    conv_d = din("convT", [DEPTH, 128, 24, 5])
    alog_d = din("a_log", [DEPTH, 16])
    dtb_d = din("dt_bias", [DEPTH, 16])
    gnw_d = din("gnwT", [DEPTH, 128, 1])
    wbd_d = din("w_bd", [DEPTH, 1024, D])
    wbg_d = din("w_bg", [DEPTH, 1024, D])
    wout_d = din("w_out", [DEPTH, D, D])
    wup_d = din("w_up", [DEPTH, D, 2 * FFN])
    wdn_d = din("w_down", [DEPTH, FFN, D])
    out_d = nc.dram_tensor("outT", [D, NT], F32, kind="ExternalOutput").ap()
    dbg_d = {k: nc.dram_tensor("dbg_" + k, list(shp), F32, kind="ExternalOutput").ap()
             for k, shp in dbg.items() if not k.startswith('_')}

    def dscr(name, shape, dt=BF16):
        return nc.dram_tensor(name, list(shape), dt).ap()

    qT_s = dscr("qT_s", [NH, 128, NT])
    kT_s = dscr("kT_s", [NH * 128, NT])
    v_s = dscr("v_s", [NT, 1024])
    g_s = dscr("g_s", [24, 128, NT])
    z_s = dscr("z_s", [NH, 128, NT])
    gg_s = dscr("gg_s", [32, 128, NT])
    gq_s = dscr("gq_s", [NH, 128, NT])
    gk_s = dscr("gk_s", [NH, 128, NT])
    gkt_s = dscr("gkt_s", [NH, NT, 128])
    gvt_s = dscr("gvt_s", [NH, NT, 128])
    o1_s = dscr("o1_s", [NH, 128, NT], F32)
    if PAIR:
        halo_snd = dscr("halo_snd", [2 * 128, 48])
        halo_rcv = dscr("halo_rcv", [2 * 128, 48])
        st_snd = dscr("st_snd", [128, NH * 128], F32)
        st_rcv = dscr("st_rcv", [128, NH * 128], F32)
        kv_snd_k = dscr("kv_snd_k", [2 * NH * 128, NT])
        kv_rcv_k = dscr("kv_rcv_k", [2 * NH * 128, NT])
        kv_snd_v = dscr("kv_snd_v", [2 * NT, 1024])
        kv_rcv_v = dscr("kv_rcv_v", [2 * NT, 1024])

    def allreduce(e, src, dst):
        return e.collective_compute("AllReduce", ALU.add, replica_groups=GROUPS, ins=[src.opt()], outs=[dst.opt()])
    GROUPS = [[2 * i, 2 * i + 1] for i in range(dbg.get('_ncores', 8) // 2)]

    with ExitStack() as stack:
        S = Sched(nc, stack)
        sb = lambda name, shape, dt=F32: nc.alloc_sbuf_tensor("sb_" + name, list(shape), dt)

        xT = sb("xT", [128, KC, NT]);            BxT = [[S.buf() for _ in range(NCH)] for _ in range(KC)]
        hT = sb("hT", [128, KC, NT], BF16);      BhT = [S.buf() for _ in range(NCH)]
        NW = 3
        wt = [sb("wt%d" % i, [128, 16 * 512], BF16) for i in range(NW)]
        Bwt = S.bufs(NW, 'wt')
        ident_f = sb("ident_f", [128, 128]);     ident_b = sb("ident_b", [128, 128], BF16)
        ones_f = sb("ones_f", [128, 128]);       ones_b = sb("ones_b", [128, 128], BF16)
        m_Ui = sb("m_Ui", [128, 128]); m_Li = sb("m_Li", [128, 128])
        m_Us = sb("m_Us", [128, 128]); m_Ls = sb("m_Ls", [128, 128])
        Bconst = S.buf('const')
        cosT = sb("cosT", [128, TT, 32]); sinT = sb("sinT", [128, TT, 32]); Brope = S.buf('rope')
        cact = sb("cact", [128, KC], BF16); Bcact = S.buf('cact')
        mod = sb("mod", [128, 96]); Bmod = S.buf('mod')
        modA = sb("modA", [128, 2, KC]); BmodA = S.buf('modA')
        lyr = sb("lyr", [128, 1024]); Blyr = S.buf('lyr')
        betaT = sb("betaT", [128, TT, 16]); gT = sb("gT", [128, TT, 16]); Bbg = S.buf('bg')
        sel = sb("sel", [128, 4]); Bsel = S.buf('sel')
        ARENA = 48 * 1024
        arena = sb("arena", [128, ARENA // 4])
        psum = [nc.alloc_psum_tensor("ps%d" % i, [128, 512], F32) for i in range(8)]
        Bps = S.bufs(8, 'ps')

        def P(i, n=512):
            return psum[i][:, 0:n]

        class Arena:
            def __init__(self):
                self.off = 0

            def reset(self):
                S.barrier()
                self.off = 0

            def get(self, shape, dt=F32):
                esz = 4 if dt == F32 or dt == I32 else 2
                n = int(np.prod(shape))
                nbytes = (n * esz + 31) // 32 * 32
                assert self.off + nbytes <= ARENA, ("arena overflow", self.off, nbytes)
                v = arena[:, self.off // 4:(self.off + nbytes) // 4]
                self.off += nbytes
                if dt != F32:
                    v = v.bitcast(dt)
                v = v[:, 0:n]
                if len(shape) == 2:
                    return v.rearrange("p (a b) -> p a b", b=shape[1])
                if len(shape) == 3:
                    return v.rearrange("p (a b c) -> p a b c", b=shape[1], c=shape[2])
                return v

        AR = Arena()

        class WStream:
            def __init__(self):
                self.descs = []
                self.issued = 0
                self.taken = 0
                self.loaded = {}

            def add(self, tag, src, kch, ncols):
                self.descs.append((tag, src, kch, ncols))

            def _issue(self, i):
                tag, src, kch, ncols = self.descs[i]
                slot = i % NW
                t = wt[slot][:, 0:kch * ncols].rearrange("p (k n) -> p k n", n=ncols)
                first = True
                for k0 in range(0, kch, 16):
                    k1 = min(kch, k0 + 16)
                    S.dma(lambda e, t=t, src=src, k0=k0, k1=k1: e.dma_start(out=t[:, k0:k1, :], in_=src[:, k0:k1, :]),
                          writes=[Bwt[slot]], q='pool', more=not first)
                    first = False
                self.loaded[i] = (t, Bwt[slot])

            def get(self, tag):
                i = self.taken
                assert self.descs[i][0] == tag, (self.descs[i][0], tag)
                while self.issued < len(self.descs) and self.issued <= i + NW - 1:
                    self._issue(self.issued)
                    self.issued += 1
                self.taken += 1
                return self.loaded.pop(i)

        WS = WStream()

        def wview(w2d, c0, n):
            return w2d.rearrange("(k p) n -> p k n", p=128)[:, :, c0:c0 + n]

        for l in range(DEPTH):
            for j in range(24):
                WS.add(('ada', l, j), wview(ada_w_d[l], j * 512, 512), KC, 512)
            for j in range(6):
                WS.add(('tm', l, j), wview(w_in_d[l], j * 512, 512), KC, 512)
            WS.add(('dir', l), wview(w_dir_d[l], 0, 32), KC, 32)
            for j in range(16):
                WS.add(('fm', l, j), wview(w_in_d[l], 3072 + j * 512 + (32 if j >= 8 else 0), 512), KC, 512)
            for j in range(4):
                WS.add(('bd', l, j), wview(wbd_d[l], j * 512, 512), 8, 512)
                WS.add(('bg', l, j), wview(wbg_d[l], j * 512, 512), 8, 512)
            for j in range(4):
                WS.add(('out', l, j), wview(wout_d[l], j * 512, 512), KC, 512)
            for n in range(NCH):
                for j in range(11):
                    WS.add(('upg', l, n, j), wview(wup_d[l], j * 512, 512), KC, 512)
                    WS.add(('upu', l, n, j), wview(wup_d[l], FFN + j * 512, 512), KC, 512)
                for m in range(16):
                    WS.add(('dn', l, n, m), wview(wdn_d[l], m * 128, 128), FC, 128)

        Bout = S.buf('out')

        def setup():
            S.op('pool', lambda e: e.memset(ident_f[:], 0.0), writes=[Bconst])
            S.op('pool', lambda e: e.affine_select(out=ident_f[:], in_=ident_f[:], pattern=[[-1, 128]],
                                                   compare_op=ALU.not_equal, fill=1.0, base=0, channel_multiplier=1),
                 reads=[Bconst], writes=[Bconst])
            S.op('pool', lambda e: e.memset(ones_f[:], 1.0), reads=[Bconst], writes=[Bconst])
            for (m, op, sgn) in ((m_Ls, ALU.is_gt, 1), (m_Li, ALU.is_ge, 1), (m_Us, ALU.is_gt, -1), (m_Ui, ALU.is_ge, -1)):
                S.op('pool', lambda e, m=m, op=op, sgn=sgn: e.affine_select(out=m[:], in_=ones_f[:], pattern=[[-sgn, 128]],
                                                                             compare_op=op, fill=0.0, base=0, channel_multiplier=sgn),
                     reads=[Bconst], writes=[Bconst])
            S.op('pool', lambda e: e.tensor_copy(ident_b[:], ident_f[:]), reads=[Bconst], writes=[Bconst])
            S.op('pool', lambda e: e.tensor_copy(ones_b[:], ones_f[:]), reads=[Bconst], writes=[Bconst])
            xv = xT_d.rearrange("(k p) n -> p k n", p=128)
            for kc in range(KC):
                for n in range(NCH):
                    S.dma(lambda e, kc=kc, n=n: e.dma_start(out=xT[:, kc, n * CH:(n + 1) * CH], in_=xv[:, kc, n * CH:(n + 1) * CH]),
                          writes=[BxT[kc][n]])
            S.dma(lambda e: e.dma_start(out=sel[:], in_=sel_d[:, :]), writes=[Bsel])
            AR.reset()
            posi = AR.get([TT], I32); posf = AR.get([TT]); invf = AR.get([32]); ang = AR.get([TT, 32]); kf = AR.get([TT, 32])
            ki = AR.get([TT, 32], I32); cf = AR.get([KC])
            Bt = S.buf('setup_t')
            S.dma(lambda e: e.dma_start(out=posi, in_=pos_d[:, :]), writes=[Bt])
            S.dma(lambda e: e.dma_start(out=invf, in_=invf_d[:, :]), writes=[Bt], more=True)
            S.dma(lambda e: e.dma_start(out=cf, in_=c_d[:, :]), writes=[Bt], more=True)
            S.op('dve', lambda e: e.tensor_copy(posf, posi), reads=[Bt], writes=[Bt])
            for t in range(TT):
                S.op('dve', lambda e, t=t: e.tensor_scalar(ang[:, t, :], invf, posf[:, t:t + 1], None, op0=ALU.mult),
                     reads=[Bt], writes=[Bt])
            def reduce_sin(dst, shift):
                S.op('dve', lambda e: e.tensor_scalar(kf, ang, shift, 1.0 / (2 * math.pi), op0=ALU.add, op1=ALU.mult), reads=[Bt, Brope], writes=[Bt])
                S.op('dve', lambda e: e.tensor_copy(ki, kf), reads=[Bt], writes=[Bt])
                S.op('dve', lambda e: e.tensor_copy(kf, ki), reads=[Bt], writes=[Bt])
                S.op('dve', lambda e: e.scalar_tensor_tensor(out=kf, in0=kf, scalar=-2 * math.pi, in1=ang, op0=ALU.mult, op1=ALU.add),
                     reads=[Bt], writes=[Bt])
                S.op('dve', lambda e: e.tensor_scalar(kf, kf, shift, None, op0=ALU.add), reads=[Bt], writes=[Bt])
                S.op('dve', lambda e: e.tensor_scalar(kf, kf, -math.pi, math.pi, op0=ALU.max, op1=ALU.min), reads=[Bt], writes=[Bt])
                S.op('act', lambda e: e.activation(out=dst, in_=kf, func=AF.Sin), reads=[Bt, Brope], writes=[Brope])
            reduce_sin(sinT[:], 0.0)
            reduce_sin(cosT[:], math.pi / 2)
            S.op('act', lambda e: e.activation(out=cact[:], in_=cf, func=AF.Silu), reads=[Bt], writes=[Bcact])

        L_QNW, L_KNW = 0, 64
        L_LAM = 128
        L_CONV = 384
        L_ALOG, L_DTB = 504, 520
        L_ADAB = 536
        L_NMW, L_NFW = 632, 648
        L_SUB, L_GNW = 664, 665
        L_LAMC = 666
        L_NEGA = 668
        L_TMP = 700

        def load_layer_params(l):
            def ld(off, n, src, more=True):
                S.dma(lambda e: e.dma_start(out=lyr[:, off:off + n], in_=src), writes=[Blyr], more=more)
            S.dma(lambda e: e.dma_start(out=lyr[:, L_QNW:L_QNW + 64], in_=qnw_d[l].partition_broadcast(128)), writes=[Blyr])
            ld(L_KNW, 64, knw_d[l].partition_broadcast(128))
            ld(L_LAM, 256, lamv_d[l].partition_broadcast(128))
            ld(L_CONV, 120, conv_d[l].rearrange("p a b -> p (a b)"))
            ld(L_ALOG, 16, alog_d[l].partition_broadcast(128))
            ld(L_DTB, 16, dtb_d[l].partition_broadcast(128))
            ld(L_ADAB, 96, ada_b_d[l])
            ld(L_NMW, 16, nmw_d[l])
            ld(L_NFW, 16, nfw_d[l])
            ld(L_SUB, 1, subln_d[l])
            ld(L_GNW, 1, gnw_d[l])
            rw = dict(reads=[Blyr], writes=[Blyr])
            S.op('dve', lambda e: e.tensor_scalar(lyr[:, L_QNW:L_QNW + 64], lyr[:, L_QNW:L_QNW + 64], 0.125, None, op0=ALU.mult), **rw)
            S.op('dve', lambda e: e.tensor_tensor(lyr[:, L_TMP:L_TMP + 64], lyr[:, L_LAM:L_LAM + 64], lyr[:, L_LAM + 64:L_LAM + 128], op=ALU.mult), **rw)
            S.op('dve', lambda e: e.tensor_tensor(lyr[:, L_TMP + 64:L_TMP + 128], lyr[:, L_LAM + 128:L_LAM + 192], lyr[:, L_LAM + 192:L_LAM + 256], op=ALU.mult), **rw)
            S.op('dve', lambda e: e.tensor_reduce(out=lyr[:, L_TMP + 128:L_TMP + 130], in_=lyr[:, L_TMP:L_TMP + 128].rearrange("p (a b) -> p a b", b=64),
                                                  axis=AX.X, op=ALU.add), **rw)
            S.op('act', lambda e: e.activation(out=lyr[:, L_TMP + 128:L_TMP + 130], in_=lyr[:, L_TMP + 128:L_TMP + 130], func=AF.Exp), **rw)
            S.op('dve', lambda e: e.tensor_tensor(lyr[:, L_LAMC:L_LAMC + 1], lyr[:, L_TMP + 128:L_TMP + 129], lyr[:, L_TMP + 129:L_TMP + 130], op=ALU.subtract), **rw)
            S.op('dve', lambda e: e.tensor_scalar(lyr[:, L_LAMC:L_LAMC + 1], lyr[:, L_LAMC:L_LAMC + 1], lam_init[l], -1.0, op0=ALU.add, op1=ALU.mult), **rw)
            S.op('act', lambda e: e.activation(out=lyr[:, L_NEGA:L_NEGA + 16], in_=lyr[:, L_ALOG:L_ALOG + 16], func=AF.Exp), **rw)
            S.op('dve', lambda e: e.tensor_scalar(lyr[:, L_NEGA:L_NEGA + 16], lyr[:, L_NEGA:L_NEGA + 16], -1.0, None, op0=ALU.mult), **rw)

        def ada_mod(l):
            pm = 7
            for j in range(24):
                w, Bw = WS.get(('ada', l, j))
                for m in range(4):
                    col = j * 4 + m
                    for kc in range(KC):
                        S.op('pe', lambda e, w=w, m=m, kc=kc, col=col: e.matmul(psum[pm][:, col:col + 1], w[:, kc, m * 128:(m + 1) * 128],
                                                                                   cact[:, kc:kc + 1], start=(kc == 0), stop=(kc == KC - 1)),
                             reads=[Bw, Bcact], writes=[Bps[pm]])
            S.op('dve', lambda e: e.tensor_tensor(mod[:], psum[pm][:, 0:96], lyr[:, L_ADAB:L_ADAB + 96], op=ALU.add),
                 reads=[Bps[pm], Blyr], writes=[Bmod])
            S.op('dve', lambda e: e.scalar_tensor_tensor(out=modA[:, 0, :], in0=mod[:, 16:32], scalar=1.0, in1=lyr[:, L_NMW:L_NMW + 16],
                                                         op0=ALU.add, op1=ALU.mult), reads=[Bmod, Blyr], writes=[BmodA])
            S.op('dve', lambda e: e.scalar_tensor_tensor(out=modA[:, 1, :], in0=mod[:, 64:80], scalar=1.0, in1=lyr[:, L_NFW:L_NFW + 16],
                                                         op0=ALU.add, op1=ALU.mult), reads=[Bmod, Blyr, BmodA], writes=[BmodA])

        def rsqrt_ops(dst, src, scale, rd, wr, eps=EPS, post=1.0):
            S.op('act', lambda e: e.activation(out=dst, in_=src, func=AF.Ln, scale=scale, bias=eps), reads=rd, writes=wr)
            S.op('act', lambda e: e.activation(out=dst, in_=dst, func=AF.Exp, scale=-0.5, bias=math.log(post)), reads=wr, writes=wr)

        def modnorm(which):
            sh0 = 0 if which == 0 else 48
            AR.reset()
            sq = [AR.get([CH]) for _ in range(2)]; Bsq = S.bufs(2, 'sq')
            rstd = AR.get([CH]); Brs = S.buf('rstd')
            tmp = [AR.get([CH]) for _ in range(2)]; Btmp = S.bufs(2, 'tmp')
            for n in range(NCH):
                cs = slice(n * CH, (n + 1) * CH)
                for kc in range(KC):
                    i = kc % 2
                    S.op('act', lambda e, i=i, kc=kc: e.activation(out=sq[i], in_=xT[:, kc, cs], func=AF.Square),
                         reads=[BxT[kc][n]], writes=[Bsq[i]])
                    S.op('pe', lambda e, i=i, kc=kc: e.matmul(P(6, CH), ones_f[:], sq[i], start=(kc == 0), stop=(kc == KC - 1)),
                         reads=[Bsq[i], Bconst], writes=[Bps[6]])
                rsqrt_ops(rstd, P(6, CH), 1.0 / D, [Bps[6]], [Brs])
                for kc in range(KC):
                    i = kc % 2
                    S.op('dve', lambda e, i=i, kc=kc: e.scalar_tensor_tensor(out=tmp[i], in0=xT[:, kc, cs], scalar=modA[:, which, kc:kc + 1], in1=rstd,
                                                                            op0=ALU.mult, op1=ALU.mult),
                         reads=[BxT[kc][n], BmodA, Brs], writes=[Btmp[i]])
                    S.op('act', lambda e, i=i, kc=kc: e.activation(out=hT[:, kc, cs], in_=tmp[i], func=AF.Identity,
                                                                   bias=mod[:, sh0 + kc:sh0 + kc + 1], scale=1.0),
                         reads=[Btmp[i], Bmod], writes=[BhT[n]])

        def fm_block(w, Bw, kch, ncols, rhs_fn, rhs_bufs_fn, evac, pbanks, cnt):
            for m in range(ncols // 128):
                for n in range(NCH):
                    pb = pbanks[cnt[0] % len(pbanks)]
                    cnt[0] += 1
                    for kc in range(kch):
                        S.op('pe', lambda e, pb=pb, m=m, n=n, kc=kc: e.matmul(P(pb, CH), w[:, kc, m * 128:(m + 1) * 128], rhs_fn(kc, n),
                                                                             start=(kc == 0), stop=(kc == kch - 1)),
                             reads=[Bw] + rhs_bufs_fn(kc, n), writes=[Bps[pb]])
                    evac(m, n, pb)

        def in_proj(l):
            AR.reset()
            sqb = [AR.get([512]) for _ in range(2)]; Bsqb = S.bufs(2, 'sqb')
            ss8 = [AR.get([8]) for _ in range(2)]; Bss8 = S.bufs(2, 'ss8')
            qn = [AR.get([512]) for _ in range(2)]; Bqn = S.bufs(2, 'qn')
            rot = [AR.get([512]) for _ in range(2)]; Brot = S.bufs(2, 'rot')
            rt2 = [AR.get([512]) for _ in range(2)]; Brt2 = S.bufs(2, 'rt2')
            qb = [AR.get([512], BF16) for _ in range(2)]; Bqb = S.bufs(2, 'qb')
            qtb = [AR.get([4, 128], BF16) for _ in range(2)]; Bqtb = S.bufs(2, 'qtb')
            vb_ = [AR.get([512], BF16) for _ in range(3)]; Bvb = S.bufs(3, 'vb')
            fo = [AR.get([CH], BF16) for _ in range(3)]; Bfo = S.bufs(3, 'fo')
            sm = AR.get([64]); Bsm = S.buf('sm')
            if PAIR:
                mk = [AR.get([2, 512], BF16) for _ in range(2)]; Bmk = S.bufs(2, 'mk')
                hm = [AR.get([2, 2], BF16) for _ in range(3)]; Bhm = S.bufs(3, 'hm')
            Bq_s = S.buf('q_s'); Bk_s = S.buf('k_s'); Bv_s = S.buf('v_s')
            cnt = [0]
            ev = [0]
            for j in range(6):
                w, Bw = WS.get(('tm', l, j))
                for tt in range(TT):
                    pb = cnt[0] % 4
                    cnt[0] += 1
                    for kc in range(KC):
                        S.op('pe', lambda e, pb=pb, tt=tt, kc=kc, w=w: e.matmul(P(pb), hT[:, kc, tt * 128:(tt + 1) * 128], w[:, kc, :],
                                                                               start=(kc == 0), stop=(kc == KC - 1)),
                             reads=[Bw, BhT[(tt * 128) // CH]], writes=[Bps[pb]])
                    if j < 4:
                        i = ev[0] % 2
                        ev[0] += 1
                        isq = (j < 2)
                        woff = L_QNW if isq else L_KNW
                        S.op('act', lambda e, i=i, pb=pb: e.activation(out=sqb[i], in_=P(pb), func=AF.Square), reads=[Bps[pb]], writes=[Bsqb[i]])
                        S.op('dve', lambda e, i=i: e.tensor_reduce(out=ss8[i], in_=sqb[i].rearrange("p (g d) -> p g d", d=64), axis=AX.X, op=ALU.add),
                             reads=[Bsqb[i]], writes=[Bss8[i]])
                        rsqrt_ops(ss8[i], ss8[i], 1.0 / 64, [Bss8[i]], [Bss8[i]])
                        S.op('dve', lambda e, i=i, pb=pb: e.tensor_tensor(qn[i].rearrange("p (g d) -> p g d", d=64), P(pb).rearrange("p (g d) -> p g d", d=64),
                                                                          bc_last(ss8[i], 64), op=ALU.mult),
                             reads=[Bps[pb], Bss8[i]], writes=[Bqn[i]])
                        S.op('dve', lambda e, i=i, woff=woff: e.tensor_tensor(qn[i].rearrange("p (g d) -> p g d", d=64), qn[i].rearrange("p (g d) -> p g d", d=64),
                                                                             bc_mid(lyr[:, woff:woff + 64], 8), op=ALU.mult),
                             reads=[Bqn[i], Blyr], writes=[Bqn[i]])
                        q4 = qn[i].rearrange("p (g t f) -> p g t f", t=2, f=32)
                        r4 = rot[i].rearrange("p (g t f) -> p g t f", t=2, f=32)
                        s4 = rt2[i].rearrange("p (g t f) -> p g t f", t=2, f=32)
                        cb = bc_mid(cosT[:, tt, :], 8)
                        sn = bc_mid(sinT[:, tt, :], 8)
                        for t_ in range(2):
                            S.op('dve', lambda e, t_=t_, q4=q4, r4=r4, cb=cb: e.tensor_tensor(r4[:, :, t_, :], q4[:, :, t_, :], cb, op=ALU.mult),
                                 reads=[Bqn[i], Brope, Brot[i]], writes=[Brot[i]])
                            S.op('dve', lambda e, t_=t_, q4=q4, s4=s4, sn=sn: e.tensor_tensor(s4[:, :, t_, :], q4[:, :, 1 - t_, :], sn, op=ALU.mult),
                                 reads=[Bqn[i], Brope, Brt2[i]], writes=[Brt2[i]])
                        b4 = qb[i].rearrange("p (g t f) -> p g t f", t=2, f=32)
                        S.op('dve', lambda e, r4=r4, s4=s4, b4=b4: e.tensor_tensor(b4[:, :, 0, :], r4[:, :, 0, :], s4[:, :, 0, :], op=ALU.subtract),
                             reads=[Brot[i], Brt2[i], Bqb[i]], writes=[Bqb[i]])
                        S.op('dve', lambda e, r4=r4, s4=s4, b4=b4: e.tensor_tensor(b4[:, :, 1, :], r4[:, :, 1, :], s4[:, :, 1, :], op=ALU.add),
                             reads=[Brot[i], Brt2[i], Bqb[i]], writes=[Bqb[i]])
                        tb = 4 + (ev[0] % 2)
                        pT = psum[tb][:].bitcast(BF16)
                        for hh in range(4):
                            S.op('pe', lambda e, hh=hh, i=i, pT=pT: e.transpose(pT[:, hh * 128:(hh + 1) * 128], qb[i][:, hh * 128:(hh + 1) * 128], ident_b[:]),
                                 reads=[Bqb[i], Bconst], writes=[Bps[tb]])
                        S.op('act', lambda e, i=i, pT=pT: e.activation(out=qtb[i], in_=pT[:, 0:512].rearrange("p (a b) -> p a b", b=128), func=AF.Copy),
                             reads=[Bps[tb]], writes=[Bqtb[i]])
                        h0 = (j % 2) * 4
                        if isq:
                            dst = qT_s[h0:h0 + 4, :, tt * 128:(tt + 1) * 128].rearrange("h p t -> p h t")
                            S.dma(lambda e, dst=dst, i=i: e.dma_start(out=dst, in_=qtb[i]), reads=[Bqtb[i]], writes=[Bq_s], store=True)
                        elif not PAIR:
                            dst = kT_s.rearrange("(h p) t -> h p t", p=128)[h0:h0 + 4, :, tt * 128:(tt + 1) * 128].rearrange("h p t -> p h t")
                            S.dma(lambda e, dst=dst, i=i: e.dma_start(out=dst, in_=qtb[i]), reads=[Bqtb[i]], writes=[Bk_s], store=True)
                        else:
                            for r in range(2):
                                S.op('dve', lambda e, i=i, r=r: e.tensor_scalar(mk[i][:, r, :], qtb[i].rearrange("p a b -> p (a b)"), sel[:, 2 + r:3 + r], None, op0=ALU.mult),
                                     reads=[Bqtb[i], Bsel, Bmk[i]], writes=[Bmk[i]])
                            for r in range(2):
                                dst = kv_snd_k.rearrange("(r h p) t -> r h p t", h=NH, p=128)[r, h0:h0 + 4, :, tt * 128:(tt + 1) * 128].rearrange("h p t -> p h t")
                                S.dma(lambda e, dst=dst, i=i, r=r: e.dma_start(out=dst, in_=mk[i][:, r, :].rearrange("p (a b) -> p a b", b=128)), reads=[Bmk[i]], writes=[Bk_s], store=True)
                    else:
                        i = ev[0] % 3
                        ev[0] += 1
                        S.op('act', lambda e, i=i, pb=pb: e.activation(out=vb_[i], in_=P(pb), func=AF.Copy), reads=[Bps[pb]], writes=[Bvb[i]])
                        c0 = (j - 4) * 512
                        if not PAIR:
                            S.dma(lambda e, i=i, tt=tt, c0=c0: e.dma_start(out=v_s[tt * 128:(tt + 1) * 128, c0:c0 + 512], in_=vb_[i]),
                                  reads=[Bvb[i]], writes=[Bv_s], store=True)
                        else:
                            i2 = ev[0] % 2
                            for r in range(2):
                                S.op('dve', lambda e, i=i, i2=i2, r=r: e.tensor_scalar(mk[i2][:, r, :], vb_[i], sel[:, 2 + r:3 + r], None, op0=ALU.mult),
                                     reads=[Bvb[i], Bsel, Bmk[i2]], writes=[Bmk[i2]])
                            for r in range(2):
                                S.dma(lambda e, i2=i2, tt=tt, c0=c0, r=r: e.dma_start(out=kv_snd_v[r * NT + tt * 128:r * NT + (tt + 1) * 128, c0:c0 + 512], in_=mk[i2][:, r, :]),
                                      reads=[Bmk[i2]], writes=[Bv_s], store=True)
            w, Bw = WS.get(('dir', l))
            for tt in range(TT):
                pb = 4 + tt % 2
                for kc in range(KC):
                    S.op('pe', lambda e, pb=pb, tt=tt, kc=kc, w=w: e.matmul(psum[pb][:, 0:32], hT[:, kc, tt * 128:(tt + 1) * 128], w[:, kc, :],
                                                                           start=(kc == 0), stop=(kc == KC - 1)),
                         reads=[Bw, BhT[(tt * 128) // CH]], writes=[Bps[pb]])
                S.op('act', lambda e, pb=pb: e.activation(out=sm[:, 0:16], in_=psum[pb][:, 0:16], func=AF.Exp, scale=-1.0), reads=[Bps[pb]], writes=[Bsm])
                S.op('dve', lambda e: e.tensor_scalar(sm[:, 0:16], sm[:, 0:16], 1.0, None, op0=ALU.add), reads=[Bsm], writes=[Bsm])
                S.op('dve', lambda e, tt=tt: e.reciprocal(betaT[:, tt, :], sm[:, 0:16]), reads=[Bsm, Bbg], writes=[Bbg])
                S.op('dve', lambda e, pb=pb: e.tensor_tensor(sm[:, 16:32], psum[pb][:, 16:32], lyr[:, L_DTB:L_DTB + 16], op=ALU.add),
                     reads=[Bps[pb], Blyr, Bsm], writes=[Bsm])
                S.op('dve', lambda e: e.tensor_scalar(sm[:, 32:48], sm[:, 16:32], 0.0, None, op0=ALU.min), reads=[Bsm], writes=[Bsm])
                S.op('dve', lambda e: e.tensor_scalar(sm[:, 48:64], sm[:, 16:32], 0.0, None, op0=ALU.max), reads=[Bsm], writes=[Bsm])
                S.op('dve', lambda e: e.tensor_tensor(sm[:, 32:48], sm[:, 32:48], sm[:, 48:64], op=ALU.subtract), reads=[Bsm], writes=[Bsm])
                S.op('act', lambda e: e.activation(out=sm[:, 32:48], in_=sm[:, 32:48], func=AF.Exp), reads=[Bsm], writes=[Bsm])
                S.op('act', lambda e: e.activation(out=sm[:, 32:48], in_=sm[:, 32:48], func=AF.Ln, bias=1.0, scale=1.0), reads=[Bsm], writes=[Bsm])
                S.op('dve', lambda e: e.tensor_tensor(sm[:, 32:48], sm[:, 32:48], sm[:, 48:64], op=ALU.add), reads=[Bsm], writes=[Bsm])
                S.op('dve', lambda e, tt=tt: e.tensor_tensor(gT[:, tt, :], sm[:, 32:48], lyr[:, L_NEGA:L_NEGA + 16], op=ALU.mult),
                     reads=[Bsm, Blyr, Bbg], writes=[Bbg])
            Bg_s = S.buf('g_s'); Bz_s = S.buf('z_s'); Bgg_s = S.buf('gg_s'); Bhs = S.buf('halo_snd')
            fcnt = [0]
            for j in range(16):
                w, Bw = WS.get(('fm', l, j))

                def evac(m, n, pb, j=j):
                    i = fcnt[0] % 3
                    fcnt[0] += 1
                    ch = j * 4 + m
                    cs = slice(n * CH, (n + 1) * CH)
                    if ch < 24:
                        S.op('dve', lambda e: e.tensor_copy(fo[i], P(pb, CH)), reads=[Bps[pb]], writes=[Bfo[i]])
                        S.dma(lambda e: e.dma_start(out=g_s[ch, :, cs], in_=fo[i]), reads=[Bfo[i]], writes=[Bg_s], store=True)
                        if PAIR and n == NCH - 1:
                            ih = ch % 3
                            for r in range(2):
                                S.op('dve', lambda e, r=r: e.tensor_scalar(hm[ih][:, r, :], fo[i][:, CH - 2:CH], sel[:, 2 + r:3 + r], None, op0=ALU.mult),
                                     reads=[Bfo[i], Bsel, Bhm[ih]], writes=[Bhm[ih]])
                            for r in range(2):
                                S.dma(lambda e, r=r: e.dma_start(out=halo_snd[r * 128:(r + 1) * 128, 2 * ch:2 * ch + 2], in_=hm[ih][:, r, :]), reads=[Bhm[ih]], writes=[Bhs], store=True)
                    elif ch < 32:
                        S.op('act', lambda e: e.activation(out=fo[i], in_=P(pb, CH), func=AF.Silu), reads=[Bps[pb]], writes=[Bfo[i]])
                        S.dma(lambda e: e.dma_start(out=z_s[ch - 24, :, cs], in_=fo[i]), reads=[Bfo[i]], writes=[Bz_s], store=True)
                    else:
                        S.op('act', lambda e: e.activation(out=fo[i], in_=P(pb, CH), func=AF.Sigmoid), reads=[Bps[pb]], writes=[Bfo[i]])
                        S.dma(lambda e: e.dma_start(out=gg_s[ch - 32, :, cs], in_=fo[i]), reads=[Bfo[i]], writes=[Bgg_s], store=True)

                fm_block(w, Bw, KC, 512, lambda kc, n: hT[:, kc, n * CH:(n + 1) * CH], lambda kc, n: [BhT[n]], evac, [0, 1, 2, 3], cnt)
            return dict(q=Bq_s, k=Bk_s, v=Bv_s, g=Bg_s, z=Bz_s, gg=Bgg_s, hs=Bhs)


        ydT = hT[:, 0:8, :]
        ygT = hT[:, 8:16, :]
        Byd = S.bufs(NH, 'yd'); Byg = S.bufs(2, 'yg')

        def gdn_prep(l, sc):
            AR.reset()
            gin = [AR.get([NT + 4], BF16) for _ in range(2)]; Bgin = S.bufs(2, 'gin')
            acc = [AR.get([NT]) for _ in range(2)]; Bacc = S.bufs(2, 'acc')
            sqt = [AR.get([CH]) for _ in range(2)]; Bsqt = S.bufs(2, 'sqt')
            rs = [AR.get([CH]) for _ in range(2)]; Brs_ = S.bufs(2, 'rs')
            nb = [AR.get([NT], BF16) for _ in range(2)]; Bnb = S.bufs(2, 'nb')
            tk = [AR.get([TT, 128], BF16) for _ in range(2)]; Btk = S.bufs(2, 'tk')
            Bgq = S.buf('gq_s'); Bgk = S.buf('gk_s'); Bgkt = S.buf('gkt_s'); Bgvt = S.buf('gvt_s')
            it = 0
            if PAIR:
                Bhr = S.buf('halo_rcv')
                S.dma(lambda e: allreduce(e, halo_snd, halo_rcv), reads=[sc['hs']], writes=[Bhr], q='pool', inc=1)
                Bkr = S.buf('kv_rcv_k'); Bvr = S.buf('kv_rcv_v')
                S.dma(lambda e: allreduce(e, kv_snd_k, kv_rcv_k), reads=[sc['k']], writes=[Bkr], q='pool', inc=1)
                S.dma(lambda e: allreduce(e, kv_snd_v, kv_rcv_v), reads=[sc['v']], writes=[Bvr], q='pool', inc=1)
                sc['k'] = Bkr
                sc['v'] = Bvr
                hl = AR.get([2, 48], BF16); hlf = AR.get([48]); hl2 = AR.get([48], BF16); Bhl = S.buf('hl')
                S.dma(lambda e: e.dma_start(out=hl, in_=halo_rcv.rearrange("(r p) x -> p r x", p=128)), reads=[Bhr], writes=[Bhl])
                S.op('dve', lambda e: e.tensor_scalar(hlf, hl[:, 0, :], sel[:, 0:1], None, op0=ALU.mult), reads=[Bhl, Bsel], writes=[Bhl])
                S.op('dve', lambda e: e.scalar_tensor_tensor(out=hl2, in0=hl[:, 1, :], scalar=sel[:, 1:2], in1=hlf, op0=ALU.mult, op1=ALU.add), reads=[Bhl, Bsel], writes=[Bhl])

                def halo_fill(gt, Bg, ch):
                    S.op('dve', lambda e: e.tensor_copy(gt[:, NT + 2:NT + 3], hl2[:, 2 * ch + 1:2 * ch + 2]), reads=[Bhl, Bg], writes=[Bg])
                    S.op('dve', lambda e: e.tensor_copy(gt[:, NT + 3:NT + 4], hl2[:, 2 * ch:2 * ch + 1]), reads=[Bhl, Bg], writes=[Bg])
            for h in range(NH):
                for qi in range(3):
                    ch = qi * 8 + h
                    i = it % 2
                    it += 1
                    S.dma(lambda e, i=i, ch=ch: e.dma_start(out=gin[i][:, 2:NT + 2], in_=g_s[ch, :, :]), reads=[sc['g']], writes=[Bgin[i]])
                    S.op('dve', lambda e, i=i: e.memset(gin[i][:, 0:2], 0.0), reads=[Bgin[i]], writes=[Bgin[i]])
                    if PAIR:
                        halo_fill(gin[i], Bgin[i], ch)
                    else:
                        S.op('dve', lambda e, i=i: e.memset(gin[i][:, NT + 2:NT + 4], 0.0), reads=[Bgin[i]], writes=[Bgin[i]])
                    cw = L_CONV + ch * 5
                    S.op('dve', lambda e, i=i, cw=cw: e.tensor_scalar(acc[i], gin[i][:, 0:NT], lyr[:, cw:cw + 1], None, op0=ALU.mult),
                         reads=[Bgin[i], Blyr], writes=[Bacc[i]])
                    for tap in range(1, 5):
                        S.op('dve', lambda e, i=i, cw=cw, tap=tap: e.scalar_tensor_tensor(out=acc[i], in0=gin[i][:, tap:tap + NT], scalar=lyr[:, cw + tap:cw + tap + 1],
                                                                                        in1=acc[i], op0=ALU.mult, op1=ALU.add),
                             reads=[Bgin[i], Blyr, Bacc[i]], writes=[Bacc[i]])
                    S.op('act', lambda e, i=i: e.activation(out=acc[i], in_=acc[i], func=AF.Silu), reads=[Bacc[i]], writes=[Bacc[i]])
                    if qi < 2:
                        post = (128.0 ** -0.5) if qi == 0 else 1.0
                        for n in range(NCH):
                            cs = slice(n * CH, (n + 1) * CH)
                            j = n % 2
                            S.op('act', lambda e, i=i, j=j, cs=cs: e.activation(out=sqt[j], in_=acc[i][:, cs], func=AF.Square), reads=[Bacc[i]], writes=[Bsqt[j]])
                            S.op('pe', lambda e, j=j: e.matmul(P(6, CH), ones_f[:], sqt[j], start=True, stop=True), reads=[Bsqt[j], Bconst], writes=[Bps[6]])
                            rsqrt_ops(rs[j], P(6, CH), 1.0, [Bps[6]], [Brs_[j]], post=post)
                            S.op('dve', lambda e, i=i, j=j, cs=cs: e.tensor_tensor(nb[i][:, cs], acc[i][:, cs], rs[j], op=ALU.mult),
                                 reads=[Bacc[i], Brs_[j], Bnb[i]], writes=[Bnb[i]])
                    else:
                        S.op('dve', lambda e, i=i: e.tensor_copy(nb[i], acc[i]), reads=[Bacc[i]], writes=[Bnb[i]])
                    if qi == 0:
                        S.dma(lambda e, i=i, h=h: e.dma_start(out=gq_s[h, :, :], in_=nb[i]), reads=[Bnb[i]], writes=[Bgq], store=True)
                    else:
                        if qi == 1:
                            S.dma(lambda e, i=i, h=h: e.dma_start(out=gk_s[h, :, :], in_=nb[i]), reads=[Bnb[i]], writes=[Bgk], store=True)
                        for t0 in range(0, TT, 4):
                            tb = 4 + ((t0 // 4) % 2)
                            pT = psum[tb][:].bitcast(BF16)
                            nt_ = min(4, TT - t0)
                            for t in range(nt_):
                                S.op('pe', lambda e, i=i, t=t, t0=t0, pT=pT: e.transpose(pT[:, t * 128:(t + 1) * 128], nb[i][:, (t0 + t) * 128:(t0 + t + 1) * 128], ident_b[:]),
                                     reads=[Bnb[i], Bconst], writes=[Bps[tb]])
                            S.op('act', lambda e, i=i, t0=t0, nt_=nt_, pT=pT: e.activation(out=tk[i][:, t0:t0 + nt_, :], in_=pT[:, 0:nt_ * 128].rearrange("p (a b) -> p a b", b=128), func=AF.Copy),
                                 reads=[Bps[tb], Btk[i]], writes=[Btk[i]])
                        dst = (gkt_s if qi == 1 else gvt_s)[h].rearrange("(t p) d -> p t d", p=128)
                        S.dma(lambda e, i=i, dst=dst: e.dma_start(out=dst, in_=tk[i]), reads=[Btk[i]], writes=[Bgkt if qi == 1 else Bgvt], store=True)
            return dict(gq=Bgq, gk=Bgk, gkt=Bgkt, gvt=Bgvt)

        Sst = sb("Sst", [128, NH, 128]); BSst = S.bufs(2, 'Sst')

        def gdn_scan(l, ph, sc, gp, Bo1, zero_state):
            AR.reset()
            TRI = m_Ui if ph == 0 else m_Li
            MST = m_Ls if ph == 0 else m_Us
            MIN = m_Ui if ph == 0 else m_Li
            tiles = list(range(TT)) if ph == 0 else list(range(TT - 1, -1, -1))
            dsl = slice(ph * 8, ph * 8 + 8)
            st8 = AR.get([64]); Bst8 = S.buf('st8')
            qT4 = [AR.get([4, 128], BF16)] * 2; kT4 = [AR.get([4, 128], BF16)] * 2
            kt4 = [AR.get([4, 128], BF16)] * 2; vt4 = [AR.get([4, 128], BF16)] * 2
            Bld = [S.buf('ld')] * 2
            Gb4 = AR.get([4, 128]); BGb = S.buf('Gb')
            ta = AR.get([4, 128]); tb_ = AR.get([4, 128]); Er = AR.get([4, 128]); Bta = S.buf('ta'); Btb = S.buf('tb'); BEr = S.buf('Er')
            L4 = AR.get([4, 128], CDT); At4 = AR.get([4, 128], BF16); BL4 = S.buf('L4'); BAt4 = S.buf('At4')
            Xb = [AR.get([4, 128], CDT) for _ in range(2)]; Yb = [AR.get([4, 128], CDT) for _ in range(2)]; Tb = [AR.get([4, 128], CDT) for _ in range(2)]
            Tfin = AR.get([4, 128], BF16); BTfin = S.buf('Tfin')
            identc = ident_f if CDT == F32 else ident_b
            BXb = S.bufs(2, 'Xb'); BYb = S.bufs(2, 'Yb'); BTb = S.bufs(2, 'Tb')
            kbg4 = AR.get([4, 128], BF16); ktl4 = AR.get([4, 128], BF16); vb4 = AR.get([4, 128], BF16)
            nw4 = AR.get([4, 128], BF16); qd4 = AR.get([4, 128], BF16); vn4 = AR.get([4, 128], BF16)
            Bkbg = S.buf('kbg'); Bktl = S.buf('ktl'); Bvb4 = S.buf('vb4'); Bnw = S.buf('nw'); Bqd = S.buf('qd'); Bvn = S.buf('vn')
            Sb4 = [AR.get([4, 128], BF16) for _ in range(2)]; BSb = S.bufs(2, 'Sb')
            ot = [AR.get([4, 128])] * 2; Bot = [S.buf('ot')] * 2
            o1t = [AR.get([4, 128])] * 2; Bo1t = [S.buf('o1t')] * 2
            zt = [AR.get([4, 128], BF16)] * 2; Bzt = [S.buf('zt')] * 2
            sq4 = AR.get([4, 128]); Bsq4 = S.buf('sq4'); rs4 = AR.get([4, 128]); Brs4 = S.buf('rs4')
            Er2 = AR.get([4, 128]); BEr2 = S.buf('Er2')
            for g in range(2):
                hs = slice(g * 4, g * 4 + 4)
                if zero_state:
                    S.op('dve', lambda e, hs=hs: e.memset(Sst[:, hs, :], 0.0), reads=[BSst[g]], writes=[BSst[g]])
                S.op('act', lambda e, g=g, hs=hs: e.activation(out=Sb4[g], in_=Sst[:, hs, :], func=AF.Copy), reads=[BSst[g]], writes=[BSb[g]])
            it = 0
            rot = [0]

            def rbank():
                rot[0] += 1
                return 4 + (rot[0] % 3)

            for tt in tiles:
                ts_ = slice(tt * 128, (tt + 1) * 128)
                S.op('pe', lambda e, tt=tt: e.matmul(psum[0][:, 0:8], TRI[:], gT[:, tt, dsl], start=True, stop=True), reads=[Bconst, Bbg], writes=[Bps[0]])
                S.op('pe', lambda e, tt=tt: e.matmul(psum[0][:, 8:16], ones_f[:], gT[:, tt, dsl], start=True, stop=True), reads=[Bconst, Bbg], writes=[Bps[0]])
                S.op('dve', lambda e: e.tensor_copy(st8[:, 0:8], psum[0][:, 0:8]), reads=[Bps[0], Bst8], writes=[Bst8])
                S.op('act', lambda e: e.activation(out=st8[:, 8:16], in_=psum[0][:, 0:8], func=AF.Exp), reads=[Bps[0], Bst8], writes=[Bst8])
                S.op('act', lambda e: e.activation(out=st8[:, 32:40], in_=psum[0][:, 0:8], func=AF.Copy, scale=-1.0), reads=[Bps[0], Bst8], writes=[Bst8])
                S.op('act', lambda e: e.activation(out=st8[:, 16:24], in_=psum[0][:, 8:16], func=AF.Exp), reads=[Bps[0], Bst8], writes=[Bst8])
                S.op('dve', lambda e: e.tensor_tensor(st8[:, 24:32], psum[0][:, 8:16], st8[:, 0:8], op=ALU.subtract), reads=[Bps[0], Bst8], writes=[Bst8])
                S.op('act', lambda e: e.activation(out=st8[:, 24:32], in_=st8[:, 24:32], func=AF.Exp), reads=[Bst8], writes=[Bst8])
                S.op('dve', lambda e, tt=tt: e.tensor_tensor(st8[:, 8:16], st8[:, 8:16], betaT[:, tt, dsl], op=ALU.mult), reads=[Bst8, Bbg], writes=[Bst8])
                if dbg.get('_cut') == 1:
                    return
                for g in range(2):
                    hs = slice(g * 4, g * 4 + 4)
                    h0 = g * 4
                    i = it % 2
                    it += 1
                    S.dma(lambda e, i=i, h0=h0, ts_=ts_: e.dma_start(out=qT4[i], in_=gq_s[h0:h0 + 4, :, ts_].rearrange("h p t -> p h t")), reads=[gp['gq']], writes=[Bld[i]])
                    S.dma(lambda e, i=i, h0=h0, ts_=ts_: e.dma_start(out=kT4[i], in_=gk_s[h0:h0 + 4, :, ts_].rearrange("h p t -> p h t")), reads=[gp['gk']], writes=[Bld[i]], more=True)
                    S.dma(lambda e, i=i, h0=h0, ts_=ts_: e.dma_start(out=kt4[i], in_=gkt_s[h0:h0 + 4, ts_, :].rearrange("h p d -> p h d")), reads=[gp['gkt']], writes=[Bld[i]], more=True)
                    S.dma(lambda e, i=i, h0=h0, ts_=ts_: e.dma_start(out=vt4[i], in_=gvt_s[h0:h0 + 4, ts_, :].rearrange("h p d -> p h d")), reads=[gp['gvt']], writes=[Bld[i]], more=True)
                    g4 = gT[:, tt, ph * 8 + h0:ph * 8 + h0 + 4]
                    be4 = betaT[:, tt, ph * 8 + h0:ph * 8 + h0 + 4]
                    gc4 = st8[:, h0:h0 + 4]; skbg4 = st8[:, 8 + h0:12 + h0]; cd4 = st8[:, 16 + h0:20 + h0]; skt4 = st8[:, 24 + h0:28 + h0]
                    if dbg.get('_cut') == 11:
                        return
                    for hh in range(4):
                        S.op('dve', lambda e, g4=g4, hh=hh: e.tensor_scalar(Gb4[:, hh, :], ones_f[:], g4[:, hh:hh + 1], None, op0=ALU.mult),
                             reads=[Bbg, BGb, Bconst], writes=[BGb])
                    for hh in range(4):
                        S.op('pe', lambda e, hh=hh: e.matmul(psum[1][:, hh * 128:(hh + 1) * 128], Gb4[:, hh, :], TRI[:], start=True, stop=True),
                             reads=[BGb, Bconst], writes=[Bps[1]])
                    if dbg.get('_cut') == 12:
                        if 'pb' in dbg_d:
                            S.op('act', lambda e: e.activation(out=Er, in_=psum[1][:].rearrange("p (a b) -> p a b", b=128), func=AF.Copy), reads=[Bps[1], BEr], writes=[BEr])
                            S.dma(lambda e: e.dma_start(out=dbg_d['pb'], in_=Er.rearrange("p a b -> p (a b)")), reads=[BEr], writes=[Bout], store=True)
                            S.dma(lambda e: e.dma_start(out=dbg_d['st8'], in_=st8), reads=[Bst8], writes=[Bout], store=True)
                            S.dma(lambda e: e.dma_start(out=dbg_d['gb'], in_=Gb4.rearrange("p a b -> p (a b)")), reads=[BGb], writes=[Bout], store=True)
                        return
                    pB = psum[1][:].rearrange("p (a b) -> p a b", b=128)
                    ngc4 = st8[:, 32 + h0:36 + h0]
                    for hh in range(4):
                        S.op('act', lambda e, hh=hh, ngc4=ngc4: e.activation(out=ta[:, hh, :], in_=psum[1][:, hh * 128:(hh + 1) * 128], func=AF.Relu,
                                                                           bias=ngc4[:, hh:hh + 1], scale=1.0), reads=[Bps[1], Bst8, Bta], writes=[Bta])
                        S.op('act', lambda e, hh=hh, gc4=gc4: e.activation(out=tb_[:, hh, :], in_=psum[1][:, hh * 128:(hh + 1) * 128], func=AF.Relu,
                                                                          bias=gc4[:, hh:hh + 1], scale=-1.0), reads=[Bps[1], Bst8, Btb], writes=[Btb])
                    S.op('act', lambda e, pB=pB: e.activation(out=Er, in_=pB, func=AF.Exp), reads=[Bps[1], BEr], writes=[BEr])
                    if dbg.get('_cut') == 13:
                        return
                    S.op('act', lambda e: e.activation(out=ta, in_=ta, func=AF.Exp, scale=-1.0), reads=[Bta], writes=[Bta])
                    S.op('act', lambda e: e.activation(out=tb_, in_=tb_, func=AF.Exp, scale=-1.0), reads=[Btb], writes=[Btb])
                    if dbg.get('_cut') == 15:
                        return
                    S.op('dve', lambda e: e.tensor_tensor(ta, ta, bc_mid(MST[:], 4), op=ALU.mult), reads=[Bta, Bconst], writes=[Bta])
                    S.op('dve', lambda e, be4=be4: e.tensor_tensor(ta, ta, bc_last(be4, 128), op=ALU.mult), reads=[Bta, Bbg], writes=[Bta])
                    S.op('dve', lambda e: e.tensor_tensor(tb_, tb_, bc_mid(MIN[:], 4), op=ALU.mult), reads=[Btb, Bconst], writes=[Btb])
                    if dbg.get('_cut') == 2:
                        return
                    for hh in range(4):
                        S.op('pe', lambda e, hh=hh, i=i: e.matmul(psum[2][:, hh * 128:(hh + 1) * 128], kT4[i][:, hh, :], kT4[i][:, hh, :], start=True, stop=True),
                             reads=[Bld[i]], writes=[Bps[2]])
                    for hh in range(4):
                        S.op('pe', lambda e, hh=hh, i=i: e.matmul(psum[3][:, hh * 128:(hh + 1) * 128], kT4[i][:, hh, :], qT4[i][:, hh, :], start=True, stop=True),
                             reads=[Bld[i]], writes=[Bps[3]])
                    pK = psum[2][:].rearrange("p (a b) -> p a b", b=128)
                    pQ = psum[3][:].rearrange("p (a b) -> p a b", b=128)
                    S.op('act', lambda e, pK=pK: e.activation(out=Er2, in_=pK, func=AF.Copy), reads=[Bps[2], BEr2], writes=[BEr2])
                    S.op('dve', lambda e: e.tensor_tensor(L4, Er2, ta, op=ALU.mult), reads=[BEr2, Bta, BL4], writes=[BL4])
                    S.op('act', lambda e, pQ=pQ: e.activation(out=Er2, in_=pQ, func=AF.Copy), reads=[Bps[3], BEr2], writes=[BEr2])
                    S.op('dve', lambda e: e.tensor_tensor(At4, Er2, tb_, op=ALU.mult), reads=[BEr2, Btb, BAt4], writes=[BAt4])
                    if dbg.get('_cut') == 3:
                        return
                    bk = rbank()
                    pT = psum[bk][:].bitcast(CDT) if CDT != F32 else psum[bk][:]
                    for hh in range(4):
                        S.op('pe', lambda e, hh=hh, pT=pT: e.transpose(pT[:, hh * 128:(hh + 1) * 128], L4[:, hh, :], identc[:]), reads=[BL4, Bconst], writes=[Bps[bk]])
                    pT3 = pT[:, 0:512].rearrange("p (a b) -> p a b", b=128)
                    S.op('act', lambda e, pT3=pT3: e.activation(out=Yb[0], in_=pT3, func=AF.Copy), reads=[Bps[bk], BYb[0]], writes=[BYb[0]])
                    S.op('dve', lambda e: e.tensor_tensor(Tb[0], bc_mid(identc[:], 4), Yb[0], op=ALU.subtract), reads=[BYb[0], Bconst, BTb[0]], writes=[BTb[0]])
                    if dbg.get('_cut') == 4:
                        return
                    Xc, BXc = L4, BL4
                    Yc, BYc = Yb[0], BYb[0]
                    Tc, BTc = Tb[0], BTb[0]
                    for lev in range(1, 7):
                        Xn, BXn = Xb[lev % 2], BXb[lev % 2]
                        Yn, BYn = Yb[lev % 2], BYb[lev % 2]
                        Tn, BTn = Tb[lev % 2], BTb[lev % 2]
                        bx = rbank()
                        for hh in range(4):
                            S.op('pe', lambda e, hh=hh, bx=bx, Xc=Xc, Yc=Yc: e.matmul(psum[bx][:, hh * 128:(hh + 1) * 128], Yc[:, hh, :], Xc[:, hh, :], start=True, stop=True),
                                 reads=[BXc, BYc], writes=[Bps[bx]])
                        if lev < 6:
                            by = rbank()
                            for hh in range(4):
                                S.op('pe', lambda e, hh=hh, by=by, Xc=Xc, Yc=Yc: e.matmul(psum[by][:, hh * 128:(hh + 1) * 128], Xc[:, hh, :], Yc[:, hh, :], start=True, stop=True),
                                     reads=[BXc, BYc], writes=[Bps[by]])
                        S.op('act', lambda e, bx=bx, Xn=Xn: e.activation(out=Xn, in_=psum[bx][:].rearrange("p (a b) -> p a b", b=128), func=AF.Copy),
                             reads=[Bps[bx], BXn], writes=[BXn])
                        if lev < 6:
                            S.op('act', lambda e, by=by, Yn=Yn: e.activation(out=Yn, in_=psum[by][:].rearrange("p (a b) -> p a b", b=128), func=AF.Copy), reads=[Bps[by], BYn], writes=[BYn])
                        bt = rbank()
                        for hh in range(4):
                            S.op('pe', lambda e, hh=hh, bt=bt, Xn=Xn, Tc=Tc: e.matmul(psum[bt][:, hh * 128:(hh + 1) * 128], Xn[:, hh, :], Tc[:, hh, :], start=True, stop=True),
                                 reads=[BXn, BTc], writes=[Bps[bt]])
                        S.op('act', lambda e, bt=bt: e.activation(out=Er2, in_=psum[bt][:].rearrange("p (a b) -> p a b", b=128), func=AF.Copy), reads=[Bps[bt], BEr2], writes=[BEr2])
                        S.op('dve', lambda e, Tn=Tn, Tc=Tc: e.tensor_tensor(Tn, Er2, Tc, op=ALU.add), reads=[BEr2, BTc, BTn], writes=[BTn])
                        Xc, BXc, Yc, BYc, Tc, BTc = Xn, BXn, Yn, BYn, Tn, BTn
                    if dbg.get('_cut') == 5:
                        return
                    S.op('act', lambda e, Tc=Tc: e.activation(out=Tfin, in_=Tc, func=AF.Copy), reads=[BTc, BTfin], writes=[BTfin])
                    Tc, BTc = Tfin, BTfin
                    S.op('dve', lambda e, i=i, skbg4=skbg4: e.tensor_tensor(kbg4, kt4[i], bc_last(skbg4, 128), op=ALU.mult), reads=[Bld[i], Bst8, Bkbg], writes=[Bkbg])
                    S.op('dve', lambda e, i=i, skt4=skt4: e.tensor_tensor(ktl4, kt4[i], bc_last(skt4, 128), op=ALU.mult), reads=[Bld[i], Bst8, Bktl], writes=[Bktl])
                    S.op('dve', lambda e, i=i, be4=be4: e.tensor_tensor(vb4, vt4[i], bc_last(be4, 128), op=ALU.mult), reads=[Bld[i], Bbg, Bvb4], writes=[Bvb4])
                    S.op('dve', lambda e, i=i: e.tensor_tensor(qd4, qT4[i], Er, op=ALU.mult), reads=[Bld[i], BEr, Bqd], writes=[Bqd])
                    bw = rbank()
                    for hh in range(4):
                        S.op('pe', lambda e, hh=hh, bw=bw, Tc=Tc: e.matmul(psum[bw][:, hh * 128:(hh + 1) * 128], kbg4[:, hh, :], Tc[:, hh, :], start=True, stop=True),
                             reads=[Bkbg, BTc], writes=[Bps[bw]])
                    S.op('act', lambda e, bw=bw: e.activation(out=nw4, in_=psum[bw][:].rearrange("p (a b) -> p a b", b=128), func=AF.Copy, scale=-1.0),
                         reads=[Bps[bw], Bnw], writes=[Bnw])
                    if dbg.get('_cut') == 6:
                        return
                    for hh in range(4):
                        S.op('pe', lambda e, hh=hh, Tc=Tc: e.matmul(psum[1][:, hh * 128:(hh + 1) * 128], Tc[:, hh, :], vb4[:, hh, :], start=True, stop=False),
                             reads=[BTc, Bvb4], writes=[Bps[1]])
                        S.op('pe', lambda e, hh=hh, g=g: e.matmul(psum[1][:, hh * 128:(hh + 1) * 128], nw4[:, hh, :], Sb4[g][:, hh, :], start=False, stop=True),
                             reads=[Bnw, BSb[g]], writes=[Bps[1]])
                    S.op('act', lambda e: e.activation(out=vn4, in_=psum[1][:].rearrange("p (a b) -> p a b", b=128), func=AF.Copy), reads=[Bps[1], Bvn], writes=[Bvn])
                    for hh in range(4):
                        S.op('pe', lambda e, hh=hh, g=g: e.matmul(psum[2][:, hh * 128:(hh + 1) * 128], Sb4[g][:, hh, :], qd4[:, hh, :], start=True, stop=False),
                             reads=[BSb[g], Bqd], writes=[Bps[2]])
                        S.op('pe', lambda e, hh=hh: e.matmul(psum[2][:, hh * 128:(hh + 1) * 128], vn4[:, hh, :], At4[:, hh, :], start=False, stop=True),
                             reads=[Bvn, BAt4], writes=[Bps[2]])
                    for hh in range(4):
                        S.op('pe', lambda e, hh=hh: e.matmul(psum[3][:, hh * 128:(hh + 1) * 128], ktl4[:, hh, :], vn4[:, hh, :], start=True, stop=True),
                             reads=[Bktl, Bvn], writes=[Bps[3]])
                    if dbg.get('_cut') == 7:
                        return
                    S.op('dve', lambda e, hs=hs, cd4=cd4: e.tensor_tensor(Sst[:, hs, :], Sst[:, hs, :], bc_last(cd4, 128), op=ALU.mult), reads=[BSst[g], Bst8], writes=[BSst[g]])
                    S.op('act', lambda e: e.activation(out=Er2, in_=psum[3][:].rearrange("p (a b) -> p a b", b=128), func=AF.Copy), reads=[Bps[3], BEr2], writes=[BEr2])
                    S.op('dve', lambda e, hs=hs: e.tensor_tensor(Sst[:, hs, :], Sst[:, hs, :], Er2, op=ALU.add), reads=[BSst[g], BEr2], writes=[BSst[g]])
                    S.op('act', lambda e, g=g, hs=hs: e.activation(out=Sb4[g], in_=Sst[:, hs, :], func=AF.Copy), reads=[BSst[g], BSb[g]], writes=[BSb[g]])
                    if dbg.get('_cut') == 8:
                        return
                    pO = psum[2][:].rearrange("p (a b) -> p a b", b=128)
                    if ph == 0:
                        S.op('act', lambda e, i=i, pO=pO: e.activation(out=ot[i], in_=pO, func=AF.Copy), reads=[Bps[2], Bot[i]], writes=[Bot[i]])
                        S.dma(lambda e, i=i, h0=h0, ts_=ts_: e.dma_start(out=o1_s[h0:h0 + 4, :, ts_].rearrange("h p t -> p h t"), in_=ot[i]),
                              reads=[Bot[i]], writes=[Bo1], store=True)
                    else:
                        S.dma(lambda e, i=i, h0=h0, ts_=ts_: e.dma_start(out=o1t[i], in_=o1_s[h0:h0 + 4, :, ts_].rearrange("h p t -> p h t")), reads=[Bo1], writes=[Bo1t[i]])
                        S.dma(lambda e, i=i, h0=h0, ts_=ts_: e.dma_start(out=zt[i], in_=z_s[h0:h0 + 4, :, ts_].rearrange("h p t -> p h t")), reads=[sc['z']], writes=[Bzt[i]])
                        S.op('act', lambda e, i=i, pO=pO: e.activation(out=ot[i], in_=pO, func=AF.Copy), reads=[Bps[2], Bot[i]], writes=[Bot[i]])
                        S.op('dve', lambda e, i=i: e.tensor_tensor(ot[i], ot[i], o1t[i], op=ALU.add), reads=[Bo1t[i], Bot[i]], writes=[Bot[i]])
                        S.op('act', lambda e, i=i: e.activation(out=sq4, in_=ot[i], func=AF.Square), reads=[Bot[i], Bsq4], writes=[Bsq4])
                        S.op('pe', lambda e: e.matmul(psum[7][:], ones_f[:], sq4.rearrange("p a b -> p (a b)"), start=True, stop=True), reads=[Bsq4, Bconst], writes=[Bps[7]])
                        rsqrt_ops(rs4.rearrange("p a b -> p (a b)"), psum[7][:], 1.0 / 128, [Bps[7], Brs4], [Brs4])
                        S.op('dve', lambda e, i=i: e.scalar_tensor_tensor(out=ot[i], in0=ot[i], scalar=lyr[:, L_GNW:L_GNW + 1], in1=rs4, op0=ALU.mult, op1=ALU.mult),
                             reads=[Bot[i], Blyr, Brs4], writes=[Bot[i]])
                        S.op('dve', lambda e, i=i, h0=h0, ts_=ts_: e.tensor_tensor(ygT[:, h0:h0 + 4, ts_], ot[i], zt[i], op=ALU.mult),
                             reads=[Bot[i], Bzt[i], Byg[g]], writes=[Byg[g]])


        def exchange(l, sc):
            AR.reset()
            Bss = S.buf('st_snd'); Bsr = S.buf('st_rcv')
            for g in range(2):
                S.dma(lambda e, g=g: e.dma_start(out=st_snd[:, g * 512:(g + 1) * 512], in_=Sst[:, g * 4:g * 4 + 4, :].rearrange("p a b -> p (a b)")),
                      reads=[BSst[g]], writes=[Bss], store=True)
            S.dma(lambda e: allreduce(e, st_snd, st_rcv), reads=[Bss], writes=[Bsr], q='pool', inc=1)
            sr = AR.get([1024]); Bsrt = S.buf('sr')
            S.dma(lambda e: e.dma_start(out=sr, in_=st_rcv[:, :]), reads=[Bsr], writes=[Bsrt])
            for g in range(2):
                gs = slice(g * 512, (g + 1) * 512)
                Sg = Sst[:, g * 4:g * 4 + 4, :].rearrange("p a b -> p (a b)")
                S.op('dve', lambda e, gs=gs, Sg=Sg: e.tensor_tensor(Sg, sr[:, gs], Sg, op=ALU.subtract), reads=[Bsrt, BSst[g]], writes=[BSst[g]])

        def attention(l, sc, nparts):
            AR.reset()
            q_sb = [AR.get([NT], BF16) for _ in range(2)]; k_sb = [AR.get([NK], BF16) for _ in range(2)]
            v_sb = [AR.get([KT, 128], BF16) for _ in range(2)]; Bqkv = S.bufs(2, 'qkv')
            pTt = [AR.get([CH], BF16) for _ in range(3)]; BpT = S.bufs(3, 'pT')
            om = [AR.get([CH]) for _ in range(2)]; Bom = S.bufs(2, 'om')
            rd = AR.get([CH]); Brd = S.buf('rd')
            oc = AR.get([CH]); Boc = S.buf('oc'); sqa = AR.get([CH]); Bsqa = S.buf('sqa'); rsa = AR.get([CH]); Brsa = S.buf('rsa')
            post = 1.0 - lam_init[l]
            pc = 0
            SK = 2
            SB = (0, 1, 7)
            steps = [(h, n, m, kt) for h in range(NH) for n in range(NCH) for m in range(2) for kt in range(KT)]
            deferred = []

            def loads(h):
                i = h % 2
                S.dma(lambda e, i=i, h=h: e.dma_start(out=q_sb[i], in_=qT_s[h, :, :]), reads=[sc['q']], writes=[Bqkv[i]])
                ksrc = (kv_rcv_k if PAIR else kT_s).rearrange("(r h p) t -> r h p t", h=NH, p=128)
                vsrc = (kv_rcv_v if PAIR else v_s).rearrange("(r t) v -> r t v", t=NT)
                for part in range(nparts):
                    S.dma(lambda e, i=i, h=h, part=part, ksrc=ksrc: e.dma_start(out=k_sb[i][:, part * NT:(part + 1) * NT], in_=ksrc[part, h, :, :]),
                          reads=[sc['k']], writes=[Bqkv[i]], more=True)
                    S.dma(lambda e, i=i, h=h, part=part, vsrc=vsrc: e.dma_start(out=v_sb[i][:, part * TT:(part + 1) * TT, :],
                                                                     in_=vsrc[part, :, h * 128:(h + 1) * 128].rearrange("(t p) v -> p t v", p=128)),
                          reads=[sc['v']], writes=[Bqkv[i]], more=True)

            def score(s):
                h, n, m, kt = steps[s]
                if n == 0 and m == 0 and kt == 0:
                    loads(h)
                i = h % 2
                sbk = SB[s % 3]
                ms = slice(m * 64, (m + 1) * 64)
                cs = slice(n * CH, (n + 1) * CH)
                S.op('pe', lambda e, i=i, ms=ms, kt=kt, sbk=sbk, cs=cs: e.matmul(P(sbk, CH), k_sb[i][ms, kt * 128:(kt + 1) * 128], q_sb[i][ms, cs], start=True, stop=True),
                     reads=[Bqkv[i]], writes=[Bps[sbk]])

            def tail1(m):
                S.op('act', lambda e, m=m: e.activation(out=rd, in_=P(4 + m, CH), func=AF.Copy), reads=[Bps[4 + m], Brd], writes=[Brd])
                S.op('dve', lambda e: e.reciprocal(rd, rd), reads=[Brd], writes=[Brd])
                S.op('act', lambda e, m=m: e.activation(out=om[m], in_=P(2 + m, CH), func=AF.Copy), reads=[Bps[2 + m], Bom[m]], writes=[Bom[m]])
                S.op('dve', lambda e, m=m: e.tensor_tensor(om[m], om[m], rd, op=ALU.mult), reads=[Brd, Bom[m]], writes=[Bom[m]])

            def fin1():
                S.op('dve', lambda e: e.scalar_tensor_tensor(out=oc, in0=om[1], scalar=lyr[:, L_LAMC:L_LAMC + 1], in1=om[0], op0=ALU.mult, op1=ALU.add),
                     reads=[Bom[0], Bom[1], Blyr, Boc], writes=[Boc])
                S.op('act', lambda e: e.activation(out=sqa, in_=oc, func=AF.Square), reads=[Boc, Bsqa], writes=[Bsqa])

            def fin2():
                S.op('pe', lambda e: e.matmul(P(6, CH), ones_f[:], sqa, start=True, stop=True), reads=[Bsqa, Bconst], writes=[Bps[6]])

            def fin3(h, cs):
                rsqrt_ops(rsa, P(6, CH), 1.0 / 128, [Bps[6], Brsa], [Brsa], post=post)
                S.op('dve', lambda e, h=h, cs=cs: e.scalar_tensor_tensor(out=ydT[:, h, cs], in0=oc, scalar=lyr[:, L_SUB:L_SUB + 1], in1=rsa, op0=ALU.mult, op1=ALU.mult),
                     reads=[Boc, Blyr, Brsa, Byd[h]], writes=[Byd[h]])

            for s in range(min(SK, len(steps))):
                score(s)
            for s in range(len(steps)):
                h, n, m, kt = steps[s]
                i = h % 2
                sbk = SB[s % 3]
                r = s % 3
                if s + SK < len(steps):
                    score(s + SK)
                S.op('act', lambda e, r=r, sbk=sbk: e.activation(out=pTt[r], in_=P(sbk, CH), func=AF.Exp), reads=[Bps[sbk]], writes=[BpT[r]])
                S.op('pe', lambda e, i=i, r=r, kt=kt, m=m: e.matmul(P(2 + m, CH), v_sb[i][:, kt, :], pTt[r], start=(kt == 0), stop=(kt == KT - 1)),
                     reads=[Bqkv[i], BpT[r]], writes=[Bps[2 + m]])
                S.op('pe', lambda e, r=r, kt=kt, m=m: e.matmul(P(4 + m, CH), ones_b[:], pTt[r], start=(kt == 0), stop=(kt == KT - 1)),
                     reads=[Bconst, BpT[r]], writes=[Bps[4 + m]])
                while deferred and deferred[0][0] <= s:
                    deferred.pop(0)[1]()
                if kt == KT - 1:
                    deferred.append((s + 1, lambda m=m: tail1(m)))
                    if m == 1:
                        cs = slice(n * CH, (n + 1) * CH)
                        deferred.append((s + 3, fin1))
                        deferred.append((s + 5, fin2))
                        deferred.append((s + 7, lambda h=h, cs=cs: fin3(h, cs)))
            while deferred:
                deferred.pop(0)[1]()

        def out_proj(l, sc):
            AR.reset()
            mg = AR.get([KC, NT], BF16); Bmg = [[S.buf() for _ in range(NCH)] for _ in range(KC)]
            gd = [AR.get([CH], BF16) for _ in range(2)]; gg = [AR.get([CH], BF16) for _ in range(2)]; Bgt = S.bufs(2, 'gt')
            t1 = [AR.get([CH]) for _ in range(2)]; Bt1 = S.bufs(2, 't1')
            t2 = [AR.get([CH]) for _ in range(2)]; Bt2 = S.bufs(2, 't2')
            it = 0
            for j in range(4):
                for which in range(2):
                    w_, Bw_ = WS.get((('bd', 'bg')[which], l, j))
                    for m in range(4):
                        mc = j * 4 + m
                        for n in range(NCH):
                            cs = slice(n * CH, (n + 1) * CH)
                            i = it % 2
                            it += 1
                            p1 = it % 4
                            src = ydT if which == 0 else ygT
                            for kc in range(8):
                                rdb = Byd[kc] if which == 0 else Byg[kc // 4]
                                S.op('pe', lambda e, kc=kc, m=m, cs=cs, p1=p1, w_=w_, src=src: e.matmul(P(p1, CH), w_[:, kc, m * 128:(m + 1) * 128], src[:, kc, cs], start=(kc == 0), stop=(kc == 7)),
                                     reads=[Bw_, rdb], writes=[Bps[p1]])
                            S.dma(lambda e, i=i, mc=mc, cs=cs, which=which: e.dma_start(out=gd[i], in_=gg_s[16 * which + mc, :, cs]), reads=[sc['gg']], writes=[Bgt[i]])
                            S.op('act', lambda e, i=i, p1=p1: e.activation(out=t1[i], in_=P(p1, CH), func=AF.Copy), reads=[Bps[p1], Bt1[i]], writes=[Bt1[i]])
                            if which == 0:
                                S.op('dve', lambda e, i=i, mc=mc, cs=cs: e.tensor_tensor(mg[:, mc, cs], t1[i], gd[i], op=ALU.mult), reads=[Bgt[i], Bt1[i], Bmg[mc][n]], writes=[Bmg[mc][n]])
                            else:
                                S.op('dve', lambda e, i=i: e.tensor_tensor(t1[i], t1[i], gd[i], op=ALU.mult), reads=[Bgt[i], Bt1[i]], writes=[Bt1[i]])
                                S.op('dve', lambda e, i=i, mc=mc, cs=cs: e.tensor_tensor(mg[:, mc, cs], mg[:, mc, cs], t1[i], op=ALU.add), reads=[Bt1[i], Bmg[mc][n]], writes=[Bmg[mc][n]])
            it = 0
            for j in range(4):
                w, Bw = WS.get(('out', l, j))
                for m in range(4):
                    mc = j * 4 + m
                    for n in range(NCH):
                        cs = slice(n * CH, (n + 1) * CH)
                        pb = 4 + it % 2
                        it += 1
                        for kc in range(KC):
                            S.op('pe', lambda e, kc=kc, m=m, cs=cs, pb=pb, w=w: e.matmul(P(pb, CH), w[:, kc, m * 128:(m + 1) * 128], mg[:, kc, cs], start=(kc == 0), stop=(kc == KC - 1)),
                                 reads=[Bw, Bmg[kc][n]], writes=[Bps[pb]])
                        i2 = it % 2
                        S.op('act', lambda e, mc=mc, pb=pb, i2=i2: e.activation(out=t1[i2], in_=P(pb, CH), func=AF.Copy, scale=mod[:, 32 + mc:33 + mc]), reads=[Bps[pb], Bmod, Bt1[i2]], writes=[Bt1[i2]])
                        S.op('dve', lambda e, mc=mc, cs=cs, i2=i2: e.tensor_tensor(xT[:, mc, cs], xT[:, mc, cs], t1[i2], op=ALU.add), reads=[Bt1[i2], BxT[mc][n]], writes=[BxT[mc][n]])

        def ffn(l):
            modnorm(1)
            AR.reset()
            aT = AR.get([FC, CH], BF16); BaT = S.bufs(FC, 'aT')
            sg_off = AR.off
            sg = [AR.get([CH], BF16) for _ in range(2)]; Bsg = S.bufs(2, 'sg')
            uc = [AR.get([CH], BF16) for _ in range(2)]; Buc = S.bufs(2, 'uc')
            xdf = arena[:, sg_off // 4:sg_off // 4 + CH]
            it = 0
            for n in range(NCH):
                cs = slice(n * CH, (n + 1) * CH)
                for j in range(11):
                    for which in range(2):
                        w_, Bw_ = WS.get((('upg', 'upu')[which], l, n, j))
                        for m in range(4):
                            jc = j * 4 + m
                            i = it % 2
                            it += 1
                            p1 = it % 4
                            for kc in range(KC):
                                S.op('pe', lambda e, kc=kc, m=m, p1=p1, w_=w_: e.matmul(P(p1, CH), w_[:, kc, m * 128:(m + 1) * 128], hT[:, kc, cs], start=(kc == 0), stop=(kc == KC - 1)),
                                     reads=[Bw_, BhT[n]], writes=[Bps[p1]])
                            if which == 0:
                                S.op('act', lambda e, p1=p1, jc=jc: e.activation(out=aT[:, jc, :], in_=P(p1, CH), func=AF.Silu), reads=[Bps[p1], BaT[jc]], writes=[BaT[jc]])
                            else:
                                S.op('act', lambda e, i=i, p1=p1: e.activation(out=uc[i], in_=P(p1, CH), func=AF.Copy), reads=[Bps[p1], Buc[i]], writes=[Buc[i]])
                                S.op('dve', lambda e, i=i, jc=jc: e.tensor_tensor(aT[:, jc, :], aT[:, jc, :], uc[i], op=ALU.mult), reads=[Buc[i], BaT[jc]], writes=[BaT[jc]])
                for m in range(KC):
                    w, Bw = WS.get(('dn', l, n, m))
                    pb = 4 + m % 2
                    for kc in range(FC):
                        S.op('pe', lambda e, kc=kc, pb=pb, w=w: e.matmul(P(pb, CH), w[:, kc, :], aT[:, kc, :], start=(kc == 0), stop=(kc == FC - 1)),
                             reads=[Bw, BaT[kc]], writes=[Bps[pb]])
                    S.op('act', lambda e, m=m, pb=pb: e.activation(out=xdf, in_=P(pb, CH), func=AF.Copy, scale=mod[:, 80 + m:81 + m]), reads=[Bps[pb], Bmod], writes=[Bsg[0], Bsg[1]])
                    S.op('dve', lambda e, m=m: e.tensor_tensor(xT[:, m, cs], xT[:, m, cs], xdf, op=ALU.add), reads=[Bsg[0], Bsg[1], BxT[m][n]], writes=[BxT[m][n]])


        setup()
        stage = dbg.get('_stage', None)
        for l in range(DEPTH):
            load_layer_params(l)
            ada_mod(l)
            modnorm(0)
            sc = in_proj(l)
            if stage == 'inproj':
                break
            gp = gdn_prep(l, sc)
            if stage == 'prep':
                break
            Bo1 = S.buf('o1_s')
            gdn_scan(l, 0, sc, gp, Bo1, True)
            if stage == 'scan0':
                break
            if PAIR:
                exchange(l, sc)
            attention(l, sc, 2 if PAIR else 1)
            if stage == 'attn':
                break
            gdn_scan(l, 1, sc, gp, Bo1, not PAIR)
            if stage == 'mixer':
                break
            out_proj(l, sc)
            if stage == 'outproj':
                break
            ffn(l)
        def dump_sb(name, src, rd):
            S.dma(lambda e: e.dma_start(out=dbg_d[name], in_=src), reads=rd, writes=[Bout], store=True)
        S.barrier()
        if 'mod' in dbg_d:
            dump_sb('mod', mod[:], [Bmod])
        if 'cos' in dbg_d:
            dump_sb('cos', cosT[:], [Brope]); dump_sb('sin', sinT[:], [Brope])
        if 'beta' in dbg_d:
            dump_sb('beta', betaT[:], [Bbg]); dump_sb('g', gT[:], [Bbg])
        if 'hT' in dbg_d:
            AR.reset()
            t = AR.get([NT]); Bt_ = S.buf()
            for kc in range(KC):
                S.op('dve', lambda e, kc=kc: e.tensor_copy(t, hT[:, kc, :]), reads=BhT, writes=[Bt_])
                S.dma(lambda e, kc=kc: e.dma_start(out=dbg_d['hT'][kc * 128:(kc + 1) * 128, :], in_=t), reads=[Bt_], writes=[Bout], store=True)
        for nm, view in (('yd', ydT), ('yg', ygT)):
            if nm in dbg_d:
                AR.reset()
                t = AR.get([NT]); Bt_ = S.buf()
                for kc in range(8):
                    S.op('dve', lambda e, kc=kc, view=view: e.tensor_copy(t, view[:, kc, :]), reads=Byd + Byg, writes=[Bt_])
                    S.dma(lambda e, kc=kc, nm=nm: e.dma_start(out=dbg_d[nm][kc * 128:(kc + 1) * 128, :], in_=t), reads=[Bt_], writes=[Bout], store=True)
        ov = out_d.rearrange("(k p) n -> p k n", p=128)
        for kc in range(KC):
            S.dma(lambda e, kc=kc: e.dma_start(out=ov[:, kc, :], in_=xT[:, kc, :]), reads=BxT[kc], writes=[Bout], store=True)
        S.final_wait('sp', [Bout])
        S.emit()
    return nc


def prep_inputs(inp, NT, PAIR, DEPTH, n_cores=8):
    f32 = np.float32
    x = np.asarray(inp['x'], f32)
    B, SEQ, _ = x.shape
    c = np.asarray(inp['c'], f32)
    pos = np.asarray(inp['positions']).astype(np.int32)
    w_in = np.asarray(inp['w_in'], f32)
    conv_w = np.asarray(inp['gdn_conv_w'], f32)
    a_log = np.asarray(inp['gdn_a_log'], f32)
    dt_bias = np.asarray(inp['gdn_dt_bias'], f32)
    invf = (np.float32(10000.0) ** (-np.arange(32, dtype=np.float32) * np.float32(2.0) / np.float32(64))).astype(f32)
    shared = {
        'invf': np.ascontiguousarray(np.broadcast_to(invf[None, :], (128, 32))),
        'ada_w': np.asarray(inp['ada_w'], f32),
        'ada_bT': np.ascontiguousarray(np.asarray(inp['ada_b'], f32).reshape(DEPTH, 96, 128).transpose(0, 2, 1)),
        'nmwT': np.ascontiguousarray(np.asarray(inp['norm_mix_w'], f32).reshape(DEPTH, 16, 128).transpose(0, 2, 1)),
        'nfwT': np.ascontiguousarray(np.asarray(inp['norm_ffn_w'], f32).reshape(DEPTH, 16, 128).transpose(0, 2, 1)),
        'w_in': w_in,
        'qn_w': np.asarray(inp['diff_qn_w'], f32),
        'kn_w': np.asarray(inp['diff_kn_w'], f32),
        'lamv': np.ascontiguousarray(np.asarray(inp['diff_lambda'], f32).reshape(DEPTH, 256)),
        'sublnT': np.ascontiguousarray(np.asarray(inp['diff_subln_w'], f32).reshape(DEPTH, 128, 1)),
        'gnwT': np.ascontiguousarray(np.asarray(inp['gdn_norm_w'], f32).reshape(DEPTH, 128, 1)),
        'w_bd': np.asarray(inp['w_branch_diff'], f32),
        'w_bg': np.asarray(inp['w_branch_gdn'], f32),
        'w_out': np.asarray(inp['w_out'], f32),
        'w_up': np.asarray(inp['ffn_w_up'], f32),
        'w_down': np.asarray(inp['ffn_w_down'], f32),
    }
    role = {}
    for r in (0, 1):
        dmap = [0, 1] if r == 0 else [1, 0]
        cols = [7168 + d * 8 + h for d in dmap for h in range(8)] + [7184 + d * 8 + h for d in dmap for h in range(8)]
        cw = conv_w if r == 0 else conv_w[:, ::-1, :]
        role[r] = {
            'w_dir': np.ascontiguousarray(w_in[:, :, cols]),
            'a_log': np.ascontiguousarray(a_log[:, dmap, :].reshape(DEPTH, 16)),
            'dt_bias': np.ascontiguousarray(dt_bias[:, dmap, :].reshape(DEPTH, 16)),
            'convT': np.ascontiguousarray(cw.reshape(DEPTH, 5, 24, 128).transpose(0, 3, 2, 1)),
            'sel': np.ascontiguousarray(np.broadcast_to(np.array([[0.0, 1.0, 1.0, 0.0] if r == 0 else [1.0, 0.0, 0.0, 1.0]], f32), (128, 4))),
        }
    maps = []
    meta = []
    for core in range(n_cores):
        if PAIR:
            b, r = core // 2, core % 2
            tok = np.arange(NT) if r == 0 else (SEQ - 1 - np.arange(NT))
        else:
            b, r = core % B, 0
            tok = np.arange(NT)
        m = dict(shared)
        m.update(role[r])
        m['xT'] = np.ascontiguousarray(x[b, tok, :].T)
        m['posT'] = np.ascontiguousarray(pos[b, tok].reshape(NT // 128, 128).T)
        m['cT'] = np.ascontiguousarray(c[b].reshape(16, 128).T)
        maps.append(m)
        meta.append((b, tok))
    return maps, meta


_PROG_CACHE = {}


def kernel(**inputs):
    x = np.asarray(inputs['x'])
    B, SEQ, _ = x.shape
    DEPTH = int(np.asarray(inputs['ada_w']).shape[0])
    NT = SEQ // 2
    key = (NT, DEPTH)
    if key not in _PROG_CACHE:
        _PROG_CACHE[key] = build_program(NT, DEPTH, True, {})
    nc = _PROG_CACHE[key]
    maps, meta = prep_inputs(inputs, NT, True, DEPTH, n_cores=8)
    res = run_bass_kernel_spmd(nc, maps, core_ids=list(range(8)))
    out = np.empty((B, SEQ, D), np.float32)
    for core in range(8):
        b, tok = meta[core]
        out[b, tok, :] = np.asarray(res.results[core]['outT'], np.float32).T
    return out
```

```python
import math
import types
from contextlib import ExitStack

import numpy as np
import concourse.bass as bass
import concourse.mybir as mybir
from concourse.bass_utils import run_bass_kernel_spmd

F32 = mybir.dt.float32
BF16 = mybir.dt.bfloat16
I32 = mybir.dt.int32
AF = mybir.ActivationFunctionType
ALU = mybir.AluOpType
AX = mybir.AxisListType

D = 2048
KC = 16
NH = 8
FFN = 5632
FC = 44
IN_COLS = 11296
EPS = 1e-6
COMPUTE = ('pe', 'act', 'dve', 'pool')


class Buf:
    __slots__ = ('name', 'w', 'r', 'sem', 'cnt')

    def __init__(self, name):
        self.name = name
        self.w = []
        self.r = []
        self.sem = None
        self.cnt = 0


def _snap(fn):
    if fn is None or fn.__closure__ is None:
        return fn
    cells = []
    for c in fn.__closure__:
        try:
            cells.append(types.CellType(c.cell_contents))
        except ValueError:
            cells.append(c)
    return types.FunctionType(fn.__code__, fn.__globals__, fn.__name__, fn.__defaults__, tuple(cells))


class Sched:
    def __init__(self, nc, stack):
        self.nc = nc
        self.stack = stack
        self.ops = {e: [] for e in COMPUTE + ('sp',)}
        self.seq = {e: 0 for e in COMPUTE}
        self.esem = {e: stack.enter_context(nc.semaphore('s_' + e)) for e in COMPUTE}
        self.waited = {e: {} for e in COMPUTE + ('sp',)}
        self.nsem = 0
        self.nbuf = 0
        self.dbufs = []
        self.named = {}

    def buf(self, name=None):
        self.nbuf += 1
        if name is None:
            return Buf('b%d' % self.nbuf)
        if name not in self.named:
            self.named[name] = Buf(name)
        return self.named[name]

    def bufs(self, n, name='b'):
        return [self.buf('%s%d' % (name, i)) for i in range(n)]

    def _dsem(self, b):
        if b.sem is None:
            b.sem = self.stack.enter_context(self.nc.semaphore('d%d' % self.nsem))
            self.nsem += 1
            self.dbufs.append(b)
        return b.sem

    def _deps(self, eng, reads, writes):
        evs = []
        for b in reads:
            evs.extend(b.w)
        for b in writes:
            evs.extend(b.w)
            evs.extend(b.r)
        waits = {}
        wd = self.waited[eng]
        for ev in evs:
            sem, val, src = ev[0], ev[1], ev[2]
            if src == 'pe' and eng == 'pe':
                continue
            if src == 'dma':
                val = ev[3].cnt
            if wd.get(sem, 0) >= val:
                continue
            if waits.get(sem, 0) < val:
                waits[sem] = val
        for sem, val in waits.items():
            wd[sem] = val
        return list(waits.items())

    def op(self, eng, fn, reads=(), writes=()):
        waits = self._deps(eng, reads, writes)
        self.seq[eng] += 1
        ev = (self.esem[eng], self.seq[eng], eng)
        for b in reads:
            b.r.append(ev)
        for b in writes:
            b.w = [ev]
            b.r = []
        self.ops[eng].append((waits, _snap(fn), self.esem[eng], 1))

    def dma(self, fn, reads=(), writes=(), q='sp', more=False, store=False, inc=16):
        d = writes[0]
        if more or store:
            saved = d.w
            d.w = []
        waits = self._deps(q, reads, writes)
        sbuf_side = reads[0] if store else d
        sem = self._dsem(sbuf_side)
        sbuf_side.cnt += inc
        ev = (sem, sbuf_side.cnt, 'dma', sbuf_side)
        for b in reads:
            b.r.append(ev)
        if more or store:
            d.w = [x for x in saved if x[0] is not sem] + [ev]
        else:
            d.w = [ev]
            d.r = []
        self.ops[q].append((waits, _snap(fn), sem, inc))

    def barrier(self):
        tgt = [(self.esem[e], self.seq[e]) for e in COMPUTE if self.seq[e] > 0]
        tgt += [(b.sem, b.cnt) for b in self.dbufs if b.cnt > 0 and not b.name.startswith('wt')]
        for eng in ('pe', 'act', 'dve', 'sp'):
            wd = self.waited[eng]
            waits = []
            for (s, v) in tgt:
                if eng in COMPUTE and s is self.esem[eng]:
                    continue
                if wd.get(s, 0) < v:
                    wd[s] = v
                    waits.append((s, v))
            if waits:
                self.ops[eng].append((waits, None, None, 0))

    def final_wait(self, eng, bufs):
        waits = self._deps(eng, bufs, ())
        self.ops[eng].append((waits, None, None, 0))

    def emit(self):
        nc = self.nc
        with nc.Block() as block:
            def run(e, name):
                for (waits, fn, sem, inc) in self.ops[name]:
                    for (s, v) in waits:
                        e.wait_ge(s, v)
                    if fn is not None:
                        fn(e).then_inc(sem, inc)

            @block.tensor
            def _(e):
                run(e, 'pe')

            @block.scalar
            def _(e):
                run(e, 'act')

            @block.vector
            def _(e):
                run(e, 'dve')

            @block.gpsimd
            def _(e):
                run(e, 'pool')

            @block.sync
            def _(e):
                run(e, 'sp')


def bc_mid(ap2, n):
    P, Fd = ap2.shape
    return ap2.unsqueeze(1).to_broadcast([P, n, Fd])


def bc_last(ap2, n):
    P, G = ap2.shape
    return ap2.unsqueeze(2).to_broadcast([P, G, n])


def build_program(NT, DEPTH, PAIR, dbg=None):
    dbg = dbg or {}
    CDT = BF16 if dbg.get('_chain16') else F32
    TT = NT // 128
    CH = min(512, NT)
    NCH = NT // CH
    NK = 2 * NT if PAIR else NT
    KT = NK // 128
    lam_init = [0.8 - 0.6 * math.exp(-0.3 * l) for l in range(DEPTH)]

    nc = bass.Bass("TRN2", target_bir_lowering=False)

    def din(name, shape, dt=F32):
        return nc.dram_tensor(name, list(shape), dt, kind="ExternalInput").ap()

    xT_d = din("xT", [D, NT])
    pos_d = din("posT", [128, TT], I32)
    c_d = din("cT", [128, KC])
    invf_d = din("invf", [128, 32])
    sel_d = din("sel", [128, 4])
    ada_w_d = din("ada_w", [DEPTH, D, 6 * D])
    ada_b_d = din("ada_bT", [DEPTH, 128, 96])
    nmw_d = din("nmwT", [DEPTH, 128, KC])
    nfw_d = din("nfwT", [DEPTH, 128, KC])
    w_in_d = din("w_in", [DEPTH, D, IN_COLS])
    w_dir_d = din("w_dir", [DEPTH, D, 32])
    qnw_d = din("qn_w", [DEPTH, 64])
    knw_d = din("kn_w", [DEPTH, 64])
    lamv_d = din("lamv", [DEPTH, 256])
    subln_d = din("sublnT", [DEPTH, 128, 1])
    conv_d = din("convT", [DEPTH, 128, 24, 5])
    alog_d = din("a_log", [DEPTH, 16])
    dtb_d = din("dt_bias", [DEPTH, 16])
    gnw_d = din("gnwT", [DEPTH, 128, 1])
    wbd_d = din("w_bd", [DEPTH, 1024, D])
    wbg_d = din("w_bg", [DEPTH, 1024, D])
    wout_d = din("w_out", [DEPTH, D, D])
    wup_d = din("w_up", [DEPTH, D, 2 * FFN])
    wdn_d = din("w_down", [DEPTH, FFN, D])
    out_d = nc.dram_tensor("outT", [D, NT], F32, kind="ExternalOutput").ap()
    dbg_d = {k: nc.dram_tensor("dbg_" + k, list(shp), F32, kind="ExternalOutput").ap()
             for k, shp in dbg.items() if not k.startswith('_')}

    def dscr(name, shape, dt=BF16):
        return nc.dram_tensor(name, list(shape), dt).ap()

    qT_s = dscr("qT_s", [NH, 128, NT])
    kT_s = dscr("kT_s", [NH * 128, NT])
    v_s = dscr("v_s", [NT, 1024])
    g_s = dscr("g_s", [24, 128, NT])
    z_s = dscr("z_s", [NH, 128, NT])
    gg_s = dscr("gg_s", [32, 128, NT])
    gq_s = dscr("gq_s", [NH, 128, NT])
    gk_s = dscr("gk_s", [NH, 128, NT])
    gkt_s = dscr("gkt_s", [NH, NT, 128])
    gvt_s = dscr("gvt_s", [NH, NT, 128])
    o1_s = dscr("o1_s", [NH, 128, NT], F32)
    if PAIR:
        halo_snd = dscr("halo_snd", [2 * 128, 48])
        halo_rcv = dscr("halo_rcv", [2 * 128, 48])
        st_snd = dscr("st_snd", [128, NH * 128], F32)
        st_rcv = dscr("st_rcv", [128, NH * 128], F32)
        kv_snd_k = dscr("kv_snd_k", [2 * NH * 128, NT])
        kv_rcv_k = dscr("kv_rcv_k", [2 * NH * 128, NT])
        kv_snd_v = dscr("kv_snd_v", [2 * NT, 1024])
        kv_rcv_v = dscr("kv_rcv_v", [2 * NT, 1024])

    def allreduce(e, src, dst):
        return e.collective_compute("AllReduce", ALU.add, replica_groups=GROUPS, ins=[src.opt()], outs=[dst.opt()])
    GROUPS = [[2 * i, 2 * i + 1] for i in range(dbg.get('_ncores', 8) // 2)]

    with ExitStack() as stack:
        S = Sched(nc, stack)
        sb = lambda name, shape, dt=F32: nc.alloc_sbuf_tensor("sb_" + name, list(shape), dt)

        xT = sb("xT", [128, KC, NT]);            BxT = [[S.buf() for _ in range(NCH)] for _ in range(KC)]
        hT = sb("hT", [128, KC, NT], BF16);      BhT = [S.buf() for _ in range(NCH)]
        NW = 3
        wt = [sb("wt%d" % i, [128, 16 * 512], BF16) for i in range(NW)]
        Bwt = S.bufs(NW, 'wt')
        ident_f = sb("ident_f", [128, 128]);     ident_b = sb("ident_b", [128, 128], BF16)
        ones_f = sb("ones_f", [128, 128]);       ones_b = sb("ones_b", [128, 128], BF16)
        m_Ui = sb("m_Ui", [128, 128]); m_Li = sb("m_Li", [128, 128])
        m_Us = sb("m_Us", [128, 128]); m_Ls = sb("m_Ls", [128, 128])
        Bconst = S.buf('const')
        cosT = sb("cosT", [128, TT, 32]); sinT = sb("sinT", [128, TT, 32]); Brope = S.buf('rope')
        cact = sb("cact", [128, KC], BF16); Bcact = S.buf('cact')
        mod = sb("mod", [128, 96]); Bmod = S.buf('mod')
        modA = sb("modA", [128, 2, KC]); BmodA = S.buf('modA')
        lyr = sb("lyr", [128, 1024]); Blyr = S.buf('lyr')
        betaT = sb("betaT", [128, TT, 16]); gT = sb("gT", [128, TT, 16]); Bbg = S.buf('bg')
        sel = sb("sel", [128, 4]); Bsel = S.buf('sel')
        ARENA = 48 * 1024
        arena = sb("arena", [128, ARENA // 4])
        psum = [nc.alloc_psum_tensor("ps%d" % i, [128, 512], F32) for i in range(8)]
        Bps = S.bufs(8, 'ps')

        def P(i, n=512):
            return psum[i][:, 0:n]

        class Arena:
            def __init__(self):
                self.off = 0

            def reset(self):
                S.barrier()
                self.off = 0

            def get(self, shape, dt=F32):
                esz = 4 if dt == F32 or dt == I32 else 2
                n = int(np.prod(shape))
                nbytes = (n * esz + 31) // 32 * 32
                assert self.off + nbytes <= ARENA, ("arena overflow", self.off, nbytes)
                v = arena[:, self.off // 4:(self.off + nbytes) // 4]
                self.off += nbytes
                if dt != F32:
                    v = v.bitcast(dt)
                v = v[:, 0:n]
                if len(shape) == 2:
                    return v.rearrange("p (a b) -> p a b", b=shape[1])
                if len(shape) == 3:
                    return v.rearrange("p (a b c) -> p a b c", b=shape[1], c=shape[2])
                return v

        AR = Arena()

        class WStream:
            def __init__(self):
                self.descs = []
                self.issued = 0
                self.taken = 0
                self.loaded = {}

            def add(self, tag, src, kch, ncols):
                self.descs.append((tag, src, kch, ncols))

            def _issue(self, i):
                tag, src, kch, ncols = self.descs[i]
                slot = i % NW
                t = wt[slot][:, 0:kch * ncols].rearrange("p (k n) -> p k n", n=ncols)
                first = True
                for k0 in range(0, kch, 16):
                    k1 = min(kch, k0 + 16)
                    S.dma(lambda e, t=t, src=src, k0=k0, k1=k1: e.dma_start(out=t[:, k0:k1, :], in_=src[:, k0:k1, :]),
                          writes=[Bwt[slot]], q='pool', more=not first)
                    first = False
                self.loaded[i] = (t, Bwt[slot])

            def get(self, tag):
                i = self.taken
                assert self.descs[i][0] == tag, (self.descs[i][0], tag)
                while self.issued < len(self.descs) and self.issued <= i + NW - 1:
                    self._issue(self.issued)
                    self.issued += 1
                self.taken += 1
                return self.loaded.pop(i)

        WS = WStream()

        def wview(w2d, c0, n):
            return w2d.rearrange("(k p) n -> p k n", p=128)[:, :, c0:c0 + n]

        for l in range(DEPTH):
            for j in range(24):
                WS.add(('ada', l, j), wview(ada_w_d[l], j * 512, 512), KC, 512)
            for j in range(6):
                WS.add(('tm', l, j), wview(w_in_d[l], j * 512, 512), KC, 512)
            WS.add(('dir', l), wview(w_dir_d[l], 0, 32), KC, 32)
            for j in range(16):
                WS.add(('fm', l, j), wview(w_in_d[l], 3072 + j * 512 + (32 if j >= 8 else 0), 512), KC, 512)
            for j in range(4):
                WS.add(('bd', l, j), wview(wbd_d[l], j * 512, 512), 8, 512)
                WS.add(('bg', l, j), wview(wbg_d[l], j * 512, 512), 8, 512)
            for j in range(4):
                WS.add(('out', l, j), wview(wout_d[l], j * 512, 512), KC, 512)
            for n in range(NCH):
                for j in range(11):
                    WS.add(('upg', l, n, j), wview(wup_d[l], j * 512, 512), KC, 512)
                    WS.add(('upu', l, n, j), wview(wup_d[l], FFN + j * 512, 512), KC, 512)
                for m in range(16):
                    WS.add(('dn', l, n, m), wview(wdn_d[l], m * 128, 128), FC, 128)

        Bout = S.buf('out')

        def setup():
            S.op('pool', lambda e: e.memset(ident_f[:], 0.0), writes=[Bconst])
            S.op('pool', lambda e: e.affine_select(out=ident_f[:], in_=ident_f[:], pattern=[[-1, 128]],
                                                   compare_op=ALU.not_equal, fill=1.0, base=0, channel_multiplier=1),
                 reads=[Bconst], writes=[Bconst])
            S.op('pool', lambda e: e.memset(ones_f[:], 1.0), reads=[Bconst], writes=[Bconst])
            for (m, op, sgn) in ((m_Ls, ALU.is_gt, 1), (m_Li, ALU.is_ge, 1), (m_Us, ALU.is_gt, -1), (m_Ui, ALU.is_ge, -1)):
                S.op('pool', lambda e, m=m, op=op, sgn=sgn: e.affine_select(out=m[:], in_=ones_f[:], pattern=[[-sgn, 128]],
                                                                             compare_op=op, fill=0.0, base=0, channel_multiplier=sgn),
                     reads=[Bconst], writes=[Bconst])
            S.op('pool', lambda e: e.tensor_copy(ident_b[:], ident_f[:]), reads=[Bconst], writes=[Bconst])
            S.op('pool', lambda e: e.tensor_copy(ones_b[:], ones_f[:]), reads=[Bconst], writes=[Bconst])
            xv = xT_d.rearrange("(k p) n -> p k n", p=128)
            for kc in range(KC):
                for n in range(NCH):
                    S.dma(lambda e, kc=kc, n=n: e.dma_start(out=xT[:, kc, n * CH:(n + 1) * CH], in_=xv[:, kc, n * CH:(n + 1) * CH]),
                          writes=[BxT[kc][n]])
            S.dma(lambda e: e.dma_start(out=sel[:], in_=sel_d[:, :]), writes=[Bsel])
            AR.reset()
            posi = AR.get([TT], I32); posf = AR.get([TT]); invf = AR.get([32]); ang = AR.get([TT, 32]); kf = AR.get([TT, 32])
            ki = AR.get([TT, 32], I32); cf = AR.get([KC])
            Bt = S.buf('setup_t')
            S.dma(lambda e: e.dma_start(out=posi, in_=pos_d[:, :]), writes=[Bt])
            S.dma(lambda e: e.dma_start(out=invf, in_=invf_d[:, :]), writes=[Bt], more=True)
            S.dma(lambda e: e.dma_start(out=cf, in_=c_d[:, :]), writes=[Bt], more=True)
            S.op('dve', lambda e: e.tensor_copy(posf, posi), reads=[Bt], writes=[Bt])
            for t in range(TT):
                S.op('dve', lambda e, t=t: e.tensor_scalar(ang[:, t, :], invf, posf[:, t:t + 1], None, op0=ALU.mult),
                     reads=[Bt], writes=[Bt])
            def reduce_sin(dst, shift):
                S.op('dve', lambda e: e.tensor_scalar(kf, ang, shift, 1.0 / (2 * math.pi), op0=ALU.add, op1=ALU.mult), reads=[Bt, Brope], writes=[Bt])
                S.op('dve', lambda e: e.tensor_copy(ki, kf), reads=[Bt], writes=[Bt])
                S.op('dve', lambda e: e.tensor_copy(kf, ki), reads=[Bt], writes=[Bt])
                S.op('dve', lambda e: e.scalar_tensor_tensor(out=kf, in0=kf, scalar=-2 * math.pi, in1=ang, op0=ALU.mult, op1=ALU.add),
                     reads=[Bt], writes=[Bt])
                S.op('dve', lambda e: e.tensor_scalar(kf, kf, shift, None, op0=ALU.add), reads=[Bt], writes=[Bt])
                S.op('dve', lambda e: e.tensor_scalar(kf, kf, -math.pi, math.pi, op0=ALU.max, op1=ALU.min), reads=[Bt], writes=[Bt])
                S.op('act', lambda e: e.activation(out=dst, in_=kf, func=AF.Sin), reads=[Bt, Brope], writes=[Brope])
            reduce_sin(sinT[:], 0.0)
            reduce_sin(cosT[:], math.pi / 2)
            S.op('act', lambda e: e.activation(out=cact[:], in_=cf, func=AF.Silu), reads=[Bt], writes=[Bcact])

        L_QNW, L_KNW = 0, 64
        L_LAM = 128
        L_CONV = 384
        L_ALOG, L_DTB = 504, 520
        L_ADAB = 536
        L_NMW, L_NFW = 632, 648
        L_SUB, L_GNW = 664, 665
        L_LAMC = 666
        L_NEGA = 668
        L_TMP = 700

        def load_layer_params(l):
            def ld(off, n, src, more=True):
                S.dma(lambda e: e.dma_start(out=lyr[:, off:off + n], in_=src), writes=[Blyr], more=more)
            S.dma(lambda e: e.dma_start(out=lyr[:, L_QNW:L_QNW + 64], in_=qnw_d[l].partition_broadcast(128)), writes=[Blyr])
            ld(L_KNW, 64, knw_d[l].partition_broadcast(128))
            ld(L_LAM, 256, lamv_d[l].partition_broadcast(128))
            ld(L_CONV, 120, conv_d[l].rearrange("p a b -> p (a b)"))
            ld(L_ALOG, 16, alog_d[l].partition_broadcast(128))
            ld(L_DTB, 16, dtb_d[l].partition_broadcast(128))
            ld(L_ADAB, 96, ada_b_d[l])
            ld(L_NMW, 16, nmw_d[l])
            ld(L_NFW, 16, nfw_d[l])
            ld(L_SUB, 1, subln_d[l])
            ld(L_GNW, 1, gnw_d[l])
            rw = dict(reads=[Blyr], writes=[Blyr])
            S.op('dve', lambda e: e.tensor_scalar(lyr[:, L_QNW:L_QNW + 64], lyr[:, L_QNW:L_QNW + 64], 0.125, None, op0=ALU.mult), **rw)
            S.op('dve', lambda e: e.tensor_tensor(lyr[:, L_TMP:L_TMP + 64], lyr[:, L_LAM:L_LAM + 64], lyr[:, L_LAM + 64:L_LAM + 128], op=ALU.mult), **rw)
            S.op('dve', lambda e: e.tensor_tensor(lyr[:, L_TMP + 64:L_TMP + 128], lyr[:, L_LAM + 128:L_LAM + 192], lyr[:, L_LAM + 192:L_LAM + 256], op=ALU.mult), **rw)
            S.op('dve', lambda e: e.tensor_reduce(out=lyr[:, L_TMP + 128:L_TMP + 130], in_=lyr[:, L_TMP:L_TMP + 128].rearrange("p (a b) -> p a b", b=64),
                                                  axis=AX.X, op=ALU.add), **rw)
            S.op('act', lambda e: e.activation(out=lyr[:, L_TMP + 128:L_TMP + 130], in_=lyr[:, L_TMP + 128:L_TMP + 130], func=AF.Exp), **rw)
            S.op('dve', lambda e: e.tensor_tensor(lyr[:, L_LAMC:L_LAMC + 1], lyr[:, L_TMP + 128:L_TMP + 129], lyr[:, L_TMP + 129:L_TMP + 130], op=ALU.subtract), **rw)
            S.op('dve', lambda e: e.tensor_scalar(lyr[:, L_LAMC:L_LAMC + 1], lyr[:, L_LAMC:L_LAMC + 1], lam_init[l], -1.0, op0=ALU.add, op1=ALU.mult), **rw)
            S.op('act', lambda e: e.activation(out=lyr[:, L_NEGA:L_NEGA + 16], in_=lyr[:, L_ALOG:L_ALOG + 16], func=AF.Exp), **rw)
            S.op('dve', lambda e: e.tensor_scalar(lyr[:, L_NEGA:L_NEGA + 16], lyr[:, L_NEGA:L_NEGA + 16], -1.0, None, op0=ALU.mult), **rw)

        def ada_mod(l):
            pm = 7
            for j in range(24):
                w, Bw = WS.get(('ada', l, j))
                for m in range(4):
                    col = j * 4 + m
                    for kc in range(KC):
                        S.op('pe', lambda e, w=w, m=m, kc=kc, col=col: e.matmul(psum[pm][:, col:col + 1], w[:, kc, m * 128:(m + 1) * 128],
                                                                                   cact[:, kc:kc + 1], start=(kc == 0), stop=(kc == KC - 1)),
                             reads=[Bw, Bcact], writes=[Bps[pm]])
            S.op('dve', lambda e: e.tensor_tensor(mod[:], psum[pm][:, 0:96], lyr[:, L_ADAB:L_ADAB + 96], op=ALU.add),
                 reads=[Bps[pm], Blyr], writes=[Bmod])
            S.op('dve', lambda e: e.scalar_tensor_tensor(out=modA[:, 0, :], in0=mod[:, 16:32], scalar=1.0, in1=lyr[:, L_NMW:L_NMW + 16],
                                                         op0=ALU.add, op1=ALU.mult), reads=[Bmod, Blyr], writes=[BmodA])
            S.op('dve', lambda e: e.scalar_tensor_tensor(out=modA[:, 1, :], in0=mod[:, 64:80], scalar=1.0, in1=lyr[:, L_NFW:L_NFW + 16],
                                                         op0=ALU.add, op1=ALU.mult), reads=[Bmod, Blyr, BmodA], writes=[BmodA])

        def rsqrt_ops(dst, src, scale, rd, wr, eps=EPS, post=1.0):
            S.op('act', lambda e: e.activation(out=dst, in_=src, func=AF.Ln, scale=scale, bias=eps), reads=rd, writes=wr)
            S.op('act', lambda e: e.activation(out=dst, in_=dst, func=AF.Exp, scale=-0.5, bias=math.log(post)), reads=wr, writes=wr)

        def modnorm(which):
            sh0 = 0 if which == 0 else 48
            AR.reset()
            sq = [AR.get([CH]) for _ in range(2)]; Bsq = S.bufs(2, 'sq')
            rstd = AR.get([CH]); Brs = S.buf('rstd')
            tmp = [AR.get([CH]) for _ in range(2)]; Btmp = S.bufs(2, 'tmp')
            for n in range(NCH):
                cs = slice(n * CH, (n + 1) * CH)
                for kc in range(KC):
                    i = kc % 2
                    S.op('act', lambda e, i=i, kc=kc: e.activation(out=sq[i], in_=xT[:, kc, cs], func=AF.Square),
                         reads=[BxT[kc][n]], writes=[Bsq[i]])
                    S.op('pe', lambda e, i=i, kc=kc: e.matmul(P(6, CH), ones_f[:], sq[i], start=(kc == 0), stop=(kc == KC - 1)),
                         reads=[Bsq[i], Bconst], writes=[Bps[6]])
                rsqrt_ops(rstd, P(6, CH), 1.0 / D, [Bps[6]], [Brs])
                for kc in range(KC):
                    i = kc % 2
                    S.op('dve', lambda e, i=i, kc=kc: e.scalar_tensor_tensor(out=tmp[i], in0=xT[:, kc, cs], scalar=modA[:, which, kc:kc + 1], in1=rstd,
                                                                            op0=ALU.mult, op1=ALU.mult),
                         reads=[BxT[kc][n], BmodA, Brs], writes=[Btmp[i]])
                    S.op('act', lambda e, i=i, kc=kc: e.activation(out=hT[:, kc, cs], in_=tmp[i], func=AF.Identity,
                                                                   bias=mod[:, sh0 + kc:sh0 + kc + 1], scale=1.0),
                         reads=[Btmp[i], Bmod], writes=[BhT[n]])

        def fm_block(w, Bw, kch, ncols, rhs_fn, rhs_bufs_fn, evac, pbanks, cnt):
            for m in range(ncols // 128):
                for n in range(NCH):
                    pb = pbanks[cnt[0] % len(pbanks)]
                    cnt[0] += 1
                    for kc in range(kch):
                        S.op('pe', lambda e, pb=pb, m=m, n=n, kc=kc: e.matmul(P(pb, CH), w[:, kc, m * 128:(m + 1) * 128], rhs_fn(kc, n),
                                                                             start=(kc == 0), stop=(kc == kch - 1)),
                             reads=[Bw] + rhs_bufs_fn(kc, n), writes=[Bps[pb]])
                    evac(m, n, pb)

        def in_proj(l):
            AR.reset()
            sqb = [AR.get([512]) for _ in range(2)]; Bsqb = S.bufs(2, 'sqb')
            ss8 = [AR.get([8]) for _ in range(2)]; Bss8 = S.bufs(2, 'ss8')
            qn = [AR.get([512]) for _ in range(2)]; Bqn = S.bufs(2, 'qn')
            rot = [AR.get([512]) for _ in range(2)]; Brot = S.bufs(2, 'rot')
            rt2 = [AR.get([512]) for _ in range(2)]; Brt2 = S.bufs(2, 'rt2')
            qb = [AR.get([512], BF16) for _ in range(2)]; Bqb = S.bufs(2, 'qb')
            qtb = [AR.get([4, 128], BF16) for _ in range(2)]; Bqtb = S.bufs(2, 'qtb')
            vb_ = [AR.get([512], BF16) for _ in range(3)]; Bvb = S.bufs(3, 'vb')
            fo = [AR.get([CH], BF16) for _ in range(3)]; Bfo = S.bufs(3, 'fo')
            sm = AR.get([64]); Bsm = S.buf('sm')
            if PAIR:
                mk = [AR.get([2, 512], BF16) for _ in range(2)]; Bmk = S.bufs(2, 'mk')
                hm = [AR.get([2, 2], BF16) for _ in range(3)]; Bhm = S.bufs(3, 'hm')
            Bq_s = S.buf('q_s'); Bk_s = S.buf('k_s'); Bv_s = S.buf('v_s')
            cnt = [0]
            ev = [0]
            pend = []
            for j in range(6):
                w, Bw = WS.get(('tm', l, j))
                for tt in range(TT):
                    pb = cnt[0] % 4
                    cnt[0] += 1
                    for kc in range(KC):
                        S.op('pe', lambda e, pb=pb, tt=tt, kc=kc, w=w: e.matmul(P(pb), hT[:, kc, tt * 128:(tt + 1) * 128], w[:, kc, :],
                                                                               start=(kc == 0), stop=(kc == KC - 1)),
                             reads=[Bw, BhT[(tt * 128) // CH]], writes=[Bps[pb]])
                    if j >= 4:
                        while pend:
                            pend.pop(0)()
                    if j < 4:
                        i = ev[0] % 2
                        ev[0] += 1
                        isq = (j < 2)
                        woff = L_QNW if isq else L_KNW
                        S.op('act', lambda e, i=i, pb=pb: e.activation(out=sqb[i], in_=P(pb), func=AF.Square), reads=[Bps[pb]], writes=[Bsqb[i]])
                        S.op('dve', lambda e, i=i: e.tensor_reduce(out=ss8[i], in_=sqb[i].rearrange("p (g d) -> p g d", d=64), axis=AX.X, op=ALU.add),
                             reads=[Bsqb[i]], writes=[Bss8[i]])
                        rsqrt_ops(ss8[i], ss8[i], 1.0 / 64, [Bss8[i]], [Bss8[i]])
                        while pend:
                            pend.pop(0)()
                        S.op('dve', lambda e, i=i, pb=pb: e.tensor_tensor(qn[i].rearrange("p (g d) -> p g d", d=64), P(pb).rearrange("p (g d) -> p g d", d=64),
                                                                          bc_last(ss8[i], 64), op=ALU.mult),
                             reads=[Bps[pb], Bss8[i]], writes=[Bqn[i]])
                        S.op('dve', lambda e, i=i, woff=woff: e.tensor_tensor(qn[i].rearrange("p (g d) -> p g d", d=64), qn[i].rearrange("p (g d) -> p g d", d=64),
                                                                             bc_mid(lyr[:, woff:woff + 64], 8), op=ALU.mult),
                             reads=[Bqn[i], Blyr], writes=[Bqn[i]])
                        q4 = qn[i].rearrange("p (g t f) -> p g t f", t=2, f=32)
                        r4 = rot[i].rearrange("p (g t f) -> p g t f", t=2, f=32)
                        s4 = rt2[i].rearrange("p (g t f) -> p g t f", t=2, f=32)
                        cb = bc_mid(cosT[:, tt, :], 8)
                        sn = bc_mid(sinT[:, tt, :], 8)
                        for t_ in range(2):
                            S.op('dve', lambda e, t_=t_, q4=q4, r4=r4, cb=cb: e.tensor_tensor(r4[:, :, t_, :], q4[:, :, t_, :], cb, op=ALU.mult),
                                 reads=[Bqn[i], Brope, Brot[i]], writes=[Brot[i]])
                            S.op('dve', lambda e, t_=t_, q4=q4, s4=s4, sn=sn: e.tensor_tensor(s4[:, :, t_, :], q4[:, :, 1 - t_, :], sn, op=ALU.mult),
                                 reads=[Bqn[i], Brope, Brt2[i]], writes=[Brt2[i]])
                        b4 = qb[i].rearrange("p (g t f) -> p g t f", t=2, f=32)
                        S.op('dve', lambda e, r4=r4, s4=s4, b4=b4: e.tensor_tensor(b4[:, :, 0, :], r4[:, :, 0, :], s4[:, :, 0, :], op=ALU.subtract),
                             reads=[Brot[i], Brt2[i], Bqb[i]], writes=[Bqb[i]])
                        S.op('dve', lambda e, r4=r4, s4=s4, b4=b4: e.tensor_tensor(b4[:, :, 1, :], r4[:, :, 1, :], s4[:, :, 1, :], op=ALU.add),
                             reads=[Brot[i], Brt2[i], Bqb[i]], writes=[Bqb[i]])
                        def fin(i=i, j=j, tt=tt, isq=isq, evv=ev[0]):
                            tb = 4 + (evv % 2)
                            pT = psum[tb][:].bitcast(BF16)
                            for hh in range(4):
                                S.op('pe', lambda e, hh=hh, i=i, pT=pT: e.transpose(pT[:, hh * 128:(hh + 1) * 128], qb[i][:, hh * 128:(hh + 1) * 128], ident_b[:]),
                                     reads=[Bqb[i], Bconst], writes=[Bps[tb]])
                            S.op('act', lambda e, i=i, pT=pT: e.activation(out=qtb[i], in_=pT[:, 0:512].rearrange("p (a b) -> p a b", b=128), func=AF.Copy),
                                 reads=[Bps[tb]], writes=[Bqtb[i]])
                            h0 = (j % 2) * 4
                            if isq:
                                dst = qT_s[h0:h0 + 4, :, tt * 128:(tt + 1) * 128].rearrange("h p t -> p h t")
                                S.dma(lambda e, dst=dst, i=i: e.dma_start(out=dst, in_=qtb[i]), reads=[Bqtb[i]], writes=[Bq_s], store=True)
                            elif not PAIR:
                                dst = kT_s.rearrange("(h p) t -> h p t", p=128)[h0:h0 + 4, :, tt * 128:(tt + 1) * 128].rearrange("h p t -> p h t")
                                S.dma(lambda e, dst=dst, i=i: e.dma_start(out=dst, in_=qtb[i]), reads=[Bqtb[i]], writes=[Bk_s], store=True)
                            else:
                                for r in range(2):
                                    S.op('dve', lambda e, i=i, r=r: e.tensor_scalar(mk[i][:, r, :], qtb[i].rearrange("p a b -> p (a b)"), sel[:, 2 + r:3 + r], None, op0=ALU.mult),
                                         reads=[Bqtb[i], Bsel, Bmk[i]], writes=[Bmk[i]])
                                for r in range(2):
                                    dst = kv_snd_k.rearrange("(r h p) t -> r h p t", h=NH, p=128)[r, h0:h0 + 4, :, tt * 128:(tt + 1) * 128].rearrange("h p t -> p h t")
                                    S.dma(lambda e, dst=dst, i=i, r=r: e.dma_start(out=dst, in_=mk[i][:, r, :].rearrange("p (a b) -> p a b", b=128)), reads=[Bmk[i]], writes=[Bk_s], store=True)
                        pend.append(fin)
                    else:
                        i = ev[0] % 3
                        ev[0] += 1
                        S.op('act', lambda e, i=i, pb=pb: e.activation(out=vb_[i], in_=P(pb), func=AF.Copy), reads=[Bps[pb]], writes=[Bvb[i]])
                        c0 = (j - 4) * 512
                        if not PAIR:
                            S.dma(lambda e, i=i, tt=tt, c0=c0: e.dma_start(out=v_s[tt * 128:(tt + 1) * 128, c0:c0 + 512], in_=vb_[i]),
                                  reads=[Bvb[i]], writes=[Bv_s], store=True)
                        else:
                            i2 = ev[0] % 2
                            for r in range(2):
                                S.op('dve', lambda e, i=i, i2=i2, r=r: e.tensor_scalar(mk[i2][:, r, :], vb_[i], sel[:, 2 + r:3 + r], None, op0=ALU.mult),
                                     reads=[Bvb[i], Bsel, Bmk[i2]], writes=[Bmk[i2]])
                            for r in range(2):
                                S.dma(lambda e, i2=i2, tt=tt, c0=c0, r=r: e.dma_start(out=kv_snd_v[r * NT + tt * 128:r * NT + (tt + 1) * 128, c0:c0 + 512], in_=mk[i2][:, r, :]),
                                      reads=[Bmk[i2]], writes=[Bv_s], store=True)
            while pend:
                pend.pop(0)()
            w, Bw = WS.get(('dir', l))
            for tt in range(TT):
                pb = 4 + tt % 2
                for kc in range(KC):
                    S.op('pe', lambda e, pb=pb, tt=tt, kc=kc, w=w: e.matmul(psum[pb][:, 0:32], hT[:, kc, tt * 128:(tt + 1) * 128], w[:, kc, :],
                                                                           start=(kc == 0), stop=(kc == KC - 1)),
                         reads=[Bw, BhT[(tt * 128) // CH]], writes=[Bps[pb]])
                S.op('act', lambda e, pb=pb: e.activation(out=sm[:, 0:16], in_=psum[pb][:, 0:16], func=AF.Exp, scale=-1.0), reads=[Bps[pb]], writes=[Bsm])
                S.op('dve', lambda e: e.tensor_scalar(sm[:, 0:16], sm[:, 0:16], 1.0, None, op0=ALU.add), reads=[Bsm], writes=[Bsm])
                S.op('dve', lambda e, tt=tt: e.reciprocal(betaT[:, tt, :], sm[:, 0:16]), reads=[Bsm, Bbg], writes=[Bbg])
                S.op('dve', lambda e, pb=pb: e.tensor_tensor(sm[:, 16:32], psum[pb][:, 16:32], lyr[:, L_DTB:L_DTB + 16], op=ALU.add),
                     reads=[Bps[pb], Blyr, Bsm], writes=[Bsm])
                S.op('dve', lambda e: e.tensor_scalar(sm[:, 32:48], sm[:, 16:32], 0.0, None, op0=ALU.min), reads=[Bsm], writes=[Bsm])
                S.op('dve', lambda e: e.tensor_scalar(sm[:, 48:64], sm[:, 16:32], 0.0, None, op0=ALU.max), reads=[Bsm], writes=[Bsm])
                S.op('dve', lambda e: e.tensor_tensor(sm[:, 32:48], sm[:, 32:48], sm[:, 48:64], op=ALU.subtract), reads=[Bsm], writes=[Bsm])
                S.op('act', lambda e: e.activation(out=sm[:, 32:48], in_=sm[:, 32:48], func=AF.Exp), reads=[Bsm], writes=[Bsm])
                S.op('act', lambda e: e.activation(out=sm[:, 32:48], in_=sm[:, 32:48], func=AF.Ln, bias=1.0, scale=1.0), reads=[Bsm], writes=[Bsm])
                S.op('dve', lambda e: e.tensor_tensor(sm[:, 32:48], sm[:, 32:48], sm[:, 48:64], op=ALU.add), reads=[Bsm], writes=[Bsm])
                S.op('dve', lambda e, tt=tt: e.tensor_tensor(gT[:, tt, :], sm[:, 32:48], lyr[:, L_NEGA:L_NEGA + 16], op=ALU.mult),
                     reads=[Bsm, Blyr, Bbg], writes=[Bbg])
            Bg_s = S.buf('g_s'); Bz_s = S.buf('z_s'); Bgg_s = S.buf('gg_s'); Bhs = S.buf('halo_snd')
            fcnt = [0]
            for j in range(16):
                w, Bw = WS.get(('fm', l, j))

                def evac(m, n, pb, j=j):
                    i = fcnt[0] % 3
                    fcnt[0] += 1
                    ch = j * 4 + m
                    cs = slice(n * CH, (n + 1) * CH)
                    if ch < 24:
                        S.op('dve', lambda e: e.tensor_copy(fo[i], P(pb, CH)), reads=[Bps[pb]], writes=[Bfo[i]])
                        S.dma(lambda e: e.dma_start(out=g_s[ch, :, cs], in_=fo[i]), reads=[Bfo[i]], writes=[Bg_s], store=True)
                        if PAIR and n == NCH - 1:
                            ih = ch % 3
                            for r in range(2):
                                S.op('dve', lambda e, r=r: e.tensor_scalar(hm[ih][:, r, :], fo[i][:, CH - 2:CH], sel[:, 2 + r:3 + r], None, op0=ALU.mult),
                                     reads=[Bfo[i], Bsel, Bhm[ih]], writes=[Bhm[ih]])
                            for r in range(2):
                                S.dma(lambda e, r=r: e.dma_start(out=halo_snd[r * 128:(r + 1) * 128, 2 * ch:2 * ch + 2], in_=hm[ih][:, r, :]), reads=[Bhm[ih]], writes=[Bhs], store=True)
                    elif ch < 32:
                        S.op('act', lambda e: e.activation(out=fo[i], in_=P(pb, CH), func=AF.Silu), reads=[Bps[pb]], writes=[Bfo[i]])
                        S.dma(lambda e: e.dma_start(out=z_s[ch - 24, :, cs], in_=fo[i]), reads=[Bfo[i]], writes=[Bz_s], store=True)
                    else:
                        S.op('act', lambda e: e.activation(out=fo[i], in_=P(pb, CH), func=AF.Sigmoid), reads=[Bps[pb]], writes=[Bfo[i]])
                        S.dma(lambda e: e.dma_start(out=gg_s[ch - 32, :, cs], in_=fo[i]), reads=[Bfo[i]], writes=[Bgg_s], store=True)

                fm_block(w, Bw, KC, 512, lambda kc, n: hT[:, kc, n * CH:(n + 1) * CH], lambda kc, n: [BhT[n]], evac, [0, 1, 2, 3], cnt)
            return dict(q=Bq_s, k=Bk_s, v=Bv_s, g=Bg_s, z=Bz_s, gg=Bgg_s, hs=Bhs)


        ydT = hT[:, 0:8, :]
        ygT = hT[:, 8:16, :]
        Byd = S.bufs(NH, 'yd'); Byg = S.bufs(2, 'yg')

        def gdn_prep(l, sc):
            AR.reset()
            gin = [AR.get([NT + 4], BF16) for _ in range(2)]; Bgin = S.bufs(2, 'gin')
            acc = [AR.get([NT]) for _ in range(2)]; Bacc = S.bufs(2, 'acc')
            sqt = [AR.get([CH]) for _ in range(2)]; Bsqt = S.bufs(2, 'sqt')
            rs = [AR.get([CH]) for _ in range(2)]; Brs_ = S.bufs(2, 'rs')
            nb = [AR.get([NT], BF16) for _ in range(2)]; Bnb = S.bufs(2, 'nb')
            tk = [AR.get([TT, 128], BF16) for _ in range(2)]; Btk = S.bufs(2, 'tk')
            Bgq = S.buf('gq_s'); Bgk = S.buf('gk_s'); Bgkt = S.buf('gkt_s'); Bgvt = S.buf('gvt_s')
            it = 0
            if PAIR:
                Bhr = S.buf('halo_rcv')
                S.dma(lambda e: allreduce(e, halo_snd, halo_rcv), reads=[sc['hs']], writes=[Bhr], q='pool', inc=1)
                Bkr = S.buf('kv_rcv_k'); Bvr = S.buf('kv_rcv_v')
                S.dma(lambda e: allreduce(e, kv_snd_k, kv_rcv_k), reads=[sc['k']], writes=[Bkr], q='pool', inc=1)
                S.dma(lambda e: allreduce(e, kv_snd_v, kv_rcv_v), reads=[sc['v']], writes=[Bvr], q='pool', inc=1)
                sc['k'] = Bkr
                sc['v'] = Bvr
                hl = AR.get([2, 48], BF16); hlf = AR.get([48]); hl2 = AR.get([48], BF16); Bhl = S.buf('hl')
                S.dma(lambda e: e.dma_start(out=hl, in_=halo_rcv.rearrange("(r p) x -> p r x", p=128)), reads=[Bhr], writes=[Bhl])
                S.op('dve', lambda e: e.tensor_scalar(hlf, hl[:, 0, :], sel[:, 0:1], None, op0=ALU.mult), reads=[Bhl, Bsel], writes=[Bhl])
                S.op('dve', lambda e: e.scalar_tensor_tensor(out=hl2, in0=hl[:, 1, :], scalar=sel[:, 1:2], in1=hlf, op0=ALU.mult, op1=ALU.add), reads=[Bhl, Bsel], writes=[Bhl])

                def halo_fill(gt, Bg, ch):
                    S.op('dve', lambda e: e.tensor_copy(gt[:, NT + 2:NT + 3], hl2[:, 2 * ch + 1:2 * ch + 2]), reads=[Bhl, Bg], writes=[Bg])
                    S.op('dve', lambda e: e.tensor_copy(gt[:, NT + 3:NT + 4], hl2[:, 2 * ch:2 * ch + 1]), reads=[Bhl, Bg], writes=[Bg])
            items = [(h, qi) for h in range(NH) for qi in range(3)]

            def stA(idx):
                h, qi = items[idx]
                ch = qi * 8 + h
                i = idx % 2
                S.dma(lambda e, i=i, ch=ch: e.dma_start(out=gin[i][:, 2:NT + 2], in_=g_s[ch, :, :]), reads=[sc['g']], writes=[Bgin[i]])
                S.op('dve', lambda e, i=i: e.memset(gin[i][:, 0:2], 0.0), reads=[Bgin[i]], writes=[Bgin[i]])
                if PAIR:
                    halo_fill(gin[i], Bgin[i], ch)
                else:
                    S.op('dve', lambda e, i=i: e.memset(gin[i][:, NT + 2:NT + 4], 0.0), reads=[Bgin[i]], writes=[Bgin[i]])
                cw = L_CONV + ch * 5
                S.op('dve', lambda e, i=i, cw=cw: e.tensor_scalar(acc[i], gin[i][:, 0:NT], lyr[:, cw:cw + 1], None, op0=ALU.mult),
                     reads=[Bgin[i], Blyr], writes=[Bacc[i]])
                for tap in range(1, 5):
                    S.op('dve', lambda e, i=i, cw=cw, tap=tap: e.scalar_tensor_tensor(out=acc[i], in0=gin[i][:, tap:tap + NT], scalar=lyr[:, cw + tap:cw + tap + 1],
                                                                                    in1=acc[i], op0=ALU.mult, op1=ALU.add),
                         reads=[Bgin[i], Blyr, Bacc[i]], writes=[Bacc[i]])

            def stS(idx):
                i = idx % 2
                S.op('act', lambda e, i=i: e.activation(out=acc[i], in_=acc[i], func=AF.Silu), reads=[Bacc[i]], writes=[Bacc[i]])

            def stB(idx):
                h, qi = items[idx]
                ch = qi * 8 + h
                i = idx % 2
                if qi < 2:
                    post = (128.0 ** -0.5) if qi == 0 else 1.0
                    for n in range(NCH):
                        cs = slice(n * CH, (n + 1) * CH)
                        j = n % 2
                        S.op('act', lambda e, i=i, j=j, cs=cs: e.activation(out=sqt[j], in_=acc[i][:, cs], func=AF.Square), reads=[Bacc[i]], writes=[Bsqt[j]])
                        S.op('pe', lambda e, j=j: e.matmul(P(6, CH), ones_f[:], sqt[j], start=True, stop=True), reads=[Bsqt[j], Bconst], writes=[Bps[6]])
                        rsqrt_ops(rs[j], P(6, CH), 1.0, [Bps[6]], [Brs_[j]], post=post)
                        S.op('dve', lambda e, i=i, j=j, cs=cs: e.tensor_tensor(nb[i][:, cs], acc[i][:, cs], rs[j], op=ALU.mult),
                             reads=[Bacc[i], Brs_[j], Bnb[i]], writes=[Bnb[i]])
                else:
                    S.op('dve', lambda e, i=i: e.tensor_copy(nb[i], acc[i]), reads=[Bacc[i]], writes=[Bnb[i]])
                if qi == 0:
                    S.dma(lambda e, i=i, h=h: e.dma_start(out=gq_s[h, :, :], in_=nb[i]), reads=[Bnb[i]], writes=[Bgq], store=True)
                else:
                    if qi == 1:
                        S.dma(lambda e, i=i, h=h: e.dma_start(out=gk_s[h, :, :], in_=nb[i]), reads=[Bnb[i]], writes=[Bgk], store=True)
                    for t0 in range(0, TT, 4):
                        tb = 4 + ((t0 // 4) % 2)
                        pT = psum[tb][:].bitcast(BF16)
                        nt_ = min(4, TT - t0)
                        for t in range(nt_):
                            S.op('pe', lambda e, i=i, t=t, t0=t0, pT=pT: e.transpose(pT[:, t * 128:(t + 1) * 128], nb[i][:, (t0 + t) * 128:(t0 + t + 1) * 128], ident_b[:]),
                                 reads=[Bnb[i], Bconst], writes=[Bps[tb]])
                        S.op('act', lambda e, i=i, t0=t0, nt_=nt_, pT=pT: e.activation(out=tk[i][:, t0:t0 + nt_, :], in_=pT[:, 0:nt_ * 128].rearrange("p (a b) -> p a b", b=128), func=AF.Copy),
                             reads=[Bps[tb], Btk[i]], writes=[Btk[i]])
                    dst = (gkt_s if qi == 1 else gvt_s)[h].rearrange("(t p) d -> p t d", p=128)
                    S.dma(lambda e, i=i, dst=dst: e.dma_start(out=dst, in_=tk[i]), reads=[Btk[i]], writes=[Bgkt if qi == 1 else Bgvt], store=True)

            stA(0)
            stS(0)
            for idx in range(len(items)):
                if idx + 1 < len(items):
                    stA(idx + 1)
                stB(idx)
                if idx + 1 < len(items):
                    stS(idx + 1)
            return dict(gq=Bgq, gk=Bgk, gkt=Bgkt, gvt=Bgvt)

        Sst = sb("Sst", [128, NH, 128]); BSst = S.bufs(2, 'Sst')

        def gdn_scan(l, ph, sc, gp, Bo1, zero_state):
            AR.reset()
            TRI = m_Ui if ph == 0 else m_Li
            MST = m_Ls if ph == 0 else m_Us
            MIN = m_Ui if ph == 0 else m_Li
            tiles = list(range(TT)) if ph == 0 else list(range(TT - 1, -1, -1))
            dsl = slice(ph * 8, ph * 8 + 8)
            st8 = AR.get([64]); Bst8 = S.buf('st8')
            qT4 = [AR.get([4, 128], BF16)] * 2; kT4 = [AR.get([4, 128], BF16)] * 2
            kt4 = [AR.get([4, 128], BF16)] * 2; vt4 = [AR.get([4, 128], BF16)] * 2
            Bld = [S.buf('ld')] * 2
            Gb4 = AR.get([4, 128]); BGb = S.buf('Gb')
            ta = AR.get([4, 128]); tb_ = AR.get([4, 128]); Er = AR.get([4, 128]); Bta = S.buf('ta'); Btb = S.buf('tb'); BEr = S.buf('Er')
            L4 = AR.get([4, 128], CDT); At4 = AR.get([4, 128], BF16); BL4 = S.buf('L4'); BAt4 = S.buf('At4')
            Xb = [AR.get([4, 128], CDT) for _ in range(2)]; Yb = [AR.get([4, 128], CDT) for _ in range(2)]; Tb = [AR.get([4, 128], CDT) for _ in range(2)]
            Tfin = AR.get([4, 128], BF16); BTfin = S.buf('Tfin')
            identc = ident_f if CDT == F32 else ident_b
            BXb = S.bufs(2, 'Xb'); BYb = S.bufs(2, 'Yb'); BTb = S.bufs(2, 'Tb')
            kbg4 = AR.get([4, 128], BF16); ktl4 = AR.get([4, 128], BF16); vb4 = AR.get([4, 128], BF16)
            nw4 = AR.get([4, 128], BF16); qd4 = AR.get([4, 128], BF16); vn4 = AR.get([4, 128], BF16)
            Bkbg = S.buf('kbg'); Bktl = S.buf('ktl'); Bvb4 = S.buf('vb4'); Bnw = S.buf('nw'); Bqd = S.buf('qd'); Bvn = S.buf('vn')
            Sb4 = [AR.get([4, 128], BF16) for _ in range(2)]; BSb = S.bufs(2, 'Sb')
            ot = [AR.get([4, 128])] * 2; Bot = [S.buf('ot')] * 2
            o1t = [AR.get([4, 128])] * 2; Bo1t = [S.buf('o1t')] * 2
            zt = [AR.get([4, 128], BF16)] * 2; Bzt = [S.buf('zt')] * 2
            sq4 = AR.get([4, 128]); Bsq4 = S.buf('sq4'); rs4 = AR.get([4, 128]); Brs4 = S.buf('rs4')
            Er2 = AR.get([4, 128]); BEr2 = S.buf('Er2')
            for g in range(2):
                hs = slice(g * 4, g * 4 + 4)
                if zero_state:
                    S.op('dve', lambda e, hs=hs: e.memset(Sst[:, hs, :], 0.0), reads=[BSst[g]], writes=[BSst[g]])
                S.op('act', lambda e, g=g, hs=hs: e.activation(out=Sb4[g], in_=Sst[:, hs, :], func=AF.Copy), reads=[BSst[g]], writes=[BSb[g]])
            it = 0
            rot = [0]

            def rbank():
                rot[0] += 1
                return 4 + (rot[0] % 3)

            for tt in tiles:
                ts_ = slice(tt * 128, (tt + 1) * 128)
                S.op('pe', lambda e, tt=tt: e.matmul(psum[0][:, 0:8], TRI[:], gT[:, tt, dsl], start=True, stop=True), reads=[Bconst, Bbg], writes=[Bps[0]])
                S.op('pe', lambda e, tt=tt: e.matmul(psum[0][:, 8:16], ones_f[:], gT[:, tt, dsl], start=True, stop=True), reads=[Bconst, Bbg], writes=[Bps[0]])
                S.op('dve', lambda e: e.tensor_copy(st8[:, 0:8], psum[0][:, 0:8]), reads=[Bps[0], Bst8], writes=[Bst8])
                S.op('act', lambda e: e.activation(out=st8[:, 8:16], in_=psum[0][:, 0:8], func=AF.Exp), reads=[Bps[0], Bst8], writes=[Bst8])
                S.op('act', lambda e: e.activation(out=st8[:, 32:40], in_=psum[0][:, 0:8], func=AF.Copy, scale=-1.0), reads=[Bps[0], Bst8], writes=[Bst8])
                S.op('act', lambda e: e.activation(out=st8[:, 16:24], in_=psum[0][:, 8:16], func=AF.Exp), reads=[Bps[0], Bst8], writes=[Bst8])
                S.op('dve', lambda e: e.tensor_tensor(st8[:, 24:32], psum[0][:, 8:16], st8[:, 0:8], op=ALU.subtract), reads=[Bps[0], Bst8], writes=[Bst8])
                S.op('act', lambda e: e.activation(out=st8[:, 24:32], in_=st8[:, 24:32], func=AF.Exp), reads=[Bst8], writes=[Bst8])
                S.op('dve', lambda e, tt=tt: e.tensor_tensor(st8[:, 8:16], st8[:, 8:16], betaT[:, tt, dsl], op=ALU.mult), reads=[Bst8, Bbg], writes=[Bst8])
                if dbg.get('_cut') == 1:
                    return
                for g in range(2):
                    hs = slice(g * 4, g * 4 + 4)
                    h0 = g * 4
                    i = it % 2
                    it += 1
                    S.dma(lambda e, i=i, h0=h0, ts_=ts_: e.dma_start(out=qT4[i], in_=gq_s[h0:h0 + 4, :, ts_].rearrange("h p t -> p h t")), reads=[gp['gq']], writes=[Bld[i]])
                    S.dma(lambda e, i=i, h0=h0, ts_=ts_: e.dma_start(out=kT4[i], in_=gk_s[h0:h0 + 4, :, ts_].rearrange("h p t -> p h t")), reads=[gp['gk']], writes=[Bld[i]], more=True)
                    S.dma(lambda e, i=i, h0=h0, ts_=ts_: e.dma_start(out=kt4[i], in_=gkt_s[h0:h0 + 4, ts_, :].rearrange("h p d -> p h d")), reads=[gp['gkt']], writes=[Bld[i]], more=True)
                    S.dma(lambda e, i=i, h0=h0, ts_=ts_: e.dma_start(out=vt4[i], in_=gvt_s[h0:h0 + 4, ts_, :].rearrange("h p d -> p h d")), reads=[gp['gvt']], writes=[Bld[i]], more=True)
                    g4 = gT[:, tt, ph * 8 + h0:ph * 8 + h0 + 4]
                    be4 = betaT[:, tt, ph * 8 + h0:ph * 8 + h0 + 4]
                    gc4 = st8[:, h0:h0 + 4]; skbg4 = st8[:, 8 + h0:12 + h0]; cd4 = st8[:, 16 + h0:20 + h0]; skt4 = st8[:, 24 + h0:28 + h0]
                    if dbg.get('_cut') == 11:
                        return
                    for hh in range(4):
                        S.op('dve', lambda e, g4=g4, hh=hh: e.tensor_scalar(Gb4[:, hh, :], ones_f[:], g4[:, hh:hh + 1], None, op0=ALU.mult),
                             reads=[Bbg, BGb, Bconst], writes=[BGb])
                    for hh in range(4):
                        S.op('pe', lambda e, hh=hh: e.matmul(psum[1][:, hh * 128:(hh + 1) * 128], Gb4[:, hh, :], TRI[:], start=True, stop=True),
                             reads=[BGb, Bconst], writes=[Bps[1]])
                    if dbg.get('_cut') == 12:
                        if 'pb' in dbg_d:
                            S.op('act', lambda e: e.activation(out=Er, in_=psum[1][:].rearrange("p (a b) -> p a b", b=128), func=AF.Copy), reads=[Bps[1], BEr], writes=[BEr])
                            S.dma(lambda e: e.dma_start(out=dbg_d['pb'], in_=Er.rearrange("p a b -> p (a b)")), reads=[BEr], writes=[Bout], store=True)
                            S.dma(lambda e: e.dma_start(out=dbg_d['st8'], in_=st8), reads=[Bst8], writes=[Bout], store=True)
                            S.dma(lambda e: e.dma_start(out=dbg_d['gb'], in_=Gb4.rearrange("p a b -> p (a b)")), reads=[BGb], writes=[Bout], store=True)
                        return
                    pB = psum[1][:].rearrange("p (a b) -> p a b", b=128)
                    ngc4 = st8[:, 32 + h0:36 + h0]
                    for hh in range(4):
                        S.op('act', lambda e, hh=hh, ngc4=ngc4: e.activation(out=ta[:, hh, :], in_=psum[1][:, hh * 128:(hh + 1) * 128], func=AF.Relu,
                                                                           bias=ngc4[:, hh:hh + 1], scale=1.0), reads=[Bps[1], Bst8, Bta], writes=[Bta])
                        S.op('act', lambda e, hh=hh, gc4=gc4: e.activation(out=tb_[:, hh, :], in_=psum[1][:, hh * 128:(hh + 1) * 128], func=AF.Relu,
                                                                          bias=gc4[:, hh:hh + 1], scale=-1.0), reads=[Bps[1], Bst8, Btb], writes=[Btb])
                    S.op('act', lambda e, pB=pB: e.activation(out=Er, in_=pB, func=AF.Exp), reads=[Bps[1], BEr], writes=[BEr])
                    if dbg.get('_cut') == 13:
                        return
                    S.op('act', lambda e: e.activation(out=ta, in_=ta, func=AF.Exp, scale=-1.0), reads=[Bta], writes=[Bta])
                    S.op('act', lambda e: e.activation(out=tb_, in_=tb_, func=AF.Exp, scale=-1.0), reads=[Btb], writes=[Btb])
                    if dbg.get('_cut') == 15:
                        return
                    S.op('dve', lambda e: e.tensor_tensor(ta, ta, bc_mid(MST[:], 4), op=ALU.mult), reads=[Bta, Bconst], writes=[Bta])
                    S.op('dve', lambda e, be4=be4: e.tensor_tensor(ta, ta, bc_last(be4, 128), op=ALU.mult), reads=[Bta, Bbg], writes=[Bta])
                    S.op('dve', lambda e: e.tensor_tensor(tb_, tb_, bc_mid(MIN[:], 4), op=ALU.mult), reads=[Btb, Bconst], writes=[Btb])
                    if dbg.get('_cut') == 2:
                        return
                    for hh in range(4):
                        S.op('pe', lambda e, hh=hh, i=i: e.matmul(psum[2][:, hh * 128:(hh + 1) * 128], kT4[i][:, hh, :], kT4[i][:, hh, :], start=True, stop=True),
                             reads=[Bld[i]], writes=[Bps[2]])
                    for hh in range(4):
                        S.op('pe', lambda e, hh=hh, i=i: e.matmul(psum[3][:, hh * 128:(hh + 1) * 128], kT4[i][:, hh, :], qT4[i][:, hh, :], start=True, stop=True),
                             reads=[Bld[i]], writes=[Bps[3]])
                    pK = psum[2][:].rearrange("p (a b) -> p a b", b=128)
                    pQ = psum[3][:].rearrange("p (a b) -> p a b", b=128)
                    S.op('act', lambda e, pK=pK: e.activation(out=Er2, in_=pK, func=AF.Copy), reads=[Bps[2], BEr2], writes=[BEr2])
                    S.op('dve', lambda e: e.tensor_tensor(L4, Er2, ta, op=ALU.mult), reads=[BEr2, Bta, BL4], writes=[BL4])
                    S.op('act', lambda e, pQ=pQ: e.activation(out=Er2, in_=pQ, func=AF.Copy), reads=[Bps[3], BEr2], writes=[BEr2])
                    S.op('dve', lambda e: e.tensor_tensor(At4, Er2, tb_, op=ALU.mult), reads=[BEr2, Btb, BAt4], writes=[BAt4])
                    if dbg.get('_cut') == 3:
                        return
                    bk = rbank()
                    pT = psum[bk][:].bitcast(CDT) if CDT != F32 else psum[bk][:]
                    for hh in range(4):
                        S.op('pe', lambda e, hh=hh, pT=pT: e.transpose(pT[:, hh * 128:(hh + 1) * 128], L4[:, hh, :], identc[:]), reads=[BL4, Bconst], writes=[Bps[bk]])
                    pT3 = pT[:, 0:512].rearrange("p (a b) -> p a b", b=128)
                    S.op('act', lambda e, pT3=pT3: e.activation(out=Yb[0], in_=pT3, func=AF.Copy), reads=[Bps[bk], BYb[0]], writes=[BYb[0]])
                    S.op('dve', lambda e: e.tensor_tensor(Tb[0], bc_mid(identc[:], 4), Yb[0], op=ALU.subtract), reads=[BYb[0], Bconst, BTb[0]], writes=[BTb[0]])
                    if dbg.get('_cut') == 4:
                        return
                    Xc, BXc = L4, BL4
                    Yc, BYc = Yb[0], BYb[0]
                    Tc, BTc = Tb[0], BTb[0]
                    for lev in range(1, 7):
                        Xn, BXn = Xb[lev % 2], BXb[lev % 2]
                        Yn, BYn = Yb[lev % 2], BYb[lev % 2]
                        Tn, BTn = Tb[lev % 2], BTb[lev % 2]
                        bx = rbank()
                        for hh in range(4):
                            S.op('pe', lambda e, hh=hh, bx=bx, Xc=Xc, Yc=Yc: e.matmul(psum[bx][:, hh * 128:(hh + 1) * 128], Yc[:, hh, :], Xc[:, hh, :], start=True, stop=True),
                                 reads=[BXc, BYc], writes=[Bps[bx]])
                        if lev < 6:
                            by = rbank()
                            for hh in range(4):
                                S.op('pe', lambda e, hh=hh, by=by, Xc=Xc, Yc=Yc: e.matmul(psum[by][:, hh * 128:(hh + 1) * 128], Xc[:, hh, :], Yc[:, hh, :], start=True, stop=True),
                                     reads=[BXc, BYc], writes=[Bps[by]])
                        S.op('act', lambda e, bx=bx, Xn=Xn: e.activation(out=Xn, in_=psum[bx][:].rearrange("p (a b) -> p a b", b=128), func=AF.Copy),
                             reads=[Bps[bx], BXn], writes=[BXn])
                        if lev < 6:
                            S.op('act', lambda e, by=by, Yn=Yn: e.activation(out=Yn, in_=psum[by][:].rearrange("p (a b) -> p a b", b=128), func=AF.Copy), reads=[Bps[by], BYn], writes=[BYn])
                        bt = rbank()
                        for hh in range(4):
                            S.op('pe', lambda e, hh=hh, bt=bt, Xn=Xn, Tc=Tc: e.matmul(psum[bt][:, hh * 128:(hh + 1) * 128], Xn[:, hh, :], Tc[:, hh, :], start=True, stop=True),
                                 reads=[BXn, BTc], writes=[Bps[bt]])
                        S.op('act', lambda e, bt=bt: e.activation(out=Er2, in_=psum[bt][:].rearrange("p (a b) -> p a b", b=128), func=AF.Copy), reads=[Bps[bt], BEr2], writes=[BEr2])
                        S.op('dve', lambda e, Tn=Tn, Tc=Tc: e.tensor_tensor(Tn, Er2, Tc, op=ALU.add), reads=[BEr2, BTc, BTn], writes=[BTn])
                        Xc, BXc, Yc, BYc, Tc, BTc = Xn, BXn, Yn, BYn, Tn, BTn
                    if dbg.get('_cut') == 5:
                        return
                    S.op('act', lambda e, Tc=Tc: e.activation(out=Tfin, in_=Tc, func=AF.Copy), reads=[BTc, BTfin], writes=[BTfin])
                    Tc, BTc = Tfin, BTfin
                    S.op('dve', lambda e, i=i, skbg4=skbg4: e.tensor_tensor(kbg4, kt4[i], bc_last(skbg4, 128), op=ALU.mult), reads=[Bld[i], Bst8, Bkbg], writes=[Bkbg])
                    S.op('dve', lambda e, i=i, skt4=skt4: e.tensor_tensor(ktl4, kt4[i], bc_last(skt4, 128), op=ALU.mult), reads=[Bld[i], Bst8, Bktl], writes=[Bktl])
                    S.op('dve', lambda e, i=i, be4=be4: e.tensor_tensor(vb4, vt4[i], bc_last(be4, 128), op=ALU.mult), reads=[Bld[i], Bbg, Bvb4], writes=[Bvb4])
                    S.op('dve', lambda e, i=i: e.tensor_tensor(qd4, qT4[i], Er, op=ALU.mult), reads=[Bld[i], BEr, Bqd], writes=[Bqd])
                    bw = rbank()
                    for hh in range(4):
                        S.op('pe', lambda e, hh=hh, bw=bw, Tc=Tc: e.matmul(psum[bw][:, hh * 128:(hh + 1) * 128], kbg4[:, hh, :], Tc[:, hh, :], start=True, stop=True),
                             reads=[Bkbg, BTc], writes=[Bps[bw]])
                    S.op('act', lambda e, bw=bw: e.activation(out=nw4, in_=psum[bw][:].rearrange("p (a b) -> p a b", b=128), func=AF.Copy, scale=-1.0),
                         reads=[Bps[bw], Bnw], writes=[Bnw])
                    if dbg.get('_cut') == 6:
                        return
                    for hh in range(4):
                        S.op('pe', lambda e, hh=hh, Tc=Tc: e.matmul(psum[1][:, hh * 128:(hh + 1) * 128], Tc[:, hh, :], vb4[:, hh, :], start=True, stop=False),
                             reads=[BTc, Bvb4], writes=[Bps[1]])
                        S.op('pe', lambda e, hh=hh, g=g: e.matmul(psum[1][:, hh * 128:(hh + 1) * 128], nw4[:, hh, :], Sb4[g][:, hh, :], start=False, stop=True),
                             reads=[Bnw, BSb[g]], writes=[Bps[1]])
                    S.op('act', lambda e: e.activation(out=vn4, in_=psum[1][:].rearrange("p (a b) -> p a b", b=128), func=AF.Copy), reads=[Bps[1], Bvn], writes=[Bvn])
                    for hh in range(4):
                        S.op('pe', lambda e, hh=hh, g=g: e.matmul(psum[2][:, hh * 128:(hh + 1) * 128], Sb4[g][:, hh, :], qd4[:, hh, :], start=True, stop=False),
                             reads=[BSb[g], Bqd], writes=[Bps[2]])
                        S.op('pe', lambda e, hh=hh: e.matmul(psum[2][:, hh * 128:(hh + 1) * 128], vn4[:, hh, :], At4[:, hh, :], start=False, stop=True),
                             reads=[Bvn, BAt4], writes=[Bps[2]])
                    for hh in range(4):
                        S.op('pe', lambda e, hh=hh: e.matmul(psum[3][:, hh * 128:(hh + 1) * 128], ktl4[:, hh, :], vn4[:, hh, :], start=True, stop=True),
                             reads=[Bktl, Bvn], writes=[Bps[3]])
                    if dbg.get('_cut') == 7:
                        return
                    S.op('dve', lambda e, hs=hs, cd4=cd4: e.tensor_tensor(Sst[:, hs, :], Sst[:, hs, :], bc_last(cd4, 128), op=ALU.mult), reads=[BSst[g], Bst8], writes=[BSst[g]])
                    S.op('act', lambda e: e.activation(out=Er2, in_=psum[3][:].rearrange("p (a b) -> p a b", b=128), func=AF.Copy), reads=[Bps[3], BEr2], writes=[BEr2])
                    S.op('dve', lambda e, hs=hs: e.tensor_tensor(Sst[:, hs, :], Sst[:, hs, :], Er2, op=ALU.add), reads=[BSst[g], BEr2], writes=[BSst[g]])
                    S.op('act', lambda e, g=g, hs=hs: e.activation(out=Sb4[g], in_=Sst[:, hs, :], func=AF.Copy), reads=[BSst[g], BSb[g]], writes=[BSb[g]])
                    if dbg.get('_cut') == 8:
                        return
                    pO = psum[2][:].rearrange("p (a b) -> p a b", b=128)
                    if ph == 0:
                        S.op('act', lambda e, i=i, pO=pO: e.activation(out=ot[i], in_=pO, func=AF.Copy), reads=[Bps[2], Bot[i]], writes=[Bot[i]])
                        S.dma(lambda e, i=i, h0=h0, ts_=ts_: e.dma_start(out=o1_s[h0:h0 + 4, :, ts_].rearrange("h p t -> p h t"), in_=ot[i]),
                              reads=[Bot[i]], writes=[Bo1], store=True)
                    else:
                        S.dma(lambda e, i=i, h0=h0, ts_=ts_: e.dma_start(out=o1t[i], in_=o1_s[h0:h0 + 4, :, ts_].rearrange("h p t -> p h t")), reads=[Bo1], writes=[Bo1t[i]])
                        S.dma(lambda e, i=i, h0=h0, ts_=ts_: e.dma_start(out=zt[i], in_=z_s[h0:h0 + 4, :, ts_].rearrange("h p t -> p h t")), reads=[sc['z']], writes=[Bzt[i]])
                        S.op('act', lambda e, i=i, pO=pO: e.activation(out=ot[i], in_=pO, func=AF.Copy), reads=[Bps[2], Bot[i]], writes=[Bot[i]])
                        S.op('dve', lambda e, i=i: e.tensor_tensor(ot[i], ot[i], o1t[i], op=ALU.add), reads=[Bo1t[i], Bot[i]], writes=[Bot[i]])
                        S.op('act', lambda e, i=i: e.activation(out=sq4, in_=ot[i], func=AF.Square), reads=[Bot[i], Bsq4], writes=[Bsq4])
                        S.op('pe', lambda e: e.matmul(psum[7][:], ones_f[:], sq4.rearrange("p a b -> p (a b)"), start=True, stop=True), reads=[Bsq4, Bconst], writes=[Bps[7]])
                        rsqrt_ops(rs4.rearrange("p a b -> p (a b)"), psum[7][:], 1.0 / 128, [Bps[7], Brs4], [Brs4])
                        S.op('dve', lambda e, i=i: e.scalar_tensor_tensor(out=ot[i], in0=ot[i], scalar=lyr[:, L_GNW:L_GNW + 1], in1=rs4, op0=ALU.mult, op1=ALU.mult),
                             reads=[Bot[i], Blyr, Brs4], writes=[Bot[i]])
                        S.op('dve', lambda e, i=i, h0=h0, ts_=ts_: e.tensor_tensor(ygT[:, h0:h0 + 4, ts_], ot[i], zt[i], op=ALU.mult),
                             reads=[Bot[i], Bzt[i], Byg[g]], writes=[Byg[g]])


        def exchange(l, sc):
            AR.reset()
            Bss = S.buf('st_snd'); Bsr = S.buf('st_rcv')
            for g in range(2):
                S.dma(lambda e, g=g: e.dma_start(out=st_snd[:, g * 512:(g + 1) * 512], in_=Sst[:, g * 4:g * 4 + 4, :].rearrange("p a b -> p (a b)")),
                      reads=[BSst[g]], writes=[Bss], store=True)
            S.dma(lambda e: allreduce(e, st_snd, st_rcv), reads=[Bss], writes=[Bsr], q='pool', inc=1)
            sr = AR.get([1024]); Bsrt = S.buf('sr')
            S.dma(lambda e: e.dma_start(out=sr, in_=st_rcv[:, :]), reads=[Bsr], writes=[Bsrt])
            for g in range(2):
                gs = slice(g * 512, (g + 1) * 512)
                Sg = Sst[:, g * 4:g * 4 + 4, :].rearrange("p a b -> p (a b)")
                S.op('dve', lambda e, gs=gs, Sg=Sg: e.tensor_tensor(Sg, sr[:, gs], Sg, op=ALU.subtract), reads=[Bsrt, BSst[g]], writes=[BSst[g]])

        def attention(l, sc, nparts):
            AR.reset()
            q_sb = [AR.get([NT], BF16) for _ in range(2)]; k_sb = [AR.get([NK], BF16) for _ in range(2)]
            v_sb = [AR.get([KT, 128], BF16) for _ in range(2)]; Bqkv = S.bufs(2, 'qkv')
            pTt = [AR.get([CH], BF16) for _ in range(3)]; BpT = S.bufs(3, 'pT')
            om = [AR.get([CH]) for _ in range(2)]; Bom = S.bufs(2, 'om')
            rd = AR.get([CH]); Brd = S.buf('rd')
            oc = AR.get([CH]); Boc = S.buf('oc'); sqa = AR.get([CH]); Bsqa = S.buf('sqa'); rsa = AR.get([CH]); Brsa = S.buf('rsa')
            post = 1.0 - lam_init[l]
            pc = 0
            SK = 2
            SB = (0, 1, 7)
            steps = [(h, n, m, kt) for h in range(NH) for n in range(NCH) for m in range(2) for kt in range(KT)]
            deferred = []

            def loads(h):
                i = h % 2
                S.dma(lambda e, i=i, h=h: e.dma_start(out=q_sb[i], in_=qT_s[h, :, :]), reads=[sc['q']], writes=[Bqkv[i]])
                ksrc = (kv_rcv_k if PAIR else kT_s).rearrange("(r h p) t -> r h p t", h=NH, p=128)
                vsrc = (kv_rcv_v if PAIR else v_s).rearrange("(r t) v -> r t v", t=NT)
                for part in range(nparts):
                    S.dma(lambda e, i=i, h=h, part=part, ksrc=ksrc: e.dma_start(out=k_sb[i][:, part * NT:(part + 1) * NT], in_=ksrc[part, h, :, :]),
                          reads=[sc['k']], writes=[Bqkv[i]], more=True)
                    S.dma(lambda e, i=i, h=h, part=part, vsrc=vsrc: e.dma_start(out=v_sb[i][:, part * TT:(part + 1) * TT, :],
                                                                     in_=vsrc[part, :, h * 128:(h + 1) * 128].rearrange("(t p) v -> p t v", p=128)),
                          reads=[sc['v']], writes=[Bqkv[i]], more=True)

            def score(s):
                h, n, m, kt = steps[s]
                if n == 0 and m == 0 and kt == 0:
                    loads(h)
                i = h % 2
                sbk = SB[s % 3]
                ms = slice(m * 64, (m + 1) * 64)
                cs = slice(n * CH, (n + 1) * CH)
                S.op('pe', lambda e, i=i, ms=ms, kt=kt, sbk=sbk, cs=cs: e.matmul(P(sbk, CH), k_sb[i][ms, kt * 128:(kt + 1) * 128], q_sb[i][ms, cs], start=True, stop=True),
                     reads=[Bqkv[i]], writes=[Bps[sbk]])

            def tail1(m):
                S.op('act', lambda e, m=m: e.activation(out=rd, in_=P(4 + m, CH), func=AF.Copy), reads=[Bps[4 + m], Brd], writes=[Brd])
                S.op('dve', lambda e: e.reciprocal(rd, rd), reads=[Brd], writes=[Brd])
                S.op('act', lambda e, m=m: e.activation(out=om[m], in_=P(2 + m, CH), func=AF.Copy), reads=[Bps[2 + m], Bom[m]], writes=[Bom[m]])
                S.op('dve', lambda e, m=m: e.tensor_tensor(om[m], om[m], rd, op=ALU.mult), reads=[Brd, Bom[m]], writes=[Bom[m]])

            def fin1():
                S.op('dve', lambda e: e.scalar_tensor_tensor(out=oc, in0=om[1], scalar=lyr[:, L_LAMC:L_LAMC + 1], in1=om[0], op0=ALU.mult, op1=ALU.add),
                     reads=[Bom[0], Bom[1], Blyr, Boc], writes=[Boc])
                S.op('act', lambda e: e.activation(out=sqa, in_=oc, func=AF.Square), reads=[Boc, Bsqa], writes=[Bsqa])

            def fin2():
                S.op('pe', lambda e: e.matmul(P(6, CH), ones_f[:], sqa, start=True, stop=True), reads=[Bsqa, Bconst], writes=[Bps[6]])

            def fin3(h, cs):
                rsqrt_ops(rsa, P(6, CH), 1.0 / 128, [Bps[6], Brsa], [Brsa], post=post)
                S.op('dve', lambda e, h=h, cs=cs: e.scalar_tensor_tensor(out=ydT[:, h, cs], in0=oc, scalar=lyr[:, L_SUB:L_SUB + 1], in1=rsa, op0=ALU.mult, op1=ALU.mult),
                     reads=[Boc, Blyr, Brsa, Byd[h]], writes=[Byd[h]])

            for s in range(min(SK, len(steps))):
                score(s)
            for s in range(len(steps)):
                h, n, m, kt = steps[s]
                i = h % 2
                sbk = SB[s % 3]
                r = s % 3
                if s + SK < len(steps):
                    score(s + SK)
                S.op('act', lambda e, r=r, sbk=sbk: e.activation(out=pTt[r], in_=P(sbk, CH), func=AF.Exp), reads=[Bps[sbk]], writes=[BpT[r]])
                S.op('pe', lambda e, i=i, r=r, kt=kt, m=m: e.matmul(P(2 + m, CH), v_sb[i][:, kt, :], pTt[r], start=(kt == 0), stop=(kt == KT - 1)),
                     reads=[Bqkv[i], BpT[r]], writes=[Bps[2 + m]])
                S.op('pe', lambda e, r=r, kt=kt, m=m: e.matmul(P(4 + m, CH), ones_b[:], pTt[r], start=(kt == 0), stop=(kt == KT - 1)),
                     reads=[Bconst, BpT[r]], writes=[Bps[4 + m]])
                while deferred and deferred[0][0] <= s:
                    deferred.pop(0)[1]()
                if kt == KT - 1:
                    deferred.append((s + 1, lambda m=m: tail1(m)))
                    if m == 1:
                        cs = slice(n * CH, (n + 1) * CH)
                        deferred.append((s + 3, fin1))
                        deferred.append((s + 5, fin2))
                        deferred.append((s + 7, lambda h=h, cs=cs: fin3(h, cs)))
            while deferred:
                deferred.pop(0)[1]()

        def out_proj(l, sc):
            AR.reset()
            mg = AR.get([KC, NT], BF16); Bmg = [[S.buf() for _ in range(NCH)] for _ in range(KC)]
            gd = [AR.get([CH], BF16) for _ in range(2)]; gg = [AR.get([CH], BF16) for _ in range(2)]; Bgt = S.bufs(2, 'gt')
            t1 = [AR.get([CH]) for _ in range(2)]; Bt1 = S.bufs(2, 't1')
            t2 = [AR.get([CH]) for _ in range(2)]; Bt2 = S.bufs(2, 't2')
            it = 0
            for j in range(4):
                for which in range(2):
                    w_, Bw_ = WS.get((('bd', 'bg')[which], l, j))
                    for m in range(4):
                        mc = j * 4 + m
                        for n in range(NCH):
                            cs = slice(n * CH, (n + 1) * CH)
                            i = it % 2
                            it += 1
                            p1 = it % 4
                            src = ydT if which == 0 else ygT
                            for kc in range(8):
                                rdb = Byd[kc] if which == 0 else Byg[kc // 4]
                                S.op('pe', lambda e, kc=kc, m=m, cs=cs, p1=p1, w_=w_, src=src: e.matmul(P(p1, CH), w_[:, kc, m * 128:(m + 1) * 128], src[:, kc, cs], start=(kc == 0), stop=(kc == 7)),
                                     reads=[Bw_, rdb], writes=[Bps[p1]])
                            S.dma(lambda e, i=i, mc=mc, cs=cs, which=which: e.dma_start(out=gd[i], in_=gg_s[16 * which + mc, :, cs]), reads=[sc['gg']], writes=[Bgt[i]])
                            S.op('act', lambda e, i=i, p1=p1: e.activation(out=t1[i], in_=P(p1, CH), func=AF.Copy), reads=[Bps[p1], Bt1[i]], writes=[Bt1[i]])
                            if which == 0:
                                S.op('dve', lambda e, i=i, mc=mc, cs=cs: e.tensor_tensor(mg[:, mc, cs], t1[i], gd[i], op=ALU.mult), reads=[Bgt[i], Bt1[i], Bmg[mc][n]], writes=[Bmg[mc][n]])
                            else:
                                S.op('dve', lambda e, i=i: e.tensor_tensor(t1[i], t1[i], gd[i], op=ALU.mult), reads=[Bgt[i], Bt1[i]], writes=[Bt1[i]])
                                S.op('dve', lambda e, i=i, mc=mc, cs=cs: e.tensor_tensor(mg[:, mc, cs], mg[:, mc, cs], t1[i], op=ALU.add), reads=[Bt1[i], Bmg[mc][n]], writes=[Bmg[mc][n]])
            it = 0
            for j in range(4):
                w, Bw = WS.get(('out', l, j))
                for m in range(4):
                    mc = j * 4 + m
                    for n in range(NCH):
                        cs = slice(n * CH, (n + 1) * CH)
                        pb = 4 + it % 2
                        it += 1
                        for kc in range(KC):
                            S.op('pe', lambda e, kc=kc, m=m, cs=cs, pb=pb, w=w: e.matmul(P(pb, CH), w[:, kc, m * 128:(m + 1) * 128], mg[:, kc, cs], start=(kc == 0), stop=(kc == KC - 1)),
                                 reads=[Bw, Bmg[kc][n]], writes=[Bps[pb]])
                        i2 = it % 2
                        S.op('act', lambda e, mc=mc, pb=pb, i2=i2: e.activation(out=t1[i2], in_=P(pb, CH), func=AF.Copy, scale=mod[:, 32 + mc:33 + mc]), reads=[Bps[pb], Bmod, Bt1[i2]], writes=[Bt1[i2]])
                        S.op('dve', lambda e, mc=mc, cs=cs, i2=i2: e.tensor_tensor(xT[:, mc, cs], xT[:, mc, cs], t1[i2], op=ALU.add), reads=[Bt1[i2], BxT[mc][n]], writes=[BxT[mc][n]])

        def ffn(l):
            modnorm(1)
            AR.reset()
            aT = AR.get([FC, CH], BF16); BaT = S.bufs(FC, 'aT')
            sg_off = AR.off
            sg = [AR.get([CH], BF16) for _ in range(2)]; Bsg = S.bufs(2, 'sg')
            uc = [AR.get([CH], BF16) for _ in range(2)]; Buc = S.bufs(2, 'uc')
            xdf = arena[:, sg_off // 4:sg_off // 4 + CH]
            it = 0
            for n in range(NCH):
                cs = slice(n * CH, (n + 1) * CH)
                for j in range(11):
                    for which in range(2):
                        w_, Bw_ = WS.get((('upg', 'upu')[which], l, n, j))
                        for m in range(4):
                            jc = j * 4 + m
                            i = it % 2
                            it += 1
                            p1 = it % 4
                            for kc in range(KC):
                                S.op('pe', lambda e, kc=kc, m=m, p1=p1, w_=w_: e.matmul(P(p1, CH), w_[:, kc, m * 128:(m + 1) * 128], hT[:, kc, cs], start=(kc == 0), stop=(kc == KC - 1)),
                                     reads=[Bw_, BhT[n]], writes=[Bps[p1]])
                            if which == 0:
                                S.op('act', lambda e, p1=p1, jc=jc: e.activation(out=aT[:, jc, :], in_=P(p1, CH), func=AF.Silu), reads=[Bps[p1], BaT[jc]], writes=[BaT[jc]])
                            else:
                                S.op('act', lambda e, i=i, p1=p1: e.activation(out=uc[i], in_=P(p1, CH), func=AF.Copy), reads=[Bps[p1], Buc[i]], writes=[Buc[i]])
                                S.op('dve', lambda e, i=i, jc=jc: e.tensor_tensor(aT[:, jc, :], aT[:, jc, :], uc[i], op=ALU.mult), reads=[Buc[i], BaT[jc]], writes=[BaT[jc]])
                for m in range(KC):
                    w, Bw = WS.get(('dn', l, n, m))
                    pb = 4 + m % 2
                    for kc in range(FC):
                        S.op('pe', lambda e, kc=kc, pb=pb, w=w: e.matmul(P(pb, CH), w[:, kc, :], aT[:, kc, :], start=(kc == 0), stop=(kc == FC - 1)),
                             reads=[Bw, BaT[kc]], writes=[Bps[pb]])
                    S.op('act', lambda e, m=m, pb=pb: e.activation(out=xdf, in_=P(pb, CH), func=AF.Copy, scale=mod[:, 80 + m:81 + m]), reads=[Bps[pb], Bmod], writes=[Bsg[0], Bsg[1]])
                    S.op('dve', lambda e, m=m: e.tensor_tensor(xT[:, m, cs], xT[:, m, cs], xdf, op=ALU.add), reads=[Bsg[0], Bsg[1], BxT[m][n]], writes=[BxT[m][n]])


        setup()
        stage = dbg.get('_stage', None)
        for l in range(DEPTH):
            load_layer_params(l)
            ada_mod(l)
            modnorm(0)
            sc = in_proj(l)
            if stage == 'inproj':
                break
            gp = gdn_prep(l, sc)
            if stage == 'prep':
                break
            Bo1 = S.buf('o1_s')
            gdn_scan(l, 0, sc, gp, Bo1, True)
            if stage == 'scan0':
                break
            if PAIR:
                exchange(l, sc)
            attention(l, sc, 2 if PAIR else 1)
            if stage == 'attn':
                break
            gdn_scan(l, 1, sc, gp, Bo1, not PAIR)
            if stage == 'mixer':
                break
            out_proj(l, sc)
            if stage == 'outproj':
                break
            ffn(l)
        def dump_sb(name, src, rd):
            S.dma(lambda e: e.dma_start(out=dbg_d[name], in_=src), reads=rd, writes=[Bout], store=True)
        S.barrier()
        if 'mod' in dbg_d:
            dump_sb('mod', mod[:], [Bmod])
        if 'cos' in dbg_d:
            dump_sb('cos', cosT[:], [Brope]); dump_sb('sin', sinT[:], [Brope])
        if 'beta' in dbg_d:
            dump_sb('beta', betaT[:], [Bbg]); dump_sb('g', gT[:], [Bbg])
        if 'hT' in dbg_d:
            AR.reset()
            t = AR.get([NT]); Bt_ = S.buf()
            for kc in range(KC):
                S.op('dve', lambda e, kc=kc: e.tensor_copy(t, hT[:, kc, :]), reads=BhT, writes=[Bt_])
                S.dma(lambda e, kc=kc: e.dma_start(out=dbg_d['hT'][kc * 128:(kc + 1) * 128, :], in_=t), reads=[Bt_], writes=[Bout], store=True)
        for nm, view in (('yd', ydT), ('yg', ygT)):
            if nm in dbg_d:
                AR.reset()
                t = AR.get([NT]); Bt_ = S.buf()
                for kc in range(8):
                    S.op('dve', lambda e, kc=kc, view=view: e.tensor_copy(t, view[:, kc, :]), reads=Byd + Byg, writes=[Bt_])
                    S.dma(lambda e, kc=kc, nm=nm: e.dma_start(out=dbg_d[nm][kc * 128:(kc + 1) * 128, :], in_=t), reads=[Bt_], writes=[Bout], store=True)
        ov = out_d.rearrange("(k p) n -> p k n", p=128)
        for kc in range(KC):
            S.dma(lambda e, kc=kc: e.dma_start(out=ov[:, kc, :], in_=xT[:, kc, :]), reads=BxT[kc], writes=[Bout], store=True)
        S.final_wait('sp', [Bout])
        S.emit()
    return nc


def prep_inputs(inp, NT, PAIR, DEPTH, n_cores=8):
    f32 = np.float32
    x = np.asarray(inp['x'], f32)
    B, SEQ, _ = x.shape
    c = np.asarray(inp['c'], f32)
    pos = np.asarray(inp['positions']).astype(np.int32)
    w_in = np.asarray(inp['w_in'], f32)
    conv_w = np.asarray(inp['gdn_conv_w'], f32)
    a_log = np.asarray(inp['gdn_a_log'], f32)
    dt_bias = np.asarray(inp['gdn_dt_bias'], f32)
    invf = (np.float32(10000.0) ** (-np.arange(32, dtype=np.float32) * np.float32(2.0) / np.float32(64))).astype(f32)
    shared = {
        'invf': np.ascontiguousarray(np.broadcast_to(invf[None, :], (128, 32))),
        'ada_w': np.asarray(inp['ada_w'], f32),
        'ada_bT': np.ascontiguousarray(np.asarray(inp['ada_b'], f32).reshape(DEPTH, 96, 128).transpose(0, 2, 1)),
        'nmwT': np.ascontiguousarray(np.asarray(inp['norm_mix_w'], f32).reshape(DEPTH, 16, 128).transpose(0, 2, 1)),
        'nfwT': np.ascontiguousarray(np.asarray(inp['norm_ffn_w'], f32).reshape(DEPTH, 16, 128).transpose(0, 2, 1)),
        'w_in': w_in,
        'qn_w': np.asarray(inp['diff_qn_w'], f32),
        'kn_w': np.asarray(inp['diff_kn_w'], f32),
        'lamv': np.ascontiguousarray(np.asarray(inp['diff_lambda'], f32).reshape(DEPTH, 256)),
        'sublnT': np.ascontiguousarray(np.asarray(inp['diff_subln_w'], f32).reshape(DEPTH, 128, 1)),
        'gnwT': np.ascontiguousarray(np.asarray(inp['gdn_norm_w'], f32).reshape(DEPTH, 128, 1)),
        'w_bd': np.asarray(inp['w_branch_diff'], f32),
        'w_bg': np.asarray(inp['w_branch_gdn'], f32),
        'w_out': np.asarray(inp['w_out'], f32),
        'w_up': np.asarray(inp['ffn_w_up'], f32),
        'w_down': np.asarray(inp['ffn_w_down'], f32),
    }
    role = {}
    for r in (0, 1):
        dmap = [0, 1] if r == 0 else [1, 0]
        cols = [7168 + d * 8 + h for d in dmap for h in range(8)] + [7184 + d * 8 + h for d in dmap for h in range(8)]
        cw = conv_w if r == 0 else conv_w[:, ::-1, :]
        role[r] = {
            'w_dir': np.ascontiguousarray(w_in[:, :, cols]),
            'a_log': np.ascontiguousarray(a_log[:, dmap, :].reshape(DEPTH, 16)),
            'dt_bias': np.ascontiguousarray(dt_bias[:, dmap, :].reshape(DEPTH, 16)),
            'convT': np.ascontiguousarray(cw.reshape(DEPTH, 5, 24, 128).transpose(0, 3, 2, 1)),
            'sel': np.ascontiguousarray(np.broadcast_to(np.array([[0.0, 1.0, 1.0, 0.0] if r == 0 else [1.0, 0.0, 0.0, 1.0]], f32), (128, 4))),
        }
    maps = []
    meta = []
    for core in range(n_cores):
        if PAIR:
            b, r = core // 2, core % 2
            tok = np.arange(NT) if r == 0 else (SEQ - 1 - np.arange(NT))
        else:
            b, r = core % B, 0
            tok = np.arange(NT)
        m = dict(shared)
        m.update(role[r])
        m['xT'] = np.ascontiguousarray(x[b, tok, :].T)
        m['posT'] = np.ascontiguousarray(pos[b, tok].reshape(NT // 128, 128).T)
        m['cT'] = np.ascontiguousarray(c[b].reshape(16, 128).T)
        maps.append(m)
        meta.append((b, tok))
    return maps, meta


_PROG_CACHE = {}


def kernel(**inputs):
    x = np.asarray(inputs['x'])
    B, SEQ, _ = x.shape
    DEPTH = int(np.asarray(inputs['ada_w']).shape[0])
    NT = SEQ // 2
    key = (NT, DEPTH)
    if key not in _PROG_CACHE:
        _PROG_CACHE[key] = build_program(NT, DEPTH, True, {})
    nc = _PROG_CACHE[key]
    maps, meta = prep_inputs(inputs, NT, True, DEPTH, n_cores=8)
    res = run_bass_kernel_spmd(nc, maps, core_ids=list(range(8)))
    out = np.empty((B, SEQ, D), np.float32)
    for core in range(8):
        b, tok = meta[core]
        out[b, tok, :] = np.asarray(res.results[core]['outT'], np.float32).T
    return out
```

```python
import math
import types
from contextlib import ExitStack

import numpy as np
import concourse.bass as bass
import concourse.mybir as mybir
from concourse.bass_utils import run_bass_kernel_spmd

F32 = mybir.dt.float32
BF16 = mybir.dt.bfloat16
I32 = mybir.dt.int32
AF = mybir.ActivationFunctionType
ALU = mybir.AluOpType
AX = mybir.AxisListType

D = 2048
KC = 16
NH = 8
FFN = 5632
FC = 44
IN_COLS = 11296
EPS = 1e-6
COMPUTE = ('pe', 'act', 'dve', 'pool')


class Buf:
    __slots__ = ('name', 'w', 'r', 'sem', 'cnt')

    def __init__(self, name):
        self.name = name
        self.w = []
        self.r = []
        self.sem = None
        self.cnt = 0


def _snap(fn):
    if fn is None or fn.__closure__ is None:
        return fn
    cells = []
    for c in fn.__closure__:
        try:
            cells.append(types.CellType(c.cell_contents))
        except ValueError:
            cells.append(c)
    return types.FunctionType(fn.__code__, fn.__globals__, fn.__name__, fn.__defaults__, tuple(cells))


class Sched:
    def __init__(self, nc, stack):
        self.nc = nc
        self.stack = stack
        self.ops = {e: [] for e in COMPUTE + ('sp',)}
        self.seq = {e: 0 for e in COMPUTE}
        self.esem = {e: stack.enter_context(nc.semaphore('s_' + e)) for e in COMPUTE}
        self.waited = {e: {} for e in COMPUTE + ('sp',)}
        self.nsem = 0
        self.nbuf = 0
        self.dbufs = []
        self.named = {}

    def buf(self, name=None):
        self.nbuf += 1
        if name is None:
            return Buf('b%d' % self.nbuf)
        if name not in self.named:
            self.named[name] = Buf(name)
        return self.named[name]

    def bufs(self, n, name='b'):
        return [self.buf('%s%d' % (name, i)) for i in range(n)]

    def _dsem(self, b):
        if b.sem is None:
            b.sem = self.stack.enter_context(self.nc.semaphore('d%d' % self.nsem))
            self.nsem += 1
            self.dbufs.append(b)
        return b.sem

    def _deps(self, eng, reads, writes):
        evs = []
        for b in reads:
            evs.extend(b.w)
        for b in writes:
            evs.extend(b.w)
            evs.extend(b.r)
        waits = {}
        wd = self.waited[eng]
        for ev in evs:
            sem, val, src = ev[0], ev[1], ev[2]
            if src == 'pe' and eng == 'pe':
                continue
            if src == 'dma':
                val = ev[3].cnt
            if wd.get(sem, 0) >= val:
                continue
            if waits.get(sem, 0) < val:
                waits[sem] = val
        for sem, val in waits.items():
            wd[sem] = val
        return list(waits.items())

    def op(self, eng, fn, reads=(), writes=()):
        waits = self._deps(eng, reads, writes)
        self.seq[eng] += 1
        ev = (self.esem[eng], self.seq[eng], eng)
        for b in reads:
            b.r.append(ev)
        for b in writes:
            b.w = [ev]
            b.r = []
        self.ops[eng].append((waits, _snap(fn), self.esem[eng], 1))

    def dma(self, fn, reads=(), writes=(), q='sp', more=False, store=False, inc=16):
        d = writes[0]
        if more or store:
            saved = d.w
            d.w = []
        waits = self._deps(q, reads, writes)
        sbuf_side = reads[0] if store else d
        sem = self._dsem(sbuf_side)
        sbuf_side.cnt += inc
        ev = (sem, sbuf_side.cnt, 'dma', sbuf_side)
        for b in reads:
            b.r.append(ev)
        if more or store:
            d.w = [x for x in saved if x[0] is not sem] + [ev]
        else:
            d.w = [ev]
            d.r = []
        self.ops[q].append((waits, _snap(fn), sem, inc))

    def barrier(self):
        tgt = [(self.esem[e], self.seq[e]) for e in COMPUTE if self.seq[e] > 0]
        tgt += [(b.sem, b.cnt) for b in self.dbufs if b.cnt > 0 and not b.name.startswith('wt')]
        for eng in ('pe', 'act', 'dve', 'sp'):
            wd = self.waited[eng]
            waits = []
            for (s, v) in tgt:
                if eng in COMPUTE and s is self.esem[eng]:
                    continue
                if wd.get(s, 0) < v:
                    wd[s] = v
                    waits.append((s, v))
            if waits:
                self.ops[eng].append((waits, None, None, 0))

    def final_wait(self, eng, bufs):
        waits = self._deps(eng, bufs, ())
        self.ops[eng].append((waits, None, None, 0))

    def emit(self):
        nc = self.nc
        with nc.Block() as block:
            def run(e, name):
                for (waits, fn, sem, inc) in self.ops[name]:
                    for (s, v) in waits:
                        e.wait_ge(s, v)
                    if fn is not None:
                        fn(e).then_inc(sem, inc)

            @block.tensor
            def _(e):
                run(e, 'pe')

            @block.scalar
            def _(e):
                run(e, 'act')

            @block.vector
            def _(e):
                run(e, 'dve')

            @block.gpsimd
            def _(e):
                run(e, 'pool')

            @block.sync
            def _(e):
                run(e, 'sp')


def bc_mid(ap2, n):
    P, Fd = ap2.shape
    return ap2.unsqueeze(1).to_broadcast([P, n, Fd])


def bc_last(ap2, n):
    P, G = ap2.shape
    return ap2.unsqueeze(2).to_broadcast([P, G, n])


def build_program(NT, DEPTH, PAIR, dbg=None):
    dbg = dbg or {}
    CDT = BF16 if dbg.get('_chain16') else F32
    TT = NT // 128
    CH = min(512, NT)
    NCH = NT // CH
    NK = 2 * NT if PAIR else NT
    KT = NK // 128
    lam_init = [0.8 - 0.6 * math.exp(-0.3 * l) for l in range(DEPTH)]

    nc = bass.Bass("TRN2", target_bir_lowering=False)

    def din(name, shape, dt=F32):
        return nc.dram_tensor(name, list(shape), dt, kind="ExternalInput").ap()

    xT_d = din("xT", [D, NT])
    pos_d = din("posT", [128, TT], I32)
    c_d = din("cT", [128, KC])
    invf_d = din("invf", [128, 32])
    sel_d = din("sel", [128, 4])
    ada_w_d = din("ada_w", [DEPTH, D, 6 * D])
    ada_b_d = din("ada_bT", [DEPTH, 128, 96])
    nmw_d = din("nmwT", [DEPTH, 128, KC])
    nfw_d = din("nfwT", [DEPTH, 128, KC])
    w_in_d = din("w_in", [DEPTH, D, IN_COLS])
    w_dir_d = din("w_dir", [DEPTH, D, 32])
    qnw_d = din("qn_w", [DEPTH, 64])
    knw_d = din("kn_w", [DEPTH, 64])
    lamv_d = din("lamv", [DEPTH, 256])
    subln_d = din("sublnT", [DEPTH, 128, 1])
    conv_d = din("convT", [DEPTH, 128, 24, 5])
    alog_d = din("a_log", [DEPTH, 16])
    dtb_d = din("dt_bias", [DEPTH, 16])
    gnw_d = din("gnwT", [DEPTH, 128, 1])
    wbd_d = din("w_bd", [DEPTH, 1024, D])
    wbg_d = din("w_bg", [DEPTH, 1024, D])
    wout_d = din("w_out", [DEPTH, D, D])
    wup_d = din("w_up", [DEPTH, D, 2 * FFN])
    wdn_d = din("w_down", [DEPTH, FFN, D])
    out_d = nc.dram_tensor("outT", [D, NT], F32, kind="ExternalOutput").ap()
    dbg_d = {k: nc.dram_tensor("dbg_" + k, list(shp), F32, kind="ExternalOutput").ap()
             for k, shp in dbg.items() if not k.startswith('_')}

    def dscr(name, shape, dt=BF16):
        return nc.dram_tensor(name, list(shape), dt).ap()

    qT_s = dscr("qT_s", [NH, 128, NT])
    kT_s = dscr("kT_s", [NH * 128, NT])
    v_s = dscr("v_s", [NT, 1024])
    g_s = dscr("g_s", [24, 128, NT])
    z_s = dscr("z_s", [NH, 128, NT])
    gg_s = dscr("gg_s", [32, 128, NT])
    gq_s = dscr("gq_s", [NH, 128, NT])
    gk_s = dscr("gk_s", [NH, 128, NT])
    gkt_s = dscr("gkt_s", [NH, NT, 128])
    gvt_s = dscr("gvt_s", [NH, NT, 128])
    o1_s = dscr("o1_s", [NH, 128, NT], F32)
    if PAIR:
        halo_snd = dscr("halo_snd", [2 * 128, 48])
        halo_rcv = dscr("halo_rcv", [2 * 128, 48])
        st_snd = dscr("st_snd", [128, NH * 128], F32)
        st_rcv = dscr("st_rcv", [128, NH * 128], F32)
        kv_snd_k = dscr("kv_snd_k", [2 * NH * 128, NT])
        kv_rcv_k = dscr("kv_rcv_k", [2 * NH * 128, NT])
        kv_snd_v = dscr("kv_snd_v", [2 * NT, 1024])
        kv_rcv_v = dscr("kv_rcv_v", [2 * NT, 1024])

    def allreduce(e, src, dst):
        return e.collective_compute("AllReduce", ALU.add, replica_groups=GROUPS, ins=[src.opt()], outs=[dst.opt()])
    GROUPS = [[2 * i, 2 * i + 1] for i in range(dbg.get('_ncores', 8) // 2)]

    with ExitStack() as stack:
        S = Sched(nc, stack)
        sb = lambda name, shape, dt=F32: nc.alloc_sbuf_tensor("sb_" + name, list(shape), dt)

        xT = sb("xT", [128, KC, NT]);            BxT = [[S.buf() for _ in range(NCH)] for _ in range(KC)]
        hT = sb("hT", [128, KC, NT], BF16);      BhT = [S.buf() for _ in range(NCH)]
        NW = 3
        wt = [sb("wt%d" % i, [128, 16 * 512], BF16) for i in range(NW)]
        Bwt = S.bufs(NW, 'wt')
        ident_f = sb("ident_f", [128, 128]);     ident_b = sb("ident_b", [128, 128], BF16)
        ones_f = sb("ones_f", [128, 128]);       ones_b = sb("ones_b", [128, 128], BF16)
        m_Ui = sb("m_Ui", [128, 128]); m_Li = sb("m_Li", [128, 128])
        m_Us = sb("m_Us", [128, 128]); m_Ls = sb("m_Ls", [128, 128])
        Bconst = S.buf('const')
        cosT = sb("cosT", [128, TT, 32]); sinT = sb("sinT", [128, TT, 32]); Brope = S.buf('rope')
        cact = sb("cact", [128, KC], BF16); Bcact = S.buf('cact')
        mod = sb("mod", [128, 96]); Bmod = S.buf('mod')
        modA = sb("modA", [128, 2, KC]); BmodA = S.buf('modA')
        lyr = sb("lyr", [128, 1024]); Blyr = S.buf('lyr')
        betaT = sb("betaT", [128, TT, 16]); gT = sb("gT", [128, TT, 16]); Bbg = S.buf('bg')
        sel = sb("sel", [128, 4]); Bsel = S.buf('sel')
        ARENA = 48 * 1024
        arena = sb("arena", [128, ARENA // 4])
        psum = [nc.alloc_psum_tensor("ps%d" % i, [128, 512], F32) for i in range(8)]
        Bps = S.bufs(8, 'ps')

        def P(i, n=512):
            return psum[i][:, 0:n]

        class Arena:
            def __init__(self):
                self.off = 0

            def reset(self):
                S.barrier()
                self.off = 0

            def get(self, shape, dt=F32):
                esz = 4 if dt == F32 or dt == I32 else 2
                n = int(np.prod(shape))
                nbytes = (n * esz + 31) // 32 * 32
                assert self.off + nbytes <= ARENA, ("arena overflow", self.off, nbytes)
                v = arena[:, self.off // 4:(self.off + nbytes) // 4]
                self.off += nbytes
                if dt != F32:
                    v = v.bitcast(dt)
                v = v[:, 0:n]
                if len(shape) == 2:
                    return v.rearrange("p (a b) -> p a b", b=shape[1])
                if len(shape) == 3:
                    return v.rearrange("p (a b c) -> p a b c", b=shape[1], c=shape[2])
                return v

        AR = Arena()

        class WStream:
            def __init__(self):
                self.descs = []
                self.issued = 0
                self.taken = 0
                self.loaded = {}

            def add(self, tag, src, kch, ncols):
                self.descs.append((tag, src, kch, ncols))

            def _issue(self, i):
                tag, src, kch, ncols = self.descs[i]
                slot = i % NW
                t = wt[slot][:, 0:kch * ncols].rearrange("p (k n) -> p k n", n=ncols)
                first = True
                for k0 in range(0, kch, 16):
                    k1 = min(kch, k0 + 16)
                    S.dma(lambda e, t=t, src=src, k0=k0, k1=k1: e.dma_start(out=t[:, k0:k1, :], in_=src[:, k0:k1, :]),
                          writes=[Bwt[slot]], q='pool', more=not first)
                    first = False
                self.loaded[i] = (t, Bwt[slot])

            def get(self, tag):
                i = self.taken
                assert self.descs[i][0] == tag, (self.descs[i][0], tag)
                while self.issued < len(self.descs) and self.issued <= i + NW - 1:
                    self._issue(self.issued)
                    self.issued += 1
                self.taken += 1
                return self.loaded.pop(i)

        WS = WStream()

        def wview(w2d, c0, n):
            return w2d.rearrange("(k p) n -> p k n", p=128)[:, :, c0:c0 + n]

        for l in range(DEPTH):
            for j in range(24):
                WS.add(('ada', l, j), wview(ada_w_d[l], j * 512, 512), KC, 512)
            for j in range(6):
                WS.add(('tm', l, j), wview(w_in_d[l], j * 512, 512), KC, 512)
            WS.add(('dir', l), wview(w_dir_d[l], 0, 32), KC, 32)
            for j in range(16):
                WS.add(('fm', l, j), wview(w_in_d[l], 3072 + j * 512 + (32 if j >= 8 else 0), 512), KC, 512)
            for j in range(4):
                WS.add(('bd', l, j), wview(wbd_d[l], j * 512, 512), 8, 512)
                WS.add(('bg', l, j), wview(wbg_d[l], j * 512, 512), 8, 512)
            for j in range(4):
                WS.add(('out', l, j), wview(wout_d[l], j * 512, 512), KC, 512)
            for n in range(NCH):
                for j in range(11):
                    WS.add(('upg', l, n, j), wview(wup_d[l], j * 512, 512), KC, 512)
                    WS.add(('upu', l, n, j), wview(wup_d[l], FFN + j * 512, 512), KC, 512)
                for m in range(16):
                    WS.add(('dn', l, n, m), wview(wdn_d[l], m * 128, 128), FC, 128)

        Bout = S.buf('out')

        def setup():
            S.op('pool', lambda e: e.memset(ident_f[:], 0.0), writes=[Bconst])
            S.op('pool', lambda e: e.affine_select(out=ident_f[:], in_=ident_f[:], pattern=[[-1, 128]],
                                                   compare_op=ALU.not_equal, fill=1.0, base=0, channel_multiplier=1),
                 reads=[Bconst], writes=[Bconst])
            S.op('pool', lambda e: e.memset(ones_f[:], 1.0), reads=[Bconst], writes=[Bconst])
            for (m, op, sgn) in ((m_Ls, ALU.is_gt, 1), (m_Li, ALU.is_ge, 1), (m_Us, ALU.is_gt, -1), (m_Ui, ALU.is_ge, -1)):
                S.op('pool', lambda e, m=m, op=op, sgn=sgn: e.affine_select(out=m[:], in_=ones_f[:], pattern=[[-sgn, 128]],
                                                                             compare_op=op, fill=0.0, base=0, channel_multiplier=sgn),
                     reads=[Bconst], writes=[Bconst])
            S.op('pool', lambda e: e.tensor_copy(ident_b[:], ident_f[:]), reads=[Bconst], writes=[Bconst])
            S.op('pool', lambda e: e.tensor_copy(ones_b[:], ones_f[:]), reads=[Bconst], writes=[Bconst])
            xv = xT_d.rearrange("(k p) n -> p k n", p=128)
            for kc in range(KC):
                for n in range(NCH):
                    S.dma(lambda e, kc=kc, n=n: e.dma_start(out=xT[:, kc, n * CH:(n + 1) * CH], in_=xv[:, kc, n * CH:(n + 1) * CH]),
                          writes=[BxT[kc][n]])
            S.dma(lambda e: e.dma_start(out=sel[:], in_=sel_d[:, :]), writes=[Bsel])
            AR.reset()
            posi = AR.get([TT], I32); posf = AR.get([TT]); invf = AR.get([32]); ang = AR.get([TT, 32]); kf = AR.get([TT, 32])
            ki = AR.get([TT, 32], I32); cf = AR.get([KC])
            Bt = S.buf('setup_t')
            S.dma(lambda e: e.dma_start(out=posi, in_=pos_d[:, :]), writes=[Bt])
            S.dma(lambda e: e.dma_start(out=invf, in_=invf_d[:, :]), writes=[Bt], more=True)
            S.dma(lambda e: e.dma_start(out=cf, in_=c_d[:, :]), writes=[Bt], more=True)
            S.op('dve', lambda e: e.tensor_copy(posf, posi), reads=[Bt], writes=[Bt])
            for t in range(TT):
                S.op('dve', lambda e, t=t: e.tensor_scalar(ang[:, t, :], invf, posf[:, t:t + 1], None, op0=ALU.mult),
                     reads=[Bt], writes=[Bt])
            def reduce_sin(dst, shift):
                S.op('dve', lambda e: e.tensor_scalar(kf, ang, shift, 1.0 / (2 * math.pi), op0=ALU.add, op1=ALU.mult), reads=[Bt, Brope], writes=[Bt])
                S.op('dve', lambda e: e.tensor_copy(ki, kf), reads=[Bt], writes=[Bt])
                S.op('dve', lambda e: e.tensor_copy(kf, ki), reads=[Bt], writes=[Bt])
                S.op('dve', lambda e: e.scalar_tensor_tensor(out=kf, in0=kf, scalar=-2 * math.pi, in1=ang, op0=ALU.mult, op1=ALU.add),
                     reads=[Bt], writes=[Bt])
                S.op('dve', lambda e: e.tensor_scalar(kf, kf, shift, None, op0=ALU.add), reads=[Bt], writes=[Bt])
                S.op('dve', lambda e: e.tensor_scalar(kf, kf, -math.pi, math.pi, op0=ALU.max, op1=ALU.min), reads=[Bt], writes=[Bt])
                S.op('act', lambda e: e.activation(out=dst, in_=kf, func=AF.Sin), reads=[Bt, Brope], writes=[Brope])
            reduce_sin(sinT[:], 0.0)
            reduce_sin(cosT[:], math.pi / 2)
            S.op('act', lambda e: e.activation(out=cact[:], in_=cf, func=AF.Silu), reads=[Bt], writes=[Bcact])

        L_QNW, L_KNW = 0, 64
        L_LAM = 128
        L_CONV = 384
        L_ALOG, L_DTB = 504, 520
        L_ADAB = 536
        L_NMW, L_NFW = 632, 648
        L_SUB, L_GNW = 664, 665
        L_LAMC = 666
        L_NEGA = 668
        L_TMP = 700

        def load_layer_params(l):
            def ld(off, n, src, more=True):
                S.dma(lambda e: e.dma_start(out=lyr[:, off:off + n], in_=src), writes=[Blyr], more=more)
            S.dma(lambda e: e.dma_start(out=lyr[:, L_QNW:L_QNW + 64], in_=qnw_d[l].partition_broadcast(128)), writes=[Blyr])
            ld(L_KNW, 64, knw_d[l].partition_broadcast(128))
            ld(L_LAM, 256, lamv_d[l].partition_broadcast(128))
            ld(L_CONV, 120, conv_d[l].rearrange("p a b -> p (a b)"))
            ld(L_ALOG, 16, alog_d[l].partition_broadcast(128))
            ld(L_DTB, 16, dtb_d[l].partition_broadcast(128))
            ld(L_ADAB, 96, ada_b_d[l])
            ld(L_NMW, 16, nmw_d[l])
            ld(L_NFW, 16, nfw_d[l])
            ld(L_SUB, 1, subln_d[l])
            ld(L_GNW, 1, gnw_d[l])
            rw = dict(reads=[Blyr], writes=[Blyr])
            S.op('dve', lambda e: e.tensor_scalar(lyr[:, L_QNW:L_QNW + 64], lyr[:, L_QNW:L_QNW + 64], 0.125, None, op0=ALU.mult), **rw)
            S.op('dve', lambda e: e.tensor_tensor(lyr[:, L_TMP:L_TMP + 64], lyr[:, L_LAM:L_LAM + 64], lyr[:, L_LAM + 64:L_LAM + 128], op=ALU.mult), **rw)
            S.op('dve', lambda e: e.tensor_tensor(lyr[:, L_TMP + 64:L_TMP + 128], lyr[:, L_LAM + 128:L_LAM + 192], lyr[:, L_LAM + 192:L_LAM + 256], op=ALU.mult), **rw)
            S.op('dve', lambda e: e.tensor_reduce(out=lyr[:, L_TMP + 128:L_TMP + 130], in_=lyr[:, L_TMP:L_TMP + 128].rearrange("p (a b) -> p a b", b=64),
                                                  axis=AX.X, op=ALU.add), **rw)
            S.op('act', lambda e: e.activation(out=lyr[:, L_TMP + 128:L_TMP + 130], in_=lyr[:, L_TMP + 128:L_TMP + 130], func=AF.Exp), **rw)
            S.op('dve', lambda e: e.tensor_tensor(lyr[:, L_LAMC:L_LAMC + 1], lyr[:, L_TMP + 128:L_TMP + 129], lyr[:, L_TMP + 129:L_TMP + 130], op=ALU.subtract), **rw)
            S.op('dve', lambda e: e.tensor_scalar(lyr[:, L_LAMC:L_LAMC + 1], lyr[:, L_LAMC:L_LAMC + 1], lam_init[l], -1.0, op0=ALU.add, op1=ALU.mult), **rw)
            S.op('act', lambda e: e.activation(out=lyr[:, L_NEGA:L_NEGA + 16], in_=lyr[:, L_ALOG:L_ALOG + 16], func=AF.Exp), **rw)
            S.op('dve', lambda e: e.tensor_scalar(lyr[:, L_NEGA:L_NEGA + 16], lyr[:, L_NEGA:L_NEGA + 16], -1.0, None, op0=ALU.mult), **rw)

        def ada_mod(l):
            pm = 7
            for j in range(24):
                w, Bw = WS.get(('ada', l, j))
                for m in range(4):
                    col = j * 4 + m
                    for kc in range(KC):
                        S.op('pe', lambda e, w=w, m=m, kc=kc, col=col: e.matmul(psum[pm][:, col:col + 1], w[:, kc, m * 128:(m + 1) * 128],
                                                                                   cact[:, kc:kc + 1], start=(kc == 0), stop=(kc == KC - 1)),
                             reads=[Bw, Bcact], writes=[Bps[pm]])
            S.op('dve', lambda e: e.tensor_tensor(mod[:], psum[pm][:, 0:96], lyr[:, L_ADAB:L_ADAB + 96], op=ALU.add),
                 reads=[Bps[pm], Blyr], writes=[Bmod])
            S.op('dve', lambda e: e.scalar_tensor_tensor(out=modA[:, 0, :], in0=mod[:, 16:32], scalar=1.0, in1=lyr[:, L_NMW:L_NMW + 16],
                                                         op0=ALU.add, op1=ALU.mult), reads=[Bmod, Blyr], writes=[BmodA])
            S.op('dve', lambda e: e.scalar_tensor_tensor(out=modA[:, 1, :], in0=mod[:, 64:80], scalar=1.0, in1=lyr[:, L_NFW:L_NFW + 16],
                                                         op0=ALU.add, op1=ALU.mult), reads=[Bmod, Blyr, BmodA], writes=[BmodA])

        def rsqrt_ops(dst, src, scale, rd, wr, eps=EPS, post=1.0):
            S.op('act', lambda e: e.activation(out=dst, in_=src, func=AF.Ln, scale=scale, bias=eps), reads=rd, writes=wr)
            S.op('act', lambda e: e.activation(out=dst, in_=dst, func=AF.Exp, scale=-0.5, bias=math.log(post)), reads=wr, writes=wr)

        def modnorm(which):
            sh0 = 0 if which == 0 else 48
            AR.reset()
            sq = [AR.get([CH]) for _ in range(2)]; Bsq = S.bufs(2, 'sq')
            rstd = AR.get([CH]); Brs = S.buf('rstd')
            tmp = [AR.get([CH]) for _ in range(2)]; Btmp = S.bufs(2, 'tmp')
            for n in range(NCH):
                cs = slice(n * CH, (n + 1) * CH)
                for kc in range(KC):
                    i = kc % 2
                    S.op('act', lambda e, i=i, kc=kc: e.activation(out=sq[i], in_=xT[:, kc, cs], func=AF.Square),
                         reads=[BxT[kc][n]], writes=[Bsq[i]])
                    S.op('pe', lambda e, i=i, kc=kc: e.matmul(P(6, CH), ones_f[:], sq[i], start=(kc == 0), stop=(kc == KC - 1)),
                         reads=[Bsq[i], Bconst], writes=[Bps[6]])
                rsqrt_ops(rstd, P(6, CH), 1.0 / D, [Bps[6]], [Brs])
                for kc in range(KC):
                    i = kc % 2
                    S.op('dve', lambda e, i=i, kc=kc: e.scalar_tensor_tensor(out=tmp[i], in0=xT[:, kc, cs], scalar=modA[:, which, kc:kc + 1], in1=rstd,
                                                                            op0=ALU.mult, op1=ALU.mult),
                         reads=[BxT[kc][n], BmodA, Brs], writes=[Btmp[i]])
                    S.op('act', lambda e, i=i, kc=kc: e.activation(out=hT[:, kc, cs], in_=tmp[i], func=AF.Identity,
                                                                   bias=mod[:, sh0 + kc:sh0 + kc + 1], scale=1.0),
                         reads=[Btmp[i], Bmod], writes=[BhT[n]])

        def fm_block(w, Bw, kch, ncols, rhs_fn, rhs_bufs_fn, evac, pbanks, cnt):
            for m in range(ncols // 128):
                for n in range(NCH):
                    pb = pbanks[cnt[0] % len(pbanks)]
                    cnt[0] += 1
                    for kc in range(kch):
                        S.op('pe', lambda e, pb=pb, m=m, n=n, kc=kc: e.matmul(P(pb, CH), w[:, kc, m * 128:(m + 1) * 128], rhs_fn(kc, n),
                                                                             start=(kc == 0), stop=(kc == kch - 1)),
                             reads=[Bw] + rhs_bufs_fn(kc, n), writes=[Bps[pb]])
                    evac(m, n, pb)

        def in_proj(l):
            AR.reset()
            sqb = [AR.get([512]) for _ in range(2)]; Bsqb = S.bufs(2, 'sqb')
            ss8 = [AR.get([8]) for _ in range(2)]; Bss8 = S.bufs(2, 'ss8')
            qn = [AR.get([512]) for _ in range(2)]; Bqn = S.bufs(2, 'qn')
            rot = [AR.get([512]) for _ in range(2)]; Brot = S.bufs(2, 'rot')
            rt2 = [AR.get([512]) for _ in range(2)]; Brt2 = S.bufs(2, 'rt2')
            qb = [AR.get([512], BF16) for _ in range(2)]; Bqb = S.bufs(2, 'qb')
            qtb = [AR.get([4, 128], BF16) for _ in range(2)]; Bqtb = S.bufs(2, 'qtb')
            vb_ = [AR.get([512], BF16) for _ in range(3)]; Bvb = S.bufs(3, 'vb')
            fo = [AR.get([CH], BF16) for _ in range(3)]; Bfo = S.bufs(3, 'fo')
            sm = AR.get([64]); Bsm = S.buf('sm')
            if PAIR:
                mk = [AR.get([2, 512], BF16) for _ in range(2)]; Bmk = S.bufs(2, 'mk')
                hm = [AR.get([2, 2], BF16) for _ in range(3)]; Bhm = S.bufs(3, 'hm')
            Bq_s = S.buf('q_s'); Bk_s = S.buf('k_s'); Bv_s = S.buf('v_s')
            cnt = [0]
            ev = [0]
            pend = []
            for j in range(6):
                w, Bw = WS.get(('tm', l, j))
                for tt in range(TT):
                    pb = cnt[0] % 4
                    cnt[0] += 1
                    for kc in range(KC):
                        S.op('pe', lambda e, pb=pb, tt=tt, kc=kc, w=w: e.matmul(P(pb), hT[:, kc, tt * 128:(tt + 1) * 128], w[:, kc, :],
                                                                               start=(kc == 0), stop=(kc == KC - 1)),
                             reads=[Bw, BhT[(tt * 128) // CH]], writes=[Bps[pb]])
                    if j >= 4:
                        while pend:
                            pend.pop(0)()
                    if j < 4:
                        i = ev[0] % 2
                        ev[0] += 1
                        isq = (j < 2)
                        woff = L_QNW if isq else L_KNW
                        S.op('act', lambda e, i=i, pb=pb: e.activation(out=sqb[i], in_=P(pb), func=AF.Square), reads=[Bps[pb]], writes=[Bsqb[i]])
                        S.op('dve', lambda e, i=i: e.tensor_reduce(out=ss8[i], in_=sqb[i].rearrange("p (g d) -> p g d", d=64), axis=AX.X, op=ALU.add),
                             reads=[Bsqb[i]], writes=[Bss8[i]])
                        rsqrt_ops(ss8[i], ss8[i], 1.0 / 64, [Bss8[i]], [Bss8[i]])
                        while pend:
                            pend.pop(0)()
                        S.op('dve', lambda e, i=i, pb=pb: e.tensor_tensor(qn[i].rearrange("p (g d) -> p g d", d=64), P(pb).rearrange("p (g d) -> p g d", d=64),
                                                                          bc_last(ss8[i], 64), op=ALU.mult),
                             reads=[Bps[pb], Bss8[i]], writes=[Bqn[i]])
                        S.op('dve', lambda e, i=i, woff=woff: e.tensor_tensor(qn[i].rearrange("p (g d) -> p g d", d=64), qn[i].rearrange("p (g d) -> p g d", d=64),
                                                                             bc_mid(lyr[:, woff:woff + 64], 8), op=ALU.mult),
                             reads=[Bqn[i], Blyr], writes=[Bqn[i]])
                        q4 = qn[i].rearrange("p (g t f) -> p g t f", t=2, f=32)
                        r4 = rot[i].rearrange("p (g t f) -> p g t f", t=2, f=32)
                        s4 = rt2[i].rearrange("p (g t f) -> p g t f", t=2, f=32)
                        cb = bc_mid(cosT[:, tt, :], 8)
                        sn = bc_mid(sinT[:, tt, :], 8)
                        for t_ in range(2):
                            S.op('dve', lambda e, t_=t_, q4=q4, r4=r4, cb=cb: e.tensor_tensor(r4[:, :, t_, :], q4[:, :, t_, :], cb, op=ALU.mult),
                                 reads=[Bqn[i], Brope, Brot[i]], writes=[Brot[i]])
                            S.op('dve', lambda e, t_=t_, q4=q4, s4=s4, sn=sn: e.tensor_tensor(s4[:, :, t_, :], q4[:, :, 1 - t_, :], sn, op=ALU.mult),
                                 reads=[Bqn[i], Brope, Brt2[i]], writes=[Brt2[i]])
                        b4 = qb[i].rearrange("p (g t f) -> p g t f", t=2, f=32)
                        S.op('dve', lambda e, r4=r4, s4=s4, b4=b4: e.tensor_tensor(b4[:, :, 0, :], r4[:, :, 0, :], s4[:, :, 0, :], op=ALU.subtract),
                             reads=[Brot[i], Brt2[i], Bqb[i]], writes=[Bqb[i]])
                        S.op('dve', lambda e, r4=r4, s4=s4, b4=b4: e.tensor_tensor(b4[:, :, 1, :], r4[:, :, 1, :], s4[:, :, 1, :], op=ALU.add),
                             reads=[Brot[i], Brt2[i], Bqb[i]], writes=[Bqb[i]])
                        def fin(i=i, j=j, tt=tt, isq=isq, evv=ev[0]):
                            tb = 4 + (evv % 2)
                            pT = psum[tb][:].bitcast(BF16)
                            for hh in range(4):
                                S.op('pe', lambda e, hh=hh, i=i, pT=pT: e.transpose(pT[:, hh * 128:(hh + 1) * 128], qb[i][:, hh * 128:(hh + 1) * 128], ident_b[:]),
                                     reads=[Bqb[i], Bconst], writes=[Bps[tb]])
                            S.op('act', lambda e, i=i, pT=pT: e.activation(out=qtb[i], in_=pT[:, 0:512].rearrange("p (a b) -> p a b", b=128), func=AF.Copy),
                                 reads=[Bps[tb]], writes=[Bqtb[i]])
                            h0 = (j % 2) * 4
                            if isq:
                                dst = qT_s[h0:h0 + 4, :, tt * 128:(tt + 1) * 128].rearrange("h p t -> p h t")
                                S.dma(lambda e, dst=dst, i=i: e.dma_start(out=dst, in_=qtb[i]), reads=[Bqtb[i]], writes=[Bq_s], store=True)
                            elif not PAIR:
                                dst = kT_s.rearrange("(h p) t -> h p t", p=128)[h0:h0 + 4, :, tt * 128:(tt + 1) * 128].rearrange("h p t -> p h t")
                                S.dma(lambda e, dst=dst, i=i: e.dma_start(out=dst, in_=qtb[i]), reads=[Bqtb[i]], writes=[Bk_s], store=True)
                            else:
                                for r in range(2):
                                    S.op('dve', lambda e, i=i, r=r: e.tensor_scalar(mk[i][:, r, :], qtb[i].rearrange("p a b -> p (a b)"), sel[:, 2 + r:3 + r], None, op0=ALU.mult),
                                         reads=[Bqtb[i], Bsel, Bmk[i]], writes=[Bmk[i]])
                                for r in range(2):
                                    dst = kv_snd_k.rearrange("(r h p) t -> r h p t", h=NH, p=128)[r, h0:h0 + 4, :, tt * 128:(tt + 1) * 128].rearrange("h p t -> p h t")
                                    S.dma(lambda e, dst=dst, i=i, r=r: e.dma_start(out=dst, in_=mk[i][:, r, :].rearrange("p (a b) -> p a b", b=128)), reads=[Bmk[i]], writes=[Bk_s], store=True)
                        pend.append(fin)
                    else:
                        i = ev[0] % 3
                        ev[0] += 1
                        S.op('act', lambda e, i=i, pb=pb: e.activation(out=vb_[i], in_=P(pb), func=AF.Copy), reads=[Bps[pb]], writes=[Bvb[i]])
                        c0 = (j - 4) * 512
                        if not PAIR:
                            S.dma(lambda e, i=i, tt=tt, c0=c0: e.dma_start(out=v_s[tt * 128:(tt + 1) * 128, c0:c0 + 512], in_=vb_[i]),
                                  reads=[Bvb[i]], writes=[Bv_s], store=True)
                        else:
                            i2 = ev[0] % 2
                            for r in range(2):
                                S.op('dve', lambda e, i=i, i2=i2, r=r: e.tensor_scalar(mk[i2][:, r, :], vb_[i], sel[:, 2 + r:3 + r], None, op0=ALU.mult),
                                     reads=[Bvb[i], Bsel, Bmk[i2]], writes=[Bmk[i2]])
                            for r in range(2):
                                S.dma(lambda e, i2=i2, tt=tt, c0=c0, r=r: e.dma_start(out=kv_snd_v[r * NT + tt * 128:r * NT + (tt + 1) * 128, c0:c0 + 512], in_=mk[i2][:, r, :]),
                                      reads=[Bmk[i2]], writes=[Bv_s], store=True)
            while pend:
                pend.pop(0)()
            w, Bw = WS.get(('dir', l))
            for tt in range(TT):
                pb = 4 + tt % 2
                for kc in range(KC):
                    S.op('pe', lambda e, pb=pb, tt=tt, kc=kc, w=w: e.matmul(psum[pb][:, 0:32], hT[:, kc, tt * 128:(tt + 1) * 128], w[:, kc, :],
                                                                           start=(kc == 0), stop=(kc == KC - 1)),
                         reads=[Bw, BhT[(tt * 128) // CH]], writes=[Bps[pb]])
                S.op('act', lambda e, pb=pb: e.activation(out=sm[:, 0:16], in_=psum[pb][:, 0:16], func=AF.Exp, scale=-1.0), reads=[Bps[pb]], writes=[Bsm])
                S.op('dve', lambda e: e.tensor_scalar(sm[:, 0:16], sm[:, 0:16], 1.0, None, op0=ALU.add), reads=[Bsm], writes=[Bsm])
                S.op('dve', lambda e, tt=tt: e.reciprocal(betaT[:, tt, :], sm[:, 0:16]), reads=[Bsm, Bbg], writes=[Bbg])
                S.op('dve', lambda e, pb=pb: e.tensor_tensor(sm[:, 16:32], psum[pb][:, 16:32], lyr[:, L_DTB:L_DTB + 16], op=ALU.add),
                     reads=[Bps[pb], Blyr, Bsm], writes=[Bsm])
                S.op('dve', lambda e: e.tensor_scalar(sm[:, 32:48], sm[:, 16:32], 0.0, None, op0=ALU.min), reads=[Bsm], writes=[Bsm])
                S.op('dve', lambda e: e.tensor_scalar(sm[:, 48:64], sm[:, 16:32], 0.0, None, op0=ALU.max), reads=[Bsm], writes=[Bsm])
                S.op('dve', lambda e: e.tensor_tensor(sm[:, 32:48], sm[:, 32:48], sm[:, 48:64], op=ALU.subtract), reads=[Bsm], writes=[Bsm])
                S.op('act', lambda e: e.activation(out=sm[:, 32:48], in_=sm[:, 32:48], func=AF.Exp), reads=[Bsm], writes=[Bsm])
                S.op('act', lambda e: e.activation(out=sm[:, 32:48], in_=sm[:, 32:48], func=AF.Ln, bias=1.0, scale=1.0), reads=[Bsm], writes=[Bsm])
                S.op('dve', lambda e: e.tensor_tensor(sm[:, 32:48], sm[:, 32:48], sm[:, 48:64], op=ALU.add), reads=[Bsm], writes=[Bsm])
                S.op('dve', lambda e, tt=tt: e.tensor_tensor(gT[:, tt, :], sm[:, 32:48], lyr[:, L_NEGA:L_NEGA + 16], op=ALU.mult),
                     reads=[Bsm, Blyr, Bbg], writes=[Bbg])
            Bg_s = S.buf('g_s'); Bz_s = S.buf('z_s'); Bgg_s = S.buf('gg_s'); Bhs = S.buf('halo_snd')
            fcnt = [0]
            for j in range(16):
                w, Bw = WS.get(('fm', l, j))

                def evac(m, n, pb, j=j):
                    i = fcnt[0] % 3
                    fcnt[0] += 1
                    ch = j * 4 + m
                    cs = slice(n * CH, (n + 1) * CH)
                    if ch < 24:
                        S.op('dve', lambda e: e.tensor_copy(fo[i], P(pb, CH)), reads=[Bps[pb]], writes=[Bfo[i]])
                        S.dma(lambda e: e.dma_start(out=g_s[ch, :, cs], in_=fo[i]), reads=[Bfo[i]], writes=[Bg_s], store=True)
                        if PAIR and n == NCH - 1:
                            ih = ch % 3
                            for r in range(2):
                                S.op('dve', lambda e, r=r: e.tensor_scalar(hm[ih][:, r, :], fo[i][:, CH - 2:CH], sel[:, 2 + r:3 + r], None, op0=ALU.mult),
                                     reads=[Bfo[i], Bsel, Bhm[ih]], writes=[Bhm[ih]])
                            for r in range(2):
                                S.dma(lambda e, r=r: e.dma_start(out=halo_snd[r * 128:(r + 1) * 128, 2 * ch:2 * ch + 2], in_=hm[ih][:, r, :]), reads=[Bhm[ih]], writes=[Bhs], store=True)
                    elif ch < 32:
                        S.op('act', lambda e: e.activation(out=fo[i], in_=P(pb, CH), func=AF.Silu), reads=[Bps[pb]], writes=[Bfo[i]])
                        S.dma(lambda e: e.dma_start(out=z_s[ch - 24, :, cs], in_=fo[i]), reads=[Bfo[i]], writes=[Bz_s], store=True)
                    else:
                        S.op('act', lambda e: e.activation(out=fo[i], in_=P(pb, CH), func=AF.Sigmoid), reads=[Bps[pb]], writes=[Bfo[i]])
                        S.dma(lambda e: e.dma_start(out=gg_s[ch - 32, :, cs], in_=fo[i]), reads=[Bfo[i]], writes=[Bgg_s], store=True)

                fm_block(w, Bw, KC, 512, lambda kc, n: hT[:, kc, n * CH:(n + 1) * CH], lambda kc, n: [BhT[n]], evac, [0, 1, 2, 3], cnt)
            return dict(q=Bq_s, k=Bk_s, v=Bv_s, g=Bg_s, z=Bz_s, gg=Bgg_s, hs=Bhs)


        ydT = hT[:, 0:8, :]
        ygT = hT[:, 8:16, :]
        Byd = S.bufs(NH, 'yd'); Byg = S.bufs(2, 'yg')

        def gdn_prep(l, sc):
            AR.reset()
            gin = [AR.get([NT + 4], BF16) for _ in range(2)]; Bgin = S.bufs(2, 'gin')
            acc = [AR.get([NT]) for _ in range(2)]; Bacc = S.bufs(2, 'acc')
            sqt = [AR.get([CH]) for _ in range(2)]; Bsqt = S.bufs(2, 'sqt')
            rs = [AR.get([CH]) for _ in range(2)]; Brs_ = S.bufs(2, 'rs')
            nb = [AR.get([NT], BF16) for _ in range(2)]; Bnb = S.bufs(2, 'nb')
            tk = [AR.get([TT, 128], BF16) for _ in range(2)]; Btk = S.bufs(2, 'tk')
            Bgq = S.buf('gq_s'); Bgk = S.buf('gk_s'); Bgkt = S.buf('gkt_s'); Bgvt = S.buf('gvt_s')
            it = 0
            if PAIR:
                Bhr = S.buf('halo_rcv')
                S.dma(lambda e: allreduce(e, halo_snd, halo_rcv), reads=[sc['hs']], writes=[Bhr], q='pool', inc=1)
                Bkr = S.buf('kv_rcv_k'); Bvr = S.buf('kv_rcv_v')
                S.dma(lambda e: allreduce(e, kv_snd_k, kv_rcv_k), reads=[sc['k']], writes=[Bkr], q='pool', inc=1)
                S.dma(lambda e: allreduce(e, kv_snd_v, kv_rcv_v), reads=[sc['v']], writes=[Bvr], q='pool', inc=1)
                sc['k'] = Bkr
                sc['v'] = Bvr
                hl = AR.get([2, 48], BF16); hlf = AR.get([48]); hl2 = AR.get([48], BF16); Bhl = S.buf('hl')
                S.dma(lambda e: e.dma_start(out=hl, in_=halo_rcv.rearrange("(r p) x -> p r x", p=128)), reads=[Bhr], writes=[Bhl])
                S.op('dve', lambda e: e.tensor_scalar(hlf, hl[:, 0, :], sel[:, 0:1], None, op0=ALU.mult), reads=[Bhl, Bsel], writes=[Bhl])
                S.op('dve', lambda e: e.scalar_tensor_tensor(out=hl2, in0=hl[:, 1, :], scalar=sel[:, 1:2], in1=hlf, op0=ALU.mult, op1=ALU.add), reads=[Bhl, Bsel], writes=[Bhl])

                def halo_fill(gt, Bg, ch):
                    S.op('dve', lambda e: e.tensor_copy(gt[:, NT + 2:NT + 3], hl2[:, 2 * ch + 1:2 * ch + 2]), reads=[Bhl, Bg], writes=[Bg])
                    S.op('dve', lambda e: e.tensor_copy(gt[:, NT + 3:NT + 4], hl2[:, 2 * ch:2 * ch + 1]), reads=[Bhl, Bg], writes=[Bg])
            items = [(h, qi) for h in range(NH) for qi in range(3)]

            def stA(idx):
                h, qi = items[idx]
                ch = qi * 8 + h
                i = idx % 2
                S.dma(lambda e, i=i, ch=ch: e.dma_start(out=gin[i][:, 2:NT + 2], in_=g_s[ch, :, :]), reads=[sc['g']], writes=[Bgin[i]])
                S.op('dve', lambda e, i=i: e.memset(gin[i][:, 0:2], 0.0), reads=[Bgin[i]], writes=[Bgin[i]])
                if PAIR:
                    halo_fill(gin[i], Bgin[i], ch)
                else:
                    S.op('dve', lambda e, i=i: e.memset(gin[i][:, NT + 2:NT + 4], 0.0), reads=[Bgin[i]], writes=[Bgin[i]])
                cw = L_CONV + ch * 5
                S.op('dve', lambda e, i=i, cw=cw: e.tensor_scalar(acc[i], gin[i][:, 0:NT], lyr[:, cw:cw + 1], None, op0=ALU.mult),
                     reads=[Bgin[i], Blyr], writes=[Bacc[i]])
                for tap in range(1, 5):
                    S.op('dve', lambda e, i=i, cw=cw, tap=tap: e.scalar_tensor_tensor(out=acc[i], in0=gin[i][:, tap:tap + NT], scalar=lyr[:, cw + tap:cw + tap + 1],
                                                                                    in1=acc[i], op0=ALU.mult, op1=ALU.add),
                         reads=[Bgin[i], Blyr, Bacc[i]], writes=[Bacc[i]])

            def stS(idx):
                i = idx % 2
                S.op('act', lambda e, i=i: e.activation(out=acc[i], in_=acc[i], func=AF.Silu), reads=[Bacc[i]], writes=[Bacc[i]])

            def stB(idx):
                h, qi = items[idx]
                ch = qi * 8 + h
                i = idx % 2
                if qi < 2:
                    post = (128.0 ** -0.5) if qi == 0 else 1.0
                    for n in range(NCH):
                        cs = slice(n * CH, (n + 1) * CH)
                        j = n % 2
                        S.op('act', lambda e, i=i, j=j, cs=cs: e.activation(out=sqt[j], in_=acc[i][:, cs], func=AF.Square), reads=[Bacc[i]], writes=[Bsqt[j]])
                        S.op('pe', lambda e, j=j: e.matmul(P(6, CH), ones_f[:], sqt[j], start=True, stop=True), reads=[Bsqt[j], Bconst], writes=[Bps[6]])
                        rsqrt_ops(rs[j], P(6, CH), 1.0, [Bps[6]], [Brs_[j]], post=post)
                        S.op('dve', lambda e, i=i, j=j, cs=cs: e.tensor_tensor(nb[i][:, cs], acc[i][:, cs], rs[j], op=ALU.mult),
                             reads=[Bacc[i], Brs_[j], Bnb[i]], writes=[Bnb[i]])
                else:
                    S.op('dve', lambda e, i=i: e.tensor_copy(nb[i], acc[i]), reads=[Bacc[i]], writes=[Bnb[i]])
                if qi == 0:
                    S.dma(lambda e, i=i, h=h: e.dma_start(out=gq_s[h, :, :], in_=nb[i]), reads=[Bnb[i]], writes=[Bgq], store=True)
                else:
                    if qi == 1:
                        S.dma(lambda e, i=i, h=h: e.dma_start(out=gk_s[h, :, :], in_=nb[i]), reads=[Bnb[i]], writes=[Bgk], store=True)
                    for t0 in range(0, TT, 4):
                        tb = 4 + ((t0 // 4) % 2)
                        pT = psum[tb][:].bitcast(BF16)
                        nt_ = min(4, TT - t0)
                        for t in range(nt_):
                            S.op('pe', lambda e, i=i, t=t, t0=t0, pT=pT: e.transpose(pT[:, t * 128:(t + 1) * 128], nb[i][:, (t0 + t) * 128:(t0 + t + 1) * 128], ident_b[:]),
                                 reads=[Bnb[i], Bconst], writes=[Bps[tb]])
                        S.op('act', lambda e, i=i, t0=t0, nt_=nt_, pT=pT: e.activation(out=tk[i][:, t0:t0 + nt_, :], in_=pT[:, 0:nt_ * 128].rearrange("p (a b) -> p a b", b=128), func=AF.Copy),
                             reads=[Bps[tb], Btk[i]], writes=[Btk[i]])
                    dst = (gkt_s if qi == 1 else gvt_s)[h].rearrange("(t p) d -> p t d", p=128)
                    S.dma(lambda e, i=i, dst=dst: e.dma_start(out=dst, in_=tk[i]), reads=[Btk[i]], writes=[Bgkt if qi == 1 else Bgvt], store=True)

            stA(0)
            stS(0)
            for idx in range(len(items)):
                if idx + 1 < len(items):
                    stA(idx + 1)
                stB(idx)
                if idx + 1 < len(items):
                    stS(idx + 1)
            return dict(gq=Bgq, gk=Bgk, gkt=Bgkt, gvt=Bgvt)

        Sst = sb("Sst", [128, NH, 128]); BSst = S.bufs(2, 'Sst')

        def gdn_scan(l, ph, sc, gp, Bo1, zero_state):
            AR.reset()
            TRI = m_Ui if ph == 0 else m_Li
            MST = m_Ls if ph == 0 else m_Us
            MIN = m_Ui if ph == 0 else m_Li
            tiles = list(range(TT)) if ph == 0 else list(range(TT - 1, -1, -1))
            dsl = slice(ph * 8, ph * 8 + 8)
            st8 = AR.get([64]); Bst8 = S.buf('st8')
            qT4 = [AR.get([4, 128], BF16)] * 2; kT4 = [AR.get([4, 128], BF16)] * 2
            kt4 = [AR.get([4, 128], BF16)] * 2; vt4 = [AR.get([4, 128], BF16)] * 2
            Bld = [S.buf('ld')] * 2
            Gb4 = AR.get([4, 128]); BGb = S.buf('Gb')
            ta = AR.get([4, 128]); tb_ = AR.get([4, 128]); Er = AR.get([4, 128]); Bta = S.buf('ta'); Btb = S.buf('tb'); BEr = S.buf('Er')
            L4 = AR.get([4, 128], CDT); At4 = AR.get([4, 128], BF16); BL4 = S.buf('L4'); BAt4 = S.buf('At4')
            Xb = [AR.get([4, 128], CDT) for _ in range(2)]; Yb = [AR.get([4, 128], CDT) for _ in range(2)]; Tb = [AR.get([4, 128], CDT) for _ in range(2)]
            Tfin = AR.get([4, 128], BF16); BTfin = S.buf('Tfin')
            identc = ident_f if CDT == F32 else ident_b
            BXb = S.bufs(2, 'Xb'); BYb = S.bufs(2, 'Yb'); BTb = S.bufs(2, 'Tb')
            kbg4 = AR.get([4, 128], BF16); ktl4 = AR.get([4, 128], BF16); vb4 = AR.get([4, 128], BF16)
            nw4 = AR.get([4, 128], BF16); qd4 = AR.get([4, 128], BF16); vn4 = AR.get([4, 128], BF16)
            Bkbg = S.buf('kbg'); Bktl = S.buf('ktl'); Bvb4 = S.buf('vb4'); Bnw = S.buf('nw'); Bqd = S.buf('qd'); Bvn = S.buf('vn')
            Sb4 = [AR.get([4, 128], BF16) for _ in range(2)]; BSb = S.bufs(2, 'Sb')
            ot = [AR.get([4, 128])] * 2; Bot = [S.buf('ot')] * 2
            o1t = [AR.get([4, 128])] * 2; Bo1t = [S.buf('o1t')] * 2
            zt = [AR.get([4, 128], BF16)] * 2; Bzt = [S.buf('zt')] * 2
            sq4 = AR.get([4, 128]); Bsq4 = S.buf('sq4'); rs4 = AR.get([4, 128]); Brs4 = S.buf('rs4')
            Er2 = AR.get([4, 128]); BEr2 = S.buf('Er2')
            for g in range(2):
                hs = slice(g * 4, g * 4 + 4)
                if zero_state:
                    S.op('dve', lambda e, hs=hs: e.memset(Sst[:, hs, :], 0.0), reads=[BSst[g]], writes=[BSst[g]])
                S.op('act', lambda e, g=g, hs=hs: e.activation(out=Sb4[g], in_=Sst[:, hs, :], func=AF.Copy), reads=[BSst[g]], writes=[BSb[g]])
            it = 0
            rot = [0]
            pend1 = []; pend2 = []

            def rbank():
                rot[0] += 1
                return 4 + (rot[0] % 3)

            for tt in tiles:
                ts_ = slice(tt * 128, (tt + 1) * 128)
                S.op('pe', lambda e, tt=tt: e.matmul(psum[0][:, 0:8], TRI[:], gT[:, tt, dsl], start=True, stop=True), reads=[Bconst, Bbg], writes=[Bps[0]])
                S.op('pe', lambda e, tt=tt: e.matmul(psum[0][:, 8:16], ones_f[:], gT[:, tt, dsl], start=True, stop=True), reads=[Bconst, Bbg], writes=[Bps[0]])
                S.op('dve', lambda e: e.tensor_copy(st8[:, 0:8], psum[0][:, 0:8]), reads=[Bps[0], Bst8], writes=[Bst8])
                S.op('act', lambda e: e.activation(out=st8[:, 8:16], in_=psum[0][:, 0:8], func=AF.Exp), reads=[Bps[0], Bst8], writes=[Bst8])
                S.op('act', lambda e: e.activation(out=st8[:, 32:40], in_=psum[0][:, 0:8], func=AF.Copy, scale=-1.0), reads=[Bps[0], Bst8], writes=[Bst8])
                S.op('act', lambda e: e.activation(out=st8[:, 16:24], in_=psum[0][:, 8:16], func=AF.Exp), reads=[Bps[0], Bst8], writes=[Bst8])
                S.op('dve', lambda e: e.tensor_tensor(st8[:, 24:32], psum[0][:, 8:16], st8[:, 0:8], op=ALU.subtract), reads=[Bps[0], Bst8], writes=[Bst8])
                S.op('act', lambda e: e.activation(out=st8[:, 24:32], in_=st8[:, 24:32], func=AF.Exp), reads=[Bst8], writes=[Bst8])
                S.op('dve', lambda e, tt=tt: e.tensor_tensor(st8[:, 8:16], st8[:, 8:16], betaT[:, tt, dsl], op=ALU.mult), reads=[Bst8, Bbg], writes=[Bst8])
                if dbg.get('_cut') == 1:
                    return
                for g in range(2):
                    hs = slice(g * 4, g * 4 + 4)
                    h0 = g * 4
                    i = it % 2
                    it += 1
                    S.dma(lambda e, i=i, h0=h0, ts_=ts_: e.dma_start(out=qT4[i], in_=gq_s[h0:h0 + 4, :, ts_].rearrange("h p t -> p h t")), reads=[gp['gq']], writes=[Bld[i]])
                    S.dma(lambda e, i=i, h0=h0, ts_=ts_: e.dma_start(out=kT4[i], in_=gk_s[h0:h0 + 4, :, ts_].rearrange("h p t -> p h t")), reads=[gp['gk']], writes=[Bld[i]], more=True)
                    S.dma(lambda e, i=i, h0=h0, ts_=ts_: e.dma_start(out=kt4[i], in_=gkt_s[h0:h0 + 4, ts_, :].rearrange("h p d -> p h d")), reads=[gp['gkt']], writes=[Bld[i]], more=True)
                    S.dma(lambda e, i=i, h0=h0, ts_=ts_: e.dma_start(out=vt4[i], in_=gvt_s[h0:h0 + 4, ts_, :].rearrange("h p d -> p h d")), reads=[gp['gvt']], writes=[Bld[i]], more=True)
                    g4 = gT[:, tt, ph * 8 + h0:ph * 8 + h0 + 4]
                    be4 = betaT[:, tt, ph * 8 + h0:ph * 8 + h0 + 4]
                    gc4 = st8[:, h0:h0 + 4]; skbg4 = st8[:, 8 + h0:12 + h0]; cd4 = st8[:, 16 + h0:20 + h0]; skt4 = st8[:, 24 + h0:28 + h0]
                    if dbg.get('_cut') == 11:
                        return
                    while pend1:
                        pend1.pop(0)()
                    for hh in range(4):
                        S.op('dve', lambda e, g4=g4, hh=hh: e.tensor_scalar(Gb4[:, hh, :], ones_f[:], g4[:, hh:hh + 1], None, op0=ALU.mult),
                             reads=[Bbg, BGb, Bconst], writes=[BGb])
                    for hh in range(4):
                        S.op('pe', lambda e, hh=hh: e.matmul(psum[1][:, hh * 128:(hh + 1) * 128], Gb4[:, hh, :], TRI[:], start=True, stop=True),
                             reads=[BGb, Bconst], writes=[Bps[1]])
                    if dbg.get('_cut') == 12:
                        if 'pb' in dbg_d:
                            S.op('act', lambda e: e.activation(out=Er, in_=psum[1][:].rearrange("p (a b) -> p a b", b=128), func=AF.Copy), reads=[Bps[1], BEr], writes=[BEr])
                            S.dma(lambda e: e.dma_start(out=dbg_d['pb'], in_=Er.rearrange("p a b -> p (a b)")), reads=[BEr], writes=[Bout], store=True)
                            S.dma(lambda e: e.dma_start(out=dbg_d['st8'], in_=st8), reads=[Bst8], writes=[Bout], store=True)
                            S.dma(lambda e: e.dma_start(out=dbg_d['gb'], in_=Gb4.rearrange("p a b -> p (a b)")), reads=[BGb], writes=[Bout], store=True)
                        return
                    pB = psum[1][:].rearrange("p (a b) -> p a b", b=128)
                    ngc4 = st8[:, 32 + h0:36 + h0]
                    for hh in range(4):
                        S.op('act', lambda e, hh=hh, ngc4=ngc4: e.activation(out=ta[:, hh, :], in_=psum[1][:, hh * 128:(hh + 1) * 128], func=AF.Relu,
                                                                           bias=ngc4[:, hh:hh + 1], scale=1.0), reads=[Bps[1], Bst8, Bta], writes=[Bta])
                        S.op('act', lambda e, hh=hh, gc4=gc4: e.activation(out=tb_[:, hh, :], in_=psum[1][:, hh * 128:(hh + 1) * 128], func=AF.Relu,
                                                                          bias=gc4[:, hh:hh + 1], scale=-1.0), reads=[Bps[1], Bst8, Btb], writes=[Btb])
                    S.op('act', lambda e, pB=pB: e.activation(out=Er, in_=pB, func=AF.Exp), reads=[Bps[1], BEr], writes=[BEr])
                    if dbg.get('_cut') == 13:
                        return
                    S.op('act', lambda e: e.activation(out=ta, in_=ta, func=AF.Exp, scale=-1.0), reads=[Bta], writes=[Bta])
                    S.op('act', lambda e: e.activation(out=tb_, in_=tb_, func=AF.Exp, scale=-1.0), reads=[Btb], writes=[Btb])
                    if dbg.get('_cut') == 15:
                        return
                    S.op('dve', lambda e: e.tensor_tensor(ta, ta, bc_mid(MST[:], 4), op=ALU.mult), reads=[Bta, Bconst], writes=[Bta])
                    S.op('dve', lambda e, be4=be4: e.tensor_tensor(ta, ta, bc_last(be4, 128), op=ALU.mult), reads=[Bta, Bbg], writes=[Bta])
                    S.op('dve', lambda e: e.tensor_tensor(tb_, tb_, bc_mid(MIN[:], 4), op=ALU.mult), reads=[Btb, Bconst], writes=[Btb])
                    if dbg.get('_cut') == 2:
                        return
                    for hh in range(4):
                        S.op('pe', lambda e, hh=hh, i=i: e.matmul(psum[2][:, hh * 128:(hh + 1) * 128], kT4[i][:, hh, :], kT4[i][:, hh, :], start=True, stop=True),
                             reads=[Bld[i]], writes=[Bps[2]])
                    for hh in range(4):
                        S.op('pe', lambda e, hh=hh, i=i: e.matmul(psum[3][:, hh * 128:(hh + 1) * 128], kT4[i][:, hh, :], qT4[i][:, hh, :], start=True, stop=True),
                             reads=[Bld[i]], writes=[Bps[3]])
                    while pend2:
                        pend2.pop(0)()
                    pK = psum[2][:].rearrange("p (a b) -> p a b", b=128)
                    pQ = psum[3][:].rearrange("p (a b) -> p a b", b=128)
                    S.op('act', lambda e, pK=pK: e.activation(out=Er2, in_=pK, func=AF.Copy), reads=[Bps[2], BEr2], writes=[BEr2])
                    S.op('dve', lambda e: e.tensor_tensor(L4, Er2, ta, op=ALU.mult), reads=[BEr2, Bta, BL4], writes=[BL4])
                    S.op('act', lambda e, pQ=pQ: e.activation(out=Er2, in_=pQ, func=AF.Copy), reads=[Bps[3], BEr2], writes=[BEr2])
                    S.op('dve', lambda e: e.tensor_tensor(At4, Er2, tb_, op=ALU.mult), reads=[BEr2, Btb, BAt4], writes=[BAt4])
                    if dbg.get('_cut') == 3:
                        return
                    bk = rbank()
                    pT = psum[bk][:].bitcast(CDT) if CDT != F32 else psum[bk][:]
                    for hh in range(4):
                        S.op('pe', lambda e, hh=hh, pT=pT: e.transpose(pT[:, hh * 128:(hh + 1) * 128], L4[:, hh, :], identc[:]), reads=[BL4, Bconst], writes=[Bps[bk]])
                    pT3 = pT[:, 0:512].rearrange("p (a b) -> p a b", b=128)
                    S.op('act', lambda e, pT3=pT3: e.activation(out=Yb[0], in_=pT3, func=AF.Copy), reads=[Bps[bk], BYb[0]], writes=[BYb[0]])
                    S.op('dve', lambda e: e.tensor_tensor(Tb[0], bc_mid(identc[:], 4), Yb[0], op=ALU.subtract), reads=[BYb[0], Bconst, BTb[0]], writes=[BTb[0]])
                    if dbg.get('_cut') == 4:
                        return
                    Xc, BXc = L4, BL4
                    Yc, BYc = Yb[0], BYb[0]
                    Tc, BTc = Tb[0], BTb[0]
                    for lev in range(1, 7):
                        Xn, BXn = Xb[lev % 2], BXb[lev % 2]
                        Yn, BYn = Yb[lev % 2], BYb[lev % 2]
                        Tn, BTn = Tb[lev % 2], BTb[lev % 2]
                        bx = rbank()
                        for hh in range(4):
                            S.op('pe', lambda e, hh=hh, bx=bx, Xc=Xc, Yc=Yc: e.matmul(psum[bx][:, hh * 128:(hh + 1) * 128], Yc[:, hh, :], Xc[:, hh, :], start=True, stop=True),
                                 reads=[BXc, BYc], writes=[Bps[bx]])
                        if lev < 6:
                            by = rbank()
                            for hh in range(4):
                                S.op('pe', lambda e, hh=hh, by=by, Xc=Xc, Yc=Yc: e.matmul(psum[by][:, hh * 128:(hh + 1) * 128], Xc[:, hh, :], Yc[:, hh, :], start=True, stop=True),
                                     reads=[BXc, BYc], writes=[Bps[by]])
                        S.op('act', lambda e, bx=bx, Xn=Xn: e.activation(out=Xn, in_=psum[bx][:].rearrange("p (a b) -> p a b", b=128), func=AF.Copy),
                             reads=[Bps[bx], BXn], writes=[BXn])
                        if lev < 6:
                            S.op('act', lambda e, by=by, Yn=Yn: e.activation(out=Yn, in_=psum[by][:].rearrange("p (a b) -> p a b", b=128), func=AF.Copy), reads=[Bps[by], BYn], writes=[BYn])
                        bt = rbank()
                        for hh in range(4):
                            S.op('pe', lambda e, hh=hh, bt=bt, Xn=Xn, Tc=Tc: e.matmul(psum[bt][:, hh * 128:(hh + 1) * 128], Xn[:, hh, :], Tc[:, hh, :], start=True, stop=True),
                                 reads=[BXn, BTc], writes=[Bps[bt]])
                        S.op('act', lambda e, bt=bt: e.activation(out=Er2, in_=psum[bt][:].rearrange("p (a b) -> p a b", b=128), func=AF.Copy), reads=[Bps[bt], BEr2], writes=[BEr2])
                        S.op('dve', lambda e, Tn=Tn, Tc=Tc: e.tensor_tensor(Tn, Er2, Tc, op=ALU.add), reads=[BEr2, BTc, BTn], writes=[BTn])
                        Xc, BXc, Yc, BYc, Tc, BTc = Xn, BXn, Yn, BYn, Tn, BTn
                    if dbg.get('_cut') == 5:
                        return
                    S.op('act', lambda e, Tc=Tc: e.activation(out=Tfin, in_=Tc, func=AF.Copy), reads=[BTc, BTfin], writes=[BTfin])
                    Tc, BTc = Tfin, BTfin
                    S.op('dve', lambda e, i=i, skbg4=skbg4: e.tensor_tensor(kbg4, kt4[i], bc_last(skbg4, 128), op=ALU.mult), reads=[Bld[i], Bst8, Bkbg], writes=[Bkbg])
                    S.op('dve', lambda e, i=i, skt4=skt4: e.tensor_tensor(ktl4, kt4[i], bc_last(skt4, 128), op=ALU.mult), reads=[Bld[i], Bst8, Bktl], writes=[Bktl])
                    S.op('dve', lambda e, i=i, be4=be4: e.tensor_tensor(vb4, vt4[i], bc_last(be4, 128), op=ALU.mult), reads=[Bld[i], Bbg, Bvb4], writes=[Bvb4])
                    S.op('dve', lambda e, i=i: e.tensor_tensor(qd4, qT4[i], Er, op=ALU.mult), reads=[Bld[i], BEr, Bqd], writes=[Bqd])
                    bw = rbank()
                    for hh in range(4):
                        S.op('pe', lambda e, hh=hh, bw=bw, Tc=Tc: e.matmul(psum[bw][:, hh * 128:(hh + 1) * 128], kbg4[:, hh, :], Tc[:, hh, :], start=True, stop=True),
                             reads=[Bkbg, BTc], writes=[Bps[bw]])
                    S.op('act', lambda e, bw=bw: e.activation(out=nw4, in_=psum[bw][:].rearrange("p (a b) -> p a b", b=128), func=AF.Copy, scale=-1.0),
                         reads=[Bps[bw], Bnw], writes=[Bnw])
                    if dbg.get('_cut') == 6:
                        return
                    for hh in range(4):
                        S.op('pe', lambda e, hh=hh, Tc=Tc: e.matmul(psum[1][:, hh * 128:(hh + 1) * 128], Tc[:, hh, :], vb4[:, hh, :], start=True, stop=False),
                             reads=[BTc, Bvb4], writes=[Bps[1]])
                        S.op('pe', lambda e, hh=hh, g=g: e.matmul(psum[1][:, hh * 128:(hh + 1) * 128], nw4[:, hh, :], Sb4[g][:, hh, :], start=False, stop=True),
                             reads=[Bnw, BSb[g]], writes=[Bps[1]])
                    S.op('act', lambda e: e.activation(out=vn4, in_=psum[1][:].rearrange("p (a b) -> p a b", b=128), func=AF.Copy), reads=[Bps[1], Bvn], writes=[Bvn])
                    for hh in range(4):
                        S.op('pe', lambda e, hh=hh, g=g: e.matmul(psum[2][:, hh * 128:(hh + 1) * 128], Sb4[g][:, hh, :], qd4[:, hh, :], start=True, stop=False),
                             reads=[BSb[g], Bqd], writes=[Bps[2]])
                        S.op('pe', lambda e, hh=hh: e.matmul(psum[2][:, hh * 128:(hh + 1) * 128], vn4[:, hh, :], At4[:, hh, :], start=False, stop=True),
                             reads=[Bvn, BAt4], writes=[Bps[2]])
                    for hh in range(4):
                        S.op('pe', lambda e, hh=hh: e.matmul(psum[3][:, hh * 128:(hh + 1) * 128], ktl4[:, hh, :], vn4[:, hh, :], start=True, stop=True),
                             reads=[Bktl, Bvn], writes=[Bps[3]])
                    if dbg.get('_cut') == 7:
                        return
                    S.op('dve', lambda e, hs=hs, cd4=cd4: e.tensor_tensor(Sst[:, hs, :], Sst[:, hs, :], bc_last(cd4, 128), op=ALU.mult), reads=[BSst[g], Bst8], writes=[BSst[g]])
                    S.op('act', lambda e: e.activation(out=Er2, in_=psum[3][:].rearrange("p (a b) -> p a b", b=128), func=AF.Copy), reads=[Bps[3], BEr2], writes=[BEr2])
                    S.op('dve', lambda e, hs=hs: e.tensor_tensor(Sst[:, hs, :], Sst[:, hs, :], Er2, op=ALU.add), reads=[BSst[g], BEr2], writes=[BSst[g]])
                    S.op('act', lambda e, g=g, hs=hs: e.activation(out=Sb4[g], in_=Sst[:, hs, :], func=AF.Copy), reads=[BSst[g], BSb[g]], writes=[BSb[g]])
                    if dbg.get('_cut') == 8:
                        return
                    pO = psum[2][:].rearrange("p (a b) -> p a b", b=128)
                    if ph == 0:
                        S.op('act', lambda e, i=i, pO=pO: e.activation(out=ot[i], in_=pO, func=AF.Copy), reads=[Bps[2], Bot[i]], writes=[Bot[i]])
                        S.dma(lambda e, i=i, h0=h0, ts_=ts_: e.dma_start(out=o1_s[h0:h0 + 4, :, ts_].rearrange("h p t -> p h t"), in_=ot[i]),
                              reads=[Bot[i]], writes=[Bo1], store=True)
                    else:
                        S.dma(lambda e, i=i, h0=h0, ts_=ts_: e.dma_start(out=o1t[i], in_=o1_s[h0:h0 + 4, :, ts_].rearrange("h p t -> p h t")), reads=[Bo1], writes=[Bo1t[i]])
                        S.dma(lambda e, i=i, h0=h0, ts_=ts_: e.dma_start(out=zt[i], in_=z_s[h0:h0 + 4, :, ts_].rearrange("h p t -> p h t")), reads=[sc['z']], writes=[Bzt[i]])
                        S.op('act', lambda e, i=i, pO=pO: e.activation(out=ot[i], in_=pO, func=AF.Copy), reads=[Bps[2], Bot[i]], writes=[Bot[i]])
                        S.op('dve', lambda e, i=i: e.tensor_tensor(ot[i], ot[i], o1t[i], op=ALU.add), reads=[Bo1t[i], Bot[i]], writes=[Bot[i]])
                        def tl1(i=i):
                            S.op('act', lambda e, i=i: e.activation(out=sq4, in_=ot[i], func=AF.Square), reads=[Bot[i], Bsq4], writes=[Bsq4])

                        def tl2(i=i, h0=h0, ts_=ts_, g=g):
                            S.op('pe', lambda e: e.matmul(psum[7][:], ones_f[:], sq4.rearrange("p a b -> p (a b)"), start=True, stop=True), reads=[Bsq4, Bconst], writes=[Bps[7]])
                            rsqrt_ops(rs4.rearrange("p a b -> p (a b)"), psum[7][:], 1.0 / 128, [Bps[7], Brs4], [Brs4])
                            S.op('dve', lambda e, i=i: e.scalar_tensor_tensor(out=ot[i], in0=ot[i], scalar=lyr[:, L_GNW:L_GNW + 1], in1=rs4, op0=ALU.mult, op1=ALU.mult),
                                 reads=[Bot[i], Blyr, Brs4], writes=[Bot[i]])
                            S.op('dve', lambda e, i=i, h0=h0, ts_=ts_: e.tensor_tensor(ygT[:, h0:h0 + 4, ts_], ot[i], zt[i], op=ALU.mult),
                                 reads=[Bot[i], Bzt[i], Byg[g]], writes=[Byg[g]])
                        pend1.append(tl1)
                        pend2.append(tl2)
            while pend1:
                pend1.pop(0)()
            while pend2:
                pend2.pop(0)()


        def exchange(l, sc):
            AR.reset()
            Bss = S.buf('st_snd'); Bsr = S.buf('st_rcv')
            for g in range(2):
                S.dma(lambda e, g=g: e.dma_start(out=st_snd[:, g * 512:(g + 1) * 512], in_=Sst[:, g * 4:g * 4 + 4, :].rearrange("p a b -> p (a b)")),
                      reads=[BSst[g]], writes=[Bss], store=True)
            S.dma(lambda e: allreduce(e, st_snd, st_rcv), reads=[Bss], writes=[Bsr], q='pool', inc=1)
            sr = AR.get([1024]); Bsrt = S.buf('sr')
            S.dma(lambda e: e.dma_start(out=sr, in_=st_rcv[:, :]), reads=[Bsr], writes=[Bsrt])
            for g in range(2):
                gs = slice(g * 512, (g + 1) * 512)
                Sg = Sst[:, g * 4:g * 4 + 4, :].rearrange("p a b -> p (a b)")
                S.op('dve', lambda e, gs=gs, Sg=Sg: e.tensor_tensor(Sg, sr[:, gs], Sg, op=ALU.subtract), reads=[Bsrt, BSst[g]], writes=[BSst[g]])

        def attention(l, sc, nparts):
            AR.reset()
            q_sb = [AR.get([NT], BF16) for _ in range(2)]; k_sb = [AR.get([NK], BF16) for _ in range(2)]
            v_sb = [AR.get([KT, 128], BF16) for _ in range(2)]; Bqkv = S.bufs(2, 'qkv')
            pTt = [AR.get([CH], BF16) for _ in range(3)]; BpT = S.bufs(3, 'pT')
            om = [AR.get([CH]) for _ in range(2)]; Bom = S.bufs(2, 'om')
            rd = AR.get([CH]); Brd = S.buf('rd')
            oc = AR.get([CH]); Boc = S.buf('oc'); sqa = AR.get([CH]); Bsqa = S.buf('sqa'); rsa = AR.get([CH]); Brsa = S.buf('rsa')
            post = 1.0 - lam_init[l]
            pc = 0
            SK = 2
            SB = (0, 1, 7)
            steps = [(h, n, m, kt) for h in range(NH) for n in range(NCH) for m in range(2) for kt in range(KT)]
            deferred = []

            def loads(h):
                i = h % 2
                S.dma(lambda e, i=i, h=h: e.dma_start(out=q_sb[i], in_=qT_s[h, :, :]), reads=[sc['q']], writes=[Bqkv[i]])
                ksrc = (kv_rcv_k if PAIR else kT_s).rearrange("(r h p) t -> r h p t", h=NH, p=128)
                vsrc = (kv_rcv_v if PAIR else v_s).rearrange("(r t) v -> r t v", t=NT)
                for part in range(nparts):
                    S.dma(lambda e, i=i, h=h, part=part, ksrc=ksrc: e.dma_start(out=k_sb[i][:, part * NT:(part + 1) * NT], in_=ksrc[part, h, :, :]),
                          reads=[sc['k']], writes=[Bqkv[i]], more=True)
                    S.dma(lambda e, i=i, h=h, part=part, vsrc=vsrc: e.dma_start(out=v_sb[i][:, part * TT:(part + 1) * TT, :],
                                                                     in_=vsrc[part, :, h * 128:(h + 1) * 128].rearrange("(t p) v -> p t v", p=128)),
                          reads=[sc['v']], writes=[Bqkv[i]], more=True)

            def score(s):
                h, n, m, kt = steps[s]
                if n == 0 and m == 0 and kt == 0:
                    loads(h)
                i = h % 2
                sbk = SB[s % 3]
                ms = slice(m * 64, (m + 1) * 64)
                cs = slice(n * CH, (n + 1) * CH)
                S.op('pe', lambda e, i=i, ms=ms, kt=kt, sbk=sbk, cs=cs: e.matmul(P(sbk, CH), k_sb[i][ms, kt * 128:(kt + 1) * 128], q_sb[i][ms, cs], start=True, stop=True),
                     reads=[Bqkv[i]], writes=[Bps[sbk]])

            def tail1(m):
                S.op('act', lambda e, m=m: e.activation(out=rd, in_=P(4 + m, CH), func=AF.Copy), reads=[Bps[4 + m], Brd], writes=[Brd])
                S.op('dve', lambda e: e.reciprocal(rd, rd), reads=[Brd], writes=[Brd])
                S.op('act', lambda e, m=m: e.activation(out=om[m], in_=P(2 + m, CH), func=AF.Copy), reads=[Bps[2 + m], Bom[m]], writes=[Bom[m]])
                S.op('dve', lambda e, m=m: e.tensor_tensor(om[m], om[m], rd, op=ALU.mult), reads=[Brd, Bom[m]], writes=[Bom[m]])

            def fin1():
                S.op('dve', lambda e: e.scalar_tensor_tensor(out=oc, in0=om[1], scalar=lyr[:, L_LAMC:L_LAMC + 1], in1=om[0], op0=ALU.mult, op1=ALU.add),
                     reads=[Bom[0], Bom[1], Blyr, Boc], writes=[Boc])
                S.op('act', lambda e: e.activation(out=sqa, in_=oc, func=AF.Square), reads=[Boc, Bsqa], writes=[Bsqa])

            def fin2():
                S.op('pe', lambda e: e.matmul(P(6, CH), ones_f[:], sqa, start=True, stop=True), reads=[Bsqa, Bconst], writes=[Bps[6]])

            def fin3(h, cs):
                rsqrt_ops(rsa, P(6, CH), 1.0 / 128, [Bps[6], Brsa], [Brsa], post=post)
                S.op('dve', lambda e, h=h, cs=cs: e.scalar_tensor_tensor(out=ydT[:, h, cs], in0=oc, scalar=lyr[:, L_SUB:L_SUB + 1], in1=rsa, op0=ALU.mult, op1=ALU.mult),
                     reads=[Boc, Blyr, Brsa, Byd[h]], writes=[Byd[h]])

            for s in range(min(SK, len(steps))):
                score(s)
            for s in range(len(steps)):
                h, n, m, kt = steps[s]
                i = h % 2
                sbk = SB[s % 3]
                r = s % 3
                if s + SK < len(steps):
                    score(s + SK)
                S.op('act', lambda e, r=r, sbk=sbk: e.activation(out=pTt[r], in_=P(sbk, CH), func=AF.Exp), reads=[Bps[sbk]], writes=[BpT[r]])
                S.op('pe', lambda e, i=i, r=r, kt=kt, m=m: e.matmul(P(2 + m, CH), v_sb[i][:, kt, :], pTt[r], start=(kt == 0), stop=(kt == KT - 1)),
                     reads=[Bqkv[i], BpT[r]], writes=[Bps[2 + m]])
                S.op('pe', lambda e, r=r, kt=kt, m=m: e.matmul(P(4 + m, CH), ones_b[:], pTt[r], start=(kt == 0), stop=(kt == KT - 1)),
                     reads=[Bconst, BpT[r]], writes=[Bps[4 + m]])
                while deferred and deferred[0][0] <= s:
                    deferred.pop(0)[1]()
                if kt == KT - 1:
                    deferred.append((s + 1, lambda m=m: tail1(m)))
                    if m == 1:
                        cs = slice(n * CH, (n + 1) * CH)
                        deferred.append((s + 3, fin1))
                        deferred.append((s + 5, fin2))
                        deferred.append((s + 7, lambda h=h, cs=cs: fin3(h, cs)))
            while deferred:
                deferred.pop(0)[1]()

        def out_proj(l, sc):
            AR.reset()
            mg = AR.get([KC, NT], BF16); Bmg = [[S.buf() for _ in range(NCH)] for _ in range(KC)]
            gd = [AR.get([CH], BF16) for _ in range(2)]; gg = [AR.get([CH], BF16) for _ in range(2)]; Bgt = S.bufs(2, 'gt')
            t1 = [AR.get([CH]) for _ in range(2)]; Bt1 = S.bufs(2, 't1')
            t2 = [AR.get([CH]) for _ in range(2)]; Bt2 = S.bufs(2, 't2')
            it = 0
            for j in range(4):
                for which in range(2):
                    w_, Bw_ = WS.get((('bd', 'bg')[which], l, j))
                    for m in range(4):
                        mc = j * 4 + m
                        for n in range(NCH):
                            cs = slice(n * CH, (n + 1) * CH)
                            i = it % 2
                            it += 1
                            p1 = it % 4
                            src = ydT if which == 0 else ygT
                            for kc in range(8):
                                rdb = Byd[kc] if which == 0 else Byg[kc // 4]
                                S.op('pe', lambda e, kc=kc, m=m, cs=cs, p1=p1, w_=w_, src=src: e.matmul(P(p1, CH), w_[:, kc, m * 128:(m + 1) * 128], src[:, kc, cs], start=(kc == 0), stop=(kc == 7)),
                                     reads=[Bw_, rdb], writes=[Bps[p1]])
                            S.dma(lambda e, i=i, mc=mc, cs=cs, which=which: e.dma_start(out=gd[i], in_=gg_s[16 * which + mc, :, cs]), reads=[sc['gg']], writes=[Bgt[i]])
                            S.op('act', lambda e, i=i, p1=p1: e.activation(out=t1[i], in_=P(p1, CH), func=AF.Copy), reads=[Bps[p1], Bt1[i]], writes=[Bt1[i]])
                            if which == 0:
                                S.op('dve', lambda e, i=i, mc=mc, cs=cs: e.tensor_tensor(mg[:, mc, cs], t1[i], gd[i], op=ALU.mult), reads=[Bgt[i], Bt1[i], Bmg[mc][n]], writes=[Bmg[mc][n]])
                            else:
                                S.op('dve', lambda e, i=i: e.tensor_tensor(t1[i], t1[i], gd[i], op=ALU.mult), reads=[Bgt[i], Bt1[i]], writes=[Bt1[i]])
                                S.op('dve', lambda e, i=i, mc=mc, cs=cs: e.tensor_tensor(mg[:, mc, cs], mg[:, mc, cs], t1[i], op=ALU.add), reads=[Bt1[i], Bmg[mc][n]], writes=[Bmg[mc][n]])
            it = 0
            for j in range(4):
                w, Bw = WS.get(('out', l, j))
                for m in range(4):
                    mc = j * 4 + m
                    for n in range(NCH):
                        cs = slice(n * CH, (n + 1) * CH)
                        pb = 4 + it % 2
                        it += 1
                        for kc in range(KC):
                            S.op('pe', lambda e, kc=kc, m=m, cs=cs, pb=pb, w=w: e.matmul(P(pb, CH), w[:, kc, m * 128:(m + 1) * 128], mg[:, kc, cs], start=(kc == 0), stop=(kc == KC - 1)),
                                 reads=[Bw, Bmg[kc][n]], writes=[Bps[pb]])
                        i2 = it % 2
                        S.op('act', lambda e, mc=mc, pb=pb, i2=i2: e.activation(out=t1[i2], in_=P(pb, CH), func=AF.Copy, scale=mod[:, 32 + mc:33 + mc]), reads=[Bps[pb], Bmod, Bt1[i2]], writes=[Bt1[i2]])
                        S.op('dve', lambda e, mc=mc, cs=cs, i2=i2: e.tensor_tensor(xT[:, mc, cs], xT[:, mc, cs], t1[i2], op=ALU.add), reads=[Bt1[i2], BxT[mc][n]], writes=[BxT[mc][n]])

        def ffn(l):
            modnorm(1)
            AR.reset()
            aT = AR.get([FC, CH], BF16); BaT = S.bufs(FC, 'aT')
            sg_off = AR.off
            sg = [AR.get([CH], BF16) for _ in range(2)]; Bsg = S.bufs(2, 'sg')
            uc = [AR.get([CH], BF16) for _ in range(2)]; Buc = S.bufs(2, 'uc')
            xdf = arena[:, sg_off // 4:sg_off // 4 + CH]
            it = 0
            for n in range(NCH):
                cs = slice(n * CH, (n + 1) * CH)
                for j in range(11):
                    for which in range(2):
                        w_, Bw_ = WS.get((('upg', 'upu')[which], l, n, j))
                        for m in range(4):
                            jc = j * 4 + m
                            i = it % 2
                            it += 1
                            p1 = it % 4
                            for kc in range(KC):
                                S.op('pe', lambda e, kc=kc, m=m, p1=p1, w_=w_: e.matmul(P(p1, CH), w_[:, kc, m * 128:(m + 1) * 128], hT[:, kc, cs], start=(kc == 0), stop=(kc == KC - 1)),
                                     reads=[Bw_, BhT[n]], writes=[Bps[p1]])
                            if which == 0:
                                S.op('act', lambda e, p1=p1, jc=jc: e.activation(out=aT[:, jc, :], in_=P(p1, CH), func=AF.Silu), reads=[Bps[p1], BaT[jc]], writes=[BaT[jc]])
                            else:
                                S.op('act', lambda e, i=i, p1=p1: e.activation(out=uc[i], in_=P(p1, CH), func=AF.Copy), reads=[Bps[p1], Buc[i]], writes=[Buc[i]])
                                S.op('dve', lambda e, i=i, jc=jc: e.tensor_tensor(aT[:, jc, :], aT[:, jc, :], uc[i], op=ALU.mult), reads=[Buc[i], BaT[jc]], writes=[BaT[jc]])
                for m in range(KC):
                    w, Bw = WS.get(('dn', l, n, m))
                    pb = 4 + m % 2
                    for kc in range(FC):
                        S.op('pe', lambda e, kc=kc, pb=pb, w=w: e.matmul(P(pb, CH), w[:, kc, :], aT[:, kc, :], start=(kc == 0), stop=(kc == FC - 1)),
                             reads=[Bw, BaT[kc]], writes=[Bps[pb]])
                    S.op('act', lambda e, m=m, pb=pb: e.activation(out=xdf, in_=P(pb, CH), func=AF.Copy, scale=mod[:, 80 + m:81 + m]), reads=[Bps[pb], Bmod], writes=[Bsg[0], Bsg[1]])
                    S.op('dve', lambda e, m=m: e.tensor_tensor(xT[:, m, cs], xT[:, m, cs], xdf, op=ALU.add), reads=[Bsg[0], Bsg[1], BxT[m][n]], writes=[BxT[m][n]])


        setup()
        stage = dbg.get('_stage', None)
        for l in range(DEPTH):
            load_layer_params(l)
            ada_mod(l)
            modnorm(0)
            sc = in_proj(l)
            if stage == 'inproj':
                break
            gp = gdn_prep(l, sc)
            if stage == 'prep':
                break
            Bo1 = S.buf('o1_s')
            gdn_scan(l, 0, sc, gp, Bo1, True)
            if stage == 'scan0':
                break
            if PAIR:
                exchange(l, sc)
            attention(l, sc, 2 if PAIR else 1)
            if stage == 'attn':
                break
            gdn_scan(l, 1, sc, gp, Bo1, not PAIR)
            if stage == 'mixer':
                break
            out_proj(l, sc)
            if stage == 'outproj':
                break
            ffn(l)
        def dump_sb(name, src, rd):
            S.dma(lambda e: e.dma_start(out=dbg_d[name], in_=src), reads=rd, writes=[Bout], store=True)
        S.barrier()
        if 'mod' in dbg_d:
            dump_sb('mod', mod[:], [Bmod])
        if 'cos' in dbg_d:
            dump_sb('cos', cosT[:], [Brope]); dump_sb('sin', sinT[:], [Brope])
        if 'beta' in dbg_d:
            dump_sb('beta', betaT[:], [Bbg]); dump_sb('g', gT[:], [Bbg])
        if 'hT' in dbg_d:
            AR.reset()
            t = AR.get([NT]); Bt_ = S.buf()
            for kc in range(KC):
                S.op('dve', lambda e, kc=kc: e.tensor_copy(t, hT[:, kc, :]), reads=BhT, writes=[Bt_])
                S.dma(lambda e, kc=kc: e.dma_start(out=dbg_d['hT'][kc * 128:(kc + 1) * 128, :], in_=t), reads=[Bt_], writes=[Bout], store=True)
        for nm, view in (('yd', ydT), ('yg', ygT)):
            if nm in dbg_d:
                AR.reset()
                t = AR.get([NT]); Bt_ = S.buf()
                for kc in range(8):
                    S.op('dve', lambda e, kc=kc, view=view: e.tensor_copy(t, view[:, kc, :]), reads=Byd + Byg, writes=[Bt_])
                    S.dma(lambda e, kc=kc, nm=nm: e.dma_start(out=dbg_d[nm][kc * 128:(kc + 1) * 128, :], in_=t), reads=[Bt_], writes=[Bout], store=True)
        ov = out_d.rearrange("(k p) n -> p k n", p=128)
        for kc in range(KC):
            S.dma(lambda e, kc=kc: e.dma_start(out=ov[:, kc, :], in_=xT[:, kc, :]), reads=BxT[kc], writes=[Bout], store=True)
        S.final_wait('sp', [Bout])
        S.emit()
    return nc


def prep_inputs(inp, NT, PAIR, DEPTH, n_cores=8):
    f32 = np.float32
    x = np.asarray(inp['x'], f32)
    B, SEQ, _ = x.shape
    c = np.asarray(inp['c'], f32)
    pos = np.asarray(inp['positions']).astype(np.int32)
    w_in = np.asarray(inp['w_in'], f32)
    conv_w = np.asarray(inp['gdn_conv_w'], f32)
    a_log = np.asarray(inp['gdn_a_log'], f32)
    dt_bias = np.asarray(inp['gdn_dt_bias'], f32)
    invf = (np.float32(10000.0) ** (-np.arange(32, dtype=np.float32) * np.float32(2.0) / np.float32(64))).astype(f32)
    shared = {
        'invf': np.ascontiguousarray(np.broadcast_to(invf[None, :], (128, 32))),
        'ada_w': np.asarray(inp['ada_w'], f32),
        'ada_bT': np.ascontiguousarray(np.asarray(inp['ada_b'], f32).reshape(DEPTH, 96, 128).transpose(0, 2, 1)),
        'nmwT': np.ascontiguousarray(np.asarray(inp['norm_mix_w'], f32).reshape(DEPTH, 16, 128).transpose(0, 2, 1)),
        'nfwT': np.ascontiguousarray(np.asarray(inp['norm_ffn_w'], f32).reshape(DEPTH, 16, 128).transpose(0, 2, 1)),
        'w_in': w_in,
        'qn_w': np.asarray(inp['diff_qn_w'], f32),
        'kn_w': np.asarray(inp['diff_kn_w'], f32),
        'lamv': np.ascontiguousarray(np.asarray(inp['diff_lambda'], f32).reshape(DEPTH, 256)),
        'sublnT': np.ascontiguousarray(np.asarray(inp['diff_subln_w'], f32).reshape(DEPTH, 128, 1)),
        'gnwT': np.ascontiguousarray(np.asarray(inp['gdn_norm_w'], f32).reshape(DEPTH, 128, 1)),
        'w_bd': np.asarray(inp['w_branch_diff'], f32),
        'w_bg': np.asarray(inp['w_branch_gdn'], f32),
        'w_out': np.asarray(inp['w_out'], f32),
        'w_up': np.asarray(inp['ffn_w_up'], f32),
        'w_down': np.asarray(inp['ffn_w_down'], f32),
    }
    role = {}
    for r in (0, 1):
        dmap = [0, 1] if r == 0 else [1, 0]
        cols = [7168 + d * 8 + h for d in dmap for h in range(8)] + [7184 + d * 8 + h for d in dmap for h in range(8)]
        cw = conv_w if r == 0 else conv_w[:, ::-1, :]
        role[r] = {
            'w_dir': np.ascontiguousarray(w_in[:, :, cols]),
            'a_log': np.ascontiguousarray(a_log[:, dmap, :].reshape(DEPTH, 16)),
            'dt_bias': np.ascontiguousarray(dt_bias[:, dmap, :].reshape(DEPTH, 16)),
            'convT': np.ascontiguousarray(cw.reshape(DEPTH, 5, 24, 128).transpose(0, 3, 2, 1)),
            'sel': np.ascontiguousarray(np.broadcast_to(np.array([[0.0, 1.0, 1.0, 0.0] if r == 0 else [1.0, 0.0, 0.0, 1.0]], f32), (128, 4))),
        }
    maps = []
    meta = []
    for core in range(n_cores):
        if PAIR:
            b, r = core // 2, core % 2
            tok = np.arange(NT) if r == 0 else (SEQ - 1 - np.arange(NT))
        else:
            b, r = core % B, 0
            tok = np.arange(NT)
        m = dict(shared)
        m.update(role[r])
        m['xT'] = np.ascontiguousarray(x[b, tok, :].T)
        m['posT'] = np.ascontiguousarray(pos[b, tok].reshape(NT // 128, 128).T)
        m['cT'] = np.ascontiguousarray(c[b].reshape(16, 128).T)
        maps.append(m)
        meta.append((b, tok))
    return maps, meta


_PROG_CACHE = {}


def kernel(**inputs):
    x = np.asarray(inputs['x'])
    B, SEQ, _ = x.shape
    DEPTH = int(np.asarray(inputs['ada_w']).shape[0])
    NT = SEQ // 2
    key = (NT, DEPTH)
    if key not in _PROG_CACHE:
        _PROG_CACHE[key] = build_program(NT, DEPTH, True, {})
    nc = _PROG_CACHE[key]
    maps, meta = prep_inputs(inputs, NT, True, DEPTH, n_cores=8)
    res = run_bass_kernel_spmd(nc, maps, core_ids=list(range(8)))
    out = np.empty((B, SEQ, D), np.float32)
    for core in range(8):
        b, tok = meta[core]
        out[b, tok, :] = np.asarray(res.results[core]['outT'], np.float32).T
    return out
```
